# Optimizing a Trainium2 kernel written in Bass

```python
import math
import functools
import jax, jax.numpy as jnp
from jax import lax
import numpy as np

D_MODEL = 1024
BATCH = 8
SEQ = 2048
DEPTH = 4
DEC_BATCH = 128
DEC_SEQ = 1
PAST_LEN = 2048
PAGE_SIZE = 128

F32 = jnp.float32
EPS = 1e-6
N_EVEN = (DEPTH + 1) // 2
N_ODD = DEPTH // 2
MIX_WIDTH = D_MODEL
WIDTH_A = MIX_WIDTH // 2
HEAD_DIM_A = 128
H_A = WIDTH_A // HEAD_DIM_A
DQK_A = HEAD_DIM_A // 2
Q_BLOCK = 128
WIDTH_B = MIX_WIDTH - WIDTH_A
HEAD_DIM_B = 128
H_B = WIDTH_B // HEAD_DIM_B
K_CONV = 4
CHUNK = 64
WIDTH_C = MIX_WIDTH // 2
GROUP_C = 16
G_C = WIDTH_C // GROUP_C
P_C = 64
WIDTH_D = MIX_WIDTH - WIDTH_C
HEAD_DIM_D = 128
H_D = WIDTH_D // HEAD_DIM_D
E_IN = 4 * WIDTH_A + 4 * WIDTH_B + 2 * H_B
O_IN = 2 * WIDTH_C + 4 * WIDTH_D

kernel_name = 'hybrid_diffattn_gdn_s5_retnet_step'


def rms_norm(x, g):
    xf = x.astype(F32)
    return xf * lax.rsqrt(jnp.mean(xf * xf, axis=-1, keepdims=True) + EPS) * g.astype(F32)


def layer_norm(x, g):
    xf = x.astype(F32)
    xc = xf - jnp.mean(xf, axis=-1, keepdims=True)
    return xc * lax.rsqrt(jnp.mean(xc * xc, axis=-1, keepdims=True) + EPS) * g.astype(F32)


def l2_normalize(x):
    return x * lax.rsqrt(jnp.sum(x * x, axis=-1, keepdims=True) + EPS)


def alibi_slopes(n_heads):
    return jnp.exp2(-8.0 * jnp.arange(1, n_heads + 1, dtype=F32) / n_heads)


def modulation(c, w_ada, b_ada):
    mod = jax.nn.silu(c) @ w_ada + b_ada
    return jnp.split(mod, 3, axis=-1)


def modulate(x, g, shift, scale):
    return (rms_norm(x, g) * (1.0 + scale[:, None, :]) + shift[:, None, :]).astype(x.dtype)


def diff_attend(q, q_pos, segments, lam):
    slopes = alibi_slopes(H_A)
    scale = DQK_A ** -0.5

    def probs(qh, half):
        parts = []
        for k, _, k_pos in segments:
            kh = k[..., half * DQK_A:(half + 1) * DQK_A].astype(F32)
            rel = q_pos[:, None] - k_pos[None, :]
            s = jnp.einsum('bqhd,bkhd->bhqk', qh, kh) * scale - slopes[:, None, None] * rel.astype(F32)
            parts.append(jnp.where(rel >= 0, s, -jnp.inf))
        return jax.nn.softmax(jnp.concatenate(parts, axis=-1), axis=-1)

    p = probs(q[..., :DQK_A], 0) - lam * probs(q[..., DQK_A:], 1)
    out = 0.0
    start = 0
    for _, v, k_pos in segments:
        n = k_pos.shape[0]
        out = out + jnp.einsum('bhqk,bkhd->bqhd', p[..., start:start + n], v.astype(F32))
        start += n
    return out


def diff_attn_prompt(q, k, v, lam):
    b, l, h, dq = q.shape
    nb = l // Q_BLOCK
    pos = jnp.arange(l, dtype=jnp.int32)
    q_blocks = q.reshape(b, nb, Q_BLOCK, h, dq).transpose(1, 0, 2, 3, 4)
    pos_blocks = pos.reshape(nb, Q_BLOCK)
    out = lax.map(lambda qp: diff_attend(qp[0], qp[1], [(k, v, pos)], lam), (q_blocks, pos_blocks))
    return out.transpose(1, 0, 2, 3, 4).reshape(b, l, h, v.shape[-1])


def diff_attn_sample(q, k, v, lam, past_k, past_v):
    past_len, l = past_k.shape[1], q.shape[1]
    past_pos = jnp.arange(past_len, dtype=jnp.int32)
    new_pos = past_len + jnp.arange(l, dtype=jnp.int32)
    return diff_attend(q, new_pos, [(past_k, past_v, past_pos), (k, v, new_pos)], lam)


def pad_time(x, n_pad):
    return jnp.pad(x, [(0, 0), (0, n_pad)] + [(0, 0)] * (x.ndim - 2))


def to_chunks(x, c):
    b, lp, h = x.shape[:3]
    rest = x.shape[3:]
    x = x.reshape((b, lp // c, c, h) + rest)
    return x.transpose((1, 0, 3, 2) + tuple(range(4, x.ndim)))


def from_chunks(o, length):
    n, b, h, c, dv = o.shape
    return o.transpose(1, 0, 3, 2, 4).reshape(b, n * c, h, dv)[:, :length]


def split_chunks(arrays, length):
    c = min(CHUNK, length)
    n_pad = -length % c
    return c, tuple(to_chunks(pad_time(t, n_pad), c) for t in arrays)


def gated_delta_chunked(q, k, v, g, beta, s0):
    length = q.shape[1]
    dv = v.shape[-1]
    c, xs = split_chunks((q, k, v, g, beta), length)
    incl = jnp.tril(jnp.ones((c, c), bool))
    strict = jnp.tril(jnp.ones((c, c), bool), -1)
    eye = jnp.eye(c, dtype=F32)

    def step(s, inp):
        qc, kc, vc, gc, bc = inp
        gam = jnp.cumsum(gc, axis=-1)
        decay = jnp.exp(jnp.where(incl, gam[..., :, None] - gam[..., None, :], -jnp.inf))
        kk = jnp.einsum('bhid,bhjd->bhij', kc, kc)
        m = eye + jnp.where(strict, bc[..., :, None] * decay * kk, 0.0)
        rhs = jnp.concatenate([bc[..., None] * vc, (bc * jnp.exp(gam))[..., None] * kc], axis=-1)
        sol = lax.linalg.triangular_solve(m, rhs, left_side=True, lower=True)
        u = sol[..., :dv] - jnp.einsum('bhik,bhkv->bhiv', sol[..., dv:], s)
        qk = jnp.einsum('bhik,bhjk->bhij', qc, kc) * decay
        o = jnp.exp(gam)[..., None] * jnp.einsum('bhik,bhkv->bhiv', qc, s) + jnp.einsum('bhij,bhjv->bhiv', qk, u)
        gl = gam[..., -1:]
        s_new = jnp.exp(gl)[..., None] * s + jnp.einsum('bhjk,bhjv->bhkv', kc * jnp.exp(gl - gam)[..., None], u)
        return s_new, o

    s_fin, o = lax.scan(step, s0, xs)
    return from_chunks(o, length), s_fin


def causal_conv(x, buf, w):
    length = x.shape[1]
    xp = jnp.concatenate([buf, x], axis=1)
    y = sum(xp[:, j:j + length] * w[j] for j in range(K_CONV))
    return y, xp[:, length:]


def decay_attn_chunked(q, k, v, logd, s0):
    length = q.shape[1]
    c, xs = split_chunks((q, k, v, logd), length)
    incl = jnp.tril(jnp.ones((c, c), bool))

    def step(s, inp):
        qc, kc, vc, lc = inp
        gam = jnp.cumsum(lc, axis=-1)
        decay = jnp.exp(jnp.where(incl, gam[..., :, None] - gam[..., None, :], -jnp.inf))
        qk = jnp.einsum('bhik,bhjk->bhij', qc, kc) * decay
        o = jnp.exp(gam)[..., None] * jnp.einsum('bhik,bhkv->bhiv', qc, s) + jnp.einsum('bhij,bhjv->bhiv', qk, vc)
        gl = gam[..., -1:]
        s_new = jnp.exp(gl)[..., None] * s + jnp.einsum('bhjk,bhjv->bhkv', kc * jnp.exp(gl - gam)[..., None], vc)
        return s_new, o

    s_fin, o = lax.scan(step, s0, xs)
    return from_chunks(o, length), s_fin


def complex_affine_combine(e1, e2):
    a1r, a1i, b1r, b1i = e1
    a2r, a2i, b2r, b2i = e2
    return (a2r * a1r - a2i * a1i, a2r * a1i + a2i * a1r,
            a2r * b1r - a2i * b1i + b2r, a2r * b1i + a2i * b1r + b2i)


def s5_scan(u, x0_re, x0_im, a_re, a_im, b_re, b_im, c_re, c_im, d_skip, log_dt):
    dt = jnp.exp(log_dt.astype(F32))[:, None]
    a_re, a_im = a_re.astype(F32), a_im.astype(F32)
    mag = jnp.exp(a_re * dt)
    ang = a_im * dt
    lb_re, lb_im = mag * jnp.cos(ang), mag * jnp.sin(ang)
    den = a_re * a_re + a_im * a_im
    f_re = ((lb_re - 1.0) * a_re + lb_im * a_im) / den
    f_im = (lb_im * a_re - (lb_re - 1.0) * a_im) / den
    bb_re = f_re[..., None] * b_re - f_im[..., None] * b_im
    bb_im = f_re[..., None] * b_im + f_im[..., None] * b_re
    bu_re = jnp.einsum('blgc,gpc->blgp', u, bb_re)
    bu_im = jnp.einsum('blgc,gpc->blgp', u, bb_im)
    bu_re = bu_re.at[:, 0].add(lb_re * x0_re - lb_im * x0_im)
    bu_im = bu_im.at[:, 0].add(lb_re * x0_im + lb_im * x0_re)
    shape = bu_re.shape
    elems = (jnp.broadcast_to(lb_re, shape), jnp.broadcast_to(lb_im, shape), bu_re, bu_im)
    _, _, xr, xi = lax.associative_scan(complex_affine_combine, elems, axis=1)
    y = (jnp.einsum('blgp,gcp->blgc', xr, c_re) - jnp.einsum('blgp,gcp->blgc', xi, c_im)
         + d_skip.reshape(G_C, GROUP_C) * u)
    return y, xr[:, -1], xi[:, -1]


def even_mixer(h, lam_init, attn_fn, conv_buf, s0, w_in, w_out, qn_g, kn_g, lam_q1, lam_k1, lam_q2, lam_k2,
               subln_g, conv_w, a_log, dt_bias, gn_b):
    b, l, _ = h.shape
    proj = (h @ w_in).astype(F32)
    sizes = [WIDTH_A, WIDTH_A, WIDTH_A, WIDTH_A, 3 * WIDTH_B, WIDTH_B, H_B, H_B]
    qa, ka, va, za, qkv_b, zb, a_b, b_b = jnp.split(proj, np.cumsum(sizes)[:-1].tolist(), axis=-1)
    qa = rms_norm(qa.reshape(b, l, H_A, 2, DQK_A), qn_g).reshape(b, l, H_A, 2 * DQK_A)
    ka = rms_norm(ka.reshape(b, l, H_A, 2, DQK_A), kn_g).reshape(b, l, H_A, 2 * DQK_A)
    va = va.reshape(b, l, H_A, HEAD_DIM_A)
    lam = (jnp.exp(jnp.sum(lam_q1.astype(F32) * lam_k1.astype(F32)))
           - jnp.exp(jnp.sum(lam_q2.astype(F32) * lam_k2.astype(F32))) + lam_init)
    oa = attn_fn(qa, ka, va, lam)
    oa = (rms_norm(oa, subln_g) * (1.0 - lam_init)).reshape(b, l, WIDTH_A) * jax.nn.silu(za)
    conv_out, conv_new = causal_conv(qkv_b, conv_buf.astype(F32), conv_w.astype(F32))
    qb, kb, vb = jnp.split(jax.nn.silu(conv_out), 3, axis=-1)
    qb = l2_normalize(qb.reshape(b, l, H_B, HEAD_DIM_B)) * (HEAD_DIM_B ** -0.5)
    kb = l2_normalize(kb.reshape(b, l, H_B, HEAD_DIM_B))
    vb = vb.reshape(b, l, H_B, HEAD_DIM_B)
    beta = jax.nn.sigmoid(b_b)
    g = -jnp.exp(a_log.astype(F32)) * jax.nn.softplus(a_b + dt_bias)
    ob, s_new = gated_delta_chunked(qb, kb, vb, g, beta, s0.astype(F32))
    ob = rms_norm(ob, gn_b).reshape(b, l, WIDTH_B) * jax.nn.silu(zb)
    out = jnp.concatenate([oa, ob], axis=-1).astype(h.dtype) @ w_out
    return out, ka, va, conv_new, s_new


def odd_mixer(h, x0_re, x0_im, r0, w_in, w_out, a_re, a_im, b_re, b_im, c_re, c_im, d_skip, log_dt, w_glu, gn_d):
    b, l, _ = h.shape
    proj = (h @ w_in).astype(F32)
    sizes = [WIDTH_C, WIDTH_C, WIDTH_D, WIDTH_D, WIDTH_D, WIDTH_D]
    u, zc, qd, kd, vd, zd = jnp.split(proj, np.cumsum(sizes)[:-1].tolist(), axis=-1)
    y, xr, xi = s5_scan(u.reshape(b, l, G_C, GROUP_C), x0_re.astype(F32), x0_im.astype(F32), a_re, a_im,
                        b_re, b_im, c_re, c_im, d_skip, log_dt)
    yg = jax.nn.gelu(y.reshape(b, l, WIDTH_C))
    oc = yg * jax.nn.sigmoid(yg @ w_glu.astype(F32)) * jax.nn.silu(zc)
    qd = qd.reshape(b, l, H_D, HEAD_DIM_D)
    kd = kd.reshape(b, l, H_D, HEAD_DIM_D) * (HEAD_DIM_D ** -0.5)
    vd = vd.reshape(b, l, H_D, HEAD_DIM_D)
    log_gamma = jnp.log1p(-jnp.exp2(-5.0 - jnp.arange(H_D, dtype=F32)))
    od, r_new = decay_attn_chunked(qd, kd, vd, jnp.broadcast_to(log_gamma, (b, l, H_D)), r0.astype(F32))
    od = layer_norm(od, gn_d).reshape(b, l, WIDTH_D) * jax.nn.silu(zd)
    out = jnp.concatenate([oc, od], axis=-1).astype(h.dtype) @ w_out
    return out, xr, xi, r_new


def setup_inputs(seed: int = 0) -> dict:
    key = jax.random.key(seed)
    ks = iter(jax.random.split(key, 48))

    def nrm(shape, scale):
        return jax.random.normal(next(ks), shape, F32) * scale

    n_pages = PAST_LEN // PAGE_SIZE
    n_phys = (5 * DEC_BATCH * n_pages + 3) // 4
    page_table = jax.random.permutation(next(ks), n_phys)[:DEC_BATCH * n_pages].reshape(DEC_BATCH, n_pages).astype(jnp.int32)
    dt_b = jnp.exp(jax.random.uniform(next(ks), (N_EVEN, H_B), F32, math.log(1e-3), math.log(1e-1)))
    return {
        'x_prompt': nrm((BATCH, SEQ, D_MODEL), 1.0),
        'x_sample': nrm((DEC_BATCH, DEC_SEQ, D_MODEL), 1.0),
        'c_prompt': nrm((BATCH, D_MODEL), 1.0),
        'c_sample': nrm((DEC_BATCH, D_MODEL), 1.0),
        'page_table': page_table,
        'cache_k': nrm((N_EVEN, n_phys, PAGE_SIZE, H_A, 2 * DQK_A), 1.0),
        'cache_v': nrm((N_EVEN, n_phys, PAGE_SIZE, H_A, HEAD_DIM_A), 1.0),
        'state_b_conv': nrm((N_EVEN, DEC_BATCH, K_CONV - 1, 3 * WIDTH_B), 1.0),
        'state_b_ssm': nrm((N_EVEN, DEC_BATCH, H_B, HEAD_DIM_B, HEAD_DIM_B), 0.1),
        'state_c_re': nrm((N_ODD, DEC_BATCH, G_C, P_C), 0.5),
        'state_c_im': nrm((N_ODD, DEC_BATCH, G_C, P_C), 0.5),
        'state_d_ret': nrm((N_ODD, DEC_BATCH, H_D, HEAD_DIM_D, HEAD_DIM_D), 0.5),
        'norm_g': 1.0 + nrm((DEPTH, D_MODEL), 0.02),
        'w_ada': nrm((DEPTH, D_MODEL, 3 * D_MODEL), 0.5 * D_MODEL ** -0.5),
        'b_ada': nrm((DEPTH, 3 * D_MODEL), 0.02),
        'w_in_e': nrm((N_EVEN, D_MODEL, E_IN), D_MODEL ** -0.5),
        'w_out_e': nrm((N_EVEN, WIDTH_A + WIDTH_B, D_MODEL), (WIDTH_A + WIDTH_B) ** -0.5),
        'qn_g': 1.0 + nrm((N_EVEN, DQK_A), 0.02),
        'kn_g': 1.0 + nrm((N_EVEN, DQK_A), 0.02),
        'lam_q1': nrm((N_EVEN, DQK_A), 0.1),
        'lam_k1': nrm((N_EVEN, DQK_A), 0.1),
        'lam_q2': nrm((N_EVEN, DQK_A), 0.1),
        'lam_k2': nrm((N_EVEN, DQK_A), 0.1),
        'subln_g': 1.0 + nrm((N_EVEN, HEAD_DIM_A), 0.02),
        'conv_w': nrm((N_EVEN, K_CONV, 3 * WIDTH_B), K_CONV ** -0.5),
        'a_log': jnp.log(jax.random.uniform(next(ks), (N_EVEN, H_B), F32, 1.0, 16.0)),
        'dt_bias': dt_b + jnp.log(-jnp.expm1(-dt_b)),
        'gn_b': 1.0 + nrm((N_EVEN, HEAD_DIM_B), 0.02),
        'w_in_o': nrm((N_ODD, D_MODEL, O_IN), D_MODEL ** -0.5),
        'w_out_o': nrm((N_ODD, WIDTH_C + WIDTH_D, D_MODEL), (WIDTH_C + WIDTH_D) ** -0.5),
        's5_a_re': -0.5 + nrm((N_ODD, G_C, P_C), 0.01),
        's5_a_im': math.pi * jnp.arange(P_C, dtype=F32) + nrm((N_ODD, G_C, P_C), 0.01),
        's5_b_re': nrm((N_ODD, G_C, P_C, GROUP_C), (2 * GROUP_C) ** -0.5),
        's5_b_im': nrm((N_ODD, G_C, P_C, GROUP_C), (2 * GROUP_C) ** -0.5),
        's5_c_re': nrm((N_ODD, G_C, GROUP_C, P_C), P_C ** -0.5),
        's5_c_im': nrm((N_ODD, G_C, GROUP_C, P_C), P_C ** -0.5),
        's5_d': nrm((N_ODD, WIDTH_C), 0.5),
        's5_log_dt': jax.random.uniform(next(ks), (N_ODD, G_C), F32, math.log(1e-3), math.log(1e-1)),
        'w_glu': nrm((N_ODD, WIDTH_C, WIDTH_C), WIDTH_C ** -0.5),
        'gn_d': 1.0 + nrm((N_ODD, HEAD_DIM_D), 0.02),
    }


def reference(x_prompt, x_sample, c_prompt, c_sample, page_table, cache_k, cache_v, state_b_conv, state_b_ssm,
              state_c_re, state_c_im, state_d_ret, norm_g, w_ada, b_ada, w_in_e, w_out_e, qn_g, kn_g,
              lam_q1, lam_k1, lam_q2, lam_k2, subln_g, conv_w, a_log, dt_bias, gn_b, w_in_o, w_out_o,
              s5_a_re, s5_a_im, s5_b_re, s5_b_im, s5_c_re, s5_c_im, s5_d, s5_log_dt, w_glu, gn_d):
    b_p, b_s = x_prompt.shape[0], x_sample.shape[0]
    n_pages, page_size = page_table.shape[1], cache_k.shape[2]
    past_len = n_pages * page_size
    xp, xs = x_prompt, x_sample
    nk_p, nv_p, nk_s, nv_s, cv_p, cv_s, dl_p, dl_s = ([] for _ in range(8))
    s5r_p, s5i_p, s5r_s, s5i_s, rt_p, rt_s = ([] for _ in range(6))
    for li in range(DEPTH):
        sh_p, sc_p, gt_p = modulation(c_prompt, w_ada[li], b_ada[li])
        sh_s, sc_s, gt_s = modulation(c_sample, w_ada[li], b_ada[li])
        hp = modulate(xp, norm_g[li], sh_p, sc_p)
        hs = modulate(xs, norm_g[li], sh_s, sc_s)
        if li % 2 == 0:
            e = li // 2
            lam_init = 0.8 - 0.6 * math.exp(-0.3 * li)
            ew = (w_in_e[e], w_out_e[e], qn_g[e], kn_g[e], lam_q1[e], lam_k1[e], lam_q2[e], lam_k2[e],
                  subln_g[e], conv_w[e], a_log[e], dt_bias[e], gn_b[e])
            out_p, k_p, v_p, c_p, s_p = even_mixer(
                hp, lam_init, diff_attn_prompt, jnp.zeros((b_p, K_CONV - 1, 3 * WIDTH_B), F32),
                jnp.zeros((b_p, H_B, HEAD_DIM_B, HEAD_DIM_B), F32), *ew)
            past_k = cache_k[e][page_table].reshape(b_s, past_len, H_A, 2 * DQK_A)
            past_v = cache_v[e][page_table].reshape(b_s, past_len, H_A, HEAD_DIM_A)
            attn_s = functools.partial(diff_attn_sample, past_k=past_k, past_v=past_v)
            out_s, k_s, v_s, c_s, s_s = even_mixer(hs, lam_init, attn_s, state_b_conv[e], state_b_ssm[e], *ew)
            nk_p.append(k_p); nv_p.append(v_p); nk_s.append(k_s); nv_s.append(v_s)
            cv_p.append(c_p); cv_s.append(c_s); dl_p.append(s_p); dl_s.append(s_s)
        else:
            o = li // 2
            ow = (w_in_o[o], w_out_o[o], s5_a_re[o], s5_a_im[o], s5_b_re[o], s5_b_im[o], s5_c_re[o], s5_c_im[o],
                  s5_d[o], s5_log_dt[o], w_glu[o], gn_d[o])
            zeros_c = jnp.zeros((b_p, G_C, P_C), F32)
            out_p, r_p, i_p, t_p = odd_mixer(hp, zeros_c, zeros_c, jnp.zeros((b_p, H_D, HEAD_DIM_D, HEAD_DIM_D), F32), *ow)
            out_s, r_s, i_s, t_s = odd_mixer(hs, state_c_re[o], state_c_im[o], state_d_ret[o], *ow)
            s5r_p.append(r_p); s5i_p.append(i_p); s5r_s.append(r_s); s5i_s.append(i_s)
            rt_p.append(t_p); rt_s.append(t_s)
        xp = xp + (gt_p[:, None, :] * out_p).astype(xp.dtype)
        xs = xs + (gt_s[:, None, :] * out_s).astype(xs.dtype)
    return (xp, xs, jnp.stack(nk_p), jnp.stack(nv_p), jnp.stack(nk_s), jnp.stack(nv_s),
            jnp.stack(cv_p), jnp.stack(cv_s), jnp.stack(dl_p), jnp.stack(dl_s),
            jnp.stack(s5r_p), jnp.stack(s5i_p), jnp.stack(s5r_s), jnp.stack(s5i_s),
            jnp.stack(rt_p), jnp.stack(rt_s))
```

```python
import math
import numpy as np
import ml_dtypes
import concourse.bass as bass
import concourse.mybir as mybir
from concourse.bass_utils import run_bass_kernel_spmd
from contextlib import ExitStack

F32 = mybir.dt.float32
BF16 = mybir.dt.bfloat16
I32 = mybir.dt.int32
AF = mybir.ActivationFunctionType
ALU = mybir.AluOpType
AX = mybir.AxisListType

ENGS = ("pe", "act", "dve", "pool", "sp")
N_DMA_SEMS = 24

D = 1024
T = 2048
NS = 16
TA = T + NS
NT = 16
NPG = 16
NPHYS = 2560
EPS = 1e-6


def _ap_region(ap):
    t = ap.tensor
    name = t.name
    shape = tuple(t.shape)
    dims = tuple(ap.ap)
    off = int(ap.offset)
    space = str(ap.space)
    if "SB" not in space.upper() and "PSUM" not in space.upper():
        ext = 0
        for st, cn in dims:
            ext += abs(st) * (cn - 1)
        return (name, 0, 1, off, off + ext + 1)
    R = 1
    for s in shape[1:]:
        R *= s
    if "PSUM" in space.upper() or name.startswith("ps"):
        return (name, 0, 128, 0, R)
    p0 = off // R
    f0 = off % R
    pst, pcn = dims[0]
    np_ = 1 if pst == 0 else pcn
    ext = 0
    for st, cn in dims[1:]:
        ext += abs(st) * (cn - 1)
    return (name, p0, p0 + np_, f0, f0 + ext + 1)


def _overlap(a, b):
    return a[0] == b[0] and a[1] < b[2] and b[1] < a[2] and a[3] < b[4] and b[3] < a[4]


def _covers(a, b):
    return a[0] == b[0] and a[1] <= b[1] and a[2] >= b[2] and a[3] <= b[3] and a[4] >= b[4]


class Prog:
    def __init__(self, nc, es):
        self.nc = nc
        self.es = es
        self.ops = {e: [] for e in ENGS}
        self.sem = {e: es.enter_context(nc.semaphore("s_" + e)) for e in ENGS}
        self.dsem = [es.enter_context(nc.semaphore("d%d" % i)) for i in range(N_DMA_SEMS)]
        self.dval = [0] * N_DMA_SEMS
        self.dnext = {"hw": 0, "sw": 0}
        self.cnt = {e: 0 for e in ENGS}
        self.seen = {e: {} for e in ENGS}
        self.recs = {}
        self.n_instr = 0

    def _semh(self, key):
        return self.dsem[key] if isinstance(key, int) else self.sem[key]

    def _need(self, eng, reads, writes):
        need = {}
        for ap in reads:
            r = _ap_region(ap)
            isps = r[0].startswith("ps")
            for rec in self.recs.get(r[0], ()):
                if (rec[3] or (isps and rec[1] != eng)) and _overlap(rec[0], r) and need.get(rec[1], 0) < rec[2]:
                    need[rec[1]] = rec[2]
        for ap in writes:
            r = _ap_region(ap)
            for rec in self.recs.get(r[0], ()):
                if _overlap(rec[0], r) and need.get(rec[1], 0) < rec[2]:
                    need[rec[1]] = rec[2]
        waits = []
        for k, v in need.items():
            if eng == "pe" and k == "pe":
                continue
            if self.seen[eng].get(k, 0) >= v:
                continue
            self.seen[eng][k] = v
            waits.append((k, v))
        return waits

    def _record(self, reads, writes, key, val):
        for ap in writes:
            r = _ap_region(ap)
            lst = self.recs.setdefault(r[0], [])
            lst[:] = [rec for rec in lst if not _covers(r, rec[0])]
            lst.append([r, key, val, True])
        for ap in reads:
            r = _ap_region(ap)
            lst = self.recs.setdefault(r[0], [])
            lst[:] = [rec for rec in lst if not (not rec[3] and rec[1] == key and rec[0] == r)]
            lst.append([r, key, val, False])

    limit = None
    trace = None
    _cap = None

    def capture(self):
        self._cap = []
        return self._cap

    def end_capture(self):
        self._cap = None

    def merge(self, streams):
        idx = [0] * len(streams)
        live = True
        while live:
            live = False
            for i, st in enumerate(streams):
                if idx[i] < len(st):
                    kind, eng, fn, r, w = st[idx[i]]
                    idx[i] += 1
                    live = True
                    (self.op if kind == "op" else self.dma)(eng, fn, r, w)

    def op(self, eng, fn, reads=(), writes=()):
        if self._cap is not None:
            self._cap.append(("op", eng, fn, reads, writes))
            return
        if self.limit is not None and self.n_instr >= self.limit:
            return
        if self.trace is not None:
            import traceback
            fr = traceback.extract_stack(limit=4)
            self.trace.append((self.n_instr, eng, [f"{f.lineno}" for f in fr[:-1]], [str(_ap_region(a)) for a in writes]))
        waits = self._need(eng, reads, writes)
        self.cnt[eng] += 1
        self.ops[eng].append((fn, waits, (eng, 1)))
        self._record(reads, writes, eng, self.cnt[eng])
        self.n_instr += 1

    def dma(self, eng, fn, reads=(), writes=()):
        if self._cap is not None:
            self._cap.append(("dma", eng, fn, reads, writes))
            return
        if self.limit is not None and self.n_instr >= self.limit:
            return
        half = N_DMA_SEMS // 2
        kind = "sw" if eng == "pool" else "hw"
        k = self.dnext[kind] + (half if kind == "sw" else 0)
        self.dnext[kind] = (self.dnext[kind] + 1) % half
        waits = self._need(eng, reads, writes)
        if self.dval[k] > 0 and self.seen[eng].get(k, 0) < self.dval[k]:
            self.seen[eng][k] = self.dval[k]
            waits.append((k, self.dval[k]))
        self.dval[k] += 16
        self.ops[eng].append((fn, waits, (k, 16)))
        self._record(reads, writes, k, self.dval[k])
        self.n_instr += 1

    def barrier(self):
        assert self._cap is None
        targets = [(e, self.cnt[e]) for e in ENGS if self.cnt[e] > 0]
        targets += [(k, self.dval[k]) for k in range(N_DMA_SEMS) if self.dval[k] > 0]
        for e in ENGS:
            waits = []
            for k, v in targets:
                if k == e or self.seen[e].get(k, 0) >= v:
                    continue
                self.seen[e][k] = v
                waits.append((k, v))
            if waits:
                self.ops[e].append((None, waits, None))
        self.recs = {}

    def emit(self):
        nc = self.nc
        prog = self
        with nc.Block() as block:
            def run(engname, eobj):
                for fn, waits, inc in prog.ops[engname]:
                    for k, v in waits:
                        eobj.wait_ge(prog._semh(k), v)
                    if fn is None:
                        continue
                    fn(eobj).then_inc(prog._semh(inc[0]), inc[1])

            @block.tensor
            def _(e):
                run("pe", e)

            @block.scalar
            def _(e):
                run("act", e)

            @block.vector
            def _(e):
                run("dve", e)

            @block.gpsimd
            def _(e):
                run("pool", e)

            @block.sync
            def _(e):
                run("sp", e)

    def mm(self, out, lhsT, rhs, start=True, stop=True):
        self.op("pe", lambda e: e.matmul(out, lhsT, rhs, start=start, stop=stop), reads=[lhsT, rhs], writes=[out])

    def tr(self, out, in_, ident):
        self.op("pe", lambda e: e.transpose(out, in_, ident), reads=[in_, ident], writes=[out])

    def act(self, out, in_, func, bias=None, scale=None):
        kw = {}
        reads = [in_]
        if bias is not None:
            kw["bias"] = bias
            if not isinstance(bias, (int, float)):
                reads.append(bias)
        if scale is not None:
            kw["scale"] = scale
            if not isinstance(scale, (int, float)):
                reads.append(scale)
        self.op("act", lambda e: e.activation(out, in_, func, **kw), reads=reads, writes=[out])

    def tt(self, out, in0, in1, op, eng="dve"):
        self.op(eng, lambda e: e.tensor_tensor(out, in0, in1, op), reads=[in0, in1], writes=[out])

    def ts(self, out, in0, s1, op0, s2=None, op1=None, eng="dve"):
        reads = [in0]
        if not isinstance(s1, (int, float)):
            reads.append(s1)
        if s2 is not None and not isinstance(s2, (int, float)):
            reads.append(s2)
        kw = {}
        if op1 is not None:
            kw["op1"] = op1
        self.op(eng, lambda e: e.tensor_scalar(out, in0, s1, s2, op0, **kw), reads=reads, writes=[out])

    def stt(self, out, in0, scalar, in1, op0, op1, eng="dve"):
        reads = [in0, in1]
        if not isinstance(scalar, (int, float)):
            reads.append(scalar)
        self.op(eng, lambda e: e.scalar_tensor_tensor(out, in0, scalar, in1, op0, op1), reads=reads, writes=[out])

    def copy(self, out, in_, eng="dve"):
        if eng == "act":
            self.op("act", lambda e: e.copy(out, in_), reads=[in_], writes=[out])
        else:
            self.op(eng, lambda e: e.tensor_copy(out, in_), reads=[in_], writes=[out])

    def memset(self, out, val, eng="dve"):
        self.op(eng, lambda e: e.memset(out, val), reads=[], writes=[out])

    def red(self, out, in_, op=None, eng="dve"):
        op = op or ALU.add
        self.op(eng, lambda e: e.tensor_reduce(out, in_, AX.X, op), reads=[in_], writes=[out])

    def recip(self, out, in_):
        self.op("dve", lambda e: e.reciprocal(out, in_), reads=[in_], writes=[out])

    def scan(self, out, d0, d1, init):
        reads = [d0, d1]
        if not isinstance(init, (int, float)):
            reads.append(init)
        self.op("dve", lambda e: e.tensor_tensor_scan(out, d0, d1, init, ALU.mult, ALU.add), reads=reads, writes=[out])

    def load(self, out, in_, eng="sp", slow=False):
        if slow:
            self.dma(eng, lambda e: e.dma_start(out=out, in_=in_, allow_slow_non_contiguous=True), reads=[in_], writes=[out])
        else:
            self.dma(eng, lambda e: e.dma_start(out=out, in_=in_), reads=[in_], writes=[out])

    store = load

    def gather(self, out, flat, idx):
        self.dma("pool", lambda e: e.indirect_dma_start(out=out, out_offset=None, in_=flat,
                                                        in_offset=bass.IndirectOffsetOnAxis(ap=idx, axis=0)),
                 reads=[idx, flat], writes=[out])


class Arena:
    def __init__(self, t, size):
        self.t = t
        self.size = size
        self.top = 0
        self.stack = []
        self.peak = 0

    def alloc(self, shape, parts=128):
        n = 1
        for s in shape:
            n *= s
        off = self.top
        self.top += n
        self.peak = max(self.peak, self.top)
        assert self.top <= self.size, ("arena overflow", self.t.name, self.top, self.size)
        ap = self.t[0:parts, off:off + n]
        if len(shape) == 2:
            ap = ap.rearrange("p (a b) -> p a b", b=shape[1])
        elif len(shape) == 3:
            ap = ap.rearrange("p (a b c) -> p a b c", b=shape[1], c=shape[2])
        elif len(shape) == 4:
            ap = ap.rearrange("p (a b c d) -> p a b c d", b=shape[1], c=shape[2], d=shape[3])
        return ap


INPUT_SPECS = [
    ("xp", (T, D), F32), ("xs", (NS, D), F32), ("cv", (NS + 1, D), F32), ("pt", (1, NS * NPG), I32),
    ("ck", (2, NPHYS * 128, 512), F32), ("cvv", (2, NPHYS * 128, 512), F32),
    ("sconv", (2, NS * 3, 1536), F32), ("sssm", (2, NS, 4, 128, 128), F32),
    ("scre", (2, NS, 2048), F32), ("scim", (2, NS, 2048), F32), ("sret", (2, NS, 4, 128, 128), F32),
    ("normg", (4, D), F32), ("wada", (4, 6, 128, 4096), F32), ("bada", (4, 3 * D), F32),
    ("wine", (2, 8, 128, 4096), F32), ("wab", (2, 128, 64), F32), ("woute", (2, 2, 128, 4096), F32),
    ("qng", (2, 1, 64), F32), ("kng", (2, 1, 64), F32), ("lamv", (2, 1, 256), F32), ("sublng", (2, 128, 1), F32),
    ("convw", (2, 4, 1536), F32), ("alog", (2, 1, 4), F32), ("dtb", (2, 1, 4), F32), ("gnb", (2, 1, 128), F32),
    ("wino", (2, 6, 128, 4096), F32), ("wouto", (2, 2, 128, 4096), F32),
    ("s5are", (2, 2048), F32), ("s5aim", (2, 2048), F32), ("s5bre", (2, 2048, 16), F32), ("s5bim", (2, 2048, 16), F32),
    ("s5cre", (2, 512, 64), F32), ("s5cim", (2, 512, 64), F32), ("s5d", (2, 512), F32), ("s5ldt", (2, 1, 32), F32),
    ("wglu", (2, 128, 2048), F32), ("gnd", (2, 1, 128), F32),
    ("c_ident", (128, 128), F32), ("c_ones", (128, 128), F32), ("c_onesb", (128, 128), BF16), ("c_trib", (128, 128), BF16),
    ("c_augq", (4, 4, T), BF16), ("c_augk", (4, 4, T), BF16), ("c_alibis", (128, NPG * 4), F32),
    ("c_mnegT", (128, 128), F32), ("c_mnegS", (128, 128), F32), ("c_LT", (128, 128), F32), ("c_LB", (128, 128), F32), ("c_cmask", (128, 8), F32), ("c_CM", (2, 128, 128), F32),
    ("c_dtc", (4, 128, 128), F32), ("c_kdc", (128, 4), F32), ("c_egc", (128, 4), F32), ("c_iota", (128, 1), F32),
    ("c_mask16", (128, 256), F32),
]
OUTPUT_SPECS = [
    ("y_p", (T, D)), ("y_s", (NS, D)), ("nk_p", (2, T, 512)), ("nv_p", (2, T, 512)), ("nk_s", (2, NS, 512)), ("nv_s", (2, NS, 512)),
    ("cv_p", (2, 3, 1536)), ("cv_s", (2, NS, 3, 1536)), ("dl_p", (2, 4, 128, 128)), ("dl_s", (2, NS, 4, 128, 128)),
    ("s5r_p", (2, 2048)), ("s5i_p", (2, 2048)), ("s5r_s", (2, NS, 2048)), ("s5i_s", (2, NS, 2048)),
    ("rt_p", (2, 4, 128, 128)), ("rt_s", (2, NS, 4, 128, 128)),
]
LOG_GAMMA = [math.log1p(-2.0 ** (-5.0 - h)) for h in range(4)]
SLOPES = [2.0 ** (-8.0 * (h + 1) / 4.0) for h in range(4)]


def make_consts():
    c = {}
    c["c_ident"] = np.eye(128, dtype=np.float32)
    c["c_ones"] = np.ones((128, 128), np.float32)
    c["c_onesb"] = np.ones((128, 128), ml_dtypes.bfloat16)
    i = np.arange(128)
    c["c_trib"] = (i[None, :] >= i[:, None]).astype(ml_dtypes.bfloat16)
    pos = np.arange(T)
    hi, lo = (pos // 128).astype(np.float64), (pos % 128).astype(np.float64)
    augq = np.zeros((4, 4, T), np.float64)
    augk = np.zeros((4, 4, T), np.float64)
    for h in range(4):
        s8 = 8.0 * SLOPES[h]
        augq[h, 0] = -s8 * 128.0 * hi
        augq[h, 1] = -s8 * lo
        augq[h, 2] = 1.0
        augq[h, 3] = 1.0
        augk[h, 0] = 1.0
        augk[h, 1] = 1.0
        augk[h, 2] = s8 * 128.0 * hi
        augk[h, 3] = s8 * lo
    c["c_augq"] = augq.astype(ml_dtypes.bfloat16)
    c["c_augk"] = augk.astype(ml_dtypes.bfloat16)
    al = np.zeros((128, NPG, 4), np.float64)
    for pg in range(NPG):
        for h in range(4):
            al[:, pg, h] = -SLOPES[h] * (T - (pg * 128 + i))
    c["c_alibis"] = al.reshape(128, NPG * 4).astype(np.float32)
    blk = i // 64
    same = blk[:, None] == blk[None, :]
    NEG = -1e30
    c["c_mnegT"] = np.where(same & (i[:, None] <= i[None, :]), 0.0, NEG).astype(np.float32)
    c["c_mnegS"] = np.where(same & (i[None, :] < i[:, None]), 0.0, NEG).astype(np.float32)
    c["c_LT"] = (same & (i[:, None] <= i[None, :])).astype(np.float32)
    c["c_LB"] = same.astype(np.float32)
    cmk = np.zeros((128, 8), np.float32)
    for l4 in range(4):
        for gp in range(2):
            cmk[l4 * 32 + gp * 16:l4 * 32 + gp * 16 + 16, l4 * 2 + gp] = 1.0
    c["c_cmask"] = cmk
    cm = np.zeros((2, 128, 128), np.float32)
    for cc in range(2):
        cm[cc, blk == cc, :] = 1.0
    c["c_CM"] = cm
    dtc = np.zeros((4, 128, 128), np.float64)
    kdc = np.zeros((128, 4), np.float64)
    egc = np.zeros((128, 4), np.float64)
    for h in range(4):
        lg = LOG_GAMMA[h]
        dd = (i[None, :] - i[:, None]).astype(np.float64)
        dtc[h] = np.where(dd >= 0, np.exp(lg * np.maximum(dd, 0)), 0.0) * (128.0 ** -0.5)
        kdc[:, h] = np.exp(lg * (127 - i)) * (128.0 ** -0.5)
        egc[:, h] = np.exp(lg * (i + 1))
    c["c_dtc"] = dtc.astype(np.float32)
    c["c_kdc"] = kdc.astype(np.float32)
    c["c_egc"] = egc.astype(np.float32)
    c["c_iota"] = i.astype(np.float32).reshape(128, 1)
    m16 = np.zeros((128, 16, 16), np.float32)
    for s in range(16):
        m16[:, s, s] = 1.0
    c["c_mask16"] = m16.reshape(128, 256)
    return c


def make_consts2(c):
    i = np.arange(128)
    ab = np.zeros((128, 4, 16), np.float64)
    for h in range(4):
        for d in range(16):
            ab[:, h, d] = SLOPES[h] * (i - 128.0 * d)
    c["c_abias"] = ab.reshape(128, 64).astype(np.float32)
    return c


INPUT_SPECS.append(("c_abias", (128, 64), F32))

NF = 9800
NB = 10240
BLOCKS = [(0, 512), (512, 512), (1024, 512), (1536, 512), (2048, 16)]


class Ctx:
    pass


def finish_o_rms(K, o, rows, gain_b, sz, dest):
    P, psn, ident = K.P, K.psn, K.ident
    sqs, ss, on = K.fo_sq, K.fo_ss, K.fo_on
    P.act(sqs[0:rows, :], o, AF.Square)
    P.red(ss[0:rows, 0:1], sqs[0:rows, :])
    P.ts(ss[0:rows, 0:1], ss[0:rows, 0:1], 1.0 / 128, ALU.mult, EPS, ALU.add)
    P.act(ss[0:rows, 0:1], ss[0:rows, 0:1], AF.Sqrt)
    P.recip(ss[0:rows, 0:1], ss[0:rows, 0:1])
    P.stt(on[0:rows, :], o, ss[0:rows, 0:1], gain_b[0:rows, :], ALU.mult, ALU.mult)
    pt = psn()
    P.tr(pt[:, 0:rows], on[0:rows, :], ident[0:rows, 0:rows])
    P.tt(dest, pt[:, 0:rows], sz, ALU.mult)


def finish_o_ln(K, o, rows, gain_b, sz, dest):
    P, psn, ident = K.P, K.psn, K.ident
    sqs, ss, on = K.fo_sq, K.fo_ss, K.fo_on
    P.red(ss[0:rows, 1:2], o)
    P.ts(ss[0:rows, 1:2], ss[0:rows, 1:2], 1.0 / 128, ALU.mult)
    P.ts(on[0:rows, :], o, ss[0:rows, 1:2], ALU.subtract)
    P.act(sqs[0:rows, :], on[0:rows, :], AF.Square)
    P.red(ss[0:rows, 0:1], sqs[0:rows, :])
    P.ts(ss[0:rows, 0:1], ss[0:rows, 0:1], 1.0 / 128, ALU.mult, EPS, ALU.add)
    P.act(ss[0:rows, 0:1], ss[0:rows, 0:1], AF.Sqrt)
    P.recip(ss[0:rows, 0:1], ss[0:rows, 0:1])
    P.stt(on[0:rows, :], on[0:rows, :], ss[0:rows, 0:1], gain_b[0:rows, :], ALU.mult, ALU.mult)
    pt = psn()
    P.tr(pt[:, 0:rows], on[0:rows, :], ident[0:rows, 0:rows])
    P.tt(dest, pt[:, 0:rows], sz, ALU.mult)


def sample_state_step(K, qT, kT, vT, eg, neg_eg, beta, EGB, s0_dram, out_dram, gain_b, szs, dest, delta, ln):
    P, AFa, psn, ident, Scope = K.P, K.AFa, K.psn, K.ident, K.Scope
    QM = AFa.alloc([16, 16])
    KM = AFa.alloc([16, 16])
    M16 = K.MASK16
    P.tt(QM[:, :, :], M16[:, :, :], qT.unsqueeze(1).to_broadcast([128, 16, 16]), ALU.mult)
    P.tt(KM[:, :, :], M16[:, :, :], kT.unsqueeze(1).to_broadcast([128, 16, 16]), ALU.mult)
    QTM = AFa.alloc([128])
    KTM = AFa.alloc([128])
    VTM = AFa.alloc([128])
    for src, dst in ((qT, QTM), (kT, KTM), (vT, VTM)):
        pt = psn()
        P.tr(pt[0:16, 0:128], src, ident[:])
        P.copy(dst[0:16, :], pt[0:16, 0:128], eng="act")
    S0 = [AFa.alloc([128]) for _ in range(2)]
    pk = psn()
    pq = psn()
    for s in range(NS):
        s0 = S0[s % 2]
        P.load(s0[:, :], s0_dram(s))
        if delta:
            P.mm(pk[0:16, 0:128], KM[:, s, :], s0[:, :], start=(s == 0), stop=(s == NS - 1))
        P.mm(pq[0:16, 0:128], QM[:, s, :], s0[:, :], start=(s == 0), stop=(s == NS - 1))
    u = AFa.alloc([128])
    if delta:
        P.stt(u[0:16, :], pk[0:16, 0:128], neg_eg, VTM[0:16, :], ALU.mult, ALU.add)
        P.ts(u[0:16, :], u[0:16, :], beta, ALU.mult)
    else:
        P.copy(u[0:16, :], VTM[0:16, :])
    tmp = AFa.alloc([128])
    qk = AFa.alloc([1])
    P.tt(tmp[0:16, :], QTM[0:16, :], KTM[0:16, :], ALU.mult)
    P.red(qk[0:16, :], tmp[0:16, :])
    o = AFa.alloc([128])
    P.ts(tmp[0:16, :], pq[0:16, 0:128], eg, ALU.mult)
    P.stt(o[0:16, :], u[0:16, :], qk[0:16, 0:1], tmp[0:16, :], ALU.mult, ALU.add)
    if ln:
        finish_o_ln(K, o[0:16, :], 16, gain_b, szs, dest)
    else:
        finish_o_rms(K, o[0:16, :], 16, gain_b, szs, dest)
    kms = [AFa.alloc([128]) for _ in range(2)]
    sn = [AFa.alloc([128]) for _ in range(2)]
    for s in range(NS):
        s0 = S0[s % 2]
        P.load(s0[:, :], s0_dram(s))
        km = kms[s % 2]
        P.ts(km[0:16, :], KTM[0:16, :], ident[0:16, s:s + 1], ALU.mult)
        ps = psn()
        P.mm(ps[:, 0:128], km[0:16, :], u[0:16, :])
        egs = EGB if isinstance(EGB, float) else EGB[:, s:s + 1]
        P.stt(sn[s % 2][:, :], s0[:, :], egs, ps[:, 0:128], ALU.mult, ALU.add)
        P.store(out_dram(s), sn[s % 2][:, :])


def gdn_heads(K, li):
    P, I, O = K.P, K.I, K.O
    XT, HT, OT, AFa, ABa, PS, psn, Scope = K.XT, K.HT, K.OT, K.AFa, K.ABa, K.PS, K.psn, K.Scope
    next_w, wv512, v3, ident, ones, onesb = K.next_w, K.wv512, K.v3, K.ident, K.ones, K.onesb
    e = li // 2
    with Scope():
        wab = next_w(("wab", e))
        wabv = wab[:, 0:64].rearrange("p (k c) -> p k c", c=8)
        AB = AFa.alloc([17, 8])
        P.memset(AB[:, :, :], 0.0)
        for n in range(17):
            rows = 128 if n < 16 else 16
            c0 = n * 128
            ps = psn()
            for k in range(8):
                P.mm(ps[0:rows, 0:8], HT[:, k, c0:c0 + rows], wabv[:, k, :], start=(k == 0), stop=(k == 7))
            P.copy(AB[0:rows, n, :], ps[0:rows, 0:8], eng=("act" if n % 2 else "dve"))
        ALG = AFa.alloc([4])
        DTB = AFa.alloc([4])
        P.load(ALG[:, :], I["alog"][e].partition_broadcast(128))
        P.load(DTB[:, :], I["dtb"][e].partition_broadcast(128))
        P.act(ALG[:, :], ALG[:, :], AF.Exp)
        P.ts(ALG[:, :], ALG[:, :], -1.0, ALU.mult)
        G = AFa.alloc([17, 4])
        BETA = AFa.alloc([17, 4])
        P.tt(G[:, :, :], AB[:, :, 0:4], DTB[:, :].unsqueeze(1).to_broadcast([128, 17, 4]), ALU.add)
        P.act(G[:, :, :], G[:, :, :], AF.Exp)
        P.act(G[:, :, :], G[:, :, :], AF.Ln, bias=ones[:, 0:1])
        P.tt(G[:, :, :], G[:, :, :], ALG[:, :].unsqueeze(1).to_broadcast([128, 17, 4]), ALU.mult)
        P.act(BETA[:, :, :], AB[:, :, 4:8], AF.Sigmoid)
        CN = {}
        for nm in ("c_LT", "c_LB", "c_mnegT", "c_mnegS"):
            CN[nm] = AFa.alloc([128])
            P.load(CN[nm][:, :], I[nm])
        CM = AFa.alloc([2, 128])
        P.load(CM[:, :, :], I["c_CM"].rearrange("c j p -> j c p"))
        Gf = G[:, 0:16, :].rearrange("p n h -> p (n h)")
        GAM = AFa.alloc([16, 4])
        GL = AFa.alloc([16, 4])
        ps = psn()
        P.mm(ps[:, 0:64], CN["c_LT"][:, :], Gf)
        P.copy(GAM[:, :, :].rearrange("p n h -> p (n h)"), ps[:, 0:64])
        ps = psn()
        P.mm(ps[:, 0:64], CN["c_LB"][:, :], Gf)
        P.copy(GL[:, :, :].rearrange("p n h -> p (n h)"), ps[:, 0:64], eng="act")
        EGLB = AFa.alloc([2, 16, 4])
        for c in range(2):
            ps = psn()
            P.mm(ps[:, 0:64], CM[:, c, :], Gf)
            P.act(EGLB[:, c, :, :].rearrange("p n h -> p (n h)"), ps[:, 0:64], AF.Exp)
        EGAM = AFa.alloc([16, 4])
        BEG = AFa.alloc([16, 4])
        NBETA = AFa.alloc([16, 4])
        NGAM = AFa.alloc([16, 4])
        ED = AFa.alloc([16, 4])
        P.act(EGAM[:, :, :], GAM[:, :, :], AF.Exp)
        P.tt(BEG[:, :, :], BETA[:, 0:16, :], EGAM[:, :, :], ALU.mult)
        P.ts(NBETA[:, :, :], BETA[:, 0:16, :], -1.0, ALU.mult)
        P.ts(NGAM[:, :, :], GAM[:, :, :], -1.0, ALU.mult)
        P.tt(ED[:, :, :], GL[:, :, :], GAM[:, :, :], ALU.subtract)
        P.act(ED[:, :, :], ED[:, :, :], AF.Exp)
        EGS = AFa.alloc([4])
        NEGS = AFa.alloc([4])
        P.act(EGS[0:16, :], G[0:16, 16, :], AF.Exp)
        P.ts(NEGS[0:16, :], EGS[0:16, :], -1.0, ALU.mult)
        CWT = AFa.alloc([12, 4])
        for tap in range(4):
            P.load(CWT[:, :, tap], I["convw"][e][tap].rearrange("(c p) -> p c", p=128), slow=True)
        CBT = AFa.alloc([12, 48])
        with Scope():
            CB = AFa.alloc([1536])
            P.load(CB[0:48, :], I["sconv"][e])
            for half in range(3):
                ps = psn()
                for q in range(4):
                    cidx = half * 4 + q
                    P.tr(ps[:, q * 64:q * 64 + 48], CB[0:48, cidx * 128:(cidx + 1) * 128], ident[0:48, 0:48])
                P.copy(CBT[:, half * 4:(half + 1) * 4, :], v3(ps[:, 0:256], 64)[:, :, 0:48])
        P.load(O["cv_s"][e, :, 0:2, :], I["sconv"][e].rearrange("(s j) c -> s j c", j=3)[:, 1:3, :])
        GNB = AFa.alloc([128])
        P.load(GNB[:, :], I["gnb"][e].partition_broadcast(128))
        K.MASK16 = AFa.alloc([16, 16])
        P.load(K.MASK16[:, :, :], v3(I["c_mask16"], 16))
        BW = 256
        NTB = T // BW

        def alloc_bufs():
            B = {}
            B["S"] = AFa.alloc([128])
            B["PRE"] = AFa.alloc([BW + 3])
            B["CAR"] = AFa.alloc([3, 3])
            B["QKV"] = [AFa.alloc([BW]) for _ in range(3)]
            B["SZ"] = AFa.alloc([BW])
            B["acc"] = AFa.alloc([BW])
            B["rs"] = B["acc"]
            B["sqb"] = ABa.alloc([BW])
            B["fo"] = (AFa.alloc([128]), AFa.alloc([2]), AFa.alloc([128]))
            return B

        def alloc_mats():
            M = {}
            M["X0"] = AFa.alloc([256])
            for nm in ("KD", "DG", "T1", "T2", "DT", "DS", "AQ", "Pm", "PTm"):
                M[nm] = AFa.alloc([128])
            M["WT"], M["UF"], M["O1"], M["OO"] = M["DG"], M["T1"], M["T2"], M["DS"]
            return M

        def l2n(B, w):
            QKV, rs, sqb_ = B["QKV"], B["rs"], B["sqb"]
            for j in (0, 1):
                P.act(sqb_[:, 0:w], QKV[j][:, 0:w], AF.Square)
                ps = psn()
                P.mm(ps[:, 0:w], onesb[:], sqb_[:, 0:w])
                P.ts(rs[:, 0:w], ps[:, 0:w], EPS, ALU.add)
                P.act(rs[:, 0:w], rs[:, 0:w], AF.Sqrt)
                P.recip(rs[:, 0:w], rs[:, 0:w])
                if j == 0:
                    P.stt(QKV[j][:, 0:w], QKV[j][:, 0:w], 128.0 ** -0.5, rs[:, 0:w], ALU.mult, ALU.mult)
                else:
                    P.tt(QKV[j][:, 0:w], QKV[j][:, 0:w], rs[:, 0:w], ALU.mult)

        def head_prompt(h, wv, B, M):
            S, PRE, CAR, QKV, SZ, acc = B["S"], B["PRE"], B["CAR"], B["QKV"], B["SZ"], B["acc"]
            K.fo_sq, K.fo_ss, K.fo_on = B["fo"]
            X0, KD, DG, T1, T2, DT, DS, AQ, Pm, PTm, WT, UF, O1, OO = [M[n] for n in
                ("X0", "KD", "DG", "T1", "T2", "DT", "DS", "AQ", "Pm", "PTm", "WT", "UF", "O1", "OO")]
            P.memset(S[:, :], 0.0)
            P.memset(CAR[:, :, :], 0.0)
            for tb in range(NTB):
                c0 = tb * BW
                for j in range(3):
                    ps = psn()
                    for k in range(8):
                        P.mm(ps[:, 0:BW], wv[:, k, j * 128:(j + 1) * 128], HT[:, k, c0:c0 + BW], start=(k == 0), stop=(k == 7))
                    P.copy(PRE[:, 0:3], CAR[:, j, :])
                    P.copy(PRE[:, 3:BW + 3], ps[:, 0:BW], eng="act")
                    cidx = j * 4 + h
                    P.ts(acc[:, :], PRE[:, 0:BW], CWT[:, cidx, 0:1], ALU.mult)
                    for tap in range(1, 4):
                        P.stt(acc[:, :], PRE[:, tap:tap + BW], CWT[:, cidx, tap:tap + 1], acc[:, :], ALU.mult, ALU.add)
                    P.act(QKV[j][:, :], acc[:, :], AF.Silu)
                    P.copy(CAR[:, j, :], PRE[:, BW:BW + 3])
                    if tb == NTB - 1:
                        P.store(O["cv_p"][e][:, j * 512 + h * 128:j * 512 + (h + 1) * 128].rearrange("t c -> c t"),
                                CAR[:, j, :], slow=True)
                l2n(B, BW)
                ps = psn()
                for k in range(8):
                    P.mm(ps[:, 0:BW], wv[:, k, 384:512], HT[:, k, c0:c0 + BW], start=(k == 0), stop=(k == 7))
                P.act(SZ[:, :], ps[:, 0:BW], AF.Silu)
                for tl in range(BW // 128):
                    n = tb * (BW // 128) + tl
                    cc = slice(tl * 128, (tl + 1) * 128)
                    qT, kT, vT = QKV[0][:, cc], QKV[1][:, cc], QKV[2][:, cc]
                    gam, ngam = GAM[:, n, h:h + 1], NGAM[:, n, h:h + 1]
                    pa = psn()
                    P.tr(pa[:, 0:128], kT, ident[:])
                    P.tr(pa[:, 128:256], vT, ident[:])
                    P.ts(X0[:, 0:128], pa[:, 128:256], BETA[:, n, h:h + 1], ALU.mult)
                    P.ts(X0[:, 128:256], pa[:, 0:128], BEG[:, n, h:h + 1], ALU.mult)
                    P.ts(KD[:, :], pa[:, 0:128], ED[:, n, h:h + 1], ALU.mult)
                    P.ts(DG[:, :], ident[:], gam, ALU.mult)
                    pr = psn()
                    P.mm(pr[:, 0:128], ones[:], DG[:, :])
                    P.tt(T1[:, :], pr[:, 0:128], CN["c_mnegT"][:, :], ALU.add)
                    P.act(DT[:, :], T1[:, :], AF.Exp, bias=ngam)
                    P.stt(T2[:, :], pr[:, 0:128], -1.0, CN["c_mnegS"][:, :], ALU.mult, ALU.add)
                    P.act(DS[:, :], T2[:, :], AF.Exp, bias=gam)
                    pq = psn()
                    P.mm(pq[:, 0:128], kT, qT)
                    P.tt(AQ[:, :], pq[:, 0:128], DT[:, :], ALU.mult)
                    pk = psn()
                    P.mm(pk[:, 0:128], kT, kT)
                    P.stt(Pm[:, :], pk[:, 0:128], NBETA[:, n, h:h + 1], DS[:, :], ALU.mult, ALU.mult)
                    pt = psn()
                    P.tr(pt[:, 0:128], Pm[:, :], ident[:])
                    P.copy(PTm[:, :], pt[:, 0:128], eng="act")
                    for lvl in range(6):
                        px = psn()
                        P.mm(px[:, 0:256], PTm[:, :], X0[:, :])
                        P.tt(X0[:, :], px[:, 0:256], X0[:, :], ALU.add)
                        if lvl < 5:
                            pp = psn()
                            P.mm(pp[:, 0:128], PTm[:, :], Pm[:, :])
                            P.mm(pp[:, 128:256], Pm[:, :], PTm[:, :])
                            P.copy(Pm[:, :], pp[:, 0:128], eng="act")
                            P.copy(PTm[:, :], pp[:, 128:256], eng="act")
                    pw = psn()
                    P.tr(pw[:, 0:128], X0[:, 128:256], ident[:])
                    P.copy(WT[:, :], pw[:, 0:128], eng="act")
                    for c in range(2):
                        r = slice(64 * c, 64 * c + 64)
                        p1 = psn()
                        P.mm(p1[:, 0:128], WT[:, :], S[:, :])
                        P.tt(UF[r, :], X0[r, 0:128], p1[r, 0:128], ALU.subtract)
                        p2 = psn()
                        P.mm(p2[:, 0:128], qT, S[:, :])
                        P.ts(O1[r, :], p2[r, 0:128], EGAM[r, n, h:h + 1], ALU.mult)
                        p3 = psn()
                        P.mm(p3[:, 0:128], KD[r, :], UF[r, :])
                        P.stt(S[:, :], S[:, :], EGLB[:, c, n, h:h + 1], p3[:, 0:128], ALU.mult, ALU.add)
                    p4 = psn()
                    P.mm(p4[:, 0:128], AQ[:, :], UF[:, :])
                    P.tt(OO[:, :], p4[:, 0:128], O1[:, :], ALU.add)
                    finish_o_rms(K, OO[:, :], 128, GNB, SZ[:, cc], OT[:, 4 + h, n * 128:(n + 1) * 128])
            P.store(O["dl_p"][e, h], S[:, :])

        def head_sample(h, wv, B):
            QKV, SZ, acc = B["QKV"], B["SZ"], B["acc"]
            K.fo_sq, K.fo_ss, K.fo_on = B["fo"]
            xs_ = AFa.alloc([16])
            xtm = AFa.alloc([128])
            for j in range(3):
                ps = psn()
                for k in range(8):
                    P.mm(ps[:, 0:16], wv[:, k, j * 128:(j + 1) * 128], HT[:, k, T:TA], start=(k == 0), stop=(k == 7))
                P.copy(xs_[:, :], ps[:, 0:16], eng="act")
                cidx = j * 4 + h
                cb = CBT[:, cidx, :].rearrange("p (s j) -> p s j", j=3)
                P.ts(acc[:, 0:16], cb[:, :, 0], CWT[:, cidx, 0:1], ALU.mult)
                P.stt(acc[:, 0:16], cb[:, :, 1], CWT[:, cidx, 1:2], acc[:, 0:16], ALU.mult, ALU.add)
                P.stt(acc[:, 0:16], cb[:, :, 2], CWT[:, cidx, 2:3], acc[:, 0:16], ALU.mult, ALU.add)
                P.stt(acc[:, 0:16], xs_[:, :], CWT[:, cidx, 3:4], acc[:, 0:16], ALU.mult, ALU.add)
                P.act(QKV[j][:, 0:16], acc[:, 0:16], AF.Silu)
                pt = psn()
                P.tr(pt[0:16, 0:128], xs_[:, :], ident[:])
                P.copy(xtm[0:16, :], pt[0:16, 0:128])
                P.store(O["cv_s"][e, :, 2, j * 512 + h * 128:j * 512 + (h + 1) * 128], xtm[0:16, :])
            l2n(B, 16)
            ps = psn()
            for k in range(8):
                P.mm(ps[:, 0:16], wv[:, k, 384:512], HT[:, k, T:TA], start=(k == 0), stop=(k == 7))
            P.act(SZ[:, 0:16], ps[:, 0:16], AF.Silu)
            DGs = AFa.alloc([16])
            P.ts(DGs[0:16, :], ident[0:16, 0:16], EGS[0:16, h:h + 1], ALU.mult)
            ps = psn()
            P.mm(ps[:, 0:16], ones[0:16, :], DGs[0:16, :])
            EGB = AFa.alloc([16])
            P.copy(EGB[:, :], ps[:, 0:16])
            sample_state_step(K, QKV[0][:, 0:16], QKV[1][:, 0:16], QKV[2][:, 0:16],
                              EGS[0:16, h:h + 1], NEGS[0:16, h:h + 1], BETA[0:16, 16, h:h + 1], EGB,
                              (lambda s, h=h: I["sssm"][e, s, h]), (lambda s, h=h: O["dl_s"][e, s, h]),
                              GNB, SZ[:, 0:16], OT[:, 4 + h, T:TA], True, False)

        NPAR = K.cfg.get("gdn_par", 2)
        for hp in range(0, 4, NPAR):
            wvs = [wv512(next_w(("eb", e, hp + i), prefetch=(i == 0))) for i in range(NPAR)]
            with Scope():
                Bs = [alloc_bufs() for _ in range(NPAR)]
                with Scope():
                    Ms = [alloc_mats() for _ in range(NPAR)]
                    streams = []
                    for i in range(NPAR):
                        K.p0, K.pn = (8 // NPAR) * i, 8 // NPAR
                        cap = P.capture()
                        head_prompt(hp + i, wvs[i], Bs[i], Ms[i])
                        P.end_capture()
                        streams.append(cap)
                    K.p0, K.pn = 0, 8
                    P.merge(streams)
                for i in range(NPAR):
                    with Scope():
                        head_sample(hp + i, wvs[i], Bs[i])


def rms_feature_gate(K, o, w, gcol, psz_fn, dest, tagf):
    P, AFa, ABa, psn, onesb = K.P, K.AFa, K.ABa, K.psn, K.onesb
    sq = K.ep_sq
    P.act(sq[:, 0:w], o[:, 0:w], AF.Square)
    ps = psn()
    P.mm(ps[:, 0:w], onesb[:], sq[:, 0:w])
    rs = K.ep_rs
    P.ts(rs[:, 0:w], ps[:, 0:w], 1.0 / 128, ALU.mult, EPS, ALU.add)
    P.act(rs[:, 0:w], rs[:, 0:w], AF.Sqrt)
    P.recip(rs[:, 0:w], rs[:, 0:w])
    P.tt(o[:, 0:w], o[:, 0:w], rs[:, 0:w], ALU.mult)
    sz = psz_fn()
    P.stt(dest, o[:, 0:w], gcol, sz, ALU.mult, ALU.mult)


def even_layer(K, li):
    P, I, O = K.P, K.I, K.O
    XT, HT, OT, AFa, ABa, PS, psn, Scope = K.XT, K.HT, K.OT, K.AFa, K.ABa, K.PS, K.psn, K.Scope
    next_w, wv512, v3, ident, ones, onesb, trib = K.next_w, K.wv512, K.v3, K.ident, K.ones, K.onesb, K.trib
    cfg = K.cfg
    e = li // 2
    lam_init = 0.8 - 0.6 * math.exp(-0.3 * li)

    with Scope():
        GQK = AFa.alloc([256])
        P.load(GQK[:, 0:64], I["qng"][e].partition_broadcast(128))
        P.load(GQK[:, 64:128], I["qng"][e].partition_broadcast(128))
        P.load(GQK[:, 128:192], I["kng"][e].partition_broadcast(128))
        P.load(GQK[:, 192:256], I["kng"][e].partition_broadcast(128))
        LV = AFa.alloc([256])
        P.load(LV[:, :], I["lamv"][e].partition_broadcast(128))
        lp = AFa.alloc([128])
        lv4 = LV[:, :].rearrange("p (a b d) -> p a b d", a=2, b=2)
        P.tt(v3(lp[:, :], 64), lv4[:, :, 0, :], lv4[:, :, 1, :], ALU.mult)
        l2 = AFa.alloc([2])
        P.red(l2[:, :], v3(lp[:, :], 64))
        P.act(l2[:, :], l2[:, :], AF.Exp)
        NLAM = AFa.alloc([1])
        P.tt(NLAM[:, :], l2[:, 1:2], l2[:, 0:1], ALU.subtract)
        P.ts(NLAM[:, :], NLAM[:, :], -lam_init, ALU.add)
        SUBG = AFa.alloc([1])
        P.load(SUBG[:, :], I["sublng"][e])
        P.ts(SUBG[:, :], SUBG[:, :], 1.0 - lam_init, ALU.mult)
        ABIAS = AFa.alloc([64])
        P.load(ABIAS[:, :], I["c_abias"])
        QS = AFa.alloc([4, 128])
        KS = AFa.alloc([4, 128])
        VS = AFa.alloc([4, 128])
        ZS = AFa.alloc([4, 16])
        VST = AFa.alloc([4, 16])
        K.ep_sq = ABa.alloc([512])
        K.ep_rs = AFa.alloc([512])

        for h in range(4 if cfg.get("attn", True) else 0):
            wv = wv512(next_w(("ea", e, h)))
            with Scope():
                QT_ = ABa.alloc([T])
                KT_ = ABa.alloc([T])
                VTM = ABa.alloc([NT, 128])
                NI = 4
                nsq = [AFa.alloc([256]) for _ in range(NI)]
                nss = [AFa.alloc([4]) for _ in range(NI)]
                nqk = [AFa.alloc([256]) for _ in range(NI)]
                nvf = [AFa.alloc([128]) for _ in range(NI)]

                def tile_ops(n, bi):
                    rows = 128 if n < 16 else 16
                    c0 = n * 128
                    ps = psn()
                    for k in range(8):
                        P.mm(ps[0:rows, 0:384], HT[:, k, c0:c0 + rows], wv[:, k, 0:384], start=(k == 0), stop=(k == 7))
                    sq, ss, qkn, vf = nsq[bi], nss[bi], nqk[bi], nvf[bi]
                    P.act(sq[0:rows, :], ps[0:rows, 0:256], AF.Square)
                    P.red(ss[0:rows, :], v3(sq[0:rows, :], 64))
                    P.ts(ss[0:rows, :], ss[0:rows, :], 1.0 / 64, ALU.mult, EPS, ALU.add)
                    P.act(ss[0:rows, :], ss[0:rows, :], AF.Sqrt)
                    P.recip(ss[0:rows, :], ss[0:rows, :])
                    P.tt(v3(qkn[0:rows, :], 64), v3(ps[0:rows, 0:256], 64),
                         ss[0:rows, :].unsqueeze(2).to_broadcast([rows, 4, 64]), ALU.mult)
                    P.tt(qkn[0:rows, :], qkn[0:rows, :], GQK[0:rows, :], ALU.mult)
                    if n < 16:
                        P.store(O["nk_p"][e, c0:c0 + 128, h * 128:(h + 1) * 128], qkn[:, 128:256])
                        P.copy(vf[:, :], ps[:, 256:384], eng="act")
                        P.store(O["nv_p"][e, c0:c0 + 128, h * 128:(h + 1) * 128], vf[:, :])
                        P.copy(VTM[:, n, :], vf[:, :])
                        pst = psn()
                        P.tr(pst[:, 0:128], qkn[:, 0:128], ident[:])
                        P.tr(pst[:, 128:256], qkn[:, 128:256], ident[:])
                        P.copy(QT_[:, c0:c0 + 128], pst[:, 0:128], eng="act")
                        P.copy(KT_[:, c0:c0 + 128], pst[:, 128:256], eng="act")
                    else:
                        P.copy(QS[0:16, h, :], qkn[0:16, 0:128])
                        P.copy(KS[0:16, h, :], qkn[0:16, 128:256])
                        P.copy(VS[0:16, h, :], ps[0:16, 256:384], eng="act")
                        pst = psn()
                        P.tr(pst[:, 0:16], VS[0:16, h, :], ident[0:16, 0:16])
                        P.copy(VST[:, h, :], pst[:, 0:16])

                for n0 in range(0, 17, NI):
                    streams = []
                    for bi, n in enumerate(range(n0, min(n0 + NI, 17))):
                        K.p0, K.pn = 2 * bi, 2
                        cap = P.capture()
                        tile_ops(n, bi)
                        P.end_capture()
                        streams.append(cap)
                    K.p0, K.pn = 0, 8
                    P.merge(streams)
                ps = psn()
                for k in range(8):
                    P.mm(ps[:, 0:16], wv[:, k, 384:512], HT[:, k, T:TA], start=(k == 0), stop=(k == 7))
                P.act(ZS[:, h, :], ps[:, 0:16], AF.Silu)
                if cfg.get("attn_stage", 3) < 2:
                    continue
                K.p0, K.pn = 0, 4
                O1, O2, D1, D2 = PS[4], PS[5], PS[6], PS[7]
                Eb = [[ABa.alloc([512]) for _ in range(2)] for _ in range(2)]
                of = AFa.alloc([512])
                t1 = AFa.alloc([512])
                t2 = AFa.alloc([512])
                szb = AFa.alloc([512])
                it = 0
                for qb in range(4):
                    q0 = qb * 512
                    nkb = 4 * qb + 4
                    for kb in range(nkb):
                        j = kb - 4 * qb
                        cs = 128 * max(j, 0)
                        s1 = psn()
                        s2 = psn()
                        P.mm(s1[:, cs:512], KT_[0:64, kb * 128:(kb + 1) * 128], QT_[0:64, q0 + cs:q0 + 512])
                        P.mm(s2[:, cs:512], KT_[64:128, kb * 128:(kb + 1) * 128], QT_[64:128, q0 + cs:q0 + 512])
                        e1, e2 = Eb[it % 2]
                        it += 1
                        for sbk in range(cs // 128, 4):
                            d = (4 * qb + sbk) - kb
                            cc = slice(sbk * 128, (sbk + 1) * 128)
                            P.act(e1[:, cc], s1[:, cc], AF.Exp, bias=ABIAS[:, h * 16 + d:h * 16 + d + 1], scale=0.125)
                            P.act(e2[:, cc], s2[:, cc], AF.Exp, bias=ABIAS[:, h * 16 + d:h * 16 + d + 1], scale=0.125)
                        if j >= 0:
                            P.tt(e1[:, cs:cs + 128], e1[:, cs:cs + 128], trib[:], ALU.mult)
                            P.tt(e2[:, cs:cs + 128], e2[:, cs:cs + 128], trib[:], ALU.mult)
                        st, sp = (kb == 0), (kb == nkb - 1)
                        P.mm(O1[:, cs:512], VTM[:, kb, :], e1[:, cs:512], start=st, stop=sp)
                        P.mm(D1[:, cs:512], onesb[:], e1[:, cs:512], start=st, stop=sp)
                        P.mm(O2[:, cs:512], VTM[:, kb, :], e2[:, cs:512], start=st, stop=sp)
                        P.mm(D2[:, cs:512], onesb[:], e2[:, cs:512], start=st, stop=sp)
                    if cfg.get("attn_stage", 3) < 3:
                        continue
                    P.recip(t1[:, :], D1[:, :])
                    P.tt(t1[:, :], O1[:, :], t1[:, :], ALU.mult)
                    P.recip(t2[:, :], D2[:, :])
                    P.tt(t2[:, :], O2[:, :], t2[:, :], ALU.mult)
                    P.stt(of[:, :], t2[:, :], NLAM[:, 0:1], t1[:, :], ALU.mult, ALU.add)

                    def psz(q0=q0, wv=wv):
                        pz = psn()
                        for k in range(8):
                            P.mm(pz[:, :], wv[:, k, 384:512], HT[:, k, q0:q0 + 512], start=(k == 0), stop=(k == 7))
                        P.act(szb[:, :], pz[:, :], AF.Silu)
                        return szb[:, :]
                    rms_feature_gate(K, of, 512, SUBG[:, 0:1], psz, OT[:, h, q0:q0 + 512], "a")
                K.p0, K.pn = 0, 8

        if cfg.get("sample_attn", True) and cfg.get("attn", True):
            with Scope():
                PTB = AFa.alloc([256])
                ptb_i = PTB[:, :].bitcast(I32)
                P.load(ptb_i, I["pt"].partition_broadcast(128))
                PTF = AFa.alloc([256])
                P.copy(PTF[:, :], ptb_i)
                IO = AFa.alloc([1])
                P.load(IO[:, :], I["c_iota"])
                P.ts(PTF[:, :], PTF[:, :], 128.0, ALU.mult, IO[:, 0:1], ALU.add)
                if e > 0:
                    P.ts(PTF[:, :], PTF[:, :], float(e * NPHYS * 128), ALU.add)
                IDX = AFa.alloc([256])
                idx_i = IDX[:, :].bitcast(I32)
                P.copy(idx_i, PTF[:, :])
                ALB = AFa.alloc([NPG, 4])
                P.load(ALB[:, :, :], v3(I["c_alibis"], 4))
                SEL = AFa.alloc([128])
                sp_ = AFa.alloc([512])
                P.tt(sp_[0:16, :], QS[0:16, :, :].rearrange("p h d -> p (h d)"), KS[0:16, :, :].rearrange("p h d -> p (h d)"), ALU.mult)
                ES = AFa.alloc([8])
                P.red(ES[0:16, :], v3(sp_[0:16, :], 64))
                P.act(ES[0:16, :], ES[0:16, :], AF.Exp, scale=0.125)
                QB = AFa.alloc([512])
                KP = [AFa.alloc([512]) for _ in range(3)]
                VP = [AFa.alloc([512]) for _ in range(3)]
                prod = AFa.alloc([512])
                SC = AFa.alloc([NPG, 8])
                EE = AFa.alloc([NPG, 8])
                PP = AFa.alloc([NPG, 4])
                rsum = AFa.alloc([8])
                rinv = AFa.alloc([8])
                esb = AFa.alloc([8])
                psl = AFa.alloc([4])
                tv = AFa.alloc([4])
                OAS = AFa.alloc([4, 16])
                psO = PS[7]
                K.p0, K.pn = 0, 7
                ck = I["ck"].rearrange("e r c -> (e r) c")
                cvv = I["cvv"].rearrange("e r c -> (e r) c")
                for s in range(NS):
                    P.ts(SEL[0:16, :], ones[0:16, :], ident[0:16, s:s + 1], ALU.mult)
                    pq = psn()
                    P.mm(pq[:, :], SEL[0:16, :], QS[0:16, :, :].rearrange("p h d -> p (h d)"))
                    P.copy(QB[:, :], pq[:, :], eng="act")
                    for pg in range(NPG):
                        kp = KP[pg % 3]
                        P.gather(kp[:, :], ck, idx_i[:, s * NPG + pg:s * NPG + pg + 1])
                        P.tt(prod[:, :], kp[:, :], QB[:, :], ALU.mult)
                        P.red(SC[:, pg, :], v3(prod[:, :], 64))
                    sc4 = SC[:, :, :].rearrange("p g (h t) -> p g h t", t=2)
                    P.stt(sc4, sc4, 0.125, ALB[:, :, :].unsqueeze(3).to_broadcast([128, NPG, 4, 2]), ALU.mult, ALU.add)
                    P.act(EE[:, :, :], SC[:, :, :], AF.Exp)
                    P.red(rsum[:, :], EE[:, :, :].rearrange("p g e -> p e g"))
                    pt_ = psn()
                    P.mm(pt_[:, 0:8], ones[:, :], rsum[:, :], start=True, stop=False)
                    P.mm(pt_[:, 0:8], SEL[0:16, :], ES[0:16, :], start=False, stop=True)
                    P.mm(pt_[:, 8:16], SEL[0:16, :], ES[0:16, :])
                    P.recip(rinv[:, :], pt_[:, 0:8])
                    P.tt(esb[:, :], pt_[:, 8:16], rinv[:, :], ALU.mult)
                    es2 = esb[:, :].rearrange("p (h t) -> p h t", t=2)
                    P.stt(psl[:, :], es2[:, :, 1], K.NLAM[:, 0:1] if False else NLAM[:, 0:1], es2[:, :, 0], ALU.mult, ALU.add)
                    P.tt(EE[:, :, :], EE[:, :, :], rinv[:, :].unsqueeze(1).to_broadcast([128, NPG, 8]), ALU.mult)
                    ee4 = EE[:, :, :].rearrange("p g (h t) -> p g h t", t=2)
                    P.stt(PP[:, :, :], ee4[:, :, :, 1], NLAM[:, 0:1], ee4[:, :, :, 0], ALU.mult, ALU.add)
                    for pg in range(NPG):
                        vp = VP[pg % 3]
                        P.gather(vp[:, :], cvv, idx_i[:, s * NPG + pg:s * NPG + pg + 1])
                        for h in range(4):
                            P.mm(psO[:, h * 16 + s:h * 16 + s + 1], vp[:, h * 128:(h + 1) * 128], PP[:, pg, h:h + 1],
                                 start=(pg == 0), stop=(pg == NPG - 1))
                    P.tt(tv[:, :], VST[:, :, s], psl[:, :], ALU.mult)
                    P.tt(OAS[:, :, s], v3(psO[:, 0:64], 16)[:, :, s], tv[:, :], ALU.add)
                K.p0, K.pn = 0, 8
                P.store(O["nk_s"][e].rearrange("s (h d) -> s h d", d=128), KS[0:16, :, :])
                P.store(O["nv_s"][e].rearrange("s (h d) -> s h d", d=128), VS[0:16, :, :])
                for h in range(4):
                    rms_feature_gate(K, OAS[:, h, :], 16, SUBG[:, 0:1], (lambda h=h: ZS[:, h, :]), OT[:, h, T:TA], "as")

    if not (cfg.get("sample_attn", True) and cfg.get("attn", True)):
        for h_ in range(4):
            P.memset(OT[:, h_, T:TA], 0.0)
    if cfg.get("gdn", True):
        gdn_heads(K, li)


def ret_heads(K, li):
    P, I, O = K.P, K.I, K.O
    XT, HT, OT, AFa, ABa, PS, psn, Scope = K.XT, K.HT, K.OT, K.AFa, K.ABa, K.PS, K.psn, K.Scope
    next_w, wv512, v3, ident, ones, onesb = K.next_w, K.wv512, K.v3, K.ident, K.ones, K.onesb
    o_ = li // 2
    with Scope():
        DTC = AFa.alloc([4, 128])
        P.load(DTC[:, :, :], I["c_dtc"].rearrange("h j i -> j h i"))
        KDC = AFa.alloc([4])
        EGC = AFa.alloc([4])
        P.load(KDC[:, :], I["c_kdc"])
        P.load(EGC[:, :], I["c_egc"])
        GND = AFa.alloc([128])
        P.load(GND[:, :], I["gnd"][o_].partition_broadcast(128))
        K.MASK16 = AFa.alloc([16, 16])
        P.load(K.MASK16[:, :, :], v3(I["c_mask16"], 16))

        def alloc_bufs():
            B = {}
            B["S"] = AFa.alloc([128])
            B["QT"] = AFa.alloc([512])
            B["KTf"] = AFa.alloc([512])
            B["SZ"] = AFa.alloc([512])
            B["mats"] = [AFa.alloc([128]) for _ in range(5)]
            B["fo"] = (AFa.alloc([128]), AFa.alloc([2]), AFa.alloc([128]))
            return B

        def head_prompt(h, wv, B):
            g128 = math.exp(128.0 * LOG_GAMMA[h])
            S, QT, KTf, SZ = B["S"], B["QT"], B["KTf"], B["SZ"]
            V, KD, AQ, O1, OO = B["mats"]
            K.fo_sq, K.fo_ss, K.fo_on = B["fo"]
            P.memset(S[:, :], 0.0)
            for tb in range(4):
                c0 = tb * 512
                for j, dst in ((0, QT), (1, KTf)):
                    ps = psn()
                    for k in range(8):
                        P.mm(ps[:, :], wv[:, k, j * 128:(j + 1) * 128], HT[:, k, c0:c0 + 512], start=(k == 0), stop=(k == 7))
                    P.copy(dst[:, :], ps[:, :], eng=("act" if j else "dve"))
                ps = psn()
                for k in range(8):
                    P.mm(ps[:, :], wv[:, k, 384:512], HT[:, k, c0:c0 + 512], start=(k == 0), stop=(k == 7))
                P.act(SZ[:, :], ps[:, :], AF.Silu)
                for tl in range(4):
                    n = tb * 4 + tl
                    cc = slice(tl * 128, (tl + 1) * 128)
                    ps = psn()
                    for k in range(8):
                        P.mm(ps[:, 0:256], HT[:, k, n * 128:(n + 1) * 128], wv[:, k, 128:384], start=(k == 0), stop=(k == 7))
                    P.copy(V[:, :], ps[:, 128:256], eng="act")
                    P.ts(KD[:, :], ps[:, 0:128], KDC[:, h:h + 1], ALU.mult)
                    pa = psn()
                    P.mm(pa[:, 0:128], KTf[:, cc], QT[:, cc])
                    P.tt(AQ[:, :], pa[:, 0:128], DTC[:, h, :], ALU.mult)
                    p2 = psn()
                    P.mm(p2[:, 0:128], QT[:, cc], S[:, :])
                    P.ts(O1[:, :], p2[:, 0:128], EGC[:, h:h + 1], ALU.mult)
                    p4 = psn()
                    P.mm(p4[:, 0:128], AQ[:, :], V[:, :])
                    P.tt(OO[:, :], p4[:, 0:128], O1[:, :], ALU.add)
                    p3 = psn()
                    P.mm(p3[:, 0:128], KD[:, :], V[:, :])
                    P.stt(S[:, :], S[:, :], g128, p3[:, 0:128], ALU.mult, ALU.add)
                    finish_o_ln(K, OO[:, :], 128, GND, SZ[:, cc], OT[:, 4 + h, n * 128:(n + 1) * 128])
            P.store(O["rt_p"][o_, h], S[:, :])

        def head_sample(h, wv, B):
            g1 = math.exp(LOG_GAMMA[h])
            SZ = B["SZ"]
            K.fo_sq, K.fo_ss, K.fo_on = B["fo"]
            QS_ = [AFa.alloc([16]) for _ in range(3)]
            for j in range(3):
                ps = psn()
                for k in range(8):
                    P.mm(ps[:, 0:16], wv[:, k, j * 128:(j + 1) * 128], HT[:, k, T:TA], start=(k == 0), stop=(k == 7))
                if j == 1:
                    P.ts(QS_[j][:, :], ps[:, 0:16], 128.0 ** -0.5, ALU.mult)
                else:
                    P.copy(QS_[j][:, :], ps[:, 0:16], eng="act")
            ps = psn()
            for k in range(8):
                P.mm(ps[:, 0:16], wv[:, k, 384:512], HT[:, k, T:TA], start=(k == 0), stop=(k == 7))
            P.act(SZ[:, 0:16], ps[:, 0:16], AF.Silu)
            sample_state_step(K, QS_[0][:, :], QS_[1][:, :], QS_[2][:, :], g1, None, None, g1,
                              (lambda s, h=h: I["sret"][o_, s, h]), (lambda s, h=h: O["rt_s"][o_, s, h]),
                              GND, SZ[:, 0:16], OT[:, 4 + h, T:TA], False, True)

        NPAR = 2
        for hp in range(0, 4, NPAR):
            wvs = [wv512(next_w(("od", o_, hp + i), prefetch=(i == 0))) for i in range(NPAR)]
            with Scope():
                Bs = [alloc_bufs() for _ in range(NPAR)]
                streams = []
                for i in range(NPAR):
                    K.p0, K.pn = (8 // NPAR) * i, 8 // NPAR
                    cap = P.capture()
                    head_prompt(hp + i, wvs[i], Bs[i])
                    P.end_capture()
                    streams.append(cap)
                K.p0, K.pn = 0, 8
                P.merge(streams)
                for i in range(NPAR):
                    with Scope():
                        head_sample(hp + i, wvs[i], Bs[i])


def s5_mixer(K, li):
    P, I, O = K.P, K.I, K.O
    XT, HT, OT, AFa, ABa, PS, psn, Scope = K.XT, K.HT, K.OT, K.AFa, K.ABa, K.PS, K.psn, K.Scope
    v3, ident, ones, onesb, wv512 = K.v3, K.ident, K.ones, K.onesb, K.wv512
    o_ = li // 2
    TC = 512
    with Scope():
        WU = ABa.alloc([4096])
        P.load(WU[:, :], I["wino"][o_, 0], eng="pool")
        wu = wv512(WU)
        ARE, AIM, LDT = AFa.alloc([16]), AFa.alloc([16]), AFa.alloc([16])
        P.load(ARE[:, :], I["s5are"][o_].rearrange("(t l) -> l t", l=128), slow=True)
        P.load(AIM[:, :], I["s5aim"][o_].rearrange("(t l) -> l t", l=128), slow=True)
        ldt2 = I["s5ldt"][o_].rearrange("o (t two) -> o two t", two=2)
        P.load(LDT[0:64, :], ldt2[:, 0, :].partition_broadcast(64), slow=True)
        P.load(LDT[64:128, :], ldt2[:, 1, :].partition_broadcast(64), slow=True)
        P.act(LDT[:, :], LDT[:, :], AF.Exp)
        LRE, TH, RR = AFa.alloc([16]), AFa.alloc([16]), AFa.alloc([16])
        P.tt(LRE[:, :], ARE[:, :], LDT[:, :], ALU.mult)
        P.tt(TH[:, :], AIM[:, :], LDT[:, :], ALU.mult)
        P.act(RR[:, :], LRE[:, :], AF.Exp)
        HPI = AFa.alloc([1])
        P.memset(HPI[:, :], math.pi / 2)
        C1, S1, t1_, t2_ = AFa.alloc([16]), AFa.alloc([16]), AFa.alloc([16]), AFa.alloc([16])
        P.act(S1[:, :], TH[:, :], AF.Sin, scale=1.0 / 16)
        P.act(C1[:, :], TH[:, :], AF.Sin, scale=1.0 / 16, bias=HPI[:, 0:1])
        for _ in range(4):
            P.tt(t1_[:, :], C1[:, :], C1[:, :], ALU.mult)
            P.tt(t2_[:, :], S1[:, :], S1[:, :], ALU.mult)
            P.stt(S1[:, :], C1[:, :], 2.0, S1[:, :], ALU.mult, ALU.mult)
            P.tt(C1[:, :], t1_[:, :], t2_[:, :], ALU.subtract)
        LBR, LBI, NLBI = AFa.alloc([16]), AFa.alloc([16]), AFa.alloc([16])
        P.tt(LBR[:, :], RR[:, :], C1[:, :], ALU.mult)
        P.tt(LBI[:, :], RR[:, :], S1[:, :], ALU.mult)
        P.ts(NLBI[:, :], LBI[:, :], -1.0, ALU.mult)
        DEN, FRE, FIM, LM1 = AFa.alloc([16]), AFa.alloc([16]), AFa.alloc([16]), AFa.alloc([16])
        P.tt(DEN[:, :], ARE[:, :], ARE[:, :], ALU.mult)
        P.tt(t1_[:, :], AIM[:, :], AIM[:, :], ALU.mult)
        P.tt(DEN[:, :], DEN[:, :], t1_[:, :], ALU.add)
        P.recip(DEN[:, :], DEN[:, :])
        P.ts(LM1[:, :], LBR[:, :], -1.0, ALU.add)
        P.tt(FRE[:, :], LM1[:, :], ARE[:, :], ALU.mult)
        P.tt(t1_[:, :], LBI[:, :], AIM[:, :], ALU.mult)
        P.tt(FRE[:, :], FRE[:, :], t1_[:, :], ALU.add)
        P.tt(FRE[:, :], FRE[:, :], DEN[:, :], ALU.mult)
        P.tt(FIM[:, :], LBI[:, :], ARE[:, :], ALU.mult)
        P.tt(t1_[:, :], LM1[:, :], AIM[:, :], ALU.mult)
        P.tt(FIM[:, :], FIM[:, :], t1_[:, :], ALU.subtract)
        P.tt(FIM[:, :], FIM[:, :], DEN[:, :], ALU.mult)
        BRE, BIM, BBR, BBI, tb_ = [AFa.alloc([16, 16]) for _ in range(5)]
        P.load(BRE[:, :, :], I["s5bre"][o_].rearrange("(t l) c -> l t c", l=128))
        P.load(BIM[:, :, :], I["s5bim"][o_].rearrange("(t l) c -> l t c", l=128))
        frb = FRE[:, :].unsqueeze(2).to_broadcast([128, 16, 16])
        fib = FIM[:, :].unsqueeze(2).to_broadcast([128, 16, 16])
        P.tt(BBR[:, :, :], BRE[:, :, :], frb, ALU.mult)
        P.tt(tb_[:, :, :], BIM[:, :, :], fib, ALU.mult)
        P.tt(BBR[:, :, :], BBR[:, :, :], tb_[:, :, :], ALU.subtract)
        P.tt(BBI[:, :, :], BRE[:, :, :], fib, ALU.mult)
        P.tt(tb_[:, :, :], BIM[:, :, :], frb, ALU.mult)
        P.tt(BBI[:, :, :], BBI[:, :, :], tb_[:, :, :], ALU.add)
        DSK = AFa.alloc([4])
        P.load(DSK[:, :], I["s5d"][o_].rearrange("(c p) -> p c", p=128), slow=True)
        XPall = AFa.alloc([16, 2])
        UT = AFa.alloc([4, TC])
        UTs = AFa.alloc([16])
        EC, ESn = AFa.alloc([TC]), AFa.alloc([TC])
        W = [AFa.alloc([TC]) for _ in range(4)]
        Bm, Cin = AFa.alloc([128]), AFa.alloc([128])
        CCr, CCi, CMK = AFa.alloc([64]), AFa.alloc([64]), AFa.alloc([8])
        P.load(CMK[:, :], I["c_cmask"])
        BLr, BLi, CLr, CLi = [AFa.alloc([128]) for _ in range(4)]
        XP = AFa.alloc([2])
        nsm = AFa.alloc([1])
        xs0 = AFa.alloc([2, 16])
        xtm = AFa.alloc([128])
        xsn = AFa.alloc([2, 16])
        xso = AFa.alloc([128])
        yt = AFa.alloc([TC])
        YA = [PS[0], PS[1], PS[2], PS[3]]
        YS = PS[4]
        K.p0, K.pn = 5, 3
        for ch in range(4):
            for tc in range(4):
                ps = psn()
                for k in range(8):
                    P.mm(ps[:, :], wu[:, k, ch * 128:(ch + 1) * 128], HT[:, k, tc * TC:(tc + 1) * TC], start=(k == 0), stop=(k == 7))
                P.copy(UT[:, tc, :], ps[:, :], eng="act")
            ps = psn()
            for k in range(8):
                P.mm(ps[:, 0:16], wu[:, k, ch * 128:(ch + 1) * 128], HT[:, k, T:TA], start=(k == 0), stop=(k == 7))
            P.copy(UTs[:, :], ps[:, 0:16], eng="act")
            P.load(CCr[:, :], I["s5cre"][o_][ch * 128:(ch + 1) * 128, :])
            P.load(CCi[:, :], I["s5cim"][o_][ch * 128:(ch + 1) * 128, :])
            for li_ in range(4):
                lt = ch * 4 + li_
                off = li_ * 32
                for src, dst, neg in ((BBR, BLr, False), (BBI, BLi, False)):
                    P.memset(Bm[:, :], 0.0)
                    P.copy(Bm[0:64, off:off + 16], src[0:64, lt, :])
                    P.copy(Bm[64:128, off + 16:off + 32], src[64:128, lt, :])
                    pt = psn()
                    P.tr(pt[:, 0:128], Bm[:, :], ident[:])
                    P.copy(dst[:, :], pt[:, 0:128], eng="act")
                for nm, dst, neg in (("s5cre", CLr, False), ("s5cim", CLi, True)):
                    ccs = CCr if nm == "s5cre" else CCi
                    P.ts(Cin[:, 0:64], ccs[:, :], CMK[:, li_ * 2:li_ * 2 + 1], ALU.mult)
                    P.ts(Cin[:, 64:128], ccs[:, :], CMK[:, li_ * 2 + 1:li_ * 2 + 2], ALU.mult)
                    pt = psn()
                    P.tr(pt[:, 0:128], Cin[:, :], ident[:])
                    if neg:
                        P.ts(dst[:, :], pt[:, 0:128], -1.0, ALU.mult)
                    else:
                        P.copy(dst[:, :], pt[:, 0:128], eng="act")
                P.copy(EC[:, 0:1], C1[:, lt:lt + 1])
                P.copy(ESn[:, 0:1], S1[:, lt:lt + 1])
                m = 1
                while m < TC:
                    cm, sm = EC[:, m - 1:m], ESn[:, m - 1:m]
                    P.ts(nsm[:, :], sm, -1.0, ALU.mult)
                    P.ts(EC[:, m:2 * m], EC[:, 0:m], cm, ALU.mult)
                    P.stt(EC[:, m:2 * m], ESn[:, 0:m], nsm[:, 0:1], EC[:, m:2 * m], ALU.mult, ALU.add)
                    P.ts(ESn[:, m:2 * m], EC[:, 0:m], sm, ALU.mult)
                    P.stt(ESn[:, m:2 * m], ESn[:, 0:m], cm, ESn[:, m:2 * m], ALU.mult, ALU.add)
                    m *= 2
                Rb = RR[:, lt:lt + 1].to_broadcast([128, TC])
                P.memset(XP[:, :], 0.0)
                for tc in range(4):
                    pbr = psn()
                    P.mm(pbr[:, :], BLr[:, :], UT[:, tc, :])
                    pbi = psn()
                    P.mm(pbi[:, :], BLi[:, :], UT[:, tc, :])
                    vr, vi, zr, zi = W
                    P.tt(vr[:, :], pbr[:, :], EC[:, :], ALU.mult)
                    P.tt(zr[:, :], pbi[:, :], ESn[:, :], ALU.mult)
                    P.tt(vr[:, :], vr[:, :], zr[:, :], ALU.add)
                    P.tt(vi[:, :], pbi[:, :], EC[:, :], ALU.mult)
                    P.tt(zr[:, :], pbr[:, :], ESn[:, :], ALU.mult)
                    P.tt(vi[:, :], vi[:, :], zr[:, :], ALU.subtract)
                    P.scan(zr[:, :], Rb, vr[:, :], XP[:, 0:1])
                    P.scan(zi[:, :], Rb, vi[:, :], XP[:, 1:2])
                    P.tt(vr[:, :], zr[:, :], EC[:, :], ALU.mult)
                    P.tt(yt[:, :], zi[:, :], ESn[:, :], ALU.mult)
                    P.tt(vr[:, :], vr[:, :], yt[:, :], ALU.subtract)
                    P.tt(vi[:, :], zr[:, :], ESn[:, :], ALU.mult)
                    P.tt(yt[:, :], zi[:, :], EC[:, :], ALU.mult)
                    P.tt(vi[:, :], vi[:, :], yt[:, :], ALU.add)
                    P.copy(XP[:, 0:1], vr[:, TC - 1:TC])
                    P.copy(XP[:, 1:2], vi[:, TC - 1:TC])
                    P.mm(YA[tc][:, :], CLr[:, :], vr[:, :], start=(li_ == 0), stop=False)
                    P.mm(YA[tc][:, :], CLi[:, :], vi[:, :], start=False, stop=(li_ == 3))
                P.copy(XPall[:, lt, :], XP[:, :])
                for ri, nm in ((0, "scre"), (1, "scim")):
                    P.load(xtm[0:16, :], I[nm][o_][:, lt * 128:(lt + 1) * 128])
                    pt = psn()
                    P.tr(pt[:, 0:16], xtm[0:16, :], ident[0:16, 0:16])
                    P.copy(xs0[:, ri, :], pt[:, 0:16])
                pbr = psn()
                P.mm(pbr[:, 0:16], BLr[:, :], UTs[:, :])
                P.mm(pbr[:, 16:32], BLi[:, :], UTs[:, :])
                P.ts(xsn[:, 0, :], xs0[:, 0, :], LBR[:, lt:lt + 1], ALU.mult)
                P.stt(xsn[:, 0, :], xs0[:, 1, :], NLBI[:, lt:lt + 1], xsn[:, 0, :], ALU.mult, ALU.add)
                P.tt(xsn[:, 0, :], xsn[:, 0, :], pbr[:, 0:16], ALU.add)
                P.ts(xsn[:, 1, :], xs0[:, 1, :], LBR[:, lt:lt + 1], ALU.mult)
                P.stt(xsn[:, 1, :], xs0[:, 0, :], LBI[:, lt:lt + 1], xsn[:, 1, :], ALU.mult, ALU.add)
                P.tt(xsn[:, 1, :], xsn[:, 1, :], pbr[:, 16:32], ALU.add)
                P.mm(YS[:, 0:16], CLr[:, :], xsn[:, 0, :], start=(li_ == 0), stop=False)
                P.mm(YS[:, 0:16], CLi[:, :], xsn[:, 1, :], start=False, stop=(li_ == 3))
                for ri, nm in ((0, "s5r_s"), (1, "s5i_s")):
                    pt = psn()
                    P.tr(pt[0:16, 0:128], xsn[:, ri, :], ident[:])
                    P.copy(xso[0:16, :], pt[0:16, 0:128])
                    P.store(O[nm][o_][:, lt * 128:(lt + 1) * 128], xso[0:16, :])
            for tc in range(4):
                P.stt(yt[:, :], UT[:, tc, :], DSK[:, ch:ch + 1], YA[tc][:, :], ALU.mult, ALU.add)
                P.act(OT[:, ch, tc * TC:(tc + 1) * TC], yt[:, :], AF.Gelu_apprx_tanh)
            P.stt(yt[:, 0:16], UTs[:, :], DSK[:, ch:ch + 1], YS[:, 0:16], ALU.mult, ALU.add)
            P.act(OT[:, ch, T:TA], yt[:, 0:16], AF.Gelu_apprx_tanh)
        P.store(O["s5r_p"][o_].rearrange("(t l) -> l t", l=128), XPall[:, :, 0], slow=True)
        P.store(O["s5i_p"][o_].rearrange("(t l) -> l t", l=128), XPall[:, :, 1], slow=True)
        K.p0, K.pn = 0, 8
    with Scope():
        WZ = ABa.alloc([4096])
        WG = ABa.alloc([2048])
        P.load(WZ[:, :], I["wino"][o_, 1], eng="pool")
        P.load(WG[:, :], I["wglu"][o_], eng="pool")
        wz = wv512(WZ)
        wg = WG[:, :].rearrange("p (k c) -> p k c", c=512)
        sg = [AFa.alloc([512]) for _ in range(4)]
        szb = AFa.alloc([512])
        for (c0, w) in BLOCKS:
            pgs = []
            for cp in range(4):
                pg = PS[cp]
                for k in range(4):
                    P.mm(pg[:, 0:w], wg[:, k, cp * 128:(cp + 1) * 128], OT[:, k, c0:c0 + w], start=(k == 0), stop=(k == 3))
                pgs.append(pg)
            for cp in range(4):
                P.act(sg[cp][:, 0:w], pgs[cp][:, 0:w], AF.Sigmoid)
            for cp in range(4):
                pz = PS[4 + cp]
                for k in range(8):
                    P.mm(pz[:, 0:w], wz[:, k, cp * 128:(cp + 1) * 128], HT[:, k, c0:c0 + w], start=(k == 0), stop=(k == 7))
                P.act(szb[:, 0:w], pz[:, 0:w], AF.Silu)
                P.tt(sg[cp][:, 0:w], sg[cp][:, 0:w], szb[:, 0:w], ALU.mult)
                P.tt(OT[:, cp, c0:c0 + w], OT[:, cp, c0:c0 + w], sg[cp][:, 0:w], ALU.mult)


def odd_layer(K, li):
    if K.cfg.get("s5", True):
        s5_mixer(K, li)
    if K.cfg.get("ret", True):
        ret_heads(K, li)


def build(cfg=None):
    cfg = cfg or {}
    nlayers = cfg.get("layers", 4)
    nc = bass.Bass("TRN2", target_bir_lowering=False)
    specs = INPUT_SPECS
    if cfg.get("small_cache"):
        specs = [(n, ((2, 256, 512) if n in ("ck", "cvv") else s), dt) for n, s, dt in INPUT_SPECS]
    I = {n: nc.dram_tensor(n, list(s), dt, kind="ExternalInput").ap() for n, s, dt in specs}
    O = {n: nc.dram_tensor(n, list(s), F32, kind="ExternalOutput").ap() for n, s in OUTPUT_SPECS}
    if cfg.get("dbg"):
        O["dbg_xt"] = nc.dram_tensor("dbg_xt", [128, 8 * TA], F32, kind="ExternalOutput").ap()
        O["dbg_ht"] = nc.dram_tensor("dbg_ht", [128, 8 * TA], BF16, kind="ExternalOutput").ap()
        O["dbg_ot"] = nc.dram_tensor("dbg_ot", [128, 8 * TA], BF16, kind="ExternalOutput").ap()
    K = Ctx()
    with ExitStack() as es:
        P = Prog(nc, es)
        P.limit = cfg.get("max_ops")
        if cfg.get("trace"):
            P.trace = []
        K.P = P

        def sbt(name, shape, dt=F32):
            return es.enter_context(nc.sbuf_tensor(name, shape, dt))

        XT = sbt("XT", [128, 8, TA])
        HT = sbt("HT", [128, 8, TA], BF16)
        OT = sbt("OT", [128, 8, TA], BF16)
        WB = [sbt("WB0", [128, 4096], BF16), sbt("WB1", [128, 4096], BF16)]
        AFa = Arena(sbt("AFa", [128, NF]), NF)
        ABa = Arena(sbt("ABa", [128, NB], BF16), NB)
        ident = sbt("ident", [128, 128])
        ones = sbt("ones", [128, 128])
        onesb = sbt("onesb", [128, 128], BF16)
        trib = sbt("trib", [128, 128], BF16)
        CT = sbt("CT", [128, 8, 17], BF16)
        MODT = sbt("MODT", [128, 24, 17])
        GS = sbt("GS", [128, 8, 17])
        BADA = sbt("BADA", [128, 24])
        NG = sbt("NG", [128, 8])
        PS = [es.enter_context(nc.psum_tensor("ps%d" % i, [128, 512], F32)) for i in range(8)]
        K.pi, K.p0, K.pn = 0, 0, 8

        def psn():
            b = PS[K.p0 + (K.pi % K.pn)]
            K.pi += 1
            return b

        class Scope:
            def __enter__(self):
                AFa.stack.append(AFa.top)
                ABa.stack.append(ABa.top)

            def __exit__(self, *a):
                P.barrier()
                AFa.top = AFa.stack.pop()
                ABa.top = ABa.stack.pop()
                return False

        def v3(ap, b):
            return ap.rearrange("p (a b) -> p a b", b=b)

        wlist = []
        for li in range(nlayers):
            for blk in range(6):
                wlist.append((("ada", li, blk), I["wada"][li, blk], 4096))
            if li % 2 == 0:
                e = li // 2
                if cfg.get("attn", True):
                    for h in range(4):
                        wlist.append((("ea", e, h), I["wine"][e, h], 4096))
                if cfg.get("gdn", True):
                    wlist.append((("wab", e), I["wab"][e], 64))
                    for h in range(4):
                        wlist.append((("eb", e, h), I["wine"][e, 4 + h], 4096))
                for blk in range(2):
                    wlist.append((("eo", e, blk), I["woute"][e, blk], 4096))
            else:
                o = li // 2
                if cfg.get("ret", True):
                    for h in range(4):
                        wlist.append((("od", o, h), I["wino"][o, 2 + h], 4096))
                for blk in range(2):
                    wlist.append((("oo", o, blk), I["wouto"][o, blk], 4096))
        K.wi, K.wissued = 0, 0

        def w_issue(i):
            tag, src, n = wlist[i]
            P.load(WB[i % 2][:, 0:n], src, eng="pool")

        def next_w(tag, prefetch=True):
            i = K.wi
            assert wlist[i][0] == tag, (wlist[i][0], tag)
            if K.wissued <= i:
                w_issue(i)
                K.wissued = i + 1
            if prefetch and i + 1 < len(wlist) and K.wissued <= i + 1:
                w_issue(i + 1)
                K.wissued = i + 2
            K.wi += 1
            return WB[i % 2]

        def wv512(wb):
            return wb[:, 0:4096].rearrange("p (k c) -> p k c", c=512)

        P.load(ident[:], I["c_ident"])
        P.load(ones[:], I["c_ones"])
        P.load(onesb[:], I["c_onesb"])
        P.load(trib[:], I["c_trib"])
        with Scope():
            xin = [AFa.alloc([1024]) for _ in range(2)]
            for n in range(17):
                rows = 128 if n < 16 else 16
                src = I["xp"][n * 128:(n + 1) * 128, :] if n < 16 else I["xs"]
                xt = xin[n % 2]
                P.load(xt[0:rows, :], src)
                for half in range(2):
                    ps = psn()
                    for j in range(4):
                        k = half * 4 + j
                        P.tr(ps[:, j * 128:j * 128 + rows], xt[0:rows, k * 128:(k + 1) * 128], ident[0:rows, 0:rows])
                    P.copy(XT[:, half * 4:(half + 1) * 4, n * 128:n * 128 + rows], v3(ps[:, :], 128)[:, :, 0:rows],
                           eng=("act" if half else "dve"))
            cvt = AFa.alloc([1024])
            P.load(cvt[0:17, :], I["cv"])
            P.act(cvt[0:17, :], cvt[0:17, :], AF.Silu)
            ps = psn()
            for k in range(8):
                P.tr(ps[:, k * 32:k * 32 + 17], cvt[0:17, k * 128:(k + 1) * 128], ident[0:17, 0:17])
            P.copy(CT[:], v3(ps[:, 0:256], 32)[:, :, 0:17])

        def modulation(li):
            P.load(BADA[:], I["bada"][li].rearrange("(c p) -> p c", p=128), slow=True)
            P.load(NG[:], I["normg"][li].rearrange("(c p) -> p c", p=128), slow=True)
            for blk in range(6):
                wv = wv512(next_w(("ada", li, blk)))
                ps = psn()
                for j in range(4):
                    for k in range(8):
                        P.mm(ps[:, j * 32:j * 32 + 17], wv[:, k, j * 128:(j + 1) * 128], CT[:, k, :],
                             start=(k == 0), stop=(k == 7))
                P.tt(MODT[:, blk * 4:(blk + 1) * 4, :], v3(ps[:, 0:128], 32)[:, :, 0:17],
                     BADA[:, blk * 4:(blk + 1) * 4].unsqueeze(2).to_broadcast([128, 4, 17]), ALU.add)
            P.ts(GS[:], MODT[:, 8:16, :], 1.0, ALU.add)
            P.tt(GS[:], GS[:], NG[:].unsqueeze(2).to_broadcast([128, 8, 17]), ALU.mult)

        def norm_ht():
            with Scope():
                RS = AFa.alloc([TA])
                sqb = [ABa.alloc([512]) for _ in range(2)]
                for (c0, w) in BLOCKS:
                    ps = psn()
                    for k in range(8):
                        sq = sqb[k % 2]
                        P.act(sq[:, 0:w], XT[:, k, c0:c0 + w], AF.Square)
                        P.mm(ps[:, 0:w], onesb[:], sq[:, 0:w], start=(k == 0), stop=(k == 7))
                    P.ts(RS[:, c0:c0 + w], ps[:, 0:w], 1.0 / D, ALU.mult, EPS, ALU.add)
                    P.act(RS[:, c0:c0 + w], RS[:, c0:c0 + w], AF.Sqrt)
                    P.recip(RS[:, c0:c0 + w], RS[:, c0:c0 + w])
                tmp = [AFa.alloc([512]) for _ in range(2)]
                i = 0
                for (c0, w) in BLOCKS:
                    for k in range(8):
                        t = tmp[i % 2]
                        i += 1
                        P.tt(t[:, 0:w], XT[:, k, c0:c0 + w], RS[:, c0:c0 + w], ALU.mult)
                        if c0 < T:
                            P.act(HT[:, k, c0:c0 + w], t[:, 0:w], AF.Identity, bias=MODT[:, k, 0:1], scale=GS[:, k, 0:1])
                        else:
                            P.tt(t[:, 0:w], t[:, 0:w], GS[:, k, 1:17], ALU.mult)
                            P.tt(HT[:, k, c0:c0 + w], t[:, 0:w], MODT[:, k, 1:17], ALU.add)

        def out_proj(tag, idx):
            with Scope():
                ts_ = AFa.alloc([16])
                for blk in range(2):
                    wv = wv512(next_w((tag, idx, blk)))
                    for j in range(4):
                        fc = blk * 4 + j
                        for (c0, w) in BLOCKS:
                            ps = psn()
                            for k in range(8):
                                P.mm(ps[:, 0:w], wv[:, k, j * 128:(j + 1) * 128], OT[:, k, c0:c0 + w],
                                     start=(k == 0), stop=(k == 7))
                            if c0 < T:
                                P.stt(XT[:, fc, c0:c0 + w], ps[:, 0:w], MODT[:, 16 + fc, 0:1], XT[:, fc, c0:c0 + w],
                                      ALU.mult, ALU.add)
                            else:
                                P.tt(ts_[:, :], ps[:, 0:16], MODT[:, 16 + fc, 1:17], ALU.mult)
                                P.tt(XT[:, fc, c0:c0 + 16], XT[:, fc, c0:c0 + 16], ts_[:, :], ALU.add)

        def final_out():
            with Scope():
                yb = [AFa.alloc([1024]) for _ in range(2)]
                for n in range(17):
                    rows = 128 if n < 16 else 16
                    yt = yb[n % 2]
                    for half in range(2):
                        ps = psn()
                        for j in range(4):
                            k = half * 4 + j
                            P.tr(ps[0:rows, j * 128:(j + 1) * 128], XT[:, k, n * 128:n * 128 + rows], ident[:])
                        P.copy(yt[0:rows, half * 512:(half + 1) * 512], ps[0:rows, :], eng=("act" if half else "dve"))
                    dst = O["y_p"][n * 128:(n + 1) * 128, :] if n < 16 else O["y_s"]
                    P.store(dst, yt[0:rows, :])

        def dbg_dump():
            if cfg.get("dbg"):
                P.store(O["dbg_xt"], XT[:].rearrange("p k t -> p (k t)"))
                P.store(O["dbg_ht"], HT[:].rearrange("p k t -> p (k t)"))
                P.store(O["dbg_ot"], OT[:].rearrange("p k t -> p (k t)"))

        K.__dict__.update(locals())
        for li in range(nlayers):
            modulation(li)
            norm_ht()
            if cfg.get("zero_ot", False) or True:
                pass
            if cfg.get("mods_only"):
                continue
            if not (cfg.get("attn", True) and cfg.get("gdn", True) and cfg.get("s5", True) and cfg.get("ret", True)):
                for k_ in range(8):
                    for (c0_, w_) in BLOCKS:
                        P.memset(OT[:, k_, c0_:c0_ + w_], 0.0)
            if li % 2 == 0:
                even_layer(K, li)
                out_proj("eo", li // 2)
            else:
                odd_layer(K, li)
                out_proj("oo", li // 2)
        final_out()
        dbg_dump()
        P.barrier()
        P.emit()
    K.n_instr = P.n_instr
    K.peaks = (AFa.peak, ABa.peak)
    return nc, K


def _wblocks(w, col_lists):
    kd = w.shape[0] // 128
    out = []
    for cols in col_lists:
        blk = w[:, cols].reshape(kd, 128, len(cols)).transpose(1, 0, 2).reshape(128, kd * len(cols))
        out.append(blk)
    return np.ascontiguousarray(np.stack(out, 0))


def _even_cols():
    lists = []
    for h in range(4):
        lists.append(np.concatenate([np.arange(h * 128, (h + 1) * 128) + off for off in (0, 512, 1024, 1536)]))
    for h in range(4):
        lists.append(np.concatenate([np.arange(h * 128, (h + 1) * 128) + off for off in (2048, 2560, 3072, 3584)]))
    return lists


def _odd_cols():
    lists = [np.arange(0, 512), np.arange(512, 1024)]
    for h in range(4):
        lists.append(np.concatenate([np.arange(h * 128, (h + 1) * 128) + off for off in (1024, 1536, 2048, 2560)]))
    return lists


def make_shared(inp):
    f = lambda a: np.ascontiguousarray(np.asarray(a, dtype=np.float32))
    sh = {}
    sh["ck"] = f(inp["cache_k"]).reshape(2, NPHYS * 128, 512)
    sh["cvv"] = f(inp["cache_v"]).reshape(2, NPHYS * 128, 512)
    sh["normg"] = f(inp["norm_g"])
    w_ada = f(inp["w_ada"])
    sh["wada"] = np.stack([_wblocks(w_ada[l], [np.arange(b * 512, (b + 1) * 512) for b in range(6)]) for l in range(4)], 0)
    sh["bada"] = f(inp["b_ada"])
    w_in_e = f(inp["w_in_e"])
    sh["wine"] = np.stack([_wblocks(w_in_e[e], _even_cols()) for e in range(2)], 0)
    sh["wab"] = np.stack([_wblocks(w_in_e[e], [np.arange(4096, 4104)])[0] for e in range(2)], 0)
    w_out_e = f(inp["w_out_e"])
    sh["woute"] = np.stack([_wblocks(w_out_e[e], [np.arange(0, 512), np.arange(512, 1024)]) for e in range(2)], 0)
    sh["qng"] = f(inp["qn_g"]).reshape(2, 1, 64)
    sh["kng"] = f(inp["kn_g"]).reshape(2, 1, 64)
    sh["lamv"] = np.ascontiguousarray(np.stack([f(inp["lam_q1"]), f(inp["lam_k1"]), f(inp["lam_q2"]), f(inp["lam_k2"])], 1)).reshape(2, 1, 256)
    sh["sublng"] = f(inp["subln_g"]).reshape(2, 128, 1)
    sh["convw"] = f(inp["conv_w"])
    sh["alog"] = f(inp["a_log"]).reshape(2, 1, 4)
    sh["dtb"] = f(inp["dt_bias"]).reshape(2, 1, 4)
    sh["gnb"] = f(inp["gn_b"]).reshape(2, 1, 128)
    w_in_o = f(inp["w_in_o"])
    sh["wino"] = np.stack([_wblocks(w_in_o[o], _odd_cols()) for o in range(2)], 0)
    w_out_o = f(inp["w_out_o"])
    sh["wouto"] = np.stack([_wblocks(w_out_o[o], [np.arange(0, 512), np.arange(512, 1024)]) for o in range(2)], 0)
    sh["s5are"] = f(inp["s5_a_re"]).reshape(2, 2048)
    sh["s5aim"] = f(inp["s5_a_im"]).reshape(2, 2048)
    sh["s5bre"] = f(inp["s5_b_re"]).reshape(2, 2048, 16)
    sh["s5bim"] = f(inp["s5_b_im"]).reshape(2, 2048, 16)
    sh["s5cre"] = f(inp["s5_c_re"]).reshape(2, 512, 64)
    sh["s5cim"] = f(inp["s5_c_im"]).reshape(2, 512, 64)
    sh["s5d"] = f(inp["s5_d"])
    sh["s5ldt"] = f(inp["s5_log_dt"]).reshape(2, 1, 32)
    w_glu = f(inp["w_glu"])
    sh["wglu"] = np.stack([_wblocks(w_glu[o], [np.arange(0, 512)])[0] for o in range(2)], 0)
    sh["gnd"] = f(inp["gn_d"]).reshape(2, 1, 128)
    sh.update(make_consts2(make_consts()))
    return sh


def make_core_inputs(inp, sh, c):
    f = lambda a: np.ascontiguousarray(np.asarray(a, dtype=np.float32))
    m = dict(sh)
    s0, s1 = c * NS, (c + 1) * NS
    m["xp"] = f(inp["x_prompt"][c])
    m["xs"] = f(inp["x_sample"][s0:s1, 0, :])
    m["cv"] = f(np.concatenate([np.asarray(inp["c_prompt"])[c:c + 1], np.asarray(inp["c_sample"])[s0:s1]], 0))
    m["pt"] = np.ascontiguousarray(np.asarray(inp["page_table"], dtype=np.int32)[s0:s1]).reshape(1, NS * NPG)
    m["sconv"] = f(np.asarray(inp["state_b_conv"])[:, s0:s1]).reshape(2, NS * 3, 1536)
    m["sssm"] = f(np.asarray(inp["state_b_ssm"])[:, s0:s1])
    m["scre"] = f(np.asarray(inp["state_c_re"])[:, s0:s1]).reshape(2, NS, 2048)
    m["scim"] = f(np.asarray(inp["state_c_im"])[:, s0:s1]).reshape(2, NS, 2048)
    m["sret"] = f(np.asarray(inp["state_d_ret"])[:, s0:s1])
    return m


_CACHE = {}


def kernel(**inp):
    ncores = 8
    if "nc" not in _CACHE:
        _CACHE["nc"] = build()[0]
    nc = _CACHE["nc"]
    sh = make_shared(inp)
    in_maps = [make_core_inputs(inp, sh, c) for c in range(ncores)]
    res = run_bass_kernel_spmd(nc, in_maps, core_ids=list(range(ncores)))
    R = res.results
    cat = lambda name, ax: np.concatenate([np.asarray(R[c][name]) for c in range(ncores)], axis=ax)
    stk = lambda name: np.stack([np.asarray(R[c][name]) for c in range(ncores)], 0)
    y_p = stk("y_p")
    y_s = cat("y_s", 0).reshape(ncores * NS, 1, D)
    nk_p = stk("nk_p").transpose(1, 0, 2, 3).reshape(2, ncores, T, 4, 128)
    nv_p = stk("nv_p").transpose(1, 0, 2, 3).reshape(2, ncores, T, 4, 128)
    nk_s = cat("nk_s", 1).reshape(2, ncores * NS, 1, 4, 128)
    nv_s = cat("nv_s", 1).reshape(2, ncores * NS, 1, 4, 128)
    cv_p = stk("cv_p").transpose(1, 0, 2, 3)
    cv_s = cat("cv_s", 1)
    dl_p = stk("dl_p").transpose(1, 0, 2, 3, 4)
    dl_s = cat("dl_s", 1)
    s5r_p = stk("s5r_p").transpose(1, 0, 2).reshape(2, ncores, 32, 64)
    s5i_p = stk("s5i_p").transpose(1, 0, 2).reshape(2, ncores, 32, 64)
    s5r_s = cat("s5r_s", 1).reshape(2, ncores * NS, 32, 64)
    s5i_s = cat("s5i_s", 1).reshape(2, ncores * NS, 32, 64)
    rt_p = stk("rt_p").transpose(1, 0, 2, 3, 4)
    rt_s = cat("rt_s", 1)
    outs = (y_p, y_s, nk_p, nv_p, nk_s, nv_s, cv_p, cv_s, dl_p, dl_s, s5r_p, s5i_p, s5r_s, s5i_s, rt_p, rt_s)
    return tuple(np.ascontiguousarray(o, dtype=np.float32) for o in outs)
```

```python
import math
import numpy as np
import ml_dtypes
import concourse.bass as bass
import concourse.mybir as mybir
from concourse.bass_utils import run_bass_kernel_spmd
from contextlib import ExitStack

F32 = mybir.dt.float32
BF16 = mybir.dt.bfloat16
I32 = mybir.dt.int32
AF = mybir.ActivationFunctionType
ALU = mybir.AluOpType
AX = mybir.AxisListType

ENGS = ("pe", "act", "dve", "pool", "sp")
N_DMA_SEMS = 24

D = 1024
T = 2048
NS = 16
TA = T + NS
NT = 16
NPG = 16
NPHYS = 2560
EPS = 1e-6


def _ap_region(ap):
    t = ap.tensor
    name = t.name
    shape = tuple(t.shape)
    dims = tuple(ap.ap)
    off = int(ap.offset)
    space = str(ap.space)
    if "SB" not in space.upper() and "PSUM" not in space.upper():
        ext = 0
        for st, cn in dims:
            ext += abs(st) * (cn - 1)
        return (name, 0, 1, off, off + ext + 1)
    R = 1
    for s in shape[1:]:
        R *= s
    if "PSUM" in space.upper() or name.startswith("ps"):
        return (name, 0, 128, 0, R)
    p0 = off // R
    f0 = off % R
    pst, pcn = dims[0]
    np_ = 1 if pst == 0 else pcn
    ext = 0
    for st, cn in dims[1:]:
        ext += abs(st) * (cn - 1)
    return (name, p0, p0 + np_, f0, f0 + ext + 1)


def _overlap(a, b):
    return a[0] == b[0] and a[1] < b[2] and b[1] < a[2] and a[3] < b[4] and b[3] < a[4]


def _covers(a, b):
    return a[0] == b[0] and a[1] <= b[1] and a[2] >= b[2] and a[3] <= b[3] and a[4] >= b[4]


class Prog:
    def __init__(self, nc, es):
        self.nc = nc
        self.es = es
        self.ops = {e: [] for e in ENGS}
        self.sem = {e: es.enter_context(nc.semaphore("s_" + e)) for e in ENGS}
        self.dsem = [es.enter_context(nc.semaphore("d%d" % i)) for i in range(N_DMA_SEMS)]
        self.dval = [0] * N_DMA_SEMS
        self.dnext = {"hw": 0, "sw": 0}
        self.cnt = {e: 0 for e in ENGS}
        self.seen = {e: {} for e in ENGS}
        self.recs = {}
        self.n_instr = 0

    def _semh(self, key):
        return self.dsem[key] if isinstance(key, int) else self.sem[key]

    def _need(self, eng, reads, writes):
        need = {}
        for ap in reads:
            r = _ap_region(ap)
            isps = r[0].startswith("ps")
            for rec in self.recs.get(r[0], ()):
                if (rec[3] or (isps and rec[1] != eng)) and _overlap(rec[0], r) and need.get(rec[1], 0) < rec[2]:
                    need[rec[1]] = rec[2]
        for ap in writes:
            r = _ap_region(ap)
            for rec in self.recs.get(r[0], ()):
                if _overlap(rec[0], r) and need.get(rec[1], 0) < rec[2]:
                    need[rec[1]] = rec[2]
        waits = []
        for k, v in need.items():
            if eng == "pe" and k == "pe":
                continue
            if self.seen[eng].get(k, 0) >= v:
                continue
            self.seen[eng][k] = v
            waits.append((k, v))
        return waits

    def _record(self, reads, writes, key, val):
        for ap in writes:
            r = _ap_region(ap)
            lst = self.recs.setdefault(r[0], [])
            lst[:] = [rec for rec in lst if not _covers(r, rec[0])]
            lst.append([r, key, val, True])
        for ap in reads:
            r = _ap_region(ap)
            lst = self.recs.setdefault(r[0], [])
            lst[:] = [rec for rec in lst if not (not rec[3] and rec[1] == key and rec[0] == r)]
            lst.append([r, key, val, False])

    limit = None
    trace = None
    _cap = None

    def capture(self):
        self._cap = []
        return self._cap

    def end_capture(self):
        self._cap = None

    def merge(self, streams):
        idx = [0] * len(streams)
        live = True
        while live:
            live = False
            for i, st in enumerate(streams):
                if idx[i] < len(st):
                    kind, eng, fn, r, w = st[idx[i]]
                    idx[i] += 1
                    live = True
                    (self.op if kind == "op" else self.dma)(eng, fn, r, w)

    def op(self, eng, fn, reads=(), writes=()):
        if self._cap is not None:
            self._cap.append(("op", eng, fn, reads, writes))
            return
        if self.limit is not None and self.n_instr >= self.limit:
            return
        if self.trace is not None:
            import traceback
            fr = traceback.extract_stack(limit=4)
            self.trace.append((self.n_instr, eng, [f"{f.lineno}" for f in fr[:-1]], [str(_ap_region(a)) for a in writes]))
        waits = self._need(eng, reads, writes)
        self.cnt[eng] += 1
        self.ops[eng].append((fn, waits, (eng, 1)))
        self._record(reads, writes, eng, self.cnt[eng])
        self.n_instr += 1

    def dma(self, eng, fn, reads=(), writes=()):
        if self._cap is not None:
            self._cap.append(("dma", eng, fn, reads, writes))
            return
        if self.limit is not None and self.n_instr >= self.limit:
            return
        half = N_DMA_SEMS // 2
        kind = "sw" if eng == "pool" else "hw"
        k = self.dnext[kind] + (half if kind == "sw" else 0)
        self.dnext[kind] = (self.dnext[kind] + 1) % half
        waits = self._need(eng, reads, writes)
        if self.dval[k] > 0 and self.seen[eng].get(k, 0) < self.dval[k]:
            self.seen[eng][k] = self.dval[k]
            waits.append((k, self.dval[k]))
        self.dval[k] += 16
        self.ops[eng].append((fn, waits, (k, 16)))
        self._record(reads, writes, k, self.dval[k])
        self.n_instr += 1

    def barrier(self):
        assert self._cap is None
        targets = [(e, self.cnt[e]) for e in ENGS if self.cnt[e] > 0]
        targets += [(k, self.dval[k]) for k in range(N_DMA_SEMS) if self.dval[k] > 0]
        for e in ENGS:
            waits = []
            for k, v in targets:
                if k == e or self.seen[e].get(k, 0) >= v:
                    continue
                self.seen[e][k] = v
                waits.append((k, v))
            if waits:
                self.ops[e].append((None, waits, None))
        self.recs = {}

    def emit(self):
        nc = self.nc
        prog = self
        with nc.Block() as block:
            def run(engname, eobj):
                for fn, waits, inc in prog.ops[engname]:
                    for k, v in waits:
                        eobj.wait_ge(prog._semh(k), v)
                    if fn is None:
                        continue
                    fn(eobj).then_inc(prog._semh(inc[0]), inc[1])

            @block.tensor
            def _(e):
                run("pe", e)

            @block.scalar
            def _(e):
                run("act", e)

            @block.vector
            def _(e):
                run("dve", e)

            @block.gpsimd
            def _(e):
                run("pool", e)

            @block.sync
            def _(e):
                run("sp", e)

    def mm(self, out, lhsT, rhs, start=True, stop=True):
        self.op("pe", lambda e: e.matmul(out, lhsT, rhs, start=start, stop=stop), reads=[lhsT, rhs], writes=[out])

    def tr(self, out, in_, ident):
        self.op("pe", lambda e: e.transpose(out, in_, ident), reads=[in_, ident], writes=[out])

    def act(self, out, in_, func, bias=None, scale=None):
        kw = {}
        reads = [in_]
        if bias is not None:
            kw["bias"] = bias
            if not isinstance(bias, (int, float)):
                reads.append(bias)
        if scale is not None:
            kw["scale"] = scale
            if not isinstance(scale, (int, float)):
                reads.append(scale)
        self.op("act", lambda e: e.activation(out, in_, func, **kw), reads=reads, writes=[out])

    def tt(self, out, in0, in1, op, eng="dve"):
        self.op(eng, lambda e: e.tensor_tensor(out, in0, in1, op), reads=[in0, in1], writes=[out])

    def ts(self, out, in0, s1, op0, s2=None, op1=None, eng="dve"):
        reads = [in0]
        if not isinstance(s1, (int, float)):
            reads.append(s1)
        if s2 is not None and not isinstance(s2, (int, float)):
            reads.append(s2)
        kw = {}
        if op1 is not None:
            kw["op1"] = op1
        self.op(eng, lambda e: e.tensor_scalar(out, in0, s1, s2, op0, **kw), reads=reads, writes=[out])

    def stt(self, out, in0, scalar, in1, op0, op1, eng="dve"):
        reads = [in0, in1]
        if not isinstance(scalar, (int, float)):
            reads.append(scalar)
        self.op(eng, lambda e: e.scalar_tensor_tensor(out, in0, scalar, in1, op0, op1), reads=reads, writes=[out])

    def copy(self, out, in_, eng="dve"):
        if eng == "act":
            self.op("act", lambda e: e.copy(out, in_), reads=[in_], writes=[out])
        else:
            self.op(eng, lambda e: e.tensor_copy(out, in_), reads=[in_], writes=[out])

    def memset(self, out, val, eng="dve"):
        self.op(eng, lambda e: e.memset(out, val), reads=[], writes=[out])

    def red(self, out, in_, op=None, eng="dve"):
        op = op or ALU.add
        self.op(eng, lambda e: e.tensor_reduce(out, in_, AX.X, op), reads=[in_], writes=[out])

    def recip(self, out, in_):
        self.op("dve", lambda e: e.reciprocal(out, in_), reads=[in_], writes=[out])

    def scan(self, out, d0, d1, init):
        reads = [d0, d1]
        if not isinstance(init, (int, float)):
            reads.append(init)
        self.op("dve", lambda e: e.tensor_tensor_scan(out, d0, d1, init, ALU.mult, ALU.add), reads=reads, writes=[out])

    def load(self, out, in_, eng="sp", slow=False):
        if slow:
            self.dma(eng, lambda e: e.dma_start(out=out, in_=in_, allow_slow_non_contiguous=True), reads=[in_], writes=[out])
        else:
            self.dma(eng, lambda e: e.dma_start(out=out, in_=in_), reads=[in_], writes=[out])

    store = load

    def gather(self, out, flat, idx):
        self.dma("pool", lambda e: e.indirect_dma_start(out=out, out_offset=None, in_=flat,
                                                        in_offset=bass.IndirectOffsetOnAxis(ap=idx, axis=0)),
                 reads=[idx, flat], writes=[out])


class Arena:
    def __init__(self, t, size):
        self.t = t
        self.size = size
        self.top = 0
        self.stack = []
        self.peak = 0

    def alloc(self, shape, parts=128):
        n = 1
        for s in shape:
            n *= s
        off = self.top
        self.top += n
        self.peak = max(self.peak, self.top)
        assert self.top <= self.size, ("arena overflow", self.t.name, self.top, self.size)
        ap = self.t[0:parts, off:off + n]
        if len(shape) == 2:
            ap = ap.rearrange("p (a b) -> p a b", b=shape[1])
        elif len(shape) == 3:
            ap = ap.rearrange("p (a b c) -> p a b c", b=shape[1], c=shape[2])
        elif len(shape) == 4:
            ap = ap.rearrange("p (a b c d) -> p a b c d", b=shape[1], c=shape[2], d=shape[3])
        return ap


INPUT_SPECS = [
    ("xp", (T, D), F32), ("xs", (NS, D), F32), ("cv", (NS + 1, D), F32), ("pt", (1, NS * NPG), I32),
    ("ck", (2, NPHYS * 128, 512), F32), ("cvv", (2, NPHYS * 128, 512), F32),
    ("sconv", (2, NS * 3, 1536), F32), ("sssm", (2, NS, 4, 128, 128), F32),
    ("scre", (2, NS, 2048), F32), ("scim", (2, NS, 2048), F32), ("sret", (2, NS, 4, 128, 128), F32),
    ("normg", (4, D), F32), ("wada", (4, 6, 128, 4096), F32), ("bada", (4, 3 * D), F32),
    ("wine", (2, 8, 128, 4096), F32), ("wab", (2, 128, 64), F32), ("woute", (2, 2, 128, 4096), F32),
    ("qng", (2, 1, 64), F32), ("kng", (2, 1, 64), F32), ("lamv", (2, 1, 256), F32), ("sublng", (2, 128, 1), F32),
    ("convw", (2, 4, 1536), F32), ("alog", (2, 1, 4), F32), ("dtb", (2, 1, 4), F32), ("gnb", (2, 1, 128), F32),
    ("wino", (2, 6, 128, 4096), F32), ("wouto", (2, 2, 128, 4096), F32),
    ("s5are", (2, 2048), F32), ("s5aim", (2, 2048), F32), ("s5bre", (2, 2048, 16), F32), ("s5bim", (2, 2048, 16), F32),
    ("s5cre", (2, 512, 64), F32), ("s5cim", (2, 512, 64), F32), ("s5d", (2, 512), F32), ("s5ldt", (2, 1, 32), F32),
    ("wglu", (2, 128, 2048), F32), ("gnd", (2, 1, 128), F32),
    ("c_ident", (128, 128), F32), ("c_ones", (128, 128), F32), ("c_onesb", (128, 128), BF16), ("c_trib", (128, 128), BF16),
    ("c_augq", (4, 4, T), BF16), ("c_augk", (4, 4, T), BF16), ("c_alibis", (128, NPG * 4), F32),
    ("c_mnegT", (128, 128), F32), ("c_mnegS", (128, 128), F32), ("c_LT", (128, 128), F32), ("c_LB", (128, 128), F32), ("c_cmask", (128, 8), F32), ("c_CM", (2, 128, 128), F32),
    ("c_dtc", (4, 128, 128), F32), ("c_kdc", (128, 4), F32), ("c_egc", (128, 4), F32), ("c_iota", (128, 1), F32),
    ("c_mask16", (128, 256), F32),
]
OUTPUT_SPECS = [
    ("y_p", (T, D)), ("y_s", (NS, D)), ("nk_p", (2, T, 512)), ("nv_p", (2, T, 512)), ("nk_s", (2, NS, 512)), ("nv_s", (2, NS, 512)),
    ("cv_p", (2, 3, 1536)), ("cv_s", (2, NS, 3, 1536)), ("dl_p", (2, 4, 128, 128)), ("dl_s", (2, NS, 4, 128, 128)),
    ("s5r_p", (2, 2048)), ("s5i_p", (2, 2048)), ("s5r_s", (2, NS, 2048)), ("s5i_s", (2, NS, 2048)),
    ("rt_p", (2, 4, 128, 128)), ("rt_s", (2, NS, 4, 128, 128)),
]
LOG_GAMMA = [math.log1p(-2.0 ** (-5.0 - h)) for h in range(4)]
SLOPES = [2.0 ** (-8.0 * (h + 1) / 4.0) for h in range(4)]


def make_consts():
    c = {}
    c["c_ident"] = np.eye(128, dtype=np.float32)
    c["c_ones"] = np.ones((128, 128), np.float32)
    c["c_onesb"] = np.ones((128, 128), ml_dtypes.bfloat16)
    i = np.arange(128)
    c["c_trib"] = (i[None, :] >= i[:, None]).astype(ml_dtypes.bfloat16)
    pos = np.arange(T)
    hi, lo = (pos // 128).astype(np.float64), (pos % 128).astype(np.float64)
    augq = np.zeros((4, 4, T), np.float64)
    augk = np.zeros((4, 4, T), np.float64)
    for h in range(4):
        s8 = 8.0 * SLOPES[h]
        augq[h, 0] = -s8 * 128.0 * hi
        augq[h, 1] = -s8 * lo
        augq[h, 2] = 1.0
        augq[h, 3] = 1.0
        augk[h, 0] = 1.0
        augk[h, 1] = 1.0
        augk[h, 2] = s8 * 128.0 * hi
        augk[h, 3] = s8 * lo
    c["c_augq"] = augq.astype(ml_dtypes.bfloat16)
    c["c_augk"] = augk.astype(ml_dtypes.bfloat16)
    al = np.zeros((128, NPG, 4), np.float64)
    for pg in range(NPG):
        for h in range(4):
            al[:, pg, h] = -SLOPES[h] * (T - (pg * 128 + i))
    c["c_alibis"] = al.reshape(128, NPG * 4).astype(np.float32)
    blk = i // 64
    same = blk[:, None] == blk[None, :]
    NEG = -1e30
    c["c_mnegT"] = np.where(same & (i[:, None] <= i[None, :]), 0.0, NEG).astype(np.float32)
    c["c_mnegS"] = np.where(same & (i[None, :] < i[:, None]), 0.0, NEG).astype(np.float32)
    c["c_LT"] = (same & (i[:, None] <= i[None, :])).astype(np.float32)
    c["c_LB"] = same.astype(np.float32)
    cmk = np.zeros((128, 8), np.float32)
    for l4 in range(4):
        for gp in range(2):
            cmk[l4 * 32 + gp * 16:l4 * 32 + gp * 16 + 16, l4 * 2 + gp] = 1.0
    c["c_cmask"] = cmk
    cm = np.zeros((2, 128, 128), np.float32)
    for cc in range(2):
        cm[cc, blk == cc, :] = 1.0
    c["c_CM"] = cm
    dtc = np.zeros((4, 128, 128), np.float64)
    kdc = np.zeros((128, 4), np.float64)
    egc = np.zeros((128, 4), np.float64)
    for h in range(4):
        lg = LOG_GAMMA[h]
        dd = (i[None, :] - i[:, None]).astype(np.float64)
        dtc[h] = np.where(dd >= 0, np.exp(lg * np.maximum(dd, 0)), 0.0) * (128.0 ** -0.5)
        kdc[:, h] = np.exp(lg * (127 - i)) * (128.0 ** -0.5)
        egc[:, h] = np.exp(lg * (i + 1))
    c["c_dtc"] = dtc.astype(np.float32)
    c["c_kdc"] = kdc.astype(np.float32)
    c["c_egc"] = egc.astype(np.float32)
    c["c_iota"] = i.astype(np.float32).reshape(128, 1)
    m16 = np.zeros((128, 16, 16), np.float32)
    for s in range(16):
        m16[:, s, s] = 1.0
    c["c_mask16"] = m16.reshape(128, 256)
    return c


def make_consts2(c):
    i = np.arange(128)
    ab = np.zeros((128, 4, 16), np.float64)
    for h in range(4):
        for d in range(16):
            ab[:, h, d] = SLOPES[h] * (i - 128.0 * d)
    c["c_abias"] = ab.reshape(128, 64).astype(np.float32)
    return c


INPUT_SPECS.append(("c_abias", (128, 64), F32))

NF = 9800
NB = 10240
BLOCKS = [(0, 512), (512, 512), (1024, 512), (1536, 512), (2048, 16)]


class Ctx:
    pass


def finish_o_rms(K, o, rows, gain_b, sz, dest):
    P, psn, ident = K.P, K.psn, K.ident
    sqs, ss, on = K.fo_sq, K.fo_ss, K.fo_on
    P.act(sqs[0:rows, :], o, AF.Square)
    P.red(ss[0:rows, 0:1], sqs[0:rows, :])
    P.ts(ss[0:rows, 0:1], ss[0:rows, 0:1], 1.0 / 128, ALU.mult, EPS, ALU.add)
    P.act(ss[0:rows, 0:1], ss[0:rows, 0:1], AF.Sqrt)
    P.recip(ss[0:rows, 0:1], ss[0:rows, 0:1])
    P.stt(on[0:rows, :], o, ss[0:rows, 0:1], gain_b[0:rows, :], ALU.mult, ALU.mult)
    pt = psn()
    P.tr(pt[:, 0:rows], on[0:rows, :], ident[0:rows, 0:rows])
    P.tt(dest, pt[:, 0:rows], sz, ALU.mult)


def finish_o_ln(K, o, rows, gain_b, sz, dest):
    P, psn, ident = K.P, K.psn, K.ident
    sqs, ss, on = K.fo_sq, K.fo_ss, K.fo_on
    P.red(ss[0:rows, 1:2], o)
    P.ts(ss[0:rows, 1:2], ss[0:rows, 1:2], 1.0 / 128, ALU.mult)
    P.ts(on[0:rows, :], o, ss[0:rows, 1:2], ALU.subtract)
    P.act(sqs[0:rows, :], on[0:rows, :], AF.Square)
    P.red(ss[0:rows, 0:1], sqs[0:rows, :])
    P.ts(ss[0:rows, 0:1], ss[0:rows, 0:1], 1.0 / 128, ALU.mult, EPS, ALU.add)
    P.act(ss[0:rows, 0:1], ss[0:rows, 0:1], AF.Sqrt)
    P.recip(ss[0:rows, 0:1], ss[0:rows, 0:1])
    P.stt(on[0:rows, :], on[0:rows, :], ss[0:rows, 0:1], gain_b[0:rows, :], ALU.mult, ALU.mult)
    pt = psn()
    P.tr(pt[:, 0:rows], on[0:rows, :], ident[0:rows, 0:rows])
    P.tt(dest, pt[:, 0:rows], sz, ALU.mult)


def sample_state_step(K, qT, kT, vT, eg, neg_eg, beta, EGB, s0_dram, out_dram, gain_b, szs, dest, delta, ln):
    P, AFa, psn, ident, Scope = K.P, K.AFa, K.psn, K.ident, K.Scope
    QM = AFa.alloc([16, 16])
    KM = AFa.alloc([16, 16])
    M16 = K.MASK16
    P.tt(QM[:, :, :], M16[:, :, :], qT.unsqueeze(1).to_broadcast([128, 16, 16]), ALU.mult)
    P.tt(KM[:, :, :], M16[:, :, :], kT.unsqueeze(1).to_broadcast([128, 16, 16]), ALU.mult)
    QTM = AFa.alloc([128])
    KTM = AFa.alloc([128])
    VTM = AFa.alloc([128])
    for src, dst in ((qT, QTM), (kT, KTM), (vT, VTM)):
        pt = psn()
        P.tr(pt[0:16, 0:128], src, ident[:])
        P.copy(dst[0:16, :], pt[0:16, 0:128], eng="act")
    S0 = [AFa.alloc([128]) for _ in range(2)]
    pk = psn()
    pq = psn()
    for s in range(NS):
        s0 = S0[s % 2]
        P.load(s0[:, :], s0_dram(s))
        if delta:
            P.mm(pk[0:16, 0:128], KM[:, s, :], s0[:, :], start=(s == 0), stop=(s == NS - 1))
        P.mm(pq[0:16, 0:128], QM[:, s, :], s0[:, :], start=(s == 0), stop=(s == NS - 1))
    u = AFa.alloc([128])
    if delta:
        P.stt(u[0:16, :], pk[0:16, 0:128], neg_eg, VTM[0:16, :], ALU.mult, ALU.add)
        P.ts(u[0:16, :], u[0:16, :], beta, ALU.mult)
    else:
        P.copy(u[0:16, :], VTM[0:16, :])
    tmp = AFa.alloc([128])
    qk = AFa.alloc([1])
    P.tt(tmp[0:16, :], QTM[0:16, :], KTM[0:16, :], ALU.mult)
    P.red(qk[0:16, :], tmp[0:16, :])
    o = AFa.alloc([128])
    P.ts(tmp[0:16, :], pq[0:16, 0:128], eg, ALU.mult)
    P.stt(o[0:16, :], u[0:16, :], qk[0:16, 0:1], tmp[0:16, :], ALU.mult, ALU.add)
    if ln:
        finish_o_ln(K, o[0:16, :], 16, gain_b, szs, dest)
    else:
        finish_o_rms(K, o[0:16, :], 16, gain_b, szs, dest)
    kms = [AFa.alloc([128]) for _ in range(2)]
    sn = [AFa.alloc([128]) for _ in range(2)]
    for s in range(NS):
        s0 = S0[s % 2]
        P.load(s0[:, :], s0_dram(s))
        km = kms[s % 2]
        P.ts(km[0:16, :], KTM[0:16, :], ident[0:16, s:s + 1], ALU.mult)
        ps = psn()
        P.mm(ps[:, 0:128], km[0:16, :], u[0:16, :])
        egs = EGB if isinstance(EGB, float) else EGB[:, s:s + 1]
        P.stt(sn[s % 2][:, :], s0[:, :], egs, ps[:, 0:128], ALU.mult, ALU.add)
        P.store(out_dram(s), sn[s % 2][:, :])


def gdn_heads(K, li):
    P, I, O = K.P, K.I, K.O
    XT, HT, OT, AFa, ABa, PS, psn, Scope = K.XT, K.HT, K.OT, K.AFa, K.ABa, K.PS, K.psn, K.Scope
    next_w, wv512, v3, ident, ones, onesb = K.next_w, K.wv512, K.v3, K.ident, K.ones, K.onesb
    e = li // 2
    with Scope():
        wab = next_w(("wab", e))
        wabv = wab[:, 0:64].rearrange("p (k c) -> p k c", c=8)
        AB = AFa.alloc([17, 8])
        P.memset(AB[:, :, :], 0.0)
        for n in range(17):
            rows = 128 if n < 16 else 16
            c0 = n * 128
            ps = psn()
            for k in range(8):
                P.mm(ps[0:rows, 0:8], HT[:, k, c0:c0 + rows], wabv[:, k, :], start=(k == 0), stop=(k == 7))
            P.copy(AB[0:rows, n, :], ps[0:rows, 0:8], eng=("act" if n % 2 else "dve"))
        ALG = AFa.alloc([4])
        DTB = AFa.alloc([4])
        P.load(ALG[:, :], I["alog"][e].partition_broadcast(128))
        P.load(DTB[:, :], I["dtb"][e].partition_broadcast(128))
        P.act(ALG[:, :], ALG[:, :], AF.Exp)
        P.ts(ALG[:, :], ALG[:, :], -1.0, ALU.mult)
        G = AFa.alloc([17, 4])
        BETA = AFa.alloc([17, 4])
        P.tt(G[:, :, :], AB[:, :, 0:4], DTB[:, :].unsqueeze(1).to_broadcast([128, 17, 4]), ALU.add)
        P.act(G[:, :, :], G[:, :, :], AF.Exp)
        P.act(G[:, :, :], G[:, :, :], AF.Ln, bias=ones[:, 0:1])
        P.tt(G[:, :, :], G[:, :, :], ALG[:, :].unsqueeze(1).to_broadcast([128, 17, 4]), ALU.mult)
        P.act(BETA[:, :, :], AB[:, :, 4:8], AF.Sigmoid)
        CN = {}
        for nm in ("c_LT", "c_LB", "c_mnegT", "c_mnegS"):
            CN[nm] = AFa.alloc([128])
            P.load(CN[nm][:, :], I[nm])
        CM = AFa.alloc([2, 128])
        P.load(CM[:, :, :], I["c_CM"].rearrange("c j p -> j c p"))
        Gf = G[:, 0:16, :].rearrange("p n h -> p (n h)")
        GAM = AFa.alloc([16, 4])
        GL = AFa.alloc([16, 4])
        ps = psn()
        P.mm(ps[:, 0:64], CN["c_LT"][:, :], Gf)
        P.copy(GAM[:, :, :].rearrange("p n h -> p (n h)"), ps[:, 0:64])
        ps = psn()
        P.mm(ps[:, 0:64], CN["c_LB"][:, :], Gf)
        P.copy(GL[:, :, :].rearrange("p n h -> p (n h)"), ps[:, 0:64], eng="act")
        EGLB = AFa.alloc([2, 16, 4])
        for c in range(2):
            ps = psn()
            P.mm(ps[:, 0:64], CM[:, c, :], Gf)
            P.act(EGLB[:, c, :, :].rearrange("p n h -> p (n h)"), ps[:, 0:64], AF.Exp)
        EGAM = AFa.alloc([16, 4])
        BEG = AFa.alloc([16, 4])
        NBETA = AFa.alloc([16, 4])
        NGAM = AFa.alloc([16, 4])
        ED = AFa.alloc([16, 4])
        P.act(EGAM[:, :, :], GAM[:, :, :], AF.Exp)
        P.tt(BEG[:, :, :], BETA[:, 0:16, :], EGAM[:, :, :], ALU.mult)
        P.ts(NBETA[:, :, :], BETA[:, 0:16, :], -1.0, ALU.mult)
        P.ts(NGAM[:, :, :], GAM[:, :, :], -1.0, ALU.mult)
        P.tt(ED[:, :, :], GL[:, :, :], GAM[:, :, :], ALU.subtract)
        P.act(ED[:, :, :], ED[:, :, :], AF.Exp)
        EGS = AFa.alloc([4])
        NEGS = AFa.alloc([4])
        P.act(EGS[0:16, :], G[0:16, 16, :], AF.Exp)
        P.ts(NEGS[0:16, :], EGS[0:16, :], -1.0, ALU.mult)
        CWT = AFa.alloc([12, 4])
        for tap in range(4):
            P.load(CWT[:, :, tap], I["convw"][e][tap].rearrange("(c p) -> p c", p=128), slow=True)
        CBT = AFa.alloc([12, 48])
        with Scope():
            CB = AFa.alloc([1536])
            P.load(CB[0:48, :], I["sconv"][e])
            for half in range(3):
                ps = psn()
                for q in range(4):
                    cidx = half * 4 + q
                    P.tr(ps[:, q * 64:q * 64 + 48], CB[0:48, cidx * 128:(cidx + 1) * 128], ident[0:48, 0:48])
                P.copy(CBT[:, half * 4:(half + 1) * 4, :], v3(ps[:, 0:256], 64)[:, :, 0:48])
        P.load(O["cv_s"][e, :, 0:2, :], I["sconv"][e].rearrange("(s j) c -> s j c", j=3)[:, 1:3, :])
        GNB = AFa.alloc([128])
        P.load(GNB[:, :], I["gnb"][e].partition_broadcast(128))
        K.MASK16 = AFa.alloc([16, 16])
        P.load(K.MASK16[:, :, :], v3(I["c_mask16"], 16))
        BW = 256
        NTB = T // BW

        def alloc_bufs():
            B = {}
            B["S"] = AFa.alloc([128])
            B["PRE"] = AFa.alloc([BW + 3])
            B["CAR"] = AFa.alloc([3, 3])
            B["QKV"] = [AFa.alloc([BW]) for _ in range(3)]
            B["SZ"] = AFa.alloc([BW])
            B["acc"] = AFa.alloc([BW])
            B["rs"] = B["acc"]
            B["sqb"] = ABa.alloc([BW])
            B["fo"] = (AFa.alloc([128]), AFa.alloc([2]), AFa.alloc([128]))
            return B

        def alloc_mats():
            M = {}
            M["X0"] = AFa.alloc([256])
            for nm in ("KD", "DG", "T1", "T2", "DT", "DS", "AQ", "Pm", "PTm"):
                M[nm] = AFa.alloc([128])
            M["WT"], M["UF"], M["O1"], M["OO"] = M["DG"], M["T1"], M["T2"], M["DS"]
            return M

        def l2n(B, w):
            QKV, rs, sqb_ = B["QKV"], B["rs"], B["sqb"]
            for j in (0, 1):
                P.act(sqb_[:, 0:w], QKV[j][:, 0:w], AF.Square)
                ps = psn()
                P.mm(ps[:, 0:w], onesb[:], sqb_[:, 0:w])
                P.ts(rs[:, 0:w], ps[:, 0:w], EPS, ALU.add)
                P.act(rs[:, 0:w], rs[:, 0:w], AF.Sqrt)
                P.recip(rs[:, 0:w], rs[:, 0:w])
                if j == 0:
                    P.stt(QKV[j][:, 0:w], QKV[j][:, 0:w], 128.0 ** -0.5, rs[:, 0:w], ALU.mult, ALU.mult)
                else:
                    P.tt(QKV[j][:, 0:w], QKV[j][:, 0:w], rs[:, 0:w], ALU.mult)

        def head_prompt(h, wv, B, M):
            S, PRE, CAR, QKV, SZ, acc = B["S"], B["PRE"], B["CAR"], B["QKV"], B["SZ"], B["acc"]
            K.fo_sq, K.fo_ss, K.fo_on = B["fo"]
            X0, KD, DG, T1, T2, DT, DS, AQ, Pm, PTm, WT, UF, O1, OO = [M[n] for n in
                ("X0", "KD", "DG", "T1", "T2", "DT", "DS", "AQ", "Pm", "PTm", "WT", "UF", "O1", "OO")]
            P.memset(S[:, :], 0.0)
            P.memset(CAR[:, :, :], 0.0)
            for tb in range(NTB):
                c0 = tb * BW
                for j in range(3):
                    ps = psn()
                    for k in range(8):
                        P.mm(ps[:, 0:BW], wv[:, k, j * 128:(j + 1) * 128], HT[:, k, c0:c0 + BW], start=(k == 0), stop=(k == 7))
                    P.copy(PRE[:, 0:3], CAR[:, j, :])
                    P.copy(PRE[:, 3:BW + 3], ps[:, 0:BW], eng="act")
                    cidx = j * 4 + h
                    P.ts(acc[:, :], PRE[:, 0:BW], CWT[:, cidx, 0:1], ALU.mult)
                    for tap in range(1, 4):
                        P.stt(acc[:, :], PRE[:, tap:tap + BW], CWT[:, cidx, tap:tap + 1], acc[:, :], ALU.mult, ALU.add)
                    P.act(QKV[j][:, :], acc[:, :], AF.Silu)
                    P.copy(CAR[:, j, :], PRE[:, BW:BW + 3])
                    if tb == NTB - 1:
                        P.store(O["cv_p"][e][:, j * 512 + h * 128:j * 512 + (h + 1) * 128].rearrange("t c -> c t"),
                                CAR[:, j, :], slow=True)
                l2n(B, BW)
                ps = psn()
                for k in range(8):
                    P.mm(ps[:, 0:BW], wv[:, k, 384:512], HT[:, k, c0:c0 + BW], start=(k == 0), stop=(k == 7))
                P.act(SZ[:, :], ps[:, 0:BW], AF.Silu)
                for tl in range(BW // 128):
                    n = tb * (BW // 128) + tl
                    cc = slice(tl * 128, (tl + 1) * 128)
                    qT, kT, vT = QKV[0][:, cc], QKV[1][:, cc], QKV[2][:, cc]
                    gam, ngam = GAM[:, n, h:h + 1], NGAM[:, n, h:h + 1]
                    pa = psn()
                    P.tr(pa[:, 0:128], kT, ident[:])
                    P.tr(pa[:, 128:256], vT, ident[:])
                    P.ts(X0[:, 0:128], pa[:, 128:256], BETA[:, n, h:h + 1], ALU.mult)
                    P.ts(X0[:, 128:256], pa[:, 0:128], BEG[:, n, h:h + 1], ALU.mult)
                    P.ts(KD[:, :], pa[:, 0:128], ED[:, n, h:h + 1], ALU.mult)
                    P.ts(DG[:, :], ident[:], gam, ALU.mult)
                    pr = psn()
                    P.mm(pr[:, 0:128], ones[:], DG[:, :])
                    P.tt(T1[:, :], pr[:, 0:128], CN["c_mnegT"][:, :], ALU.add)
                    P.act(DT[:, :], T1[:, :], AF.Exp, bias=ngam)
                    P.stt(T2[:, :], pr[:, 0:128], -1.0, CN["c_mnegS"][:, :], ALU.mult, ALU.add)
                    P.act(DS[:, :], T2[:, :], AF.Exp, bias=gam)
                    pq = psn()
                    P.mm(pq[:, 0:128], kT, qT)
                    P.tt(AQ[:, :], pq[:, 0:128], DT[:, :], ALU.mult)
                    pk = psn()
                    P.mm(pk[:, 0:128], kT, kT)
                    P.stt(Pm[:, :], pk[:, 0:128], NBETA[:, n, h:h + 1], DS[:, :], ALU.mult, ALU.mult)
                    pt = psn()
                    P.tr(pt[:, 0:128], Pm[:, :], ident[:])
                    P.copy(PTm[:, :], pt[:, 0:128], eng="act")
                    for lvl in range(6):
                        px = psn()
                        P.mm(px[:, 0:256], PTm[:, :], X0[:, :])
                        P.tt(X0[:, :], px[:, 0:256], X0[:, :], ALU.add)
                        if lvl < 5:
                            pp = psn()
                            P.mm(pp[:, 0:128], PTm[:, :], Pm[:, :])
                            P.mm(pp[:, 128:256], Pm[:, :], PTm[:, :])
                            P.copy(Pm[:, :], pp[:, 0:128], eng="act")
                            P.copy(PTm[:, :], pp[:, 128:256], eng="act")
                    pw = psn()
                    P.tr(pw[:, 0:128], X0[:, 128:256], ident[:])
                    P.copy(WT[:, :], pw[:, 0:128], eng="act")
                    for c in range(2):
                        r = slice(64 * c, 64 * c + 64)
                        p1 = psn()
                        P.mm(p1[:, 0:128], WT[:, :], S[:, :])
                        P.tt(UF[r, :], X0[r, 0:128], p1[r, 0:128], ALU.subtract)
                        p2 = psn()
                        P.mm(p2[:, 0:128], qT, S[:, :])
                        P.ts(O1[r, :], p2[r, 0:128], EGAM[r, n, h:h + 1], ALU.mult)
                        p3 = psn()
                        P.mm(p3[:, 0:128], KD[r, :], UF[r, :])
                        P.stt(S[:, :], S[:, :], EGLB[:, c, n, h:h + 1], p3[:, 0:128], ALU.mult, ALU.add)
                    p4 = psn()
                    P.mm(p4[:, 0:128], AQ[:, :], UF[:, :])
                    P.tt(OO[:, :], p4[:, 0:128], O1[:, :], ALU.add)
                    finish_o_rms(K, OO[:, :], 128, GNB, SZ[:, cc], OT[:, 4 + h, n * 128:(n + 1) * 128])
            P.store(O["dl_p"][e, h], S[:, :])

        def head_sample(h, wv, B):
            QKV, SZ, acc = B["QKV"], B["SZ"], B["acc"]
            K.fo_sq, K.fo_ss, K.fo_on = B["fo"]
            xs_ = AFa.alloc([16])
            xtm = AFa.alloc([128])
            for j in range(3):
                ps = psn()
                for k in range(8):
                    P.mm(ps[:, 0:16], wv[:, k, j * 128:(j + 1) * 128], HT[:, k, T:TA], start=(k == 0), stop=(k == 7))
                P.copy(xs_[:, :], ps[:, 0:16], eng="act")
                cidx = j * 4 + h
                cb = CBT[:, cidx, :].rearrange("p (s j) -> p s j", j=3)
                P.ts(acc[:, 0:16], cb[:, :, 0], CWT[:, cidx, 0:1], ALU.mult)
                P.stt(acc[:, 0:16], cb[:, :, 1], CWT[:, cidx, 1:2], acc[:, 0:16], ALU.mult, ALU.add)
                P.stt(acc[:, 0:16], cb[:, :, 2], CWT[:, cidx, 2:3], acc[:, 0:16], ALU.mult, ALU.add)
                P.stt(acc[:, 0:16], xs_[:, :], CWT[:, cidx, 3:4], acc[:, 0:16], ALU.mult, ALU.add)
                P.act(QKV[j][:, 0:16], acc[:, 0:16], AF.Silu)
                pt = psn()
                P.tr(pt[0:16, 0:128], xs_[:, :], ident[:])
                P.copy(xtm[0:16, :], pt[0:16, 0:128])
                P.store(O["cv_s"][e, :, 2, j * 512 + h * 128:j * 512 + (h + 1) * 128], xtm[0:16, :])
            l2n(B, 16)
            ps = psn()
            for k in range(8):
                P.mm(ps[:, 0:16], wv[:, k, 384:512], HT[:, k, T:TA], start=(k == 0), stop=(k == 7))
            P.act(SZ[:, 0:16], ps[:, 0:16], AF.Silu)
            DGs = AFa.alloc([16])
            P.ts(DGs[0:16, :], ident[0:16, 0:16], EGS[0:16, h:h + 1], ALU.mult)
            ps = psn()
            P.mm(ps[:, 0:16], ones[0:16, :], DGs[0:16, :])
            EGB = AFa.alloc([16])
            P.copy(EGB[:, :], ps[:, 0:16])
            sample_state_step(K, QKV[0][:, 0:16], QKV[1][:, 0:16], QKV[2][:, 0:16],
                              EGS[0:16, h:h + 1], NEGS[0:16, h:h + 1], BETA[0:16, 16, h:h + 1], EGB,
                              (lambda s, h=h: I["sssm"][e, s, h]), (lambda s, h=h: O["dl_s"][e, s, h]),
                              GNB, SZ[:, 0:16], OT[:, 4 + h, T:TA], True, False)

        NPAR = K.cfg.get("gdn_par", 2)
        for hp in range(0, 4, NPAR):
            wvs = [wv512(next_w(("eb", e, hp + i), prefetch=(i == 0))) for i in range(NPAR)]
            with Scope():
                Bs = [alloc_bufs() for _ in range(NPAR)]
                with Scope():
                    Ms = [alloc_mats() for _ in range(NPAR)]
                    streams = []
                    for i in range(NPAR):
                        K.p0, K.pn = (8 // NPAR) * i, 8 // NPAR
                        cap = P.capture()
                        head_prompt(hp + i, wvs[i], Bs[i], Ms[i])
                        P.end_capture()
                        streams.append(cap)
                    K.p0, K.pn = 0, 8
                    P.merge(streams)
                for i in range(NPAR):
                    with Scope():
                        head_sample(hp + i, wvs[i], Bs[i])


def rms_feature_gate(K, o, w, gcol, psz_fn, dest, tagf):
    P, AFa, ABa, psn, onesb = K.P, K.AFa, K.ABa, K.psn, K.onesb
    sq = K.ep_sq
    P.act(sq[:, 0:w], o[:, 0:w], AF.Square)
    ps = psn()
    P.mm(ps[:, 0:w], onesb[:], sq[:, 0:w])
    rs = K.ep_rs
    P.ts(rs[:, 0:w], ps[:, 0:w], 1.0 / 128, ALU.mult, EPS, ALU.add)
    P.act(rs[:, 0:w], rs[:, 0:w], AF.Sqrt)
    P.recip(rs[:, 0:w], rs[:, 0:w])
    P.tt(o[:, 0:w], o[:, 0:w], rs[:, 0:w], ALU.mult)
    sz = psz_fn()
    P.stt(dest, o[:, 0:w], gcol, sz, ALU.mult, ALU.mult)


def even_layer(K, li):
    P, I, O = K.P, K.I, K.O
    XT, HT, OT, AFa, ABa, PS, psn, Scope = K.XT, K.HT, K.OT, K.AFa, K.ABa, K.PS, K.psn, K.Scope
    next_w, wv512, v3, ident, ones, onesb, trib = K.next_w, K.wv512, K.v3, K.ident, K.ones, K.onesb, K.trib
    cfg = K.cfg
    e = li // 2
    lam_init = 0.8 - 0.6 * math.exp(-0.3 * li)

    with Scope():
        GQK = AFa.alloc([256])
        P.load(GQK[:, 0:64], I["qng"][e].partition_broadcast(128))
        P.load(GQK[:, 64:128], I["qng"][e].partition_broadcast(128))
        P.load(GQK[:, 128:192], I["kng"][e].partition_broadcast(128))
        P.load(GQK[:, 192:256], I["kng"][e].partition_broadcast(128))
        LV = AFa.alloc([256])
        P.load(LV[:, :], I["lamv"][e].partition_broadcast(128))
        lp = AFa.alloc([128])
        lv4 = LV[:, :].rearrange("p (a b d) -> p a b d", a=2, b=2)
        P.tt(v3(lp[:, :], 64), lv4[:, :, 0, :], lv4[:, :, 1, :], ALU.mult)
        l2 = AFa.alloc([2])
        P.red(l2[:, :], v3(lp[:, :], 64))
        P.act(l2[:, :], l2[:, :], AF.Exp)
        NLAM = AFa.alloc([1])
        P.tt(NLAM[:, :], l2[:, 1:2], l2[:, 0:1], ALU.subtract)
        P.ts(NLAM[:, :], NLAM[:, :], -lam_init, ALU.add)
        SUBG = AFa.alloc([1])
        P.load(SUBG[:, :], I["sublng"][e])
        P.ts(SUBG[:, :], SUBG[:, :], 1.0 - lam_init, ALU.mult)
        ABIAS = AFa.alloc([64])
        P.load(ABIAS[:, :], I["c_abias"])
        QS = AFa.alloc([4, 128])
        KS = AFa.alloc([4, 128])
        VS = AFa.alloc([4, 128])
        ZS = AFa.alloc([4, 16])
        VST = AFa.alloc([4, 16])
        K.ep_sq = ABa.alloc([512])
        K.ep_rs = AFa.alloc([512])

        for h in range(4 if cfg.get("attn", True) else 0):
            wv = wv512(next_w(("ea", e, h)))
            with Scope():
                QT_ = ABa.alloc([T])
                KT_ = ABa.alloc([T])
                VTM = ABa.alloc([NT, 128])
                NI = 4
                nsq = [AFa.alloc([256]) for _ in range(NI)]
                nss = [AFa.alloc([4]) for _ in range(NI)]
                nqk = [AFa.alloc([256]) for _ in range(NI)]
                nvf = [AFa.alloc([128]) for _ in range(NI)]

                def tile_ops(n, bi):
                    rows = 128 if n < 16 else 16
                    c0 = n * 128
                    ps = psn()
                    for k in range(8):
                        P.mm(ps[0:rows, 0:384], HT[:, k, c0:c0 + rows], wv[:, k, 0:384], start=(k == 0), stop=(k == 7))
                    sq, ss, qkn, vf = nsq[bi], nss[bi], nqk[bi], nvf[bi]
                    P.act(sq[0:rows, :], ps[0:rows, 0:256], AF.Square)
                    P.red(ss[0:rows, :], v3(sq[0:rows, :], 64))
                    P.ts(ss[0:rows, :], ss[0:rows, :], 1.0 / 64, ALU.mult, EPS, ALU.add)
                    P.act(ss[0:rows, :], ss[0:rows, :], AF.Sqrt)
                    P.recip(ss[0:rows, :], ss[0:rows, :])
                    P.tt(v3(qkn[0:rows, :], 64), v3(ps[0:rows, 0:256], 64),
                         ss[0:rows, :].unsqueeze(2).to_broadcast([rows, 4, 64]), ALU.mult)
                    P.tt(qkn[0:rows, :], qkn[0:rows, :], GQK[0:rows, :], ALU.mult)
                    if n < 16:
                        P.store(O["nk_p"][e, c0:c0 + 128, h * 128:(h + 1) * 128], qkn[:, 128:256])
                        P.copy(vf[:, :], ps[:, 256:384], eng="act")
                        P.store(O["nv_p"][e, c0:c0 + 128, h * 128:(h + 1) * 128], vf[:, :])
                        P.copy(VTM[:, n, :], vf[:, :])
                        pst = psn()
                        P.tr(pst[:, 0:128], qkn[:, 0:128], ident[:])
                        P.tr(pst[:, 128:256], qkn[:, 128:256], ident[:])
                        P.copy(QT_[:, c0:c0 + 128], pst[:, 0:128], eng="act")
                        P.copy(KT_[:, c0:c0 + 128], pst[:, 128:256], eng="act")
                    else:
                        P.copy(QS[0:16, h, :], qkn[0:16, 0:128])
                        P.copy(KS[0:16, h, :], qkn[0:16, 128:256])
                        P.copy(VS[0:16, h, :], ps[0:16, 256:384], eng="act")
                        pst = psn()
                        P.tr(pst[:, 0:16], VS[0:16, h, :], ident[0:16, 0:16])
                        P.copy(VST[:, h, :], pst[:, 0:16])

                for n0 in range(0, 17, NI):
                    streams = []
                    for bi, n in enumerate(range(n0, min(n0 + NI, 17))):
                        K.p0, K.pn = 2 * bi, 2
                        cap = P.capture()
                        tile_ops(n, bi)
                        P.end_capture()
                        streams.append(cap)
                    K.p0, K.pn = 0, 8
                    P.merge(streams)
                ps = psn()
                for k in range(8):
                    P.mm(ps[:, 0:16], wv[:, k, 384:512], HT[:, k, T:TA], start=(k == 0), stop=(k == 7))
                P.act(ZS[:, h, :], ps[:, 0:16], AF.Silu)
                if cfg.get("attn_stage", 3) < 2:
                    continue
                K.p0, K.pn = 0, 4
                O1, O2, D1, D2 = PS[4], PS[5], PS[6], PS[7]
                Eb = [[ABa.alloc([512]) for _ in range(2)] for _ in range(2)]
                of = AFa.alloc([512])
                t1 = AFa.alloc([512])
                t2 = AFa.alloc([512])
                szb = AFa.alloc([512])
                it = 0
                for qb in range(4):
                    q0 = qb * 512
                    nkb = 4 * qb + 4
                    for kb in range(nkb):
                        j = kb - 4 * qb
                        cs = 128 * max(j, 0)
                        s1 = psn()
                        s2 = psn()
                        P.mm(s1[:, cs:512], KT_[0:64, kb * 128:(kb + 1) * 128], QT_[0:64, q0 + cs:q0 + 512])
                        P.mm(s2[:, cs:512], KT_[64:128, kb * 128:(kb + 1) * 128], QT_[64:128, q0 + cs:q0 + 512])
                        e1, e2 = Eb[it % 2]
                        it += 1
                        for sbk in range(cs // 128, 4):
                            d = (4 * qb + sbk) - kb
                            cc = slice(sbk * 128, (sbk + 1) * 128)
                            P.act(e1[:, cc], s1[:, cc], AF.Exp, bias=ABIAS[:, h * 16 + d:h * 16 + d + 1], scale=0.125)
                            P.act(e2[:, cc], s2[:, cc], AF.Exp, bias=ABIAS[:, h * 16 + d:h * 16 + d + 1], scale=0.125)
                        if j >= 0:
                            P.tt(e1[:, cs:cs + 128], e1[:, cs:cs + 128], trib[:], ALU.mult)
                            P.tt(e2[:, cs:cs + 128], e2[:, cs:cs + 128], trib[:], ALU.mult)
                        st, sp = (kb == 0), (kb == nkb - 1)
                        P.mm(O1[:, cs:512], VTM[:, kb, :], e1[:, cs:512], start=st, stop=sp)
                        P.mm(D1[:, cs:512], onesb[:], e1[:, cs:512], start=st, stop=sp)
                        P.mm(O2[:, cs:512], VTM[:, kb, :], e2[:, cs:512], start=st, stop=sp)
                        P.mm(D2[:, cs:512], onesb[:], e2[:, cs:512], start=st, stop=sp)
                    if cfg.get("attn_stage", 3) < 3:
                        continue
                    P.recip(t1[:, :], D1[:, :])
                    P.tt(t1[:, :], O1[:, :], t1[:, :], ALU.mult)
                    P.recip(t2[:, :], D2[:, :])
                    P.tt(t2[:, :], O2[:, :], t2[:, :], ALU.mult)
                    P.stt(of[:, :], t2[:, :], NLAM[:, 0:1], t1[:, :], ALU.mult, ALU.add)

                    def psz(q0=q0, wv=wv):
                        pz = psn()
                        for k in range(8):
                            P.mm(pz[:, :], wv[:, k, 384:512], HT[:, k, q0:q0 + 512], start=(k == 0), stop=(k == 7))
                        P.act(szb[:, :], pz[:, :], AF.Silu)
                        return szb[:, :]
                    rms_feature_gate(K, of, 512, SUBG[:, 0:1], psz, OT[:, h, q0:q0 + 512], "a")
                K.p0, K.pn = 0, 8

        if cfg.get("sample_attn", True) and cfg.get("attn", True):
            with Scope():
                PTB = AFa.alloc([256])
                ptb_i = PTB[:, :].bitcast(I32)
                P.load(ptb_i, I["pt"].partition_broadcast(128))
                PTF = AFa.alloc([256])
                P.copy(PTF[:, :], ptb_i)
                IO = AFa.alloc([1])
                P.load(IO[:, :], I["c_iota"])
                P.ts(PTF[:, :], PTF[:, :], 128.0, ALU.mult, IO[:, 0:1], ALU.add)
                if e > 0:
                    P.ts(PTF[:, :], PTF[:, :], float(e * NPHYS * 128), ALU.add)
                IDX = AFa.alloc([256])
                idx_i = IDX[:, :].bitcast(I32)
                P.copy(idx_i, PTF[:, :])
                ALB = AFa.alloc([NPG, 4])
                P.load(ALB[:, :, :], v3(I["c_alibis"], 4))
                SEL = AFa.alloc([128])
                sp_ = AFa.alloc([512])
                P.tt(sp_[0:16, :], QS[0:16, :, :].rearrange("p h d -> p (h d)"), KS[0:16, :, :].rearrange("p h d -> p (h d)"), ALU.mult)
                ES = AFa.alloc([8])
                P.red(ES[0:16, :], v3(sp_[0:16, :], 64))
                P.act(ES[0:16, :], ES[0:16, :], AF.Exp, scale=0.125)
                QB = AFa.alloc([512])
                NPB = 8
                KP = [ABa.alloc([512]) for _ in range(NPB)]
                VP = [ABa.alloc([512]) for _ in range(NPB)]
                PPb = ABa.alloc([NPG, 4])
                prod = AFa.alloc([512])
                SC = AFa.alloc([NPG, 8])
                EE = AFa.alloc([NPG, 8])
                PP = AFa.alloc([NPG, 4])
                rsum = AFa.alloc([8])
                rinv = AFa.alloc([8])
                esb = AFa.alloc([8])
                psl = AFa.alloc([4])
                tv = AFa.alloc([4])
                OAS = AFa.alloc([4, 16])
                psO = PS[7]
                K.p0, K.pn = 0, 7
                ck = I["ck"].rearrange("e r c -> (e r) c")
                cvv = I["cvv"].rearrange("e r c -> (e r) c")
                for s in range(NS):
                    P.ts(SEL[0:16, :], ones[0:16, :], ident[0:16, s:s + 1], ALU.mult)
                    pq = psn()
                    P.mm(pq[:, :], SEL[0:16, :], QS[0:16, :, :].rearrange("p h d -> p (h d)"))
                    P.copy(QB[:, :], pq[:, :], eng="act")
                    for pg in range(NPG):
                        kp = KP[(s * NPG + pg) % NPB]
                        P.gather(kp[:, :], ck, idx_i[:, s * NPG + pg:s * NPG + pg + 1])
                        P.tt(prod[:, :], kp[:, :], QB[:, :], ALU.mult)
                        P.red(SC[:, pg, :], v3(prod[:, :], 64))
                    sc4 = SC[:, :, :].rearrange("p g (h t) -> p g h t", t=2)
                    P.stt(sc4, sc4, 0.125, ALB[:, :, :].unsqueeze(3).to_broadcast([128, NPG, 4, 2]), ALU.mult, ALU.add)
                    P.act(EE[:, :, :], SC[:, :, :], AF.Exp)
                    P.red(rsum[:, :], EE[:, :, :].rearrange("p g e -> p e g"))
                    pt_ = psn()
                    P.mm(pt_[:, 0:8], ones[:, :], rsum[:, :], start=True, stop=False)
                    P.mm(pt_[:, 0:8], SEL[0:16, :], ES[0:16, :], start=False, stop=True)
                    P.mm(pt_[:, 8:16], SEL[0:16, :], ES[0:16, :])
                    P.recip(rinv[:, :], pt_[:, 0:8])
                    P.tt(esb[:, :], pt_[:, 8:16], rinv[:, :], ALU.mult)
                    es2 = esb[:, :].rearrange("p (h t) -> p h t", t=2)
                    P.stt(psl[:, :], es2[:, :, 1], K.NLAM[:, 0:1] if False else NLAM[:, 0:1], es2[:, :, 0], ALU.mult, ALU.add)
                    P.tt(EE[:, :, :], EE[:, :, :], rinv[:, :].unsqueeze(1).to_broadcast([128, NPG, 8]), ALU.mult)
                    ee4 = EE[:, :, :].rearrange("p g (h t) -> p g h t", t=2)
                    P.stt(PP[:, :, :], ee4[:, :, :, 1], NLAM[:, 0:1], ee4[:, :, :, 0], ALU.mult, ALU.add)
                    P.copy(PPb[:, :, :], PP[:, :, :], eng="act")
                    for pg in range(NPG):
                        vp = VP[(s * NPG + pg) % NPB]
                        P.gather(vp[:, :], cvv, idx_i[:, s * NPG + pg:s * NPG + pg + 1])
                        for h in range(4):
                            P.mm(psO[:, h * 16 + s:h * 16 + s + 1], vp[:, h * 128:(h + 1) * 128], PPb[:, pg, h:h + 1],
                                 start=(pg == 0), stop=(pg == NPG - 1))
                    P.tt(tv[:, :], VST[:, :, s], psl[:, :], ALU.mult)
                    P.tt(OAS[:, :, s], v3(psO[:, 0:64], 16)[:, :, s], tv[:, :], ALU.add)
                K.p0, K.pn = 0, 8
                P.store(O["nk_s"][e].rearrange("s (h d) -> s h d", d=128), KS[0:16, :, :])
                P.store(O["nv_s"][e].rearrange("s (h d) -> s h d", d=128), VS[0:16, :, :])
                for h in range(4):
                    rms_feature_gate(K, OAS[:, h, :], 16, SUBG[:, 0:1], (lambda h=h: ZS[:, h, :]), OT[:, h, T:TA], "as")

    if not (cfg.get("sample_attn", True) and cfg.get("attn", True)):
        for h_ in range(4):
            P.memset(OT[:, h_, T:TA], 0.0)
    if cfg.get("gdn", True):
        gdn_heads(K, li)


def ret_heads(K, li):
    P, I, O = K.P, K.I, K.O
    XT, HT, OT, AFa, ABa, PS, psn, Scope = K.XT, K.HT, K.OT, K.AFa, K.ABa, K.PS, K.psn, K.Scope
    next_w, wv512, v3, ident, ones, onesb = K.next_w, K.wv512, K.v3, K.ident, K.ones, K.onesb
    o_ = li // 2
    with Scope():
        DTC = AFa.alloc([4, 128])
        P.load(DTC[:, :, :], I["c_dtc"].rearrange("h j i -> j h i"))
        KDC = AFa.alloc([4])
        EGC = AFa.alloc([4])
        P.load(KDC[:, :], I["c_kdc"])
        P.load(EGC[:, :], I["c_egc"])
        GND = AFa.alloc([128])
        P.load(GND[:, :], I["gnd"][o_].partition_broadcast(128))
        K.MASK16 = AFa.alloc([16, 16])
        P.load(K.MASK16[:, :, :], v3(I["c_mask16"], 16))

        def alloc_bufs():
            B = {}
            B["S"] = AFa.alloc([128])
            B["QT"] = AFa.alloc([512])
            B["KTf"] = AFa.alloc([512])
            B["SZ"] = AFa.alloc([512])
            B["mats"] = [AFa.alloc([128]) for _ in range(5)]
            B["fo"] = (AFa.alloc([128]), AFa.alloc([2]), AFa.alloc([128]))
            return B

        def head_prompt(h, wv, B):
            g128 = math.exp(128.0 * LOG_GAMMA[h])
            S, QT, KTf, SZ = B["S"], B["QT"], B["KTf"], B["SZ"]
            V, KD, AQ, O1, OO = B["mats"]
            K.fo_sq, K.fo_ss, K.fo_on = B["fo"]
            P.memset(S[:, :], 0.0)
            for tb in range(4):
                c0 = tb * 512
                for j, dst in ((0, QT), (1, KTf)):
                    ps = psn()
                    for k in range(8):
                        P.mm(ps[:, :], wv[:, k, j * 128:(j + 1) * 128], HT[:, k, c0:c0 + 512], start=(k == 0), stop=(k == 7))
                    P.copy(dst[:, :], ps[:, :], eng=("act" if j else "dve"))
                ps = psn()
                for k in range(8):
                    P.mm(ps[:, :], wv[:, k, 384:512], HT[:, k, c0:c0 + 512], start=(k == 0), stop=(k == 7))
                P.act(SZ[:, :], ps[:, :], AF.Silu)
                for tl in range(4):
                    n = tb * 4 + tl
                    cc = slice(tl * 128, (tl + 1) * 128)
                    ps = psn()
                    for k in range(8):
                        P.mm(ps[:, 0:256], HT[:, k, n * 128:(n + 1) * 128], wv[:, k, 128:384], start=(k == 0), stop=(k == 7))
                    P.copy(V[:, :], ps[:, 128:256], eng="act")
                    P.ts(KD[:, :], ps[:, 0:128], KDC[:, h:h + 1], ALU.mult)
                    pa = psn()
                    P.mm(pa[:, 0:128], KTf[:, cc], QT[:, cc])
                    P.tt(AQ[:, :], pa[:, 0:128], DTC[:, h, :], ALU.mult)
                    p2 = psn()
                    P.mm(p2[:, 0:128], QT[:, cc], S[:, :])
                    P.ts(O1[:, :], p2[:, 0:128], EGC[:, h:h + 1], ALU.mult)
                    p4 = psn()
                    P.mm(p4[:, 0:128], AQ[:, :], V[:, :])
                    P.tt(OO[:, :], p4[:, 0:128], O1[:, :], ALU.add)
                    p3 = psn()
                    P.mm(p3[:, 0:128], KD[:, :], V[:, :])
                    P.stt(S[:, :], S[:, :], g128, p3[:, 0:128], ALU.mult, ALU.add)
                    finish_o_ln(K, OO[:, :], 128, GND, SZ[:, cc], OT[:, 4 + h, n * 128:(n + 1) * 128])
            P.store(O["rt_p"][o_, h], S[:, :])

        def head_sample(h, wv, B):
            g1 = math.exp(LOG_GAMMA[h])
            SZ = B["SZ"]
            K.fo_sq, K.fo_ss, K.fo_on = B["fo"]
            QS_ = [AFa.alloc([16]) for _ in range(3)]
            for j in range(3):
                ps = psn()
                for k in range(8):
                    P.mm(ps[:, 0:16], wv[:, k, j * 128:(j + 1) * 128], HT[:, k, T:TA], start=(k == 0), stop=(k == 7))
                if j == 1:
                    P.ts(QS_[j][:, :], ps[:, 0:16], 128.0 ** -0.5, ALU.mult)
                else:
                    P.copy(QS_[j][:, :], ps[:, 0:16], eng="act")
            ps = psn()
            for k in range(8):
                P.mm(ps[:, 0:16], wv[:, k, 384:512], HT[:, k, T:TA], start=(k == 0), stop=(k == 7))
            P.act(SZ[:, 0:16], ps[:, 0:16], AF.Silu)
            sample_state_step(K, QS_[0][:, :], QS_[1][:, :], QS_[2][:, :], g1, None, None, g1,
                              (lambda s, h=h: I["sret"][o_, s, h]), (lambda s, h=h: O["rt_s"][o_, s, h]),
                              GND, SZ[:, 0:16], OT[:, 4 + h, T:TA], False, True)

        NPAR = 2
        for hp in range(0, 4, NPAR):
            wvs = [wv512(next_w(("od", o_, hp + i), prefetch=(i == 0))) for i in range(NPAR)]
            with Scope():
                Bs = [alloc_bufs() for _ in range(NPAR)]
                streams = []
                for i in range(NPAR):
                    K.p0, K.pn = (8 // NPAR) * i, 8 // NPAR
                    cap = P.capture()
                    head_prompt(hp + i, wvs[i], Bs[i])
                    P.end_capture()
                    streams.append(cap)
                K.p0, K.pn = 0, 8
                P.merge(streams)
                for i in range(NPAR):
                    with Scope():
                        head_sample(hp + i, wvs[i], Bs[i])


def s5_mixer(K, li):
    P, I, O = K.P, K.I, K.O
    XT, HT, OT, AFa, ABa, PS, psn, Scope = K.XT, K.HT, K.OT, K.AFa, K.ABa, K.PS, K.psn, K.Scope
    v3, ident, ones, onesb, wv512 = K.v3, K.ident, K.ones, K.onesb, K.wv512
    o_ = li // 2
    TC = 512
    with Scope():
        WU = ABa.alloc([4096])
        P.load(WU[:, :], I["wino"][o_, 0], eng="pool")
        wu = wv512(WU)
        ARE, AIM, LDT = AFa.alloc([16]), AFa.alloc([16]), AFa.alloc([16])
        P.load(ARE[:, :], I["s5are"][o_].rearrange("(t l) -> l t", l=128), slow=True)
        P.load(AIM[:, :], I["s5aim"][o_].rearrange("(t l) -> l t", l=128), slow=True)
        ldt2 = I["s5ldt"][o_].rearrange("o (t two) -> o two t", two=2)
        P.load(LDT[0:64, :], ldt2[:, 0, :].partition_broadcast(64), slow=True)
        P.load(LDT[64:128, :], ldt2[:, 1, :].partition_broadcast(64), slow=True)
        P.act(LDT[:, :], LDT[:, :], AF.Exp)
        LRE, TH, RR = AFa.alloc([16]), AFa.alloc([16]), AFa.alloc([16])
        P.tt(LRE[:, :], ARE[:, :], LDT[:, :], ALU.mult)
        P.tt(TH[:, :], AIM[:, :], LDT[:, :], ALU.mult)
        P.act(RR[:, :], LRE[:, :], AF.Exp)
        HPI = AFa.alloc([1])
        P.memset(HPI[:, :], math.pi / 2)
        C1, S1, t1_, t2_ = AFa.alloc([16]), AFa.alloc([16]), AFa.alloc([16]), AFa.alloc([16])
        P.act(S1[:, :], TH[:, :], AF.Sin, scale=1.0 / 16)
        P.act(C1[:, :], TH[:, :], AF.Sin, scale=1.0 / 16, bias=HPI[:, 0:1])
        for _ in range(4):
            P.tt(t1_[:, :], C1[:, :], C1[:, :], ALU.mult)
            P.tt(t2_[:, :], S1[:, :], S1[:, :], ALU.mult)
            P.stt(S1[:, :], C1[:, :], 2.0, S1[:, :], ALU.mult, ALU.mult)
            P.tt(C1[:, :], t1_[:, :], t2_[:, :], ALU.subtract)
        LBR, LBI, NLBI = AFa.alloc([16]), AFa.alloc([16]), AFa.alloc([16])
        P.tt(LBR[:, :], RR[:, :], C1[:, :], ALU.mult)
        P.tt(LBI[:, :], RR[:, :], S1[:, :], ALU.mult)
        P.ts(NLBI[:, :], LBI[:, :], -1.0, ALU.mult)
        DEN, FRE, FIM, LM1 = AFa.alloc([16]), AFa.alloc([16]), AFa.alloc([16]), AFa.alloc([16])
        P.tt(DEN[:, :], ARE[:, :], ARE[:, :], ALU.mult)
        P.tt(t1_[:, :], AIM[:, :], AIM[:, :], ALU.mult)
        P.tt(DEN[:, :], DEN[:, :], t1_[:, :], ALU.add)
        P.recip(DEN[:, :], DEN[:, :])
        P.ts(LM1[:, :], LBR[:, :], -1.0, ALU.add)
        P.tt(FRE[:, :], LM1[:, :], ARE[:, :], ALU.mult)
        P.tt(t1_[:, :], LBI[:, :], AIM[:, :], ALU.mult)
        P.tt(FRE[:, :], FRE[:, :], t1_[:, :], ALU.add)
        P.tt(FRE[:, :], FRE[:, :], DEN[:, :], ALU.mult)
        P.tt(FIM[:, :], LBI[:, :], ARE[:, :], ALU.mult)
        P.tt(t1_[:, :], LM1[:, :], AIM[:, :], ALU.mult)
        P.tt(FIM[:, :], FIM[:, :], t1_[:, :], ALU.subtract)
        P.tt(FIM[:, :], FIM[:, :], DEN[:, :], ALU.mult)
        BRE, BIM, BBR, BBI, tb_ = [AFa.alloc([16, 16]) for _ in range(5)]
        P.load(BRE[:, :, :], I["s5bre"][o_].rearrange("(t l) c -> l t c", l=128))
        P.load(BIM[:, :, :], I["s5bim"][o_].rearrange("(t l) c -> l t c", l=128))
        frb = FRE[:, :].unsqueeze(2).to_broadcast([128, 16, 16])
        fib = FIM[:, :].unsqueeze(2).to_broadcast([128, 16, 16])
        P.tt(BBR[:, :, :], BRE[:, :, :], frb, ALU.mult)
        P.tt(tb_[:, :, :], BIM[:, :, :], fib, ALU.mult)
        P.tt(BBR[:, :, :], BBR[:, :, :], tb_[:, :, :], ALU.subtract)
        P.tt(BBI[:, :, :], BRE[:, :, :], fib, ALU.mult)
        P.tt(tb_[:, :, :], BIM[:, :, :], frb, ALU.mult)
        P.tt(BBI[:, :, :], BBI[:, :, :], tb_[:, :, :], ALU.add)
        DSK = AFa.alloc([4])
        P.load(DSK[:, :], I["s5d"][o_].rearrange("(c p) -> p c", p=128), slow=True)
        XPall = AFa.alloc([16, 2])
        UT = AFa.alloc([4, TC])
        UTs = AFa.alloc([16])
        EC, ESn = AFa.alloc([TC]), AFa.alloc([TC])
        W = [AFa.alloc([TC]) for _ in range(4)]
        Bm, Cin = AFa.alloc([128]), AFa.alloc([128])
        CCr, CCi, CMK = AFa.alloc([64]), AFa.alloc([64]), AFa.alloc([8])
        P.load(CMK[:, :], I["c_cmask"])
        BLr, BLi, CLr, CLi = [AFa.alloc([128]) for _ in range(4)]
        XP = AFa.alloc([2])
        nsm = AFa.alloc([1])
        xs0 = AFa.alloc([2, 16])
        xtm = AFa.alloc([128])
        xsn = AFa.alloc([2, 16])
        xso = AFa.alloc([128])
        yt = AFa.alloc([TC])
        YA = [PS[0], PS[1], PS[2], PS[3]]
        YS = PS[4]
        K.p0, K.pn = 5, 3
        for ch in range(4):
            for tc in range(4):
                ps = psn()
                for k in range(8):
                    P.mm(ps[:, :], wu[:, k, ch * 128:(ch + 1) * 128], HT[:, k, tc * TC:(tc + 1) * TC], start=(k == 0), stop=(k == 7))
                P.copy(UT[:, tc, :], ps[:, :], eng="act")
            ps = psn()
            for k in range(8):
                P.mm(ps[:, 0:16], wu[:, k, ch * 128:(ch + 1) * 128], HT[:, k, T:TA], start=(k == 0), stop=(k == 7))
            P.copy(UTs[:, :], ps[:, 0:16], eng="act")
            P.load(CCr[:, :], I["s5cre"][o_][ch * 128:(ch + 1) * 128, :])
            P.load(CCi[:, :], I["s5cim"][o_][ch * 128:(ch + 1) * 128, :])
            for li_ in range(4):
                lt = ch * 4 + li_
                off = li_ * 32
                for src, dst, neg in ((BBR, BLr, False), (BBI, BLi, False)):
                    P.memset(Bm[:, :], 0.0)
                    P.copy(Bm[0:64, off:off + 16], src[0:64, lt, :])
                    P.copy(Bm[64:128, off + 16:off + 32], src[64:128, lt, :])
                    pt = psn()
                    P.tr(pt[:, 0:128], Bm[:, :], ident[:])
                    P.copy(dst[:, :], pt[:, 0:128], eng="act")
                for nm, dst, neg in (("s5cre", CLr, False), ("s5cim", CLi, True)):
                    ccs = CCr if nm == "s5cre" else CCi
                    P.ts(Cin[:, 0:64], ccs[:, :], CMK[:, li_ * 2:li_ * 2 + 1], ALU.mult)
                    P.ts(Cin[:, 64:128], ccs[:, :], CMK[:, li_ * 2 + 1:li_ * 2 + 2], ALU.mult)
                    pt = psn()
                    P.tr(pt[:, 0:128], Cin[:, :], ident[:])
                    if neg:
                        P.ts(dst[:, :], pt[:, 0:128], -1.0, ALU.mult)
                    else:
                        P.copy(dst[:, :], pt[:, 0:128], eng="act")
                P.copy(EC[:, 0:1], C1[:, lt:lt + 1])
                P.copy(ESn[:, 0:1], S1[:, lt:lt + 1])
                m = 1
                while m < TC:
                    cm, sm = EC[:, m - 1:m], ESn[:, m - 1:m]
                    P.ts(nsm[:, :], sm, -1.0, ALU.mult)
                    P.ts(EC[:, m:2 * m], EC[:, 0:m], cm, ALU.mult)
                    P.stt(EC[:, m:2 * m], ESn[:, 0:m], nsm[:, 0:1], EC[:, m:2 * m], ALU.mult, ALU.add)
                    P.ts(ESn[:, m:2 * m], EC[:, 0:m], sm, ALU.mult)
                    P.stt(ESn[:, m:2 * m], ESn[:, 0:m], cm, ESn[:, m:2 * m], ALU.mult, ALU.add)
                    m *= 2
                Rb = RR[:, lt:lt + 1].to_broadcast([128, TC])
                P.memset(XP[:, :], 0.0)
                for tc in range(4):
                    pbr = psn()
                    P.mm(pbr[:, :], BLr[:, :], UT[:, tc, :])
                    pbi = psn()
                    P.mm(pbi[:, :], BLi[:, :], UT[:, tc, :])
                    vr, vi, zr, zi = W
                    P.tt(vr[:, :], pbr[:, :], EC[:, :], ALU.mult)
                    P.tt(zr[:, :], pbi[:, :], ESn[:, :], ALU.mult)
                    P.tt(vr[:, :], vr[:, :], zr[:, :], ALU.add)
                    P.tt(vi[:, :], pbi[:, :], EC[:, :], ALU.mult)
                    P.tt(zr[:, :], pbr[:, :], ESn[:, :], ALU.mult)
                    P.tt(vi[:, :], vi[:, :], zr[:, :], ALU.subtract)
                    P.scan(zr[:, :], Rb, vr[:, :], XP[:, 0:1])
                    P.scan(zi[:, :], Rb, vi[:, :], XP[:, 1:2])
                    P.tt(vr[:, :], zr[:, :], EC[:, :], ALU.mult)
                    P.tt(yt[:, :], zi[:, :], ESn[:, :], ALU.mult)
                    P.tt(vr[:, :], vr[:, :], yt[:, :], ALU.subtract)
                    P.tt(vi[:, :], zr[:, :], ESn[:, :], ALU.mult)
                    P.tt(yt[:, :], zi[:, :], EC[:, :], ALU.mult)
                    P.tt(vi[:, :], vi[:, :], yt[:, :], ALU.add)
                    P.copy(XP[:, 0:1], vr[:, TC - 1:TC])
                    P.copy(XP[:, 1:2], vi[:, TC - 1:TC])
                    P.mm(YA[tc][:, :], CLr[:, :], vr[:, :], start=(li_ == 0), stop=False)
                    P.mm(YA[tc][:, :], CLi[:, :], vi[:, :], start=False, stop=(li_ == 3))
                P.copy(XPall[:, lt, :], XP[:, :])
                for ri, nm in ((0, "scre"), (1, "scim")):
                    P.load(xtm[0:16, :], I[nm][o_][:, lt * 128:(lt + 1) * 128])
                    pt = psn()
                    P.tr(pt[:, 0:16], xtm[0:16, :], ident[0:16, 0:16])
                    P.copy(xs0[:, ri, :], pt[:, 0:16])
                pbr = psn()
                P.mm(pbr[:, 0:16], BLr[:, :], UTs[:, :])
                P.mm(pbr[:, 16:32], BLi[:, :], UTs[:, :])
                P.ts(xsn[:, 0, :], xs0[:, 0, :], LBR[:, lt:lt + 1], ALU.mult)
                P.stt(xsn[:, 0, :], xs0[:, 1, :], NLBI[:, lt:lt + 1], xsn[:, 0, :], ALU.mult, ALU.add)
                P.tt(xsn[:, 0, :], xsn[:, 0, :], pbr[:, 0:16], ALU.add)
                P.ts(xsn[:, 1, :], xs0[:, 1, :], LBR[:, lt:lt + 1], ALU.mult)
                P.stt(xsn[:, 1, :], xs0[:, 0, :], LBI[:, lt:lt + 1], xsn[:, 1, :], ALU.mult, ALU.add)
                P.tt(xsn[:, 1, :], xsn[:, 1, :], pbr[:, 16:32], ALU.add)
                P.mm(YS[:, 0:16], CLr[:, :], xsn[:, 0, :], start=(li_ == 0), stop=False)
                P.mm(YS[:, 0:16], CLi[:, :], xsn[:, 1, :], start=False, stop=(li_ == 3))
                for ri, nm in ((0, "s5r_s"), (1, "s5i_s")):
                    pt = psn()
                    P.tr(pt[0:16, 0:128], xsn[:, ri, :], ident[:])
                    P.copy(xso[0:16, :], pt[0:16, 0:128])
                    P.store(O[nm][o_][:, lt * 128:(lt + 1) * 128], xso[0:16, :])
            for tc in range(4):
                P.stt(yt[:, :], UT[:, tc, :], DSK[:, ch:ch + 1], YA[tc][:, :], ALU.mult, ALU.add)
                P.act(OT[:, ch, tc * TC:(tc + 1) * TC], yt[:, :], AF.Gelu_apprx_tanh)
            P.stt(yt[:, 0:16], UTs[:, :], DSK[:, ch:ch + 1], YS[:, 0:16], ALU.mult, ALU.add)
            P.act(OT[:, ch, T:TA], yt[:, 0:16], AF.Gelu_apprx_tanh)
        P.store(O["s5r_p"][o_].rearrange("(t l) -> l t", l=128), XPall[:, :, 0], slow=True)
        P.store(O["s5i_p"][o_].rearrange("(t l) -> l t", l=128), XPall[:, :, 1], slow=True)
        K.p0, K.pn = 0, 8
    with Scope():
        WZ = ABa.alloc([4096])
        WG = ABa.alloc([2048])
        P.load(WZ[:, :], I["wino"][o_, 1], eng="pool")
        P.load(WG[:, :], I["wglu"][o_], eng="pool")
        wz = wv512(WZ)
        wg = WG[:, :].rearrange("p (k c) -> p k c", c=512)
        sg = [AFa.alloc([512]) for _ in range(4)]
        szb = AFa.alloc([512])
        for (c0, w) in BLOCKS:
            pgs = []
            for cp in range(4):
                pg = PS[cp]
                for k in range(4):
                    P.mm(pg[:, 0:w], wg[:, k, cp * 128:(cp + 1) * 128], OT[:, k, c0:c0 + w], start=(k == 0), stop=(k == 3))
                pgs.append(pg)
            for cp in range(4):
                P.act(sg[cp][:, 0:w], pgs[cp][:, 0:w], AF.Sigmoid)
            for cp in range(4):
                pz = PS[4 + cp]
                for k in range(8):
                    P.mm(pz[:, 0:w], wz[:, k, cp * 128:(cp + 1) * 128], HT[:, k, c0:c0 + w], start=(k == 0), stop=(k == 7))
                P.act(szb[:, 0:w], pz[:, 0:w], AF.Silu)
                P.tt(sg[cp][:, 0:w], sg[cp][:, 0:w], szb[:, 0:w], ALU.mult)
                P.tt(OT[:, cp, c0:c0 + w], OT[:, cp, c0:c0 + w], sg[cp][:, 0:w], ALU.mult)


def odd_layer(K, li):
    if K.cfg.get("s5", True):
        s5_mixer(K, li)
    if K.cfg.get("ret", True):
        ret_heads(K, li)


def build(cfg=None):
    cfg = cfg or {}
    nlayers = cfg.get("layers", 4)
    nc = bass.Bass("TRN2", target_bir_lowering=False)
    specs = INPUT_SPECS
    if cfg.get("small_cache"):
        specs = [(n, ((2, 256, 512) if n in ("ck", "cvv") else s), dt) for n, s, dt in INPUT_SPECS]
    I = {n: nc.dram_tensor(n, list(s), dt, kind="ExternalInput").ap() for n, s, dt in specs}
    O = {n: nc.dram_tensor(n, list(s), F32, kind="ExternalOutput").ap() for n, s in OUTPUT_SPECS}
    if cfg.get("dbg"):
        O["dbg_xt"] = nc.dram_tensor("dbg_xt", [128, 8 * TA], F32, kind="ExternalOutput").ap()
        O["dbg_ht"] = nc.dram_tensor("dbg_ht", [128, 8 * TA], BF16, kind="ExternalOutput").ap()
        O["dbg_ot"] = nc.dram_tensor("dbg_ot", [128, 8 * TA], BF16, kind="ExternalOutput").ap()
    K = Ctx()
    with ExitStack() as es:
        P = Prog(nc, es)
        P.limit = cfg.get("max_ops")
        if cfg.get("trace"):
            P.trace = []
        K.P = P

        def sbt(name, shape, dt=F32):
            return es.enter_context(nc.sbuf_tensor(name, shape, dt))

        XT = sbt("XT", [128, 8, TA])
        HT = sbt("HT", [128, 8, TA], BF16)
        OT = sbt("OT", [128, 8, TA], BF16)
        WB = [sbt("WB0", [128, 4096], BF16), sbt("WB1", [128, 4096], BF16)]
        AFa = Arena(sbt("AFa", [128, NF]), NF)
        ABa = Arena(sbt("ABa", [128, NB], BF16), NB)
        ident = sbt("ident", [128, 128])
        ones = sbt("ones", [128, 128])
        onesb = sbt("onesb", [128, 128], BF16)
        trib = sbt("trib", [128, 128], BF16)
        CT = sbt("CT", [128, 8, 17], BF16)
        MODT = sbt("MODT", [128, 24, 17])
        GS = sbt("GS", [128, 8, 17])
        BADA = sbt("BADA", [128, 24])
        NG = sbt("NG", [128, 8])
        PS = [es.enter_context(nc.psum_tensor("ps%d" % i, [128, 512], F32)) for i in range(8)]
        K.pi, K.p0, K.pn = 0, 0, 8

        def psn():
            b = PS[K.p0 + (K.pi % K.pn)]
            K.pi += 1
            return b

        class Scope:
            def __enter__(self):
                AFa.stack.append(AFa.top)
                ABa.stack.append(ABa.top)

            def __exit__(self, *a):
                P.barrier()
                AFa.top = AFa.stack.pop()
                ABa.top = ABa.stack.pop()
                return False

        def v3(ap, b):
            return ap.rearrange("p (a b) -> p a b", b=b)

        wlist = []
        for li in range(nlayers):
            for blk in range(6):
                wlist.append((("ada", li, blk), I["wada"][li, blk], 4096))
            if li % 2 == 0:
                e = li // 2
                if cfg.get("attn", True):
                    for h in range(4):
                        wlist.append((("ea", e, h), I["wine"][e, h], 4096))
                if cfg.get("gdn", True):
                    wlist.append((("wab", e), I["wab"][e], 64))
                    for h in range(4):
                        wlist.append((("eb", e, h), I["wine"][e, 4 + h], 4096))
                for blk in range(2):
                    wlist.append((("eo", e, blk), I["woute"][e, blk], 4096))
            else:
                o = li // 2
                if cfg.get("ret", True):
                    for h in range(4):
                        wlist.append((("od", o, h), I["wino"][o, 2 + h], 4096))
                for blk in range(2):
                    wlist.append((("oo", o, blk), I["wouto"][o, blk], 4096))
        K.wi, K.wissued = 0, 0

        def w_issue(i):
            tag, src, n = wlist[i]
            P.load(WB[i % 2][:, 0:n], src, eng="pool")

        def next_w(tag, prefetch=True):
            i = K.wi
            assert wlist[i][0] == tag, (wlist[i][0], tag)
            if K.wissued <= i:
                w_issue(i)
                K.wissued = i + 1
            if prefetch and i + 1 < len(wlist) and K.wissued <= i + 1:
                w_issue(i + 1)
                K.wissued = i + 2
            K.wi += 1
            return WB[i % 2]

        def wv512(wb):
            return wb[:, 0:4096].rearrange("p (k c) -> p k c", c=512)

        P.load(ident[:], I["c_ident"])
        P.load(ones[:], I["c_ones"])
        P.load(onesb[:], I["c_onesb"])
        P.load(trib[:], I["c_trib"])
        with Scope():
            xin = [AFa.alloc([1024]) for _ in range(2)]
            for n in range(17):
                rows = 128 if n < 16 else 16
                src = I["xp"][n * 128:(n + 1) * 128, :] if n < 16 else I["xs"]
                xt = xin[n % 2]
                P.load(xt[0:rows, :], src)
                for half in range(2):
                    ps = psn()
                    for j in range(4):
                        k = half * 4 + j
                        P.tr(ps[:, j * 128:j * 128 + rows], xt[0:rows, k * 128:(k + 1) * 128], ident[0:rows, 0:rows])
                    P.copy(XT[:, half * 4:(half + 1) * 4, n * 128:n * 128 + rows], v3(ps[:, :], 128)[:, :, 0:rows],
                           eng=("act" if half else "dve"))
            cvt = AFa.alloc([1024])
            P.load(cvt[0:17, :], I["cv"])
            P.act(cvt[0:17, :], cvt[0:17, :], AF.Silu)
            ps = psn()
            for k in range(8):
                P.tr(ps[:, k * 32:k * 32 + 17], cvt[0:17, k * 128:(k + 1) * 128], ident[0:17, 0:17])
            P.copy(CT[:], v3(ps[:, 0:256], 32)[:, :, 0:17])

        def modulation(li):
            P.load(BADA[:], I["bada"][li].rearrange("(c p) -> p c", p=128), slow=True)
            P.load(NG[:], I["normg"][li].rearrange("(c p) -> p c", p=128), slow=True)
            for blk in range(6):
                wv = wv512(next_w(("ada", li, blk)))
                ps = psn()
                for j in range(4):
                    for k in range(8):
                        P.mm(ps[:, j * 32:j * 32 + 17], wv[:, k, j * 128:(j + 1) * 128], CT[:, k, :],
                             start=(k == 0), stop=(k == 7))
                P.tt(MODT[:, blk * 4:(blk + 1) * 4, :], v3(ps[:, 0:128], 32)[:, :, 0:17],
                     BADA[:, blk * 4:(blk + 1) * 4].unsqueeze(2).to_broadcast([128, 4, 17]), ALU.add)
            P.ts(GS[:], MODT[:, 8:16, :], 1.0, ALU.add)
            P.tt(GS[:], GS[:], NG[:].unsqueeze(2).to_broadcast([128, 8, 17]), ALU.mult)

        def norm_ht():
            with Scope():
                RS = AFa.alloc([TA])
                sqb = [ABa.alloc([512]) for _ in range(2)]
                for (c0, w) in BLOCKS:
                    ps = psn()
                    for k in range(8):
                        sq = sqb[k % 2]
                        P.act(sq[:, 0:w], XT[:, k, c0:c0 + w], AF.Square)
                        P.mm(ps[:, 0:w], onesb[:], sq[:, 0:w], start=(k == 0), stop=(k == 7))
                    P.ts(RS[:, c0:c0 + w], ps[:, 0:w], 1.0 / D, ALU.mult, EPS, ALU.add)
                    P.act(RS[:, c0:c0 + w], RS[:, c0:c0 + w], AF.Sqrt)
                    P.recip(RS[:, c0:c0 + w], RS[:, c0:c0 + w])
                tmp = [AFa.alloc([512]) for _ in range(2)]
                i = 0
                for (c0, w) in BLOCKS:
                    for k in range(8):
                        t = tmp[i % 2]
                        i += 1
                        P.tt(t[:, 0:w], XT[:, k, c0:c0 + w], RS[:, c0:c0 + w], ALU.mult)
                        if c0 < T:
                            P.act(HT[:, k, c0:c0 + w], t[:, 0:w], AF.Identity, bias=MODT[:, k, 0:1], scale=GS[:, k, 0:1])
                        else:
                            P.tt(t[:, 0:w], t[:, 0:w], GS[:, k, 1:17], ALU.mult)
                            P.tt(HT[:, k, c0:c0 + w], t[:, 0:w], MODT[:, k, 1:17], ALU.add)

        def out_proj(tag, idx):
            with Scope():
                ts_ = AFa.alloc([16])
                for blk in range(2):
                    wv = wv512(next_w((tag, idx, blk)))
                    for j in range(4):
                        fc = blk * 4 + j
                        for (c0, w) in BLOCKS:
                            ps = psn()
                            for k in range(8):
                                P.mm(ps[:, 0:w], wv[:, k, j * 128:(j + 1) * 128], OT[:, k, c0:c0 + w],
                                     start=(k == 0), stop=(k == 7))
                            if c0 < T:
                                P.stt(XT[:, fc, c0:c0 + w], ps[:, 0:w], MODT[:, 16 + fc, 0:1], XT[:, fc, c0:c0 + w],
                                      ALU.mult, ALU.add)
                            else:
                                P.tt(ts_[:, :], ps[:, 0:16], MODT[:, 16 + fc, 1:17], ALU.mult)
                                P.tt(XT[:, fc, c0:c0 + 16], XT[:, fc, c0:c0 + 16], ts_[:, :], ALU.add)

        def final_out():
            with Scope():
                yb = [AFa.alloc([1024]) for _ in range(2)]
                for n in range(17):
                    rows = 128 if n < 16 else 16
                    yt = yb[n % 2]
                    for half in range(2):
                        ps = psn()
                        for j in range(4):
                            k = half * 4 + j
                            P.tr(ps[0:rows, j * 128:(j + 1) * 128], XT[:, k, n * 128:n * 128 + rows], ident[:])
                        P.copy(yt[0:rows, half * 512:(half + 1) * 512], ps[0:rows, :], eng=("act" if half else "dve"))
                    dst = O["y_p"][n * 128:(n + 1) * 128, :] if n < 16 else O["y_s"]
                    P.store(dst, yt[0:rows, :])

        def dbg_dump():
            if cfg.get("dbg"):
                P.store(O["dbg_xt"], XT[:].rearrange("p k t -> p (k t)"))
                P.store(O["dbg_ht"], HT[:].rearrange("p k t -> p (k t)"))
                P.store(O["dbg_ot"], OT[:].rearrange("p k t -> p (k t)"))

        K.__dict__.update(locals())
        for li in range(nlayers):
            modulation(li)
            norm_ht()
            if cfg.get("zero_ot", False) or True:
                pass
            if cfg.get("mods_only"):
                continue
            if not (cfg.get("attn", True) and cfg.get("gdn", True) and cfg.get("s5", True) and cfg.get("ret", True)):
                for k_ in range(8):
                    for (c0_, w_) in BLOCKS:
                        P.memset(OT[:, k_, c0_:c0_ + w_], 0.0)
            if li % 2 == 0:
                even_layer(K, li)
                out_proj("eo", li // 2)
            else:
                odd_layer(K, li)
                out_proj("oo", li // 2)
        final_out()
        dbg_dump()
        P.barrier()
        P.emit()
    K.n_instr = P.n_instr
    K.peaks = (AFa.peak, ABa.peak)
    return nc, K


def _wblocks(w, col_lists):
    kd = w.shape[0] // 128
    out = []
    for cols in col_lists:
        blk = w[:, cols].reshape(kd, 128, len(cols)).transpose(1, 0, 2).reshape(128, kd * len(cols))
        out.append(blk)
    return np.ascontiguousarray(np.stack(out, 0))


def _even_cols():
    lists = []
    for h in range(4):
        lists.append(np.concatenate([np.arange(h * 128, (h + 1) * 128) + off for off in (0, 512, 1024, 1536)]))
    for h in range(4):
        lists.append(np.concatenate([np.arange(h * 128, (h + 1) * 128) + off for off in (2048, 2560, 3072, 3584)]))
    return lists


def _odd_cols():
    lists = [np.arange(0, 512), np.arange(512, 1024)]
    for h in range(4):
        lists.append(np.concatenate([np.arange(h * 128, (h + 1) * 128) + off for off in (1024, 1536, 2048, 2560)]))
    return lists


def make_shared(inp):
    f = lambda a: np.ascontiguousarray(np.asarray(a, dtype=np.float32))
    sh = {}
    sh["ck"] = f(inp["cache_k"]).reshape(2, NPHYS * 128, 512)
    sh["cvv"] = f(inp["cache_v"]).reshape(2, NPHYS * 128, 512)
    sh["normg"] = f(inp["norm_g"])
    w_ada = f(inp["w_ada"])
    sh["wada"] = np.stack([_wblocks(w_ada[l], [np.arange(b * 512, (b + 1) * 512) for b in range(6)]) for l in range(4)], 0)
    sh["bada"] = f(inp["b_ada"])
    w_in_e = f(inp["w_in_e"])
    sh["wine"] = np.stack([_wblocks(w_in_e[e], _even_cols()) for e in range(2)], 0)
    sh["wab"] = np.stack([_wblocks(w_in_e[e], [np.arange(4096, 4104)])[0] for e in range(2)], 0)
    w_out_e = f(inp["w_out_e"])
    sh["woute"] = np.stack([_wblocks(w_out_e[e], [np.arange(0, 512), np.arange(512, 1024)]) for e in range(2)], 0)
    sh["qng"] = f(inp["qn_g"]).reshape(2, 1, 64)
    sh["kng"] = f(inp["kn_g"]).reshape(2, 1, 64)
    sh["lamv"] = np.ascontiguousarray(np.stack([f(inp["lam_q1"]), f(inp["lam_k1"]), f(inp["lam_q2"]), f(inp["lam_k2"])], 1)).reshape(2, 1, 256)
    sh["sublng"] = f(inp["subln_g"]).reshape(2, 128, 1)
    sh["convw"] = f(inp["conv_w"])
    sh["alog"] = f(inp["a_log"]).reshape(2, 1, 4)
    sh["dtb"] = f(inp["dt_bias"]).reshape(2, 1, 4)
    sh["gnb"] = f(inp["gn_b"]).reshape(2, 1, 128)
    w_in_o = f(inp["w_in_o"])
    sh["wino"] = np.stack([_wblocks(w_in_o[o], _odd_cols()) for o in range(2)], 0)
    w_out_o = f(inp["w_out_o"])
    sh["wouto"] = np.stack([_wblocks(w_out_o[o], [np.arange(0, 512), np.arange(512, 1024)]) for o in range(2)], 0)
    sh["s5are"] = f(inp["s5_a_re"]).reshape(2, 2048)
    sh["s5aim"] = f(inp["s5_a_im"]).reshape(2, 2048)
    sh["s5bre"] = f(inp["s5_b_re"]).reshape(2, 2048, 16)
    sh["s5bim"] = f(inp["s5_b_im"]).reshape(2, 2048, 16)
    sh["s5cre"] = f(inp["s5_c_re"]).reshape(2, 512, 64)
    sh["s5cim"] = f(inp["s5_c_im"]).reshape(2, 512, 64)
    sh["s5d"] = f(inp["s5_d"])
    sh["s5ldt"] = f(inp["s5_log_dt"]).reshape(2, 1, 32)
    w_glu = f(inp["w_glu"])
    sh["wglu"] = np.stack([_wblocks(w_glu[o], [np.arange(0, 512)])[0] for o in range(2)], 0)
    sh["gnd"] = f(inp["gn_d"]).reshape(2, 1, 128)
    sh.update(make_consts2(make_consts()))
    return sh


def make_core_inputs(inp, sh, c):
    f = lambda a: np.ascontiguousarray(np.asarray(a, dtype=np.float32))
    m = dict(sh)
    s0, s1 = c * NS, (c + 1) * NS
    m["xp"] = f(inp["x_prompt"][c])
    m["xs"] = f(inp["x_sample"][s0:s1, 0, :])
    m["cv"] = f(np.concatenate([np.asarray(inp["c_prompt"])[c:c + 1], np.asarray(inp["c_sample"])[s0:s1]], 0))
    m["pt"] = np.ascontiguousarray(np.asarray(inp["page_table"], dtype=np.int32)[s0:s1]).reshape(1, NS * NPG)
    m["sconv"] = f(np.asarray(inp["state_b_conv"])[:, s0:s1]).reshape(2, NS * 3, 1536)
    m["sssm"] = f(np.asarray(inp["state_b_ssm"])[:, s0:s1])
    m["scre"] = f(np.asarray(inp["state_c_re"])[:, s0:s1]).reshape(2, NS, 2048)
    m["scim"] = f(np.asarray(inp["state_c_im"])[:, s0:s1]).reshape(2, NS, 2048)
    m["sret"] = f(np.asarray(inp["state_d_ret"])[:, s0:s1])
    return m


_CACHE = {}


def kernel(**inp):
    ncores = 8
    if "nc" not in _CACHE:
        _CACHE["nc"] = build()[0]
    nc = _CACHE["nc"]
    sh = make_shared(inp)
    in_maps = [make_core_inputs(inp, sh, c) for c in range(ncores)]
    res = run_bass_kernel_spmd(nc, in_maps, core_ids=list(range(ncores)))
    R = res.results
    cat = lambda name, ax: np.concatenate([np.asarray(R[c][name]) for c in range(ncores)], axis=ax)
    stk = lambda name: np.stack([np.asarray(R[c][name]) for c in range(ncores)], 0)
    y_p = stk("y_p")
    y_s = cat("y_s", 0).reshape(ncores * NS, 1, D)
    nk_p = stk("nk_p").transpose(1, 0, 2, 3).reshape(2, ncores, T, 4, 128)
    nv_p = stk("nv_p").transpose(1, 0, 2, 3).reshape(2, ncores, T, 4, 128)
    nk_s = cat("nk_s", 1).reshape(2, ncores * NS, 1, 4, 128)
    nv_s = cat("nv_s", 1).reshape(2, ncores * NS, 1, 4, 128)
    cv_p = stk("cv_p").transpose(1, 0, 2, 3)
    cv_s = cat("cv_s", 1)
    dl_p = stk("dl_p").transpose(1, 0, 2, 3, 4)
    dl_s = cat("dl_s", 1)
    s5r_p = stk("s5r_p").transpose(1, 0, 2).reshape(2, ncores, 32, 64)
    s5i_p = stk("s5i_p").transpose(1, 0, 2).reshape(2, ncores, 32, 64)
    s5r_s = cat("s5r_s", 1).reshape(2, ncores * NS, 32, 64)
    s5i_s = cat("s5i_s", 1).reshape(2, ncores * NS, 32, 64)
    rt_p = stk("rt_p").transpose(1, 0, 2, 3, 4)
    rt_s = cat("rt_s", 1)
    outs = (y_p, y_s, nk_p, nv_p, nk_s, nv_s, cv_p, cv_s, dl_p, dl_s, s5r_p, s5i_p, s5r_s, s5i_s, rt_p, rt_s)
    return tuple(np.ascontiguousarray(o, dtype=np.float32) for o in outs)
```

```python
import math
import numpy as np
import ml_dtypes
import concourse.bass as bass
import concourse.mybir as mybir
from concourse.bass_utils import run_bass_kernel_spmd
from contextlib import ExitStack

F32 = mybir.dt.float32
BF16 = mybir.dt.bfloat16
I32 = mybir.dt.int32
AF = mybir.ActivationFunctionType
ALU = mybir.AluOpType
AX = mybir.AxisListType

ENGS = ("pe", "act", "dve", "pool", "sp")
N_DMA_SEMS = 24

D = 1024
T = 2048
NS = 16
TA = T + NS
NT = 16
NPG = 16
NPHYS = 2560
EPS = 1e-6


def _ap_region(ap):
    t = ap.tensor
    name = t.name
    shape = tuple(t.shape)
    dims = tuple(ap.ap)
    off = int(ap.offset)
    space = str(ap.space)
    if "SB" not in space.upper() and "PSUM" not in space.upper():
        ext = 0
        for st, cn in dims:
            ext += abs(st) * (cn - 1)
        return (name, 0, 1, off, off + ext + 1)
    R = 1
    for s in shape[1:]:
        R *= s
    if "PSUM" in space.upper() or name.startswith("ps"):
        return (name, 0, 128, 0, R)
    p0 = off // R
    f0 = off % R
    pst, pcn = dims[0]
    np_ = 1 if pst == 0 else pcn
    ext = 0
    for st, cn in dims[1:]:
        ext += abs(st) * (cn - 1)
    return (name, p0, p0 + np_, f0, f0 + ext + 1)


def _overlap(a, b):
    return a[0] == b[0] and a[1] < b[2] and b[1] < a[2] and a[3] < b[4] and b[3] < a[4]


def _covers(a, b):
    return a[0] == b[0] and a[1] <= b[1] and a[2] >= b[2] and a[3] <= b[3] and a[4] >= b[4]


class Prog:
    def __init__(self, nc, es):
        self.nc = nc
        self.es = es
        self.ops = {e: [] for e in ENGS}
        self.sem = {e: es.enter_context(nc.semaphore("s_" + e)) for e in ENGS}
        self.dsem = [es.enter_context(nc.semaphore("d%d" % i)) for i in range(N_DMA_SEMS)]
        self.dval = [0] * N_DMA_SEMS
        self.dnext = {"hw": 0, "sw": 0}
        self.cnt = {e: 0 for e in ENGS}
        self.seen = {e: {} for e in ENGS}
        self.recs = {}
        self.n_instr = 0

    def _semh(self, key):
        return self.dsem[key] if isinstance(key, int) else self.sem[key]

    def _need(self, eng, reads, writes):
        need = {}
        for ap in reads:
            r = _ap_region(ap)
            isps = r[0].startswith("ps")
            for rec in self.recs.get(r[0], ()):
                if (rec[3] or (isps and rec[1] != eng)) and _overlap(rec[0], r) and need.get(rec[1], 0) < rec[2]:
                    need[rec[1]] = rec[2]
        for ap in writes:
            r = _ap_region(ap)
            for rec in self.recs.get(r[0], ()):
                if _overlap(rec[0], r) and need.get(rec[1], 0) < rec[2]:
                    need[rec[1]] = rec[2]
        waits = []
        for k, v in need.items():
            if eng == "pe" and k == "pe":
                continue
            if self.seen[eng].get(k, 0) >= v:
                continue
            self.seen[eng][k] = v
            waits.append((k, v))
        return waits

    def _record(self, reads, writes, key, val):
        for ap in writes:
            r = _ap_region(ap)
            lst = self.recs.setdefault(r[0], [])
            lst[:] = [rec for rec in lst if not _covers(r, rec[0])]
            lst.append([r, key, val, True])
        for ap in reads:
            r = _ap_region(ap)
            lst = self.recs.setdefault(r[0], [])
            lst[:] = [rec for rec in lst if not (not rec[3] and rec[1] == key and rec[0] == r)]
            lst.append([r, key, val, False])

    limit = None
    trace = None
    _cap = None

    def capture(self):
        self._cap = []
        return self._cap

    def end_capture(self):
        self._cap = None

    def merge(self, streams):
        idx = [0] * len(streams)
        live = True
        while live:
            live = False
            for i, st in enumerate(streams):
                if idx[i] < len(st):
                    kind, eng, fn, r, w = st[idx[i]]
                    idx[i] += 1
                    live = True
                    (self.op if kind == "op" else self.dma)(eng, fn, r, w)

    def op(self, eng, fn, reads=(), writes=()):
        if self._cap is not None:
            self._cap.append(("op", eng, fn, reads, writes))
            return
        if self.limit is not None and self.n_instr >= self.limit:
            return
        if self.trace is not None:
            import traceback
            fr = traceback.extract_stack(limit=4)
            self.trace.append((self.n_instr, eng, [f"{f.lineno}" for f in fr[:-1]], [str(_ap_region(a)) for a in writes]))
        waits = self._need(eng, reads, writes)
        self.cnt[eng] += 1
        self.ops[eng].append((fn, waits, (eng, 1)))
        self._record(reads, writes, eng, self.cnt[eng])
        self.n_instr += 1

    def dma(self, eng, fn, reads=(), writes=()):
        if self._cap is not None:
            self._cap.append(("dma", eng, fn, reads, writes))
            return
        if self.limit is not None and self.n_instr >= self.limit:
            return
        half = N_DMA_SEMS // 2
        kind = "sw" if eng == "pool" else "hw"
        k = self.dnext[kind] + (half if kind == "sw" else 0)
        self.dnext[kind] = (self.dnext[kind] + 1) % half
        waits = self._need(eng, reads, writes)
        if self.dval[k] > 0 and self.seen[eng].get(k, 0) < self.dval[k]:
            self.seen[eng][k] = self.dval[k]
            waits.append((k, self.dval[k]))
        self.dval[k] += 16
        self.ops[eng].append((fn, waits, (k, 16)))
        self._record(reads, writes, k, self.dval[k])
        self.n_instr += 1

    def barrier(self):
        assert self._cap is None
        targets = [(e, self.cnt[e]) for e in ENGS if self.cnt[e] > 0]
        targets += [(k, self.dval[k]) for k in range(N_DMA_SEMS) if self.dval[k] > 0]
        for e in ENGS:
            waits = []
            for k, v in targets:
                if k == e or self.seen[e].get(k, 0) >= v:
                    continue
                self.seen[e][k] = v
                waits.append((k, v))
            if waits:
                self.ops[e].append((None, waits, None))
        self.recs = {}

    def emit(self):
        nc = self.nc
        prog = self
        with nc.Block() as block:
            def run(engname, eobj):
                for fn, waits, inc in prog.ops[engname]:
                    for k, v in waits:
                        eobj.wait_ge(prog._semh(k), v)
                    if fn is None:
                        continue
                    fn(eobj).then_inc(prog._semh(inc[0]), inc[1])

            @block.tensor
            def _(e):
                run("pe", e)

            @block.scalar
            def _(e):
                run("act", e)

            @block.vector
            def _(e):
                run("dve", e)

            @block.gpsimd
            def _(e):
                run("pool", e)

            @block.sync
            def _(e):
                run("sp", e)

    def mm(self, out, lhsT, rhs, start=True, stop=True):
        self.op("pe", lambda e: e.matmul(out, lhsT, rhs, start=start, stop=stop), reads=[lhsT, rhs], writes=[out])

    def tr(self, out, in_, ident):
        self.op("pe", lambda e: e.transpose(out, in_, ident), reads=[in_, ident], writes=[out])

    def act(self, out, in_, func, bias=None, scale=None):
        kw = {}
        reads = [in_]
        if bias is not None:
            kw["bias"] = bias
            if not isinstance(bias, (int, float)):
                reads.append(bias)
        if scale is not None:
            kw["scale"] = scale
            if not isinstance(scale, (int, float)):
                reads.append(scale)
        self.op("act", lambda e: e.activation(out, in_, func, **kw), reads=reads, writes=[out])

    def tt(self, out, in0, in1, op, eng="dve"):
        self.op(eng, lambda e: e.tensor_tensor(out, in0, in1, op), reads=[in0, in1], writes=[out])

    def ts(self, out, in0, s1, op0, s2=None, op1=None, eng="dve"):
        reads = [in0]
        if not isinstance(s1, (int, float)):
            reads.append(s1)
        if s2 is not None and not isinstance(s2, (int, float)):
            reads.append(s2)
        kw = {}
        if op1 is not None:
            kw["op1"] = op1
        self.op(eng, lambda e: e.tensor_scalar(out, in0, s1, s2, op0, **kw), reads=reads, writes=[out])

    def stt(self, out, in0, scalar, in1, op0, op1, eng="dve"):
        reads = [in0, in1]
        if not isinstance(scalar, (int, float)):
            reads.append(scalar)
        self.op(eng, lambda e: e.scalar_tensor_tensor(out, in0, scalar, in1, op0, op1), reads=reads, writes=[out])

    def copy(self, out, in_, eng="dve"):
        if eng == "act":
            self.op("act", lambda e: e.copy(out, in_), reads=[in_], writes=[out])
        else:
            self.op(eng, lambda e: e.tensor_copy(out, in_), reads=[in_], writes=[out])

    def memset(self, out, val, eng="dve"):
        self.op(eng, lambda e: e.memset(out, val), reads=[], writes=[out])

    def red(self, out, in_, op=None, eng="dve"):
        op = op or ALU.add
        self.op(eng, lambda e: e.tensor_reduce(out, in_, AX.X, op), reads=[in_], writes=[out])

    def recip(self, out, in_):
        self.op("dve", lambda e: e.reciprocal(out, in_), reads=[in_], writes=[out])

    def scan(self, out, d0, d1, init):
        reads = [d0, d1]
        if not isinstance(init, (int, float)):
            reads.append(init)
        self.op("dve", lambda e: e.tensor_tensor_scan(out, d0, d1, init, ALU.mult, ALU.add), reads=reads, writes=[out])

    def load(self, out, in_, eng="sp", slow=False):
        if slow:
            self.dma(eng, lambda e: e.dma_start(out=out, in_=in_, allow_slow_non_contiguous=True), reads=[in_], writes=[out])
        else:
            self.dma(eng, lambda e: e.dma_start(out=out, in_=in_), reads=[in_], writes=[out])

    store = load

    def gather(self, out, flat, idx):
        self.dma("pool", lambda e: e.indirect_dma_start(out=out, out_offset=None, in_=flat,
                                                        in_offset=bass.IndirectOffsetOnAxis(ap=idx, axis=0)),
                 reads=[idx, flat], writes=[out])


class Arena:
    def __init__(self, t, size):
        self.t = t
        self.size = size
        self.top = 0
        self.stack = []
        self.peak = 0

    def alloc(self, shape, parts=128):
        n = 1
        for s in shape:
            n *= s
        off = self.top
        self.top += n
        self.peak = max(self.peak, self.top)
        assert self.top <= self.size, ("arena overflow", self.t.name, self.top, self.size)
        ap = self.t[0:parts, off:off + n]
        if len(shape) == 2:
            ap = ap.rearrange("p (a b) -> p a b", b=shape[1])
        elif len(shape) == 3:
            ap = ap.rearrange("p (a b c) -> p a b c", b=shape[1], c=shape[2])
        elif len(shape) == 4:
            ap = ap.rearrange("p (a b c d) -> p a b c d", b=shape[1], c=shape[2], d=shape[3])
        return ap


INPUT_SPECS = [
    ("xp", (T, D), F32), ("xs", (NS, D), F32), ("cv", (NS + 1, D), F32), ("pt", (1, NS * NPG), I32),
    ("ck", (2, NPHYS * 128, 512), F32), ("cvv", (2, NPHYS * 128, 512), F32),
    ("sconv", (2, NS * 3, 1536), F32), ("sssm", (2, NS, 4, 128, 128), F32),
    ("scre", (2, NS, 2048), F32), ("scim", (2, NS, 2048), F32), ("sret", (2, NS, 4, 128, 128), F32),
    ("normg", (4, D), F32), ("wada", (4, 6, 128, 4096), F32), ("bada", (4, 3 * D), F32),
    ("wine", (2, 8, 128, 4096), F32), ("wab", (2, 128, 64), F32), ("woute", (2, 2, 128, 4096), F32),
    ("qng", (2, 1, 64), F32), ("kng", (2, 1, 64), F32), ("lamv", (2, 1, 256), F32), ("sublng", (2, 128, 1), F32),
    ("convw", (2, 4, 1536), F32), ("alog", (2, 1, 4), F32), ("dtb", (2, 1, 4), F32), ("gnb", (2, 1, 128), F32),
    ("wino", (2, 6, 128, 4096), F32), ("wouto", (2, 2, 128, 4096), F32),
    ("s5are", (2, 2048), F32), ("s5aim", (2, 2048), F32), ("s5bre", (2, 2048, 16), F32), ("s5bim", (2, 2048, 16), F32),
    ("s5cre", (2, 512, 64), F32), ("s5cim", (2, 512, 64), F32), ("s5d", (2, 512), F32), ("s5ldt", (2, 1, 32), F32),
    ("wglu", (2, 128, 2048), F32), ("gnd", (2, 1, 128), F32),
    ("c_ident", (128, 128), F32), ("c_ones", (128, 128), F32), ("c_onesb", (128, 128), BF16), ("c_trib", (128, 128), BF16),
    ("c_augq", (4, 4, T), BF16), ("c_augk", (4, 4, T), BF16), ("c_alibis", (128, NPG * 4), F32),
    ("c_mnegT", (128, 128), F32), ("c_mnegS", (128, 128), F32), ("c_LT", (128, 128), F32), ("c_LB", (128, 128), F32), ("c_cmask", (128, 8), F32), ("c_CM", (2, 128, 128), F32),
    ("c_dtc", (4, 128, 128), F32), ("c_kdc", (128, 4), F32), ("c_egc", (128, 4), F32), ("c_iota", (128, 1), F32),
    ("c_mask16", (128, 256), F32),
]
OUTPUT_SPECS = [
    ("y_p", (T, D)), ("y_s", (NS, D)), ("nk_p", (2, T, 512)), ("nv_p", (2, T, 512)), ("nk_s", (2, NS, 512)), ("nv_s", (2, NS, 512)),
    ("cv_p", (2, 3, 1536)), ("cv_s", (2, NS, 3, 1536)), ("dl_p", (2, 4, 128, 128)), ("dl_s", (2, NS, 4, 128, 128)),
    ("s5r_p", (2, 2048)), ("s5i_p", (2, 2048)), ("s5r_s", (2, NS, 2048)), ("s5i_s", (2, NS, 2048)),
    ("rt_p", (2, 4, 128, 128)), ("rt_s", (2, NS, 4, 128, 128)),
]
LOG_GAMMA = [math.log1p(-2.0 ** (-5.0 - h)) for h in range(4)]
SLOPES = [2.0 ** (-8.0 * (h + 1) / 4.0) for h in range(4)]


def make_consts():
    c = {}
    c["c_ident"] = np.eye(128, dtype=np.float32)
    c["c_ones"] = np.ones((128, 128), np.float32)
    c["c_onesb"] = np.ones((128, 128), ml_dtypes.bfloat16)
    i = np.arange(128)
    c["c_trib"] = (i[None, :] >= i[:, None]).astype(ml_dtypes.bfloat16)
    pos = np.arange(T)
    hi, lo = (pos // 128).astype(np.float64), (pos % 128).astype(np.float64)
    augq = np.zeros((4, 4, T), np.float64)
    augk = np.zeros((4, 4, T), np.float64)
    for h in range(4):
        s8 = 8.0 * SLOPES[h]
        augq[h, 0] = -s8 * 128.0 * hi
        augq[h, 1] = -s8 * lo
        augq[h, 2] = 1.0
        augq[h, 3] = 1.0
        augk[h, 0] = 1.0
        augk[h, 1] = 1.0
        augk[h, 2] = s8 * 128.0 * hi
        augk[h, 3] = s8 * lo
    c["c_augq"] = augq.astype(ml_dtypes.bfloat16)
    c["c_augk"] = augk.astype(ml_dtypes.bfloat16)
    al = np.zeros((128, NPG, 4), np.float64)
    for pg in range(NPG):
        for h in range(4):
            al[:, pg, h] = -SLOPES[h] * (T - (pg * 128 + i))
    c["c_alibis"] = al.reshape(128, NPG * 4).astype(np.float32)
    blk = i // 64
    same = blk[:, None] == blk[None, :]
    NEG = -1e30
    c["c_mnegT"] = np.where(same & (i[:, None] <= i[None, :]), 0.0, NEG).astype(np.float32)
    c["c_mnegS"] = np.where(same & (i[None, :] < i[:, None]), 0.0, NEG).astype(np.float32)
    c["c_LT"] = (same & (i[:, None] <= i[None, :])).astype(np.float32)
    c["c_LB"] = same.astype(np.float32)
    cmk = np.zeros((128, 8), np.float32)
    for l4 in range(4):
        for gp in range(2):
            cmk[l4 * 32 + gp * 16:l4 * 32 + gp * 16 + 16, l4 * 2 + gp] = 1.0
    c["c_cmask"] = cmk
    cm = np.zeros((2, 128, 128), np.float32)
    for cc in range(2):
        cm[cc, blk == cc, :] = 1.0
    c["c_CM"] = cm
    dtc = np.zeros((4, 128, 128), np.float64)
    kdc = np.zeros((128, 4), np.float64)
    egc = np.zeros((128, 4), np.float64)
    for h in range(4):
        lg = LOG_GAMMA[h]
        dd = (i[None, :] - i[:, None]).astype(np.float64)
        dtc[h] = np.where(dd >= 0, np.exp(lg * np.maximum(dd, 0)), 0.0) * (128.0 ** -0.5)
        kdc[:, h] = np.exp(lg * (127 - i)) * (128.0 ** -0.5)
        egc[:, h] = np.exp(lg * (i + 1))
    c["c_dtc"] = dtc.astype(np.float32)
    c["c_kdc"] = kdc.astype(np.float32)
    c["c_egc"] = egc.astype(np.float32)
    c["c_iota"] = i.astype(np.float32).reshape(128, 1)
    m16 = np.zeros((128, 16, 16), np.float32)
    for s in range(16):
        m16[:, s, s] = 1.0
    c["c_mask16"] = m16.reshape(128, 256)
    return c


def make_consts2(c):
    i = np.arange(128)
    ab = np.zeros((128, 16 + 3 * 19), np.float64)
    for d in range(16):
        ab[:, d] = SLOPES[0] * (i - 128.0 * d)
    for h in range(1, 4):
        for d in range(-3, 16):
            ab[:, 16 + (h - 1) * 19 + d + 3] = SLOPES[h] * (i - 128.0 * d)
    c["c_abias"] = ab.astype(np.float32)
    return c


INPUT_SPECS.append(("c_abias", (128, 73), F32))

NF = 9800
NB = 10240
BLOCKS = [(0, 512), (512, 512), (1024, 512), (1536, 512), (2048, 16)]


class Ctx:
    pass


def finish_o_rms(K, o, rows, gain_b, sz, dest):
    P, psn, ident = K.P, K.psn, K.ident
    sqs, ss, on = K.fo_sq, K.fo_ss, K.fo_on
    P.act(sqs[0:rows, :], o, AF.Square)
    P.red(ss[0:rows, 0:1], sqs[0:rows, :])
    P.ts(ss[0:rows, 0:1], ss[0:rows, 0:1], 1.0 / 128, ALU.mult, EPS, ALU.add)
    P.act(ss[0:rows, 0:1], ss[0:rows, 0:1], AF.Sqrt)
    P.recip(ss[0:rows, 0:1], ss[0:rows, 0:1])
    P.stt(on[0:rows, :], o, ss[0:rows, 0:1], gain_b[0:rows, :], ALU.mult, ALU.mult)
    pt = psn()
    P.tr(pt[:, 0:rows], on[0:rows, :], ident[0:rows, 0:rows])
    P.tt(dest, pt[:, 0:rows], sz, ALU.mult)


def finish_o_ln(K, o, rows, gain_b, sz, dest):
    P, psn, ident = K.P, K.psn, K.ident
    sqs, ss, on = K.fo_sq, K.fo_ss, K.fo_on
    P.red(ss[0:rows, 1:2], o)
    P.ts(ss[0:rows, 1:2], ss[0:rows, 1:2], 1.0 / 128, ALU.mult)
    P.ts(on[0:rows, :], o, ss[0:rows, 1:2], ALU.subtract)
    P.act(sqs[0:rows, :], on[0:rows, :], AF.Square)
    P.red(ss[0:rows, 0:1], sqs[0:rows, :])
    P.ts(ss[0:rows, 0:1], ss[0:rows, 0:1], 1.0 / 128, ALU.mult, EPS, ALU.add)
    P.act(ss[0:rows, 0:1], ss[0:rows, 0:1], AF.Sqrt)
    P.recip(ss[0:rows, 0:1], ss[0:rows, 0:1])
    P.stt(on[0:rows, :], on[0:rows, :], ss[0:rows, 0:1], gain_b[0:rows, :], ALU.mult, ALU.mult)
    pt = psn()
    P.tr(pt[:, 0:rows], on[0:rows, :], ident[0:rows, 0:rows])
    P.tt(dest, pt[:, 0:rows], sz, ALU.mult)


def sample_state_step(K, qT, kT, vT, eg, neg_eg, beta, EGB, s0_dram, out_dram, gain_b, szs, dest, delta, ln):
    P, AFa, psn, ident, Scope = K.P, K.AFa, K.psn, K.ident, K.Scope
    QM = AFa.alloc([16, 16])
    KM = AFa.alloc([16, 16])
    M16 = K.MASK16
    P.tt(QM[:, :, :], M16[:, :, :], qT.unsqueeze(1).to_broadcast([128, 16, 16]), ALU.mult)
    P.tt(KM[:, :, :], M16[:, :, :], kT.unsqueeze(1).to_broadcast([128, 16, 16]), ALU.mult)
    QTM = AFa.alloc([128])
    KTM = AFa.alloc([128])
    VTM = AFa.alloc([128])
    for src, dst in ((qT, QTM), (kT, KTM), (vT, VTM)):
        pt = psn()
        P.tr(pt[0:16, 0:128], src, ident[:])
        P.copy(dst[0:16, :], pt[0:16, 0:128], eng="act")
    S0 = [AFa.alloc([128]) for _ in range(2)]
    pk = psn()
    pq = psn()
    for s in range(NS):
        s0 = S0[s % 2]
        P.load(s0[:, :], s0_dram(s))
        if delta:
            P.mm(pk[0:16, 0:128], KM[:, s, :], s0[:, :], start=(s == 0), stop=(s == NS - 1))
        P.mm(pq[0:16, 0:128], QM[:, s, :], s0[:, :], start=(s == 0), stop=(s == NS - 1))
    u = AFa.alloc([128])
    if delta:
        P.stt(u[0:16, :], pk[0:16, 0:128], neg_eg, VTM[0:16, :], ALU.mult, ALU.add)
        P.ts(u[0:16, :], u[0:16, :], beta, ALU.mult)
    else:
        P.copy(u[0:16, :], VTM[0:16, :])
    tmp = AFa.alloc([128])
    qk = AFa.alloc([1])
    P.tt(tmp[0:16, :], QTM[0:16, :], KTM[0:16, :], ALU.mult)
    P.red(qk[0:16, :], tmp[0:16, :])
    o = AFa.alloc([128])
    P.ts(tmp[0:16, :], pq[0:16, 0:128], eg, ALU.mult)
    P.stt(o[0:16, :], u[0:16, :], qk[0:16, 0:1], tmp[0:16, :], ALU.mult, ALU.add)
    if ln:
        finish_o_ln(K, o[0:16, :], 16, gain_b, szs, dest)
    else:
        finish_o_rms(K, o[0:16, :], 16, gain_b, szs, dest)
    kms = [AFa.alloc([128]) for _ in range(2)]
    sn = [AFa.alloc([128]) for _ in range(2)]
    for s in range(NS):
        s0 = S0[s % 2]
        P.load(s0[:, :], s0_dram(s))
        km = kms[s % 2]
        P.ts(km[0:16, :], KTM[0:16, :], ident[0:16, s:s + 1], ALU.mult)
        ps = psn()
        P.mm(ps[:, 0:128], km[0:16, :], u[0:16, :])
        egs = EGB if isinstance(EGB, float) else EGB[:, s:s + 1]
        P.stt(sn[s % 2][:, :], s0[:, :], egs, ps[:, 0:128], ALU.mult, ALU.add)
        P.store(out_dram(s), sn[s % 2][:, :])


def gdn_heads(K, li):
    P, I, O = K.P, K.I, K.O
    XT, HT, OT, AFa, ABa, PS, psn, Scope = K.XT, K.HT, K.OT, K.AFa, K.ABa, K.PS, K.psn, K.Scope
    next_w, wv512, v3, ident, ones, onesb = K.next_w, K.wv512, K.v3, K.ident, K.ones, K.onesb
    e = li // 2
    with Scope():
        wab = next_w(("wab", e))
        wabv = wab[:, 0:64].rearrange("p (k c) -> p k c", c=8)
        AB = AFa.alloc([17, 8])
        P.memset(AB[:, :, :], 0.0)
        for n in range(17):
            rows = 128 if n < 16 else 16
            c0 = n * 128
            ps = psn()
            for k in range(8):
                P.mm(ps[0:rows, 0:8], HT[:, k, c0:c0 + rows], wabv[:, k, :], start=(k == 0), stop=(k == 7))
            P.copy(AB[0:rows, n, :], ps[0:rows, 0:8], eng=("act" if n % 2 else "dve"))
        ALG = AFa.alloc([4])
        DTB = AFa.alloc([4])
        P.load(ALG[:, :], I["alog"][e].partition_broadcast(128))
        P.load(DTB[:, :], I["dtb"][e].partition_broadcast(128))
        P.act(ALG[:, :], ALG[:, :], AF.Exp)
        P.ts(ALG[:, :], ALG[:, :], -1.0, ALU.mult)
        G = AFa.alloc([17, 4])
        BETA = AFa.alloc([17, 4])
        P.tt(G[:, :, :], AB[:, :, 0:4], DTB[:, :].unsqueeze(1).to_broadcast([128, 17, 4]), ALU.add)
        P.act(G[:, :, :], G[:, :, :], AF.Exp)
        P.act(G[:, :, :], G[:, :, :], AF.Ln, bias=ones[:, 0:1])
        P.tt(G[:, :, :], G[:, :, :], ALG[:, :].unsqueeze(1).to_broadcast([128, 17, 4]), ALU.mult)
        P.act(BETA[:, :, :], AB[:, :, 4:8], AF.Sigmoid)
        CN = {}
        for nm in ("c_LT", "c_LB", "c_mnegT", "c_mnegS"):
            CN[nm] = AFa.alloc([128])
            P.load(CN[nm][:, :], I[nm])
        CM = AFa.alloc([2, 128])
        P.load(CM[:, :, :], I["c_CM"].rearrange("c j p -> j c p"))
        Gf = G[:, 0:16, :].rearrange("p n h -> p (n h)")
        GAM = AFa.alloc([16, 4])
        GL = AFa.alloc([16, 4])
        ps = psn()
        P.mm(ps[:, 0:64], CN["c_LT"][:, :], Gf)
        P.copy(GAM[:, :, :].rearrange("p n h -> p (n h)"), ps[:, 0:64])
        ps = psn()
        P.mm(ps[:, 0:64], CN["c_LB"][:, :], Gf)
        P.copy(GL[:, :, :].rearrange("p n h -> p (n h)"), ps[:, 0:64], eng="act")
        EGLB = AFa.alloc([2, 16, 4])
        for c in range(2):
            ps = psn()
            P.mm(ps[:, 0:64], CM[:, c, :], Gf)
            P.act(EGLB[:, c, :, :].rearrange("p n h -> p (n h)"), ps[:, 0:64], AF.Exp)
        EGAM = AFa.alloc([16, 4])
        BEG = AFa.alloc([16, 4])
        NBETA = AFa.alloc([16, 4])
        NGAM = AFa.alloc([16, 4])
        ED = AFa.alloc([16, 4])
        P.act(EGAM[:, :, :], GAM[:, :, :], AF.Exp)
        P.tt(BEG[:, :, :], BETA[:, 0:16, :], EGAM[:, :, :], ALU.mult)
        P.ts(NBETA[:, :, :], BETA[:, 0:16, :], -1.0, ALU.mult)
        P.ts(NGAM[:, :, :], GAM[:, :, :], -1.0, ALU.mult)
        P.tt(ED[:, :, :], GL[:, :, :], GAM[:, :, :], ALU.subtract)
        P.act(ED[:, :, :], ED[:, :, :], AF.Exp)
        EGS = AFa.alloc([4])
        NEGS = AFa.alloc([4])
        P.act(EGS[0:16, :], G[0:16, 16, :], AF.Exp)
        P.ts(NEGS[0:16, :], EGS[0:16, :], -1.0, ALU.mult)
        CWT = AFa.alloc([12, 4])
        for tap in range(4):
            P.load(CWT[:, :, tap], I["convw"][e][tap].rearrange("(c p) -> p c", p=128), slow=True)
        CBT = AFa.alloc([12, 48])
        with Scope():
            CB = AFa.alloc([1536])
            P.load(CB[0:48, :], I["sconv"][e])
            for half in range(3):
                ps = psn()
                for q in range(4):
                    cidx = half * 4 + q
                    P.tr(ps[:, q * 64:q * 64 + 48], CB[0:48, cidx * 128:(cidx + 1) * 128], ident[0:48, 0:48])
                P.copy(CBT[:, half * 4:(half + 1) * 4, :], v3(ps[:, 0:256], 64)[:, :, 0:48])
        P.load(O["cv_s"][e, :, 0:2, :], I["sconv"][e].rearrange("(s j) c -> s j c", j=3)[:, 1:3, :])
        GNB = AFa.alloc([128])
        P.load(GNB[:, :], I["gnb"][e].partition_broadcast(128))
        K.MASK16 = AFa.alloc([16, 16])
        P.load(K.MASK16[:, :, :], v3(I["c_mask16"], 16))
        BW = 256
        NTB = T // BW

        def alloc_bufs():
            B = {}
            B["S"] = AFa.alloc([128])
            B["PRE"] = AFa.alloc([BW + 3])
            B["CAR"] = AFa.alloc([3, 3])
            B["QKV"] = [AFa.alloc([BW]) for _ in range(3)]
            B["SZ"] = AFa.alloc([BW])
            B["acc"] = AFa.alloc([BW])
            B["rs"] = B["acc"]
            B["sqb"] = ABa.alloc([BW])
            B["fo"] = (AFa.alloc([128]), AFa.alloc([2]), AFa.alloc([128]))
            return B

        def alloc_mats():
            M = {}
            M["X0"] = AFa.alloc([256])
            for nm in ("KD", "DG", "T1", "T2", "DT", "DS", "AQ", "Pm", "PTm"):
                M[nm] = AFa.alloc([128])
            M["WT"], M["UF"], M["O1"], M["OO"] = M["DG"], M["T1"], M["T2"], M["DS"]
            return M

        def l2n(B, w):
            QKV, rs, sqb_ = B["QKV"], B["rs"], B["sqb"]
            for j in (0, 1):
                P.act(sqb_[:, 0:w], QKV[j][:, 0:w], AF.Square)
                ps = psn()
                P.mm(ps[:, 0:w], onesb[:], sqb_[:, 0:w])
                P.ts(rs[:, 0:w], ps[:, 0:w], EPS, ALU.add)
                P.act(rs[:, 0:w], rs[:, 0:w], AF.Sqrt)
                P.recip(rs[:, 0:w], rs[:, 0:w])
                if j == 0:
                    P.stt(QKV[j][:, 0:w], QKV[j][:, 0:w], 128.0 ** -0.5, rs[:, 0:w], ALU.mult, ALU.mult)
                else:
                    P.tt(QKV[j][:, 0:w], QKV[j][:, 0:w], rs[:, 0:w], ALU.mult)

        def head_prompt(h, wv, B, M):
            S, PRE, CAR, QKV, SZ, acc = B["S"], B["PRE"], B["CAR"], B["QKV"], B["SZ"], B["acc"]
            K.fo_sq, K.fo_ss, K.fo_on = B["fo"]
            X0, KD, DG, T1, T2, DT, DS, AQ, Pm, PTm, WT, UF, O1, OO = [M[n] for n in
                ("X0", "KD", "DG", "T1", "T2", "DT", "DS", "AQ", "Pm", "PTm", "WT", "UF", "O1", "OO")]
            P.memset(S[:, :], 0.0)
            P.memset(CAR[:, :, :], 0.0)
            for tb in range(NTB):
                c0 = tb * BW
                for j in range(3):
                    ps = psn()
                    for k in range(8):
                        P.mm(ps[:, 0:BW], wv[:, k, j * 128:(j + 1) * 128], HT[:, k, c0:c0 + BW], start=(k == 0), stop=(k == 7))
                    P.copy(PRE[:, 0:3], CAR[:, j, :])
                    P.copy(PRE[:, 3:BW + 3], ps[:, 0:BW], eng="act")
                    cidx = j * 4 + h
                    P.ts(acc[:, :], PRE[:, 0:BW], CWT[:, cidx, 0:1], ALU.mult)
                    for tap in range(1, 4):
                        P.stt(acc[:, :], PRE[:, tap:tap + BW], CWT[:, cidx, tap:tap + 1], acc[:, :], ALU.mult, ALU.add)
                    P.act(QKV[j][:, :], acc[:, :], AF.Silu)
                    P.copy(CAR[:, j, :], PRE[:, BW:BW + 3])
                    if tb == NTB - 1:
                        P.store(O["cv_p"][e][:, j * 512 + h * 128:j * 512 + (h + 1) * 128].rearrange("t c -> c t"),
                                CAR[:, j, :], slow=True)
                l2n(B, BW)
                ps = psn()
                for k in range(8):
                    P.mm(ps[:, 0:BW], wv[:, k, 384:512], HT[:, k, c0:c0 + BW], start=(k == 0), stop=(k == 7))
                P.act(SZ[:, :], ps[:, 0:BW], AF.Silu)
                for tl in range(BW // 128):
                    n = tb * (BW // 128) + tl
                    cc = slice(tl * 128, (tl + 1) * 128)
                    qT, kT, vT = QKV[0][:, cc], QKV[1][:, cc], QKV[2][:, cc]
                    gam, ngam = GAM[:, n, h:h + 1], NGAM[:, n, h:h + 1]
                    pa = psn()
                    P.tr(pa[:, 0:128], kT, ident[:])
                    P.tr(pa[:, 128:256], vT, ident[:])
                    P.ts(X0[:, 0:128], pa[:, 128:256], BETA[:, n, h:h + 1], ALU.mult)
                    P.ts(X0[:, 128:256], pa[:, 0:128], BEG[:, n, h:h + 1], ALU.mult)
                    P.ts(KD[:, :], pa[:, 0:128], ED[:, n, h:h + 1], ALU.mult)
                    P.ts(DG[:, :], ident[:], gam, ALU.mult)
                    pr = psn()
                    P.mm(pr[:, 0:128], ones[:], DG[:, :])
                    P.tt(T1[:, :], pr[:, 0:128], CN["c_mnegT"][:, :], ALU.add)
                    P.act(DT[:, :], T1[:, :], AF.Exp, bias=ngam)
                    P.stt(T2[:, :], pr[:, 0:128], -1.0, CN["c_mnegS"][:, :], ALU.mult, ALU.add)
                    P.act(DS[:, :], T2[:, :], AF.Exp, bias=gam)
                    pq = psn()
                    P.mm(pq[:, 0:128], kT, qT)
                    P.tt(AQ[:, :], pq[:, 0:128], DT[:, :], ALU.mult)
                    pk = psn()
                    P.mm(pk[:, 0:128], kT, kT)
                    P.stt(Pm[:, :], pk[:, 0:128], NBETA[:, n, h:h + 1], DS[:, :], ALU.mult, ALU.mult)
                    pt = psn()
                    P.tr(pt[:, 0:128], Pm[:, :], ident[:])
                    P.copy(PTm[:, :], pt[:, 0:128], eng="act")
                    for lvl in range(6):
                        px = psn()
                        P.mm(px[:, 0:256], PTm[:, :], X0[:, :])
                        P.tt(X0[:, :], px[:, 0:256], X0[:, :], ALU.add)
                        if lvl < 5:
                            pp = psn()
                            P.mm(pp[:, 0:128], PTm[:, :], Pm[:, :])
                            P.mm(pp[:, 128:256], Pm[:, :], PTm[:, :])
                            P.copy(Pm[:, :], pp[:, 0:128], eng="act")
                            P.copy(PTm[:, :], pp[:, 128:256], eng="act")
                    pw = psn()
                    P.tr(pw[:, 0:128], X0[:, 128:256], ident[:])
                    P.copy(WT[:, :], pw[:, 0:128], eng="act")
                    for c in range(2):
                        r = slice(64 * c, 64 * c + 64)
                        p1 = psn()
                        P.mm(p1[:, 0:128], WT[:, :], S[:, :])
                        P.tt(UF[r, :], X0[r, 0:128], p1[r, 0:128], ALU.subtract)
                        p2 = psn()
                        P.mm(p2[:, 0:128], qT, S[:, :])
                        P.ts(O1[r, :], p2[r, 0:128], EGAM[r, n, h:h + 1], ALU.mult)
                        p3 = psn()
                        P.mm(p3[:, 0:128], KD[r, :], UF[r, :])
                        P.stt(S[:, :], S[:, :], EGLB[:, c, n, h:h + 1], p3[:, 0:128], ALU.mult, ALU.add)
                    p4 = psn()
                    P.mm(p4[:, 0:128], AQ[:, :], UF[:, :])
                    P.tt(OO[:, :], p4[:, 0:128], O1[:, :], ALU.add)
                    finish_o_rms(K, OO[:, :], 128, GNB, SZ[:, cc], OT[:, 4 + h, n * 128:(n + 1) * 128])
            P.store(O["dl_p"][e, h], S[:, :])

        def head_sample(h, wv, B):
            QKV, SZ, acc = B["QKV"], B["SZ"], B["acc"]
            K.fo_sq, K.fo_ss, K.fo_on = B["fo"]
            xs_ = AFa.alloc([16])
            xtm = AFa.alloc([128])
            for j in range(3):
                ps = psn()
                for k in range(8):
                    P.mm(ps[:, 0:16], wv[:, k, j * 128:(j + 1) * 128], HT[:, k, T:TA], start=(k == 0), stop=(k == 7))
                P.copy(xs_[:, :], ps[:, 0:16], eng="act")
                cidx = j * 4 + h
                cb = CBT[:, cidx, :].rearrange("p (s j) -> p s j", j=3)
                P.ts(acc[:, 0:16], cb[:, :, 0], CWT[:, cidx, 0:1], ALU.mult)
                P.stt(acc[:, 0:16], cb[:, :, 1], CWT[:, cidx, 1:2], acc[:, 0:16], ALU.mult, ALU.add)
                P.stt(acc[:, 0:16], cb[:, :, 2], CWT[:, cidx, 2:3], acc[:, 0:16], ALU.mult, ALU.add)
                P.stt(acc[:, 0:16], xs_[:, :], CWT[:, cidx, 3:4], acc[:, 0:16], ALU.mult, ALU.add)
                P.act(QKV[j][:, 0:16], acc[:, 0:16], AF.Silu)
                pt = psn()
                P.tr(pt[0:16, 0:128], xs_[:, :], ident[:])
                P.copy(xtm[0:16, :], pt[0:16, 0:128])
                P.store(O["cv_s"][e, :, 2, j * 512 + h * 128:j * 512 + (h + 1) * 128], xtm[0:16, :])
            l2n(B, 16)
            ps = psn()
            for k in range(8):
                P.mm(ps[:, 0:16], wv[:, k, 384:512], HT[:, k, T:TA], start=(k == 0), stop=(k == 7))
            P.act(SZ[:, 0:16], ps[:, 0:16], AF.Silu)
            DGs = AFa.alloc([16])
            P.ts(DGs[0:16, :], ident[0:16, 0:16], EGS[0:16, h:h + 1], ALU.mult)
            ps = psn()
            P.mm(ps[:, 0:16], ones[0:16, :], DGs[0:16, :])
            EGB = AFa.alloc([16])
            P.copy(EGB[:, :], ps[:, 0:16])
            sample_state_step(K, QKV[0][:, 0:16], QKV[1][:, 0:16], QKV[2][:, 0:16],
                              EGS[0:16, h:h + 1], NEGS[0:16, h:h + 1], BETA[0:16, 16, h:h + 1], EGB,
                              (lambda s, h=h: I["sssm"][e, s, h]), (lambda s, h=h: O["dl_s"][e, s, h]),
                              GNB, SZ[:, 0:16], OT[:, 4 + h, T:TA], True, False)

        NPAR = K.cfg.get("gdn_par", 2)
        for hp in range(0, 4, NPAR):
            wvs = [wv512(next_w(("eb", e, hp + i), prefetch=(i == 0))) for i in range(NPAR)]
            with Scope():
                Bs = [alloc_bufs() for _ in range(NPAR)]
                with Scope():
                    Ms = [alloc_mats() for _ in range(NPAR)]
                    streams = []
                    for i in range(NPAR):
                        K.p0, K.pn = (8 // NPAR) * i, 8 // NPAR
                        cap = P.capture()
                        head_prompt(hp + i, wvs[i], Bs[i], Ms[i])
                        P.end_capture()
                        streams.append(cap)
                    K.p0, K.pn = 0, 8
                    P.merge(streams)
                for i in range(NPAR):
                    with Scope():
                        head_sample(hp + i, wvs[i], Bs[i])


def rms_feature_gate(K, o, w, gcol, psz_fn, dest, tagf):
    P, AFa, ABa, psn, onesb = K.P, K.AFa, K.ABa, K.psn, K.onesb
    sq = K.ep_sq
    P.act(sq[:, 0:w], o[:, 0:w], AF.Square)
    ps = psn()
    P.mm(ps[:, 0:w], onesb[:], sq[:, 0:w])
    rs = K.ep_rs
    P.ts(rs[:, 0:w], ps[:, 0:w], 1.0 / 128, ALU.mult, EPS, ALU.add)
    P.act(rs[:, 0:w], rs[:, 0:w], AF.Sqrt)
    P.recip(rs[:, 0:w], rs[:, 0:w])
    P.tt(o[:, 0:w], o[:, 0:w], rs[:, 0:w], ALU.mult)
    sz = psz_fn()
    P.stt(dest, o[:, 0:w], gcol, sz, ALU.mult, ALU.mult)


def even_layer(K, li):
    P, I, O = K.P, K.I, K.O
    XT, HT, OT, AFa, ABa, PS, psn, Scope = K.XT, K.HT, K.OT, K.AFa, K.ABa, K.PS, K.psn, K.Scope
    next_w, wv512, v3, ident, ones, onesb, trib = K.next_w, K.wv512, K.v3, K.ident, K.ones, K.onesb, K.trib
    cfg = K.cfg
    e = li // 2
    lam_init = 0.8 - 0.6 * math.exp(-0.3 * li)

    with Scope():
        GQK = AFa.alloc([256])
        P.load(GQK[:, 0:64], I["qng"][e].partition_broadcast(128))
        P.load(GQK[:, 64:128], I["qng"][e].partition_broadcast(128))
        P.load(GQK[:, 128:192], I["kng"][e].partition_broadcast(128))
        P.load(GQK[:, 192:256], I["kng"][e].partition_broadcast(128))
        LV = AFa.alloc([256])
        P.load(LV[:, :], I["lamv"][e].partition_broadcast(128))
        lp = AFa.alloc([128])
        lv4 = LV[:, :].rearrange("p (a b d) -> p a b d", a=2, b=2)
        P.tt(v3(lp[:, :], 64), lv4[:, :, 0, :], lv4[:, :, 1, :], ALU.mult)
        l2 = AFa.alloc([2])
        P.red(l2[:, :], v3(lp[:, :], 64))
        P.act(l2[:, :], l2[:, :], AF.Exp)
        NLAM = AFa.alloc([1])
        P.tt(NLAM[:, :], l2[:, 1:2], l2[:, 0:1], ALU.subtract)
        P.ts(NLAM[:, :], NLAM[:, :], -lam_init, ALU.add)
        SUBG = AFa.alloc([1])
        P.load(SUBG[:, :], I["sublng"][e])
        P.ts(SUBG[:, :], SUBG[:, :], 1.0 - lam_init, ALU.mult)
        ABIAS = AFa.alloc([73])
        P.load(ABIAS[:, :], I["c_abias"])
        QS = AFa.alloc([4, 128])
        KS = AFa.alloc([4, 128])
        VS = AFa.alloc([4, 128])
        ZS = AFa.alloc([4, 16])
        VST = AFa.alloc([4, 16])
        K.ep_sq = ABa.alloc([512])
        K.ep_rs = AFa.alloc([512])

        for h in range(4 if cfg.get("attn", True) else 0):
            wv = wv512(next_w(("ea", e, h)))
            with Scope():
                QT_ = ABa.alloc([T])
                KT_ = ABa.alloc([T])
                VTM = ABa.alloc([NT, 128])
                NI = 4
                nsq = [AFa.alloc([256]) for _ in range(NI)]
                nss = [AFa.alloc([4]) for _ in range(NI)]
                nqk = [AFa.alloc([256]) for _ in range(NI)]
                nvf = [AFa.alloc([128]) for _ in range(NI)]

                def tile_ops(n, bi):
                    rows = 128 if n < 16 else 16
                    c0 = n * 128
                    ps = psn()
                    for k in range(8):
                        P.mm(ps[0:rows, 0:384], HT[:, k, c0:c0 + rows], wv[:, k, 0:384], start=(k == 0), stop=(k == 7))
                    sq, ss, qkn, vf = nsq[bi], nss[bi], nqk[bi], nvf[bi]
                    P.act(sq[0:rows, :], ps[0:rows, 0:256], AF.Square)
                    P.red(ss[0:rows, :], v3(sq[0:rows, :], 64))
                    P.ts(ss[0:rows, :], ss[0:rows, :], 1.0 / 64, ALU.mult, EPS, ALU.add)
                    P.act(ss[0:rows, :], ss[0:rows, :], AF.Sqrt)
                    P.recip(ss[0:rows, :], ss[0:rows, :])
                    P.tt(v3(qkn[0:rows, :], 64), v3(ps[0:rows, 0:256], 64),
                         ss[0:rows, :].unsqueeze(2).to_broadcast([rows, 4, 64]), ALU.mult)
                    P.tt(qkn[0:rows, :], qkn[0:rows, :], GQK[0:rows, :], ALU.mult)
                    if n < 16:
                        P.store(O["nk_p"][e, c0:c0 + 128, h * 128:(h + 1) * 128], qkn[:, 128:256])
                        P.copy(vf[:, :], ps[:, 256:384], eng="act")
                        P.store(O["nv_p"][e, c0:c0 + 128, h * 128:(h + 1) * 128], vf[:, :])
                        P.copy(VTM[:, n, :], vf[:, :])
                        pst = psn()
                        P.tr(pst[:, 0:128], qkn[:, 0:128], ident[:])
                        P.tr(pst[:, 128:256], qkn[:, 128:256], ident[:])
                        P.copy(QT_[:, c0:c0 + 128], pst[:, 0:128], eng="act")
                        P.copy(KT_[:, c0:c0 + 128], pst[:, 128:256], eng="act")
                    else:
                        P.copy(QS[0:16, h, :], qkn[0:16, 0:128])
                        P.copy(KS[0:16, h, :], qkn[0:16, 128:256])
                        P.copy(VS[0:16, h, :], ps[0:16, 256:384], eng="act")
                        pst = psn()
                        P.tr(pst[:, 0:16], VS[0:16, h, :], ident[0:16, 0:16])
                        P.copy(VST[:, h, :], pst[:, 0:16])

                for n0 in range(0, 17, NI):
                    streams = []
                    for bi, n in enumerate(range(n0, min(n0 + NI, 17))):
                        K.p0, K.pn = 2 * bi, 2
                        cap = P.capture()
                        tile_ops(n, bi)
                        P.end_capture()
                        streams.append(cap)
                    K.p0, K.pn = 0, 8
                    P.merge(streams)
                ps = psn()
                for k in range(8):
                    P.mm(ps[:, 0:16], wv[:, k, 384:512], HT[:, k, T:TA], start=(k == 0), stop=(k == 7))
                P.act(ZS[:, h, :], ps[:, 0:16], AF.Silu)
                if cfg.get("attn_stage", 3) < 2:
                    continue
                K.p0, K.pn = 0, 4
                O1, O2, D1, D2 = PS[4], PS[5], PS[6], PS[7]
                Eb = [[ABa.alloc([512]) for _ in range(2)] for _ in range(2)]
                of = AFa.alloc([512])
                t1 = AFa.alloc([512])
                t2 = AFa.alloc([512])
                szb = AFa.alloc([512])
                iters = [(qb, kb) for qb in range(4) for kb in range(4 * qb + 4)]
                SB = [(PS[0], PS[1]), (PS[2], PS[3])]

                def emit_S(idx):
                    qb, kb = iters[idx]
                    q0 = qb * 512
                    cs = 128 * max(kb - 4 * qb, 0)
                    s1, s2 = SB[idx % 2]
                    P.mm(s1[:, cs:512], KT_[0:64, kb * 128:(kb + 1) * 128], QT_[0:64, q0 + cs:q0 + 512])
                    P.mm(s2[:, cs:512], KT_[64:128, kb * 128:(kb + 1) * 128], QT_[64:128, q0 + cs:q0 + 512])

                emit_S(0)
                for idx, (qb, kb) in enumerate(iters):
                    q0 = qb * 512
                    nkb = 4 * qb + 4
                    j = kb - 4 * qb
                    cs = 128 * max(j, 0)
                    s1, s2 = SB[idx % 2]
                    if idx + 1 < len(iters):
                        emit_S(idx + 1)
                    e1, e2 = Eb[idx % 2]
                    if h == 0:
                        for sbk in range(cs // 128, 4):
                            d = (4 * qb + sbk) - kb
                            cc = slice(sbk * 128, (sbk + 1) * 128)
                            P.act(e1[:, cc], s1[:, cc], AF.Exp, bias=ABIAS[:, d:d + 1], scale=0.125)
                            P.act(e2[:, cc], s2[:, cc], AF.Exp, bias=ABIAS[:, d:d + 1], scale=0.125)
                    else:
                        d = 4 * qb - kb + 3
                        bcol = 16 + (h - 1) * 19 + d
                        P.act(e1[:, cs:512], s1[:, cs:512], AF.Exp, bias=ABIAS[:, bcol:bcol + 1], scale=0.125)
                        P.act(e2[:, cs:512], s2[:, cs:512], AF.Exp, bias=ABIAS[:, bcol:bcol + 1], scale=0.125)
                    if j >= 0:
                        P.tt(e1[:, cs:cs + 128], e1[:, cs:cs + 128], trib[:], ALU.mult)
                        P.tt(e2[:, cs:cs + 128], e2[:, cs:cs + 128], trib[:], ALU.mult)
                    st, sp = (kb == 0), (kb == nkb - 1)
                    P.mm(O1[:, cs:512], VTM[:, kb, :], e1[:, cs:512], start=st, stop=sp)
                    P.mm(D1[:, cs:512], onesb[:], e1[:, cs:512], start=st, stop=sp)
                    P.mm(O2[:, cs:512], VTM[:, kb, :], e2[:, cs:512], start=st, stop=sp)
                    P.mm(D2[:, cs:512], onesb[:], e2[:, cs:512], start=st, stop=sp)
                    if kb != nkb - 1:
                        continue
                    if cfg.get("attn_stage", 3) < 3:
                        continue
                    P.recip(t1[:, :], D1[:, :])
                    P.tt(t1[:, :], O1[:, :], t1[:, :], ALU.mult)
                    P.recip(t2[:, :], D2[:, :])
                    P.tt(t2[:, :], O2[:, :], t2[:, :], ALU.mult)
                    P.stt(of[:, :], t2[:, :], NLAM[:, 0:1], t1[:, :], ALU.mult, ALU.add)
                    K.forced = [s1, s2]

                    def psz(q0=q0, wv=wv):
                        pz = psn()
                        for k in range(8):
                            P.mm(pz[:, :], wv[:, k, 384:512], HT[:, k, q0:q0 + 512], start=(k == 0), stop=(k == 7))
                        P.act(szb[:, :], pz[:, :], AF.Silu)
                        return szb[:, :]
                    rms_feature_gate(K, of, 512, SUBG[:, 0:1], psz, OT[:, h, q0:q0 + 512], "a")
                    assert not K.forced
                K.p0, K.pn = 0, 8

        if cfg.get("sample_attn", True) and cfg.get("attn", True):
            with Scope():
                PTB = AFa.alloc([256])
                ptb_i = PTB[:, :].bitcast(I32)
                P.load(ptb_i, I["pt"].partition_broadcast(128))
                PTF = AFa.alloc([256])
                P.copy(PTF[:, :], ptb_i)
                IO = AFa.alloc([1])
                P.load(IO[:, :], I["c_iota"])
                P.ts(PTF[:, :], PTF[:, :], 128.0, ALU.mult, IO[:, 0:1], ALU.add)
                if e > 0:
                    P.ts(PTF[:, :], PTF[:, :], float(e * NPHYS * 128), ALU.add)
                IDX = AFa.alloc([256])
                idx_i = IDX[:, :].bitcast(I32)
                P.copy(idx_i, PTF[:, :])
                ALB = AFa.alloc([NPG, 4])
                P.load(ALB[:, :, :], v3(I["c_alibis"], 4))
                SEL = AFa.alloc([128])
                sp_ = AFa.alloc([512])
                P.tt(sp_[0:16, :], QS[0:16, :, :].rearrange("p h d -> p (h d)"), KS[0:16, :, :].rearrange("p h d -> p (h d)"), ALU.mult)
                ES = AFa.alloc([8])
                P.red(ES[0:16, :], v3(sp_[0:16, :], 64))
                P.act(ES[0:16, :], ES[0:16, :], AF.Exp, scale=0.125)
                QB = AFa.alloc([512])
                NPB = 8
                KP = [ABa.alloc([512]) for _ in range(NPB)]
                VP = [ABa.alloc([512]) for _ in range(NPB)]
                PPb = ABa.alloc([NPG, 4])
                prod = AFa.alloc([512])
                SC = AFa.alloc([NPG, 8])
                EE = AFa.alloc([NPG, 8])
                PP = AFa.alloc([NPG, 4])
                rsum = AFa.alloc([8])
                rinv = AFa.alloc([8])
                esb = AFa.alloc([8])
                psl = AFa.alloc([4])
                tv = AFa.alloc([4])
                OAS = AFa.alloc([4, 16])
                psO = PS[7]
                K.p0, K.pn = 0, 7
                ck = I["ck"].rearrange("e r c -> (e r) c")
                cvv = I["cvv"].rearrange("e r c -> (e r) c")
                for s in range(NS):
                    P.ts(SEL[0:16, :], ones[0:16, :], ident[0:16, s:s + 1], ALU.mult)
                    pq = psn()
                    P.mm(pq[:, :], SEL[0:16, :], QS[0:16, :, :].rearrange("p h d -> p (h d)"))
                    P.copy(QB[:, :], pq[:, :], eng="act")
                    for pg in range(NPG):
                        kp = KP[(s * NPG + pg) % NPB]
                        P.gather(kp[:, :], ck, idx_i[:, s * NPG + pg:s * NPG + pg + 1])
                        P.tt(prod[:, :], kp[:, :], QB[:, :], ALU.mult)
                        P.red(SC[:, pg, :], v3(prod[:, :], 64))
                    sc4 = SC[:, :, :].rearrange("p g (h t) -> p g h t", t=2)
                    P.stt(sc4, sc4, 0.125, ALB[:, :, :].unsqueeze(3).to_broadcast([128, NPG, 4, 2]), ALU.mult, ALU.add)
                    P.act(EE[:, :, :], SC[:, :, :], AF.Exp)
                    P.red(rsum[:, :], EE[:, :, :].rearrange("p g e -> p e g"))
                    pt_ = psn()
                    P.mm(pt_[:, 0:8], ones[:, :], rsum[:, :], start=True, stop=False)
                    P.mm(pt_[:, 0:8], SEL[0:16, :], ES[0:16, :], start=False, stop=True)
                    P.mm(pt_[:, 8:16], SEL[0:16, :], ES[0:16, :])
                    P.recip(rinv[:, :], pt_[:, 0:8])
                    P.tt(esb[:, :], pt_[:, 8:16], rinv[:, :], ALU.mult)
                    es2 = esb[:, :].rearrange("p (h t) -> p h t", t=2)
                    P.stt(psl[:, :], es2[:, :, 1], K.NLAM[:, 0:1] if False else NLAM[:, 0:1], es2[:, :, 0], ALU.mult, ALU.add)
                    P.tt(EE[:, :, :], EE[:, :, :], rinv[:, :].unsqueeze(1).to_broadcast([128, NPG, 8]), ALU.mult)
                    ee4 = EE[:, :, :].rearrange("p g (h t) -> p g h t", t=2)
                    P.stt(PP[:, :, :], ee4[:, :, :, 1], NLAM[:, 0:1], ee4[:, :, :, 0], ALU.mult, ALU.add)
                    P.copy(PPb[:, :, :], PP[:, :, :], eng="act")
                    for pg in range(NPG):
                        vp = VP[(s * NPG + pg) % NPB]
                        P.gather(vp[:, :], cvv, idx_i[:, s * NPG + pg:s * NPG + pg + 1])
                        for h in range(4):
                            P.mm(psO[:, h * 16 + s:h * 16 + s + 1], vp[:, h * 128:(h + 1) * 128], PPb[:, pg, h:h + 1],
                                 start=(pg == 0), stop=(pg == NPG - 1))
                    P.tt(tv[:, :], VST[:, :, s], psl[:, :], ALU.mult)
                    P.tt(OAS[:, :, s], v3(psO[:, 0:64], 16)[:, :, s], tv[:, :], ALU.add)
                K.p0, K.pn = 0, 8
                P.store(O["nk_s"][e].rearrange("s (h d) -> s h d", d=128), KS[0:16, :, :])
                P.store(O["nv_s"][e].rearrange("s (h d) -> s h d", d=128), VS[0:16, :, :])
                for h in range(4):
                    rms_feature_gate(K, OAS[:, h, :], 16, SUBG[:, 0:1], (lambda h=h: ZS[:, h, :]), OT[:, h, T:TA], "as")

    if not (cfg.get("sample_attn", True) and cfg.get("attn", True)):
        for h_ in range(4):
            P.memset(OT[:, h_, T:TA], 0.0)
    if cfg.get("gdn", True):
        gdn_heads(K, li)


def ret_heads(K, li):
    P, I, O = K.P, K.I, K.O
    XT, HT, OT, AFa, ABa, PS, psn, Scope = K.XT, K.HT, K.OT, K.AFa, K.ABa, K.PS, K.psn, K.Scope
    next_w, wv512, v3, ident, ones, onesb = K.next_w, K.wv512, K.v3, K.ident, K.ones, K.onesb
    o_ = li // 2
    with Scope():
        DTC = AFa.alloc([4, 128])
        P.load(DTC[:, :, :], I["c_dtc"].rearrange("h j i -> j h i"))
        KDC = AFa.alloc([4])
        EGC = AFa.alloc([4])
        P.load(KDC[:, :], I["c_kdc"])
        P.load(EGC[:, :], I["c_egc"])
        GND = AFa.alloc([128])
        P.load(GND[:, :], I["gnd"][o_].partition_broadcast(128))
        K.MASK16 = AFa.alloc([16, 16])
        P.load(K.MASK16[:, :, :], v3(I["c_mask16"], 16))

        def alloc_bufs():
            B = {}
            B["S"] = AFa.alloc([128])
            B["QT"] = AFa.alloc([512])
            B["KTf"] = AFa.alloc([512])
            B["SZ"] = AFa.alloc([512])
            B["mats"] = [AFa.alloc([128]) for _ in range(5)]
            B["fo"] = (AFa.alloc([128]), AFa.alloc([2]), AFa.alloc([128]))
            return B

        def head_prompt(h, wv, B):
            g128 = math.exp(128.0 * LOG_GAMMA[h])
            S, QT, KTf, SZ = B["S"], B["QT"], B["KTf"], B["SZ"]
            V, KD, AQ, O1, OO = B["mats"]
            K.fo_sq, K.fo_ss, K.fo_on = B["fo"]
            P.memset(S[:, :], 0.0)
            for tb in range(4):
                c0 = tb * 512
                for j, dst in ((0, QT), (1, KTf)):
                    ps = psn()
                    for k in range(8):
                        P.mm(ps[:, :], wv[:, k, j * 128:(j + 1) * 128], HT[:, k, c0:c0 + 512], start=(k == 0), stop=(k == 7))
                    P.copy(dst[:, :], ps[:, :], eng=("act" if j else "dve"))
                ps = psn()
                for k in range(8):
                    P.mm(ps[:, :], wv[:, k, 384:512], HT[:, k, c0:c0 + 512], start=(k == 0), stop=(k == 7))
                P.act(SZ[:, :], ps[:, :], AF.Silu)
                for tl in range(4):
                    n = tb * 4 + tl
                    cc = slice(tl * 128, (tl + 1) * 128)
                    ps = psn()
                    for k in range(8):
                        P.mm(ps[:, 0:256], HT[:, k, n * 128:(n + 1) * 128], wv[:, k, 128:384], start=(k == 0), stop=(k == 7))
                    P.copy(V[:, :], ps[:, 128:256], eng="act")
                    P.ts(KD[:, :], ps[:, 0:128], KDC[:, h:h + 1], ALU.mult)
                    pa = psn()
                    P.mm(pa[:, 0:128], KTf[:, cc], QT[:, cc])
                    P.tt(AQ[:, :], pa[:, 0:128], DTC[:, h, :], ALU.mult)
                    p2 = psn()
                    P.mm(p2[:, 0:128], QT[:, cc], S[:, :])
                    P.ts(O1[:, :], p2[:, 0:128], EGC[:, h:h + 1], ALU.mult)
                    p4 = psn()
                    P.mm(p4[:, 0:128], AQ[:, :], V[:, :])
                    P.tt(OO[:, :], p4[:, 0:128], O1[:, :], ALU.add)
                    p3 = psn()
                    P.mm(p3[:, 0:128], KD[:, :], V[:, :])
                    P.stt(S[:, :], S[:, :], g128, p3[:, 0:128], ALU.mult, ALU.add)
                    finish_o_ln(K, OO[:, :], 128, GND, SZ[:, cc], OT[:, 4 + h, n * 128:(n + 1) * 128])
            P.store(O["rt_p"][o_, h], S[:, :])

        def head_sample(h, wv, B):
            g1 = math.exp(LOG_GAMMA[h])
            SZ = B["SZ"]
            K.fo_sq, K.fo_ss, K.fo_on = B["fo"]
            QS_ = [AFa.alloc([16]) for _ in range(3)]
            for j in range(3):
                ps = psn()
                for k in range(8):
                    P.mm(ps[:, 0:16], wv[:, k, j * 128:(j + 1) * 128], HT[:, k, T:TA], start=(k == 0), stop=(k == 7))
                if j == 1:
                    P.ts(QS_[j][:, :], ps[:, 0:16], 128.0 ** -0.5, ALU.mult)
                else:
                    P.copy(QS_[j][:, :], ps[:, 0:16], eng="act")
            ps = psn()
            for k in range(8):
                P.mm(ps[:, 0:16], wv[:, k, 384:512], HT[:, k, T:TA], start=(k == 0), stop=(k == 7))
            P.act(SZ[:, 0:16], ps[:, 0:16], AF.Silu)
            sample_state_step(K, QS_[0][:, :], QS_[1][:, :], QS_[2][:, :], g1, None, None, g1,
                              (lambda s, h=h: I["sret"][o_, s, h]), (lambda s, h=h: O["rt_s"][o_, s, h]),
                              GND, SZ[:, 0:16], OT[:, 4 + h, T:TA], False, True)

        NPAR = 2
        for hp in range(0, 4, NPAR):
            wvs = [wv512(next_w(("od", o_, hp + i), prefetch=(i == 0))) for i in range(NPAR)]
            with Scope():
                Bs = [alloc_bufs() for _ in range(NPAR)]
                streams = []
                for i in range(NPAR):
                    K.p0, K.pn = (8 // NPAR) * i, 8 // NPAR
                    cap = P.capture()
                    head_prompt(hp + i, wvs[i], Bs[i])
                    P.end_capture()
                    streams.append(cap)
                K.p0, K.pn = 0, 8
                P.merge(streams)
                for i in range(NPAR):
                    with Scope():
                        head_sample(hp + i, wvs[i], Bs[i])


def s5_mixer(K, li):
    P, I, O = K.P, K.I, K.O
    XT, HT, OT, AFa, ABa, PS, psn, Scope = K.XT, K.HT, K.OT, K.AFa, K.ABa, K.PS, K.psn, K.Scope
    v3, ident, ones, onesb, wv512 = K.v3, K.ident, K.ones, K.onesb, K.wv512
    o_ = li // 2
    TC = 512
    with Scope():
        WU = ABa.alloc([4096])
        P.load(WU[:, :], I["wino"][o_, 0], eng="pool")
        wu = wv512(WU)
        ARE, AIM, LDT = AFa.alloc([16]), AFa.alloc([16]), AFa.alloc([16])
        P.load(ARE[:, :], I["s5are"][o_].rearrange("(t l) -> l t", l=128), slow=True)
        P.load(AIM[:, :], I["s5aim"][o_].rearrange("(t l) -> l t", l=128), slow=True)
        ldt2 = I["s5ldt"][o_].rearrange("o (t two) -> o two t", two=2)
        P.load(LDT[0:64, :], ldt2[:, 0, :].partition_broadcast(64), slow=True)
        P.load(LDT[64:128, :], ldt2[:, 1, :].partition_broadcast(64), slow=True)
        P.act(LDT[:, :], LDT[:, :], AF.Exp)
        LRE, TH, RR = AFa.alloc([16]), AFa.alloc([16]), AFa.alloc([16])
        P.tt(LRE[:, :], ARE[:, :], LDT[:, :], ALU.mult)
        P.tt(TH[:, :], AIM[:, :], LDT[:, :], ALU.mult)
        P.act(RR[:, :], LRE[:, :], AF.Exp)
        HPI = AFa.alloc([1])
        P.memset(HPI[:, :], math.pi / 2)
        C1, S1, t1_, t2_ = AFa.alloc([16]), AFa.alloc([16]), AFa.alloc([16]), AFa.alloc([16])
        P.act(S1[:, :], TH[:, :], AF.Sin, scale=1.0 / 16)
        P.act(C1[:, :], TH[:, :], AF.Sin, scale=1.0 / 16, bias=HPI[:, 0:1])
        for _ in range(4):
            P.tt(t1_[:, :], C1[:, :], C1[:, :], ALU.mult)
            P.tt(t2_[:, :], S1[:, :], S1[:, :], ALU.mult)
            P.stt(S1[:, :], C1[:, :], 2.0, S1[:, :], ALU.mult, ALU.mult)
            P.tt(C1[:, :], t1_[:, :], t2_[:, :], ALU.subtract)
        LBR, LBI, NLBI = AFa.alloc([16]), AFa.alloc([16]), AFa.alloc([16])
        P.tt(LBR[:, :], RR[:, :], C1[:, :], ALU.mult)
        P.tt(LBI[:, :], RR[:, :], S1[:, :], ALU.mult)
        P.ts(NLBI[:, :], LBI[:, :], -1.0, ALU.mult)
        DEN, FRE, FIM, LM1 = AFa.alloc([16]), AFa.alloc([16]), AFa.alloc([16]), AFa.alloc([16])
        P.tt(DEN[:, :], ARE[:, :], ARE[:, :], ALU.mult)
        P.tt(t1_[:, :], AIM[:, :], AIM[:, :], ALU.mult)
        P.tt(DEN[:, :], DEN[:, :], t1_[:, :], ALU.add)
        P.recip(DEN[:, :], DEN[:, :])
        P.ts(LM1[:, :], LBR[:, :], -1.0, ALU.add)
        P.tt(FRE[:, :], LM1[:, :], ARE[:, :], ALU.mult)
        P.tt(t1_[:, :], LBI[:, :], AIM[:, :], ALU.mult)
        P.tt(FRE[:, :], FRE[:, :], t1_[:, :], ALU.add)
        P.tt(FRE[:, :], FRE[:, :], DEN[:, :], ALU.mult)
        P.tt(FIM[:, :], LBI[:, :], ARE[:, :], ALU.mult)
        P.tt(t1_[:, :], LM1[:, :], AIM[:, :], ALU.mult)
        P.tt(FIM[:, :], FIM[:, :], t1_[:, :], ALU.subtract)
        P.tt(FIM[:, :], FIM[:, :], DEN[:, :], ALU.mult)
        BRE, BIM, BBR, BBI, tb_ = [AFa.alloc([16, 16]) for _ in range(5)]
        P.load(BRE[:, :, :], I["s5bre"][o_].rearrange("(t l) c -> l t c", l=128))
        P.load(BIM[:, :, :], I["s5bim"][o_].rearrange("(t l) c -> l t c", l=128))
        frb = FRE[:, :].unsqueeze(2).to_broadcast([128, 16, 16])
        fib = FIM[:, :].unsqueeze(2).to_broadcast([128, 16, 16])
        P.tt(BBR[:, :, :], BRE[:, :, :], frb, ALU.mult)
        P.tt(tb_[:, :, :], BIM[:, :, :], fib, ALU.mult)
        P.tt(BBR[:, :, :], BBR[:, :, :], tb_[:, :, :], ALU.subtract)
        P.tt(BBI[:, :, :], BRE[:, :, :], fib, ALU.mult)
        P.tt(tb_[:, :, :], BIM[:, :, :], frb, ALU.mult)
        P.tt(BBI[:, :, :], BBI[:, :, :], tb_[:, :, :], ALU.add)
        DSK = AFa.alloc([4])
        P.load(DSK[:, :], I["s5d"][o_].rearrange("(c p) -> p c", p=128), slow=True)
        XPall = AFa.alloc([16, 2])
        UT = AFa.alloc([4, TC])
        UTs = AFa.alloc([16])
        EC, ESn = AFa.alloc([TC]), AFa.alloc([TC])
        W = [AFa.alloc([TC]) for _ in range(4)]
        Bm, Cin = AFa.alloc([128]), AFa.alloc([128])
        CCr, CCi, CMK = AFa.alloc([64]), AFa.alloc([64]), AFa.alloc([8])
        P.load(CMK[:, :], I["c_cmask"])
        BLr, BLi, CLr, CLi = [AFa.alloc([128]) for _ in range(4)]
        XP = AFa.alloc([2])
        nsm = AFa.alloc([1])
        xs0 = AFa.alloc([2, 16])
        xtm = AFa.alloc([128])
        xsn = AFa.alloc([2, 16])
        xso = AFa.alloc([128])
        yt = AFa.alloc([TC])
        YA = [PS[0], PS[1], PS[2], PS[3]]
        YS = PS[4]
        K.p0, K.pn = 5, 3
        for ch in range(4):
            for tc in range(4):
                ps = psn()
                for k in range(8):
                    P.mm(ps[:, :], wu[:, k, ch * 128:(ch + 1) * 128], HT[:, k, tc * TC:(tc + 1) * TC], start=(k == 0), stop=(k == 7))
                P.copy(UT[:, tc, :], ps[:, :], eng="act")
            ps = psn()
            for k in range(8):
                P.mm(ps[:, 0:16], wu[:, k, ch * 128:(ch + 1) * 128], HT[:, k, T:TA], start=(k == 0), stop=(k == 7))
            P.copy(UTs[:, :], ps[:, 0:16], eng="act")
            P.load(CCr[:, :], I["s5cre"][o_][ch * 128:(ch + 1) * 128, :])
            P.load(CCi[:, :], I["s5cim"][o_][ch * 128:(ch + 1) * 128, :])
            for li_ in range(4):
                lt = ch * 4 + li_
                off = li_ * 32
                for src, dst, neg in ((BBR, BLr, False), (BBI, BLi, False)):
                    P.memset(Bm[:, :], 0.0)
                    P.copy(Bm[0:64, off:off + 16], src[0:64, lt, :])
                    P.copy(Bm[64:128, off + 16:off + 32], src[64:128, lt, :])
                    pt = psn()
                    P.tr(pt[:, 0:128], Bm[:, :], ident[:])
                    P.copy(dst[:, :], pt[:, 0:128], eng="act")
                for nm, dst, neg in (("s5cre", CLr, False), ("s5cim", CLi, True)):
                    ccs = CCr if nm == "s5cre" else CCi
                    P.ts(Cin[:, 0:64], ccs[:, :], CMK[:, li_ * 2:li_ * 2 + 1], ALU.mult)
                    P.ts(Cin[:, 64:128], ccs[:, :], CMK[:, li_ * 2 + 1:li_ * 2 + 2], ALU.mult)
                    pt = psn()
                    P.tr(pt[:, 0:128], Cin[:, :], ident[:])
                    if neg:
                        P.ts(dst[:, :], pt[:, 0:128], -1.0, ALU.mult)
                    else:
                        P.copy(dst[:, :], pt[:, 0:128], eng="act")
                P.copy(EC[:, 0:1], C1[:, lt:lt + 1])
                P.copy(ESn[:, 0:1], S1[:, lt:lt + 1])
                m = 1
                while m < TC:
                    cm, sm = EC[:, m - 1:m], ESn[:, m - 1:m]
                    P.ts(nsm[:, :], sm, -1.0, ALU.mult)
                    P.ts(EC[:, m:2 * m], EC[:, 0:m], cm, ALU.mult)
                    P.stt(EC[:, m:2 * m], ESn[:, 0:m], nsm[:, 0:1], EC[:, m:2 * m], ALU.mult, ALU.add)
                    P.ts(ESn[:, m:2 * m], EC[:, 0:m], sm, ALU.mult)
                    P.stt(ESn[:, m:2 * m], ESn[:, 0:m], cm, ESn[:, m:2 * m], ALU.mult, ALU.add)
                    m *= 2
                Rb = RR[:, lt:lt + 1].to_broadcast([128, TC])
                P.memset(XP[:, :], 0.0)
                for tc in range(4):
                    pbr = psn()
                    P.mm(pbr[:, :], BLr[:, :], UT[:, tc, :])
                    pbi = psn()
                    P.mm(pbi[:, :], BLi[:, :], UT[:, tc, :])
                    vr, vi, zr, zi = W
                    P.tt(vr[:, :], pbr[:, :], EC[:, :], ALU.mult)
                    P.tt(zr[:, :], pbi[:, :], ESn[:, :], ALU.mult)
                    P.tt(vr[:, :], vr[:, :], zr[:, :], ALU.add)
                    P.tt(vi[:, :], pbi[:, :], EC[:, :], ALU.mult)
                    P.tt(zr[:, :], pbr[:, :], ESn[:, :], ALU.mult)
                    P.tt(vi[:, :], vi[:, :], zr[:, :], ALU.subtract)
                    P.scan(zr[:, :], Rb, vr[:, :], XP[:, 0:1])
                    P.scan(zi[:, :], Rb, vi[:, :], XP[:, 1:2])
                    P.tt(vr[:, :], zr[:, :], EC[:, :], ALU.mult)
                    P.tt(yt[:, :], zi[:, :], ESn[:, :], ALU.mult)
                    P.tt(vr[:, :], vr[:, :], yt[:, :], ALU.subtract)
                    P.tt(vi[:, :], zr[:, :], ESn[:, :], ALU.mult)
                    P.tt(yt[:, :], zi[:, :], EC[:, :], ALU.mult)
                    P.tt(vi[:, :], vi[:, :], yt[:, :], ALU.add)
                    P.copy(XP[:, 0:1], vr[:, TC - 1:TC])
                    P.copy(XP[:, 1:2], vi[:, TC - 1:TC])
                    P.mm(YA[tc][:, :], CLr[:, :], vr[:, :], start=(li_ == 0), stop=False)
                    P.mm(YA[tc][:, :], CLi[:, :], vi[:, :], start=False, stop=(li_ == 3))
                P.copy(XPall[:, lt, :], XP[:, :])
                for ri, nm in ((0, "scre"), (1, "scim")):
                    P.load(xtm[0:16, :], I[nm][o_][:, lt * 128:(lt + 1) * 128])
                    pt = psn()
                    P.tr(pt[:, 0:16], xtm[0:16, :], ident[0:16, 0:16])
                    P.copy(xs0[:, ri, :], pt[:, 0:16])
                pbr = psn()
                P.mm(pbr[:, 0:16], BLr[:, :], UTs[:, :])
                P.mm(pbr[:, 16:32], BLi[:, :], UTs[:, :])
                P.ts(xsn[:, 0, :], xs0[:, 0, :], LBR[:, lt:lt + 1], ALU.mult)
                P.stt(xsn[:, 0, :], xs0[:, 1, :], NLBI[:, lt:lt + 1], xsn[:, 0, :], ALU.mult, ALU.add)
                P.tt(xsn[:, 0, :], xsn[:, 0, :], pbr[:, 0:16], ALU.add)
                P.ts(xsn[:, 1, :], xs0[:, 1, :], LBR[:, lt:lt + 1], ALU.mult)
                P.stt(xsn[:, 1, :], xs0[:, 0, :], LBI[:, lt:lt + 1], xsn[:, 1, :], ALU.mult, ALU.add)
                P.tt(xsn[:, 1, :], xsn[:, 1, :], pbr[:, 16:32], ALU.add)
                P.mm(YS[:, 0:16], CLr[:, :], xsn[:, 0, :], start=(li_ == 0), stop=False)
                P.mm(YS[:, 0:16], CLi[:, :], xsn[:, 1, :], start=False, stop=(li_ == 3))
                for ri, nm in ((0, "s5r_s"), (1, "s5i_s")):
                    pt = psn()
                    P.tr(pt[0:16, 0:128], xsn[:, ri, :], ident[:])
                    P.copy(xso[0:16, :], pt[0:16, 0:128])
                    P.store(O[nm][o_][:, lt * 128:(lt + 1) * 128], xso[0:16, :])
            for tc in range(4):
                P.stt(yt[:, :], UT[:, tc, :], DSK[:, ch:ch + 1], YA[tc][:, :], ALU.mult, ALU.add)
                P.act(OT[:, ch, tc * TC:(tc + 1) * TC], yt[:, :], AF.Gelu_apprx_tanh)
            P.stt(yt[:, 0:16], UTs[:, :], DSK[:, ch:ch + 1], YS[:, 0:16], ALU.mult, ALU.add)
            P.act(OT[:, ch, T:TA], yt[:, 0:16], AF.Gelu_apprx_tanh)
        P.store(O["s5r_p"][o_].rearrange("(t l) -> l t", l=128), XPall[:, :, 0], slow=True)
        P.store(O["s5i_p"][o_].rearrange("(t l) -> l t", l=128), XPall[:, :, 1], slow=True)
        K.p0, K.pn = 0, 8
    with Scope():
        WZ = ABa.alloc([4096])
        WG = ABa.alloc([2048])
        P.load(WZ[:, :], I["wino"][o_, 1], eng="pool")
        P.load(WG[:, :], I["wglu"][o_], eng="pool")
        wz = wv512(WZ)
        wg = WG[:, :].rearrange("p (k c) -> p k c", c=512)
        sg = [AFa.alloc([512]) for _ in range(4)]
        szb = AFa.alloc([512])
        for (c0, w) in BLOCKS:
            pgs = []
            for cp in range(4):
                pg = PS[cp]
                for k in range(4):
                    P.mm(pg[:, 0:w], wg[:, k, cp * 128:(cp + 1) * 128], OT[:, k, c0:c0 + w], start=(k == 0), stop=(k == 3))
                pgs.append(pg)
            for cp in range(4):
                P.act(sg[cp][:, 0:w], pgs[cp][:, 0:w], AF.Sigmoid)
            for cp in range(4):
                pz = PS[4 + cp]
                for k in range(8):
                    P.mm(pz[:, 0:w], wz[:, k, cp * 128:(cp + 1) * 128], HT[:, k, c0:c0 + w], start=(k == 0), stop=(k == 7))
                P.act(szb[:, 0:w], pz[:, 0:w], AF.Silu)
                P.tt(sg[cp][:, 0:w], sg[cp][:, 0:w], szb[:, 0:w], ALU.mult)
                P.tt(OT[:, cp, c0:c0 + w], OT[:, cp, c0:c0 + w], sg[cp][:, 0:w], ALU.mult)


def odd_layer(K, li):
    if K.cfg.get("s5", True):
        s5_mixer(K, li)
    if K.cfg.get("ret", True):
        ret_heads(K, li)


def build(cfg=None):
    cfg = cfg or {}
    nlayers = cfg.get("layers", 4)
    nc = bass.Bass("TRN2", target_bir_lowering=False)
    specs = INPUT_SPECS
    if cfg.get("small_cache"):
        specs = [(n, ((2, 256, 512) if n in ("ck", "cvv") else s), dt) for n, s, dt in INPUT_SPECS]
    I = {n: nc.dram_tensor(n, list(s), dt, kind="ExternalInput").ap() for n, s, dt in specs}
    O = {n: nc.dram_tensor(n, list(s), F32, kind="ExternalOutput").ap() for n, s in OUTPUT_SPECS}
    if cfg.get("dbg"):
        O["dbg_xt"] = nc.dram_tensor("dbg_xt", [128, 8 * TA], F32, kind="ExternalOutput").ap()
        O["dbg_ht"] = nc.dram_tensor("dbg_ht", [128, 8 * TA], BF16, kind="ExternalOutput").ap()
        O["dbg_ot"] = nc.dram_tensor("dbg_ot", [128, 8 * TA], BF16, kind="ExternalOutput").ap()
    K = Ctx()
    with ExitStack() as es:
        P = Prog(nc, es)
        P.limit = cfg.get("max_ops")
        if cfg.get("trace"):
            P.trace = []
        K.P = P

        def sbt(name, shape, dt=F32):
            return es.enter_context(nc.sbuf_tensor(name, shape, dt))

        XT = sbt("XT", [128, 8, TA])
        HT = sbt("HT", [128, 8, TA], BF16)
        OT = sbt("OT", [128, 8, TA], BF16)
        WB = [sbt("WB0", [128, 4096], BF16), sbt("WB1", [128, 4096], BF16)]
        AFa = Arena(sbt("AFa", [128, NF]), NF)
        ABa = Arena(sbt("ABa", [128, NB], BF16), NB)
        ident = sbt("ident", [128, 128])
        ones = sbt("ones", [128, 128])
        onesb = sbt("onesb", [128, 128], BF16)
        trib = sbt("trib", [128, 128], BF16)
        CT = sbt("CT", [128, 8, 17], BF16)
        MODT = sbt("MODT", [128, 24, 17])
        GS = sbt("GS", [128, 8, 17])
        BADA = sbt("BADA", [128, 24])
        NG = sbt("NG", [128, 8])
        PS = [es.enter_context(nc.psum_tensor("ps%d" % i, [128, 512], F32)) for i in range(8)]
        K.pi, K.p0, K.pn = 0, 0, 8

        K.forced = []

        def psn():
            if K.forced:
                return K.forced.pop(0)
            b = PS[K.p0 + (K.pi % K.pn)]
            K.pi += 1
            return b

        class Scope:
            def __enter__(self):
                AFa.stack.append(AFa.top)
                ABa.stack.append(ABa.top)

            def __exit__(self, *a):
                P.barrier()
                AFa.top = AFa.stack.pop()
                ABa.top = ABa.stack.pop()
                return False

        def v3(ap, b):
            return ap.rearrange("p (a b) -> p a b", b=b)

        wlist = []
        for li in range(nlayers):
            for blk in range(6):
                wlist.append((("ada", li, blk), I["wada"][li, blk], 4096))
            if li % 2 == 0:
                e = li // 2
                if cfg.get("attn", True):
                    for h in range(4):
                        wlist.append((("ea", e, h), I["wine"][e, h], 4096))
                if cfg.get("gdn", True):
                    wlist.append((("wab", e), I["wab"][e], 64))
                    for h in range(4):
                        wlist.append((("eb", e, h), I["wine"][e, 4 + h], 4096))
                for blk in range(2):
                    wlist.append((("eo", e, blk), I["woute"][e, blk], 4096))
            else:
                o = li // 2
                if cfg.get("ret", True):
                    for h in range(4):
                        wlist.append((("od", o, h), I["wino"][o, 2 + h], 4096))
                for blk in range(2):
                    wlist.append((("oo", o, blk), I["wouto"][o, blk], 4096))
        K.wi, K.wissued = 0, 0

        def w_issue(i):
            tag, src, n = wlist[i]
            P.load(WB[i % 2][:, 0:n], src, eng="pool")

        def next_w(tag, prefetch=True):
            i = K.wi
            assert wlist[i][0] == tag, (wlist[i][0], tag)
            if K.wissued <= i:
                w_issue(i)
                K.wissued = i + 1
            if prefetch and i + 1 < len(wlist) and K.wissued <= i + 1:
                w_issue(i + 1)
                K.wissued = i + 2
            K.wi += 1
            return WB[i % 2]

        def wv512(wb):
            return wb[:, 0:4096].rearrange("p (k c) -> p k c", c=512)

        P.load(ident[:], I["c_ident"])
        P.load(ones[:], I["c_ones"])
        P.load(onesb[:], I["c_onesb"])
        P.load(trib[:], I["c_trib"])
        with Scope():
            xin = [AFa.alloc([1024]) for _ in range(2)]
            for n in range(17):
                rows = 128 if n < 16 else 16
                src = I["xp"][n * 128:(n + 1) * 128, :] if n < 16 else I["xs"]
                xt = xin[n % 2]
                P.load(xt[0:rows, :], src)
                for half in range(2):
                    ps = psn()
                    for j in range(4):
                        k = half * 4 + j
                        P.tr(ps[:, j * 128:j * 128 + rows], xt[0:rows, k * 128:(k + 1) * 128], ident[0:rows, 0:rows])
                    P.copy(XT[:, half * 4:(half + 1) * 4, n * 128:n * 128 + rows], v3(ps[:, :], 128)[:, :, 0:rows],
                           eng=("act" if half else "dve"))
            cvt = AFa.alloc([1024])
            P.load(cvt[0:17, :], I["cv"])
            P.act(cvt[0:17, :], cvt[0:17, :], AF.Silu)
            ps = psn()
            for k in range(8):
                P.tr(ps[:, k * 32:k * 32 + 17], cvt[0:17, k * 128:(k + 1) * 128], ident[0:17, 0:17])
            P.copy(CT[:], v3(ps[:, 0:256], 32)[:, :, 0:17])

        def modulation(li):
            P.load(BADA[:], I["bada"][li].rearrange("(c p) -> p c", p=128), slow=True)
            P.load(NG[:], I["normg"][li].rearrange("(c p) -> p c", p=128), slow=True)
            for blk in range(6):
                wv = wv512(next_w(("ada", li, blk)))
                ps = psn()
                for j in range(4):
                    for k in range(8):
                        P.mm(ps[:, j * 32:j * 32 + 17], wv[:, k, j * 128:(j + 1) * 128], CT[:, k, :],
                             start=(k == 0), stop=(k == 7))
                P.tt(MODT[:, blk * 4:(blk + 1) * 4, :], v3(ps[:, 0:128], 32)[:, :, 0:17],
                     BADA[:, blk * 4:(blk + 1) * 4].unsqueeze(2).to_broadcast([128, 4, 17]), ALU.add)
            P.ts(GS[:], MODT[:, 8:16, :], 1.0, ALU.add)
            P.tt(GS[:], GS[:], NG[:].unsqueeze(2).to_broadcast([128, 8, 17]), ALU.mult)

        def norm_ht():
            with Scope():
                RS = AFa.alloc([TA])
                sqb = [ABa.alloc([512]) for _ in range(2)]
                for (c0, w) in BLOCKS:
                    ps = psn()
                    for k in range(8):
                        sq = sqb[k % 2]
                        P.act(sq[:, 0:w], XT[:, k, c0:c0 + w], AF.Square)
                        P.mm(ps[:, 0:w], onesb[:], sq[:, 0:w], start=(k == 0), stop=(k == 7))
                    P.ts(RS[:, c0:c0 + w], ps[:, 0:w], 1.0 / D, ALU.mult, EPS, ALU.add)
                    P.act(RS[:, c0:c0 + w], RS[:, c0:c0 + w], AF.Sqrt)
                    P.recip(RS[:, c0:c0 + w], RS[:, c0:c0 + w])
                tmp = [AFa.alloc([512]) for _ in range(2)]
                i = 0
                for (c0, w) in BLOCKS:
                    for k in range(8):
                        t = tmp[i % 2]
                        i += 1
                        P.tt(t[:, 0:w], XT[:, k, c0:c0 + w], RS[:, c0:c0 + w], ALU.mult)
                        if c0 < T:
                            P.act(HT[:, k, c0:c0 + w], t[:, 0:w], AF.Identity, bias=MODT[:, k, 0:1], scale=GS[:, k, 0:1])
                        else:
                            P.tt(t[:, 0:w], t[:, 0:w], GS[:, k, 1:17], ALU.mult)
                            P.tt(HT[:, k, c0:c0 + w], t[:, 0:w], MODT[:, k, 1:17], ALU.add)

        def out_proj(tag, idx):
            with Scope():
                ts_ = AFa.alloc([16])
                for blk in range(2):
                    wv = wv512(next_w((tag, idx, blk)))
                    for j in range(4):
                        fc = blk * 4 + j
                        for (c0, w) in BLOCKS:
                            ps = psn()
                            for k in range(8):
                                P.mm(ps[:, 0:w], wv[:, k, j * 128:(j + 1) * 128], OT[:, k, c0:c0 + w],
                                     start=(k == 0), stop=(k == 7))
                            if c0 < T:
                                P.stt(XT[:, fc, c0:c0 + w], ps[:, 0:w], MODT[:, 16 + fc, 0:1], XT[:, fc, c0:c0 + w],
                                      ALU.mult, ALU.add)
                            else:
                                P.tt(ts_[:, :], ps[:, 0:16], MODT[:, 16 + fc, 1:17], ALU.mult)
                                P.tt(XT[:, fc, c0:c0 + 16], XT[:, fc, c0:c0 + 16], ts_[:, :], ALU.add)

        def final_out():
            with Scope():
                yb = [AFa.alloc([1024]) for _ in range(2)]
                for n in range(17):
                    rows = 128 if n < 16 else 16
                    yt = yb[n % 2]
                    for half in range(2):
                        ps = psn()
                        for j in range(4):
                            k = half * 4 + j
                            P.tr(ps[0:rows, j * 128:(j + 1) * 128], XT[:, k, n * 128:n * 128 + rows], ident[:])
                        P.copy(yt[0:rows, half * 512:(half + 1) * 512], ps[0:rows, :], eng=("act" if half else "dve"))
                    dst = O["y_p"][n * 128:(n + 1) * 128, :] if n < 16 else O["y_s"]
                    P.store(dst, yt[0:rows, :])

        def dbg_dump():
            if cfg.get("dbg"):
                P.store(O["dbg_xt"], XT[:].rearrange("p k t -> p (k t)"))
                P.store(O["dbg_ht"], HT[:].rearrange("p k t -> p (k t)"))
                P.store(O["dbg_ot"], OT[:].rearrange("p k t -> p (k t)"))

        K.__dict__.update(locals())
        for li in range(nlayers):
            modulation(li)
            norm_ht()
            if cfg.get("zero_ot", False) or True:
                pass
            if cfg.get("mods_only"):
                continue
            if not (cfg.get("attn", True) and cfg.get("gdn", True) and cfg.get("s5", True) and cfg.get("ret", True)):
                for k_ in range(8):
                    for (c0_, w_) in BLOCKS:
                        P.memset(OT[:, k_, c0_:c0_ + w_], 0.0)
            if li % 2 == 0:
                even_layer(K, li)
                out_proj("eo", li // 2)
            else:
                odd_layer(K, li)
                out_proj("oo", li // 2)
        final_out()
        dbg_dump()
        P.barrier()
        P.emit()
    K.n_instr = P.n_instr
    K.peaks = (AFa.peak, ABa.peak)
    return nc, K


def _wblocks(w, col_lists):
    kd = w.shape[0] // 128
    out = []
    for cols in col_lists:
        blk = w[:, cols].reshape(kd, 128, len(cols)).transpose(1, 0, 2).reshape(128, kd * len(cols))
        out.append(blk)
    return np.ascontiguousarray(np.stack(out, 0))


def _even_cols():
    lists = []
    for h in range(4):
        lists.append(np.concatenate([np.arange(h * 128, (h + 1) * 128) + off for off in (0, 512, 1024, 1536)]))
    for h in range(4):
        lists.append(np.concatenate([np.arange(h * 128, (h + 1) * 128) + off for off in (2048, 2560, 3072, 3584)]))
    return lists


def _odd_cols():
    lists = [np.arange(0, 512), np.arange(512, 1024)]
    for h in range(4):
        lists.append(np.concatenate([np.arange(h * 128, (h + 1) * 128) + off for off in (1024, 1536, 2048, 2560)]))
    return lists


def make_shared(inp):
    f = lambda a: np.ascontiguousarray(np.asarray(a, dtype=np.float32))
    sh = {}
    sh["ck"] = f(inp["cache_k"]).reshape(2, NPHYS * 128, 512)
    sh["cvv"] = f(inp["cache_v"]).reshape(2, NPHYS * 128, 512)
    sh["normg"] = f(inp["norm_g"])
    w_ada = f(inp["w_ada"])
    sh["wada"] = np.stack([_wblocks(w_ada[l], [np.arange(b * 512, (b + 1) * 512) for b in range(6)]) for l in range(4)], 0)
    sh["bada"] = f(inp["b_ada"])
    w_in_e = f(inp["w_in_e"])
    sh["wine"] = np.stack([_wblocks(w_in_e[e], _even_cols()) for e in range(2)], 0)
    sh["wab"] = np.stack([_wblocks(w_in_e[e], [np.arange(4096, 4104)])[0] for e in range(2)], 0)
    w_out_e = f(inp["w_out_e"])
    sh["woute"] = np.stack([_wblocks(w_out_e[e], [np.arange(0, 512), np.arange(512, 1024)]) for e in range(2)], 0)
    sh["qng"] = f(inp["qn_g"]).reshape(2, 1, 64)
    sh["kng"] = f(inp["kn_g"]).reshape(2, 1, 64)
    sh["lamv"] = np.ascontiguousarray(np.stack([f(inp["lam_q1"]), f(inp["lam_k1"]), f(inp["lam_q2"]), f(inp["lam_k2"])], 1)).reshape(2, 1, 256)
    sh["sublng"] = f(inp["subln_g"]).reshape(2, 128, 1)
    sh["convw"] = f(inp["conv_w"])
    sh["alog"] = f(inp["a_log"]).reshape(2, 1, 4)
    sh["dtb"] = f(inp["dt_bias"]).reshape(2, 1, 4)
    sh["gnb"] = f(inp["gn_b"]).reshape(2, 1, 128)
    w_in_o = f(inp["w_in_o"])
    sh["wino"] = np.stack([_wblocks(w_in_o[o], _odd_cols()) for o in range(2)], 0)
    w_out_o = f(inp["w_out_o"])
    sh["wouto"] = np.stack([_wblocks(w_out_o[o], [np.arange(0, 512), np.arange(512, 1024)]) for o in range(2)], 0)
    sh["s5are"] = f(inp["s5_a_re"]).reshape(2, 2048)
    sh["s5aim"] = f(inp["s5_a_im"]).reshape(2, 2048)
    sh["s5bre"] = f(inp["s5_b_re"]).reshape(2, 2048, 16)
    sh["s5bim"] = f(inp["s5_b_im"]).reshape(2, 2048, 16)
    sh["s5cre"] = f(inp["s5_c_re"]).reshape(2, 512, 64)
    sh["s5cim"] = f(inp["s5_c_im"]).reshape(2, 512, 64)
    sh["s5d"] = f(inp["s5_d"])
    sh["s5ldt"] = f(inp["s5_log_dt"]).reshape(2, 1, 32)
    w_glu = f(inp["w_glu"])
    sh["wglu"] = np.stack([_wblocks(w_glu[o], [np.arange(0, 512)])[0] for o in range(2)], 0)
    sh["gnd"] = f(inp["gn_d"]).reshape(2, 1, 128)
    sh.update(make_consts2(make_consts()))
    return sh


def make_core_inputs(inp, sh, c):
    f = lambda a: np.ascontiguousarray(np.asarray(a, dtype=np.float32))
    m = dict(sh)
    s0, s1 = c * NS, (c + 1) * NS
    m["xp"] = f(inp["x_prompt"][c])
    m["xs"] = f(inp["x_sample"][s0:s1, 0, :])
    m["cv"] = f(np.concatenate([np.asarray(inp["c_prompt"])[c:c + 1], np.asarray(inp["c_sample"])[s0:s1]], 0))
    m["pt"] = np.ascontiguousarray(np.asarray(inp["page_table"], dtype=np.int32)[s0:s1]).reshape(1, NS * NPG)
    m["sconv"] = f(np.asarray(inp["state_b_conv"])[:, s0:s1]).reshape(2, NS * 3, 1536)
    m["sssm"] = f(np.asarray(inp["state_b_ssm"])[:, s0:s1])
    m["scre"] = f(np.asarray(inp["state_c_re"])[:, s0:s1]).reshape(2, NS, 2048)
    m["scim"] = f(np.asarray(inp["state_c_im"])[:, s0:s1]).reshape(2, NS, 2048)
    m["sret"] = f(np.asarray(inp["state_d_ret"])[:, s0:s1])
    return m


_CACHE = {}


def kernel(**inp):
    ncores = 8
    if "nc" not in _CACHE:
        _CACHE["nc"] = build()[0]
    nc = _CACHE["nc"]
    sh = make_shared(inp)
    in_maps = [make_core_inputs(inp, sh, c) for c in range(ncores)]
    res = run_bass_kernel_spmd(nc, in_maps, core_ids=list(range(ncores)))
    R = res.results
    cat = lambda name, ax: np.concatenate([np.asarray(R[c][name]) for c in range(ncores)], axis=ax)
    stk = lambda name: np.stack([np.asarray(R[c][name]) for c in range(ncores)], 0)
    y_p = stk("y_p")
    y_s = cat("y_s", 0).reshape(ncores * NS, 1, D)
    nk_p = stk("nk_p").transpose(1, 0, 2, 3).reshape(2, ncores, T, 4, 128)
    nv_p = stk("nv_p").transpose(1, 0, 2, 3).reshape(2, ncores, T, 4, 128)
    nk_s = cat("nk_s", 1).reshape(2, ncores * NS, 1, 4, 128)
    nv_s = cat("nv_s", 1).reshape(2, ncores * NS, 1, 4, 128)
    cv_p = stk("cv_p").transpose(1, 0, 2, 3)
    cv_s = cat("cv_s", 1)
    dl_p = stk("dl_p").transpose(1, 0, 2, 3, 4)
    dl_s = cat("dl_s", 1)
    s5r_p = stk("s5r_p").transpose(1, 0, 2).reshape(2, ncores, 32, 64)
    s5i_p = stk("s5i_p").transpose(1, 0, 2).reshape(2, ncores, 32, 64)
    s5r_s = cat("s5r_s", 1).reshape(2, ncores * NS, 32, 64)
    s5i_s = cat("s5i_s", 1).reshape(2, ncores * NS, 32, 64)
    rt_p = stk("rt_p").transpose(1, 0, 2, 3, 4)
    rt_s = cat("rt_s", 1)
    outs = (y_p, y_s, nk_p, nv_p, nk_s, nv_s, cv_p, cv_s, dl_p, dl_s, s5r_p, s5i_p, s5r_s, s5i_s, rt_p, rt_s)
    return tuple(np.ascontiguousarray(o, dtype=np.float32) for o in outs)
```

```python
import math
import numpy as np
import ml_dtypes
import concourse.bass as bass
import concourse.mybir as mybir
from concourse.bass_utils import run_bass_kernel_spmd
from contextlib import ExitStack

F32 = mybir.dt.float32
BF16 = mybir.dt.bfloat16
I32 = mybir.dt.int32
AF = mybir.ActivationFunctionType
ALU = mybir.AluOpType
AX = mybir.AxisListType

ENGS = ("pe", "act", "dve", "pool", "sp")
N_DMA_SEMS = 24

D = 1024
T = 2048
NS = 16
TA = T + NS
NT = 16
NPG = 16
NPHYS = 2560
EPS = 1e-6


def _ap_region(ap):
    t = ap.tensor
    name = t.name
    shape = tuple(t.shape)
    dims = tuple(ap.ap)
    off = int(ap.offset)
    space = str(ap.space)
    if "SB" not in space.upper() and "PSUM" not in space.upper():
        ext = 0
        for st, cn in dims:
            ext += abs(st) * (cn - 1)
        return (name, 0, 1, off, off + ext + 1)
    R = 1
    for s in shape[1:]:
        R *= s
    if "PSUM" in space.upper() or name.startswith("ps"):
        return (name, 0, 128, 0, R)
    p0 = off // R
    f0 = off % R
    pst, pcn = dims[0]
    np_ = 1 if pst == 0 else pcn
    ext = 0
    for st, cn in dims[1:]:
        ext += abs(st) * (cn - 1)
    return (name, p0, p0 + np_, f0, f0 + ext + 1)


def _overlap(a, b):
    return a[0] == b[0] and a[1] < b[2] and b[1] < a[2] and a[3] < b[4] and b[3] < a[4]


def _covers(a, b):
    return a[0] == b[0] and a[1] <= b[1] and a[2] >= b[2] and a[3] <= b[3] and a[4] >= b[4]


class Prog:
    def __init__(self, nc, es):
        self.nc = nc
        self.es = es
        self.ops = {e: [] for e in ENGS}
        self.sem = {e: es.enter_context(nc.semaphore("s_" + e)) for e in ENGS}
        self.dsem = [es.enter_context(nc.semaphore("d%d" % i)) for i in range(N_DMA_SEMS)]
        self.dval = [0] * N_DMA_SEMS
        self.dnext = {"hw": 0, "sw": 0}
        self.cnt = {e: 0 for e in ENGS}
        self.seen = {e: {} for e in ENGS}
        self.recs = {}
        self.n_instr = 0

    def _semh(self, key):
        return self.dsem[key] if isinstance(key, int) else self.sem[key]

    def _need(self, eng, reads, writes):
        need = {}
        for ap in reads:
            r = _ap_region(ap)
            isps = r[0].startswith("ps")
            for rec in self.recs.get(r[0], ()):
                if (rec[3] or (isps and rec[1] != eng)) and _overlap(rec[0], r) and need.get(rec[1], 0) < rec[2]:
                    need[rec[1]] = rec[2]
        for ap in writes:
            r = _ap_region(ap)
            for rec in self.recs.get(r[0], ()):
                if _overlap(rec[0], r) and need.get(rec[1], 0) < rec[2]:
                    need[rec[1]] = rec[2]
        waits = []
        for k, v in need.items():
            if eng == "pe" and k == "pe":
                continue
            if self.seen[eng].get(k, 0) >= v:
                continue
            self.seen[eng][k] = v
            waits.append((k, v))
        return waits

    def _record(self, reads, writes, key, val):
        for ap in writes:
            r = _ap_region(ap)
            lst = self.recs.setdefault(r[0], [])
            lst[:] = [rec for rec in lst if not _covers(r, rec[0])]
            lst.append([r, key, val, True])
        for ap in reads:
            r = _ap_region(ap)
            lst = self.recs.setdefault(r[0], [])
            lst[:] = [rec for rec in lst if not (not rec[3] and rec[1] == key and rec[0] == r)]
            lst.append([r, key, val, False])

    limit = None
    trace = None
    _cap = None

    def capture(self):
        self._cap = []
        return self._cap

    def end_capture(self):
        self._cap = None

    def merge(self, streams):
        idx = [0] * len(streams)
        live = True
        while live:
            live = False
            for i, st in enumerate(streams):
                if idx[i] < len(st):
                    kind, eng, fn, r, w = st[idx[i]]
                    idx[i] += 1
                    live = True
                    (self.op if kind == "op" else self.dma)(eng, fn, r, w)

    def op(self, eng, fn, reads=(), writes=()):
        if self._cap is not None:
            self._cap.append(("op", eng, fn, reads, writes))
            return
        if self.limit is not None and self.n_instr >= self.limit:
            return
        if self.trace is not None:
            import traceback
            fr = traceback.extract_stack(limit=4)
            self.trace.append((self.n_instr, eng, [f"{f.lineno}" for f in fr[:-1]], [str(_ap_region(a)) for a in writes]))
        waits = self._need(eng, reads, writes)
        self.cnt[eng] += 1
        self.ops[eng].append((fn, waits, (eng, 1)))
        self._record(reads, writes, eng, self.cnt[eng])
        self.n_instr += 1

    def dma(self, eng, fn, reads=(), writes=()):
        if self._cap is not None:
            self._cap.append(("dma", eng, fn, reads, writes))
            return
        if self.limit is not None and self.n_instr >= self.limit:
            return
        half = N_DMA_SEMS // 2
        kind = "sw" if eng == "pool" else "hw"
        k = self.dnext[kind] + (half if kind == "sw" else 0)
        self.dnext[kind] = (self.dnext[kind] + 1) % half
        waits = self._need(eng, reads, writes)
        if self.dval[k] > 0 and self.seen[eng].get(k, 0) < self.dval[k]:
            self.seen[eng][k] = self.dval[k]
            waits.append((k, self.dval[k]))
        self.dval[k] += 16
        self.ops[eng].append((fn, waits, (k, 16)))
        self._record(reads, writes, k, self.dval[k])
        self.n_instr += 1

    def barrier(self):
        assert self._cap is None
        targets = [(e, self.cnt[e]) for e in ENGS if self.cnt[e] > 0]
        targets += [(k, self.dval[k]) for k in range(N_DMA_SEMS) if self.dval[k] > 0]
        for e in ENGS:
            waits = []
            for k, v in targets:
                if k == e or self.seen[e].get(k, 0) >= v:
                    continue
                self.seen[e][k] = v
                waits.append((k, v))
            if waits:
                self.ops[e].append((None, waits, None))
        self.recs = {}

    def emit(self):
        nc = self.nc
        prog = self
        with nc.Block() as block:
            def run(engname, eobj):
                for fn, waits, inc in prog.ops[engname]:
                    for k, v in waits:
                        eobj.wait_ge(prog._semh(k), v)
                    if fn is None:
                        continue
                    fn(eobj).then_inc(prog._semh(inc[0]), inc[1])

            @block.tensor
            def _(e):
                run("pe", e)

            @block.scalar
            def _(e):
                run("act", e)

            @block.vector
            def _(e):
                run("dve", e)

            @block.gpsimd
            def _(e):
                run("pool", e)

            @block.sync
            def _(e):
                run("sp", e)

    def mm(self, out, lhsT, rhs, start=True, stop=True):
        self.op("pe", lambda e: e.matmul(out, lhsT, rhs, start=start, stop=stop), reads=[lhsT, rhs], writes=[out])

    def tr(self, out, in_, ident):
        self.op("pe", lambda e: e.transpose(out, in_, ident), reads=[in_, ident], writes=[out])

    def act(self, out, in_, func, bias=None, scale=None):
        kw = {}
        reads = [in_]
        if bias is not None:
            kw["bias"] = bias
            if not isinstance(bias, (int, float)):
                reads.append(bias)
        if scale is not None:
            kw["scale"] = scale
            if not isinstance(scale, (int, float)):
                reads.append(scale)
        self.op("act", lambda e: e.activation(out, in_, func, **kw), reads=reads, writes=[out])

    def tt(self, out, in0, in1, op, eng="dve"):
        self.op(eng, lambda e: e.tensor_tensor(out, in0, in1, op), reads=[in0, in1], writes=[out])

    def ts(self, out, in0, s1, op0, s2=None, op1=None, eng="dve"):
        reads = [in0]
        if not isinstance(s1, (int, float)):
            reads.append(s1)
        if s2 is not None and not isinstance(s2, (int, float)):
            reads.append(s2)
        kw = {}
        if op1 is not None:
            kw["op1"] = op1
        self.op(eng, lambda e: e.tensor_scalar(out, in0, s1, s2, op0, **kw), reads=reads, writes=[out])

    def stt(self, out, in0, scalar, in1, op0, op1, eng="dve"):
        reads = [in0, in1]
        if not isinstance(scalar, (int, float)):
            reads.append(scalar)
        self.op(eng, lambda e: e.scalar_tensor_tensor(out, in0, scalar, in1, op0, op1), reads=reads, writes=[out])

    def copy(self, out, in_, eng="dve"):
        if eng == "act":
            self.op("act", lambda e: e.copy(out, in_), reads=[in_], writes=[out])
        else:
            self.op(eng, lambda e: e.tensor_copy(out, in_), reads=[in_], writes=[out])

    def memset(self, out, val, eng="dve"):
        self.op(eng, lambda e: e.memset(out, val), reads=[], writes=[out])

    def red(self, out, in_, op=None, eng="dve"):
        op = op or ALU.add
        self.op(eng, lambda e: e.tensor_reduce(out, in_, AX.X, op), reads=[in_], writes=[out])

    def recip(self, out, in_):
        self.op("dve", lambda e: e.reciprocal(out, in_), reads=[in_], writes=[out])

    def scan(self, out, d0, d1, init):
        reads = [d0, d1]
        if not isinstance(init, (int, float)):
            reads.append(init)
        self.op("dve", lambda e: e.tensor_tensor_scan(out, d0, d1, init, ALU.mult, ALU.add), reads=reads, writes=[out])

    def load(self, out, in_, eng="sp", slow=False):
        if slow:
            self.dma(eng, lambda e: e.dma_start(out=out, in_=in_, allow_slow_non_contiguous=True), reads=[in_], writes=[out])
        else:
            self.dma(eng, lambda e: e.dma_start(out=out, in_=in_), reads=[in_], writes=[out])

    store = load

    def gather(self, out, flat, idx):
        self.dma("pool", lambda e: e.indirect_dma_start(out=out, out_offset=None, in_=flat,
                                                        in_offset=bass.IndirectOffsetOnAxis(ap=idx, axis=0)),
                 reads=[idx, flat], writes=[out])


class Arena:
    def __init__(self, t, size):
        self.t = t
        self.size = size
        self.top = 0
        self.stack = []
        self.peak = 0

    def alloc(self, shape, parts=128):
        n = 1
        for s in shape:
            n *= s
        off = self.top
        self.top += n
        self.peak = max(self.peak, self.top)
        assert self.top <= self.size, ("arena overflow", self.t.name, self.top, self.size)
        ap = self.t[0:parts, off:off + n]
        if len(shape) == 2:
            ap = ap.rearrange("p (a b) -> p a b", b=shape[1])
        elif len(shape) == 3:
            ap = ap.rearrange("p (a b c) -> p a b c", b=shape[1], c=shape[2])
        elif len(shape) == 4:
            ap = ap.rearrange("p (a b c d) -> p a b c d", b=shape[1], c=shape[2], d=shape[3])
        return ap


INPUT_SPECS = [
    ("xp", (T, D), F32), ("xs", (NS, D), F32), ("cv", (NS + 1, D), F32), ("pt", (1, NS * NPG), I32),
    ("ck", (2, NPHYS * 128, 512), F32), ("cvv", (2, NPHYS * 128, 512), F32),
    ("sconv", (2, NS * 3, 1536), F32), ("sssm", (2, NS, 4, 128, 128), F32),
    ("scre", (2, NS, 2048), F32), ("scim", (2, NS, 2048), F32), ("sret", (2, NS, 4, 128, 128), F32),
    ("normg", (4, D), F32), ("wada", (4, 6, 128, 4096), F32), ("bada", (4, 3 * D), F32),
    ("wine", (2, 8, 128, 4096), F32), ("wab", (2, 128, 64), F32), ("woute", (2, 2, 128, 4096), F32),
    ("qng", (2, 1, 64), F32), ("kng", (2, 1, 64), F32), ("lamv", (2, 1, 256), F32), ("sublng", (2, 128, 1), F32),
    ("convw", (2, 4, 1536), F32), ("alog", (2, 1, 4), F32), ("dtb", (2, 1, 4), F32), ("gnb", (2, 1, 128), F32),
    ("wino", (2, 6, 128, 4096), F32), ("wouto", (2, 2, 128, 4096), F32),
    ("s5are", (2, 2048), F32), ("s5aim", (2, 2048), F32), ("s5bre", (2, 2048, 16), F32), ("s5bim", (2, 2048, 16), F32),
    ("s5cre", (2, 512, 64), F32), ("s5cim", (2, 512, 64), F32), ("s5d", (2, 512), F32), ("s5ldt", (2, 1, 32), F32),
    ("wglu", (2, 128, 2048), F32), ("gnd", (2, 1, 128), F32),
    ("c_ident", (128, 128), F32), ("c_ones", (128, 128), F32), ("c_onesb", (128, 128), BF16), ("c_trib", (128, 128), BF16),
    ("c_augq", (4, 4, T), BF16), ("c_augk", (4, 4, T), BF16), ("c_alibis", (128, NPG * 4), F32),
    ("c_mnegT", (128, 128), F32), ("c_mnegS", (128, 128), F32), ("c_LT", (128, 128), F32), ("c_LB", (128, 128), F32), ("c_cmask", (128, 8), F32), ("c_CM", (2, 128, 128), F32),
    ("c_dtc", (4, 128, 128), F32), ("c_kdc", (128, 4), F32), ("c_egc", (128, 4), F32), ("c_iota", (128, 1), F32),
    ("c_mask16", (128, 256), F32),
]
OUTPUT_SPECS = [
    ("y_p", (T, D)), ("y_s", (NS, D)), ("nk_p", (2, T, 512)), ("nv_p", (2, T, 512)), ("nk_s", (2, NS, 512)), ("nv_s", (2, NS, 512)),
    ("cv_p", (2, 3, 1536)), ("cv_s", (2, NS, 3, 1536)), ("dl_p", (2, 4, 128, 128)), ("dl_s", (2, NS, 4, 128, 128)),
    ("s5r_p", (2, 2048)), ("s5i_p", (2, 2048)), ("s5r_s", (2, NS, 2048)), ("s5i_s", (2, NS, 2048)),
    ("rt_p", (2, 4, 128, 128)), ("rt_s", (2, NS, 4, 128, 128)),
]
LOG_GAMMA = [math.log1p(-2.0 ** (-5.0 - h)) for h in range(4)]
SLOPES = [2.0 ** (-8.0 * (h + 1) / 4.0) for h in range(4)]


def make_consts():
    c = {}
    c["c_ident"] = np.eye(128, dtype=np.float32)
    c["c_ones"] = np.ones((128, 128), np.float32)
    c["c_onesb"] = np.ones((128, 128), ml_dtypes.bfloat16)
    i = np.arange(128)
    c["c_trib"] = (i[None, :] >= i[:, None]).astype(ml_dtypes.bfloat16)
    pos = np.arange(T)
    hi, lo = (pos // 128).astype(np.float64), (pos % 128).astype(np.float64)
    augq = np.zeros((4, 4, T), np.float64)
    augk = np.zeros((4, 4, T), np.float64)
    for h in range(4):
        s8 = 8.0 * SLOPES[h]
        augq[h, 0] = -s8 * 128.0 * hi
        augq[h, 1] = -s8 * lo
        augq[h, 2] = 1.0
        augq[h, 3] = 1.0
        augk[h, 0] = 1.0
        augk[h, 1] = 1.0
        augk[h, 2] = s8 * 128.0 * hi
        augk[h, 3] = s8 * lo
    c["c_augq"] = augq.astype(ml_dtypes.bfloat16)
    c["c_augk"] = augk.astype(ml_dtypes.bfloat16)
    al = np.zeros((128, NPG, 4), np.float64)
    for pg in range(NPG):
        for h in range(4):
            al[:, pg, h] = -SLOPES[h] * (T - (pg * 128 + i))
    c["c_alibis"] = al.reshape(128, NPG * 4).astype(np.float32)
    blk = i // 64
    same = blk[:, None] == blk[None, :]
    NEG = -1e30
    c["c_mnegT"] = np.where(same & (i[:, None] <= i[None, :]), 0.0, NEG).astype(np.float32)
    c["c_mnegS"] = np.where(same & (i[None, :] < i[:, None]), 0.0, NEG).astype(np.float32)
    c["c_LT"] = (same & (i[:, None] <= i[None, :])).astype(np.float32)
    c["c_LB"] = same.astype(np.float32)
    cmk = np.zeros((128, 8), np.float32)
    for l4 in range(4):
        for gp in range(2):
            cmk[l4 * 32 + gp * 16:l4 * 32 + gp * 16 + 16, l4 * 2 + gp] = 1.0
    c["c_cmask"] = cmk
    cm = np.zeros((2, 128, 128), np.float32)
    for cc in range(2):
        cm[cc, blk == cc, :] = 1.0
    c["c_CM"] = cm
    dtc = np.zeros((4, 128, 128), np.float64)
    kdc = np.zeros((128, 4), np.float64)
    egc = np.zeros((128, 4), np.float64)
    for h in range(4):
        lg = LOG_GAMMA[h]
        dd = (i[None, :] - i[:, None]).astype(np.float64)
        dtc[h] = np.where(dd >= 0, np.exp(lg * np.maximum(dd, 0)), 0.0) * (128.0 ** -0.5)
        kdc[:, h] = np.exp(lg * (127 - i)) * (128.0 ** -0.5)
        egc[:, h] = np.exp(lg * (i + 1))
    c["c_dtc"] = dtc.astype(np.float32)
    c["c_kdc"] = kdc.astype(np.float32)
    c["c_egc"] = egc.astype(np.float32)
    c["c_iota"] = i.astype(np.float32).reshape(128, 1)
    m16 = np.zeros((128, 16, 16), np.float32)
    for s in range(16):
        m16[:, s, s] = 1.0
    c["c_mask16"] = m16.reshape(128, 256)
    return c


def make_consts2(c):
    i = np.arange(128)
    ab = np.zeros((128, 16 + 3 * 19), np.float64)
    for d in range(16):
        ab[:, d] = SLOPES[0] * (i - 128.0 * d)
    for h in range(1, 4):
        for d in range(-3, 16):
            ab[:, 16 + (h - 1) * 19 + d + 3] = SLOPES[h] * (i - 128.0 * d)
    c["c_abias"] = ab.astype(np.float32)
    return c


INPUT_SPECS.append(("c_abias", (128, 73), F32))

NF = 9800
NB = 10240
BLOCKS = [(0, 512), (512, 512), (1024, 512), (1536, 512), (2048, 16)]


class Ctx:
    pass


def finish_o_rms(K, o, rows, gain_b, sz, dest):
    P, psn, ident = K.P, K.psn, K.ident
    sqs, ss, on = K.fo_sq, K.fo_ss, K.fo_on
    P.act(sqs[0:rows, :], o, AF.Square)
    P.red(ss[0:rows, 0:1], sqs[0:rows, :])
    P.ts(ss[0:rows, 0:1], ss[0:rows, 0:1], 1.0 / 128, ALU.mult, EPS, ALU.add)
    P.act(ss[0:rows, 0:1], ss[0:rows, 0:1], AF.Sqrt)
    P.recip(ss[0:rows, 0:1], ss[0:rows, 0:1])
    P.stt(on[0:rows, :], o, ss[0:rows, 0:1], gain_b[0:rows, :], ALU.mult, ALU.mult)
    pt = psn()
    P.tr(pt[:, 0:rows], on[0:rows, :], ident[0:rows, 0:rows])
    P.tt(dest, pt[:, 0:rows], sz, ALU.mult)


def finish_o_ln(K, o, rows, gain_b, sz, dest):
    P, psn, ident = K.P, K.psn, K.ident
    sqs, ss, on = K.fo_sq, K.fo_ss, K.fo_on
    P.red(ss[0:rows, 1:2], o)
    P.ts(ss[0:rows, 1:2], ss[0:rows, 1:2], 1.0 / 128, ALU.mult)
    P.ts(on[0:rows, :], o, ss[0:rows, 1:2], ALU.subtract)
    P.act(sqs[0:rows, :], on[0:rows, :], AF.Square)
    P.red(ss[0:rows, 0:1], sqs[0:rows, :])
    P.ts(ss[0:rows, 0:1], ss[0:rows, 0:1], 1.0 / 128, ALU.mult, EPS, ALU.add)
    P.act(ss[0:rows, 0:1], ss[0:rows, 0:1], AF.Sqrt)
    P.recip(ss[0:rows, 0:1], ss[0:rows, 0:1])
    P.stt(on[0:rows, :], on[0:rows, :], ss[0:rows, 0:1], gain_b[0:rows, :], ALU.mult, ALU.mult)
    pt = psn()
    P.tr(pt[:, 0:rows], on[0:rows, :], ident[0:rows, 0:rows])
    P.tt(dest, pt[:, 0:rows], sz, ALU.mult)


def sample_state_step(K, qT, kT, vT, eg, neg_eg, beta, EGB, s0_dram, out_dram, gain_b, szs, dest, delta, ln):
    P, AFa, psn, ident, Scope = K.P, K.AFa, K.psn, K.ident, K.Scope
    QM = AFa.alloc([16, 16])
    KM = AFa.alloc([16, 16])
    M16 = K.MASK16
    P.tt(QM[:, :, :], M16[:, :, :], qT.unsqueeze(1).to_broadcast([128, 16, 16]), ALU.mult)
    P.tt(KM[:, :, :], M16[:, :, :], kT.unsqueeze(1).to_broadcast([128, 16, 16]), ALU.mult)
    QTM = AFa.alloc([128])
    KTM = AFa.alloc([128])
    VTM = AFa.alloc([128])
    for src, dst in ((qT, QTM), (kT, KTM), (vT, VTM)):
        pt = psn()
        P.tr(pt[0:16, 0:128], src, ident[:])
        P.copy(dst[0:16, :], pt[0:16, 0:128], eng="act")
    S0 = [AFa.alloc([128]) for _ in range(2)]
    pk = psn()
    pq = psn()
    for s in range(NS):
        s0 = S0[s % 2]
        P.load(s0[:, :], s0_dram(s))
        if delta:
            P.mm(pk[0:16, 0:128], KM[:, s, :], s0[:, :], start=(s == 0), stop=(s == NS - 1))
        P.mm(pq[0:16, 0:128], QM[:, s, :], s0[:, :], start=(s == 0), stop=(s == NS - 1))
    u = AFa.alloc([128])
    if delta:
        P.stt(u[0:16, :], pk[0:16, 0:128], neg_eg, VTM[0:16, :], ALU.mult, ALU.add)
        P.ts(u[0:16, :], u[0:16, :], beta, ALU.mult)
    else:
        P.copy(u[0:16, :], VTM[0:16, :])
    tmp = AFa.alloc([128])
    qk = AFa.alloc([1])
    P.tt(tmp[0:16, :], QTM[0:16, :], KTM[0:16, :], ALU.mult)
    P.red(qk[0:16, :], tmp[0:16, :])
    o = AFa.alloc([128])
    P.ts(tmp[0:16, :], pq[0:16, 0:128], eg, ALU.mult)
    P.stt(o[0:16, :], u[0:16, :], qk[0:16, 0:1], tmp[0:16, :], ALU.mult, ALU.add)
    if ln:
        finish_o_ln(K, o[0:16, :], 16, gain_b, szs, dest)
    else:
        finish_o_rms(K, o[0:16, :], 16, gain_b, szs, dest)
    kms = [AFa.alloc([128]) for _ in range(2)]
    sn = [AFa.alloc([128]) for _ in range(2)]
    for s in range(NS):
        s0 = S0[s % 2]
        P.load(s0[:, :], s0_dram(s))
        km = kms[s % 2]
        P.ts(km[0:16, :], KTM[0:16, :], ident[0:16, s:s + 1], ALU.mult)
        ps = psn()
        P.mm(ps[:, 0:128], km[0:16, :], u[0:16, :])
        egs = EGB if isinstance(EGB, float) else EGB[:, s:s + 1]
        P.stt(sn[s % 2][:, :], s0[:, :], egs, ps[:, 0:128], ALU.mult, ALU.add)
        P.store(out_dram(s), sn[s % 2][:, :])


def gdn_heads(K, li):
    P, I, O = K.P, K.I, K.O
    XT, HT, OT, AFa, ABa, PS, psn, Scope = K.XT, K.HT, K.OT, K.AFa, K.ABa, K.PS, K.psn, K.Scope
    next_w, wv512, v3, ident, ones, onesb = K.next_w, K.wv512, K.v3, K.ident, K.ones, K.onesb
    e = li // 2
    with Scope():
        wab = next_w(("wab", e))
        wabv = wab[:, 0:64].rearrange("p (k c) -> p k c", c=8)
        AB = AFa.alloc([17, 8])
        P.memset(AB[:, :, :], 0.0)
        for n in range(17):
            rows = 128 if n < 16 else 16
            c0 = n * 128
            ps = psn()
            for k in range(8):
                P.mm(ps[0:rows, 0:8], HT[:, k, c0:c0 + rows], wabv[:, k, :], start=(k == 0), stop=(k == 7))
            P.copy(AB[0:rows, n, :], ps[0:rows, 0:8], eng=("act" if n % 2 else "dve"))
        ALG = AFa.alloc([4])
        DTB = AFa.alloc([4])
        P.load(ALG[:, :], I["alog"][e].partition_broadcast(128))
        P.load(DTB[:, :], I["dtb"][e].partition_broadcast(128))
        P.act(ALG[:, :], ALG[:, :], AF.Exp)
        P.ts(ALG[:, :], ALG[:, :], -1.0, ALU.mult)
        G = AFa.alloc([17, 4])
        BETA = AFa.alloc([17, 4])
        P.tt(G[:, :, :], AB[:, :, 0:4], DTB[:, :].unsqueeze(1).to_broadcast([128, 17, 4]), ALU.add)
        P.act(G[:, :, :], G[:, :, :], AF.Exp)
        P.act(G[:, :, :], G[:, :, :], AF.Ln, bias=ones[:, 0:1])
        P.tt(G[:, :, :], G[:, :, :], ALG[:, :].unsqueeze(1).to_broadcast([128, 17, 4]), ALU.mult)
        P.act(BETA[:, :, :], AB[:, :, 4:8], AF.Sigmoid)
        CN = {}
        for nm in ("c_LT", "c_LB", "c_mnegT", "c_mnegS"):
            CN[nm] = AFa.alloc([128])
            P.load(CN[nm][:, :], I[nm])
        CM = AFa.alloc([2, 128])
        P.load(CM[:, :, :], I["c_CM"].rearrange("c j p -> j c p"))
        Gf = G[:, 0:16, :].rearrange("p n h -> p (n h)")
        GAM = AFa.alloc([16, 4])
        GL = AFa.alloc([16, 4])
        ps = psn()
        P.mm(ps[:, 0:64], CN["c_LT"][:, :], Gf)
        P.copy(GAM[:, :, :].rearrange("p n h -> p (n h)"), ps[:, 0:64])
        ps = psn()
        P.mm(ps[:, 0:64], CN["c_LB"][:, :], Gf)
        P.copy(GL[:, :, :].rearrange("p n h -> p (n h)"), ps[:, 0:64], eng="act")
        EGLB = AFa.alloc([2, 16, 4])
        for c in range(2):
            ps = psn()
            P.mm(ps[:, 0:64], CM[:, c, :], Gf)
            P.act(EGLB[:, c, :, :].rearrange("p n h -> p (n h)"), ps[:, 0:64], AF.Exp)
        EGAM = AFa.alloc([16, 4])
        BEG = AFa.alloc([16, 4])
        NBETA = AFa.alloc([16, 4])
        NGAM = AFa.alloc([16, 4])
        ED = AFa.alloc([16, 4])
        P.act(EGAM[:, :, :], GAM[:, :, :], AF.Exp)
        P.tt(BEG[:, :, :], BETA[:, 0:16, :], EGAM[:, :, :], ALU.mult)
        P.ts(NBETA[:, :, :], BETA[:, 0:16, :], -1.0, ALU.mult)
        P.ts(NGAM[:, :, :], GAM[:, :, :], -1.0, ALU.mult)
        P.tt(ED[:, :, :], GL[:, :, :], GAM[:, :, :], ALU.subtract)
        P.act(ED[:, :, :], ED[:, :, :], AF.Exp)
        EGS = AFa.alloc([4])
        NEGS = AFa.alloc([4])
        P.act(EGS[0:16, :], G[0:16, 16, :], AF.Exp)
        P.ts(NEGS[0:16, :], EGS[0:16, :], -1.0, ALU.mult)
        CWT = AFa.alloc([12, 4])
        for tap in range(4):
            P.load(CWT[:, :, tap], I["convw"][e][tap].rearrange("(c p) -> p c", p=128), slow=True)
        CBT = AFa.alloc([12, 48])
        with Scope():
            CB = AFa.alloc([1536])
            P.load(CB[0:48, :], I["sconv"][e])
            for half in range(3):
                ps = psn()
                for q in range(4):
                    cidx = half * 4 + q
                    P.tr(ps[:, q * 64:q * 64 + 48], CB[0:48, cidx * 128:(cidx + 1) * 128], ident[0:48, 0:48])
                P.copy(CBT[:, half * 4:(half + 1) * 4, :], v3(ps[:, 0:256], 64)[:, :, 0:48])
        P.load(O["cv_s"][e, :, 0:2, :], I["sconv"][e].rearrange("(s j) c -> s j c", j=3)[:, 1:3, :])
        GNB = AFa.alloc([128])
        P.load(GNB[:, :], I["gnb"][e].partition_broadcast(128))
        K.MASK16 = AFa.alloc([16, 16])
        P.load(K.MASK16[:, :, :], v3(I["c_mask16"], 16))
        BW = 256
        NTB = T // BW

        def alloc_bufs():
            B = {}
            B["S"] = AFa.alloc([128])
            B["PRE"] = AFa.alloc([BW + 3])
            B["CAR"] = AFa.alloc([3, 3])
            B["QKV"] = [AFa.alloc([BW]) for _ in range(3)]
            B["SZ"] = AFa.alloc([BW])
            B["acc"] = AFa.alloc([BW])
            B["rs"] = B["acc"]
            B["sqb"] = ABa.alloc([BW])
            B["fo"] = (AFa.alloc([128]), AFa.alloc([2]), AFa.alloc([128]))
            return B

        def alloc_mats():
            M = {}
            M["X0"] = AFa.alloc([256])
            for nm in ("KD", "DG", "T1", "T2", "DT", "DS", "AQ"):
                M[nm] = AFa.alloc([128])
            M["PP2"] = AFa.alloc([256])
            M["Pm"], M["PTm"] = M["PP2"][:, 0:128], M["PP2"][:, 128:256]
            M["WT"], M["UF"], M["O1"], M["OO"] = M["DG"], M["T1"], M["T2"], M["DS"]
            return M

        def l2n(B, w):
            QKV, rs, sqb_ = B["QKV"], B["rs"], B["sqb"]
            for j in (0, 1):
                P.act(sqb_[:, 0:w], QKV[j][:, 0:w], AF.Square)
                ps = psn()
                P.mm(ps[:, 0:w], onesb[:], sqb_[:, 0:w])
                P.ts(rs[:, 0:w], ps[:, 0:w], EPS, ALU.add)
                P.act(rs[:, 0:w], rs[:, 0:w], AF.Sqrt)
                P.recip(rs[:, 0:w], rs[:, 0:w])
                if j == 0:
                    P.stt(QKV[j][:, 0:w], QKV[j][:, 0:w], 128.0 ** -0.5, rs[:, 0:w], ALU.mult, ALU.mult)
                else:
                    P.tt(QKV[j][:, 0:w], QKV[j][:, 0:w], rs[:, 0:w], ALU.mult)

        def head_prompt(h, wv, B, M):
            S, PRE, CAR, QKV, SZ, acc = B["S"], B["PRE"], B["CAR"], B["QKV"], B["SZ"], B["acc"]
            K.fo_sq, K.fo_ss, K.fo_on = B["fo"]
            X0, KD, DG, T1, T2, DT, DS, AQ, Pm, PTm, WT, UF, O1, OO = [M[n] for n in
                ("X0", "KD", "DG", "T1", "T2", "DT", "DS", "AQ", "Pm", "PTm", "WT", "UF", "O1", "OO")]
            P.memset(S[:, :], 0.0)
            P.memset(CAR[:, :, :], 0.0)
            for tb in range(NTB):
                c0 = tb * BW
                for j in range(3):
                    ps = psn()
                    for k in range(8):
                        P.mm(ps[:, 0:BW], wv[:, k, j * 128:(j + 1) * 128], HT[:, k, c0:c0 + BW], start=(k == 0), stop=(k == 7))
                    P.copy(PRE[:, 0:3], CAR[:, j, :])
                    P.copy(PRE[:, 3:BW + 3], ps[:, 0:BW], eng="act")
                    cidx = j * 4 + h
                    P.ts(acc[:, :], PRE[:, 0:BW], CWT[:, cidx, 0:1], ALU.mult)
                    for tap in range(1, 4):
                        P.stt(acc[:, :], PRE[:, tap:tap + BW], CWT[:, cidx, tap:tap + 1], acc[:, :], ALU.mult, ALU.add)
                    P.act(QKV[j][:, :], acc[:, :], AF.Silu)
                    P.copy(CAR[:, j, :], PRE[:, BW:BW + 3])
                    if tb == NTB - 1:
                        P.store(O["cv_p"][e][:, j * 512 + h * 128:j * 512 + (h + 1) * 128].rearrange("t c -> c t"),
                                CAR[:, j, :], slow=True)
                l2n(B, BW)
                ps = psn()
                for k in range(8):
                    P.mm(ps[:, 0:BW], wv[:, k, 384:512], HT[:, k, c0:c0 + BW], start=(k == 0), stop=(k == 7))
                P.act(SZ[:, :], ps[:, 0:BW], AF.Silu)
                for tl in range(BW // 128):
                    n = tb * (BW // 128) + tl
                    cc = slice(tl * 128, (tl + 1) * 128)
                    qT, kT, vT = QKV[0][:, cc], QKV[1][:, cc], QKV[2][:, cc]
                    gam, ngam = GAM[:, n, h:h + 1], NGAM[:, n, h:h + 1]
                    pa = psn()
                    P.tr(pa[:, 0:128], kT, ident[:])
                    P.tr(pa[:, 128:256], vT, ident[:])
                    P.ts(X0[:, 0:128], pa[:, 128:256], BETA[:, n, h:h + 1], ALU.mult)
                    P.ts(X0[:, 128:256], pa[:, 0:128], BEG[:, n, h:h + 1], ALU.mult)
                    P.ts(KD[:, :], pa[:, 0:128], ED[:, n, h:h + 1], ALU.mult)
                    P.ts(DG[:, :], ident[:], gam, ALU.mult)
                    pr = psn()
                    P.mm(pr[:, 0:128], ones[:], DG[:, :])
                    P.tt(T1[:, :], pr[:, 0:128], CN["c_mnegT"][:, :], ALU.add)
                    P.act(DT[:, :], T1[:, :], AF.Exp, bias=ngam)
                    P.stt(T2[:, :], pr[:, 0:128], -1.0, CN["c_mnegS"][:, :], ALU.mult, ALU.add)
                    P.act(DS[:, :], T2[:, :], AF.Exp, bias=gam)
                    pq = psn()
                    P.mm(pq[:, 0:128], kT, qT)
                    P.tt(AQ[:, :], pq[:, 0:128], DT[:, :], ALU.mult)
                    pk = psn()
                    P.mm(pk[:, 0:128], kT, kT)
                    P.stt(Pm[:, :], pk[:, 0:128], NBETA[:, n, h:h + 1], DS[:, :], ALU.mult, ALU.mult)
                    pt = psn()
                    P.tr(pt[:, 0:128], Pm[:, :], ident[:])
                    P.copy(PTm[:, :], pt[:, 0:128], eng="act")
                    for lvl in range(6):
                        px = psn()
                        P.mm(px[:, 0:256], PTm[:, :], X0[:, :])
                        P.tt(X0[:, :], px[:, 0:256], X0[:, :], ALU.add)
                        if lvl < 5:
                            pp = psn()
                            P.mm(pp[:, 0:128], PTm[:, :], Pm[:, :])
                            P.mm(pp[:, 128:256], Pm[:, :], PTm[:, :])
                            P.copy(M["PP2"][:, :], pp[:, 0:256], eng="act")
                    pw = psn()
                    P.tr(pw[:, 0:128], X0[:, 128:256], ident[:])
                    P.copy(WT[:, :], pw[:, 0:128], eng="act")
                    for c in range(2):
                        r = slice(64 * c, 64 * c + 64)
                        p1 = psn()
                        P.mm(p1[:, 0:128], WT[:, :], S[:, :])
                        P.tt(UF[r, :], X0[r, 0:128], p1[r, 0:128], ALU.subtract)
                        p2 = psn()
                        P.mm(p2[:, 0:128], qT, S[:, :])
                        P.ts(O1[r, :], p2[r, 0:128], EGAM[r, n, h:h + 1], ALU.mult)
                        p3 = psn()
                        P.mm(p3[:, 0:128], KD[r, :], UF[r, :])
                        P.stt(S[:, :], S[:, :], EGLB[:, c, n, h:h + 1], p3[:, 0:128], ALU.mult, ALU.add)
                    p4 = psn()
                    P.mm(p4[:, 0:128], AQ[:, :], UF[:, :])
                    P.tt(OO[:, :], p4[:, 0:128], O1[:, :], ALU.add)
                    finish_o_rms(K, OO[:, :], 128, GNB, SZ[:, cc], OT[:, 4 + h, n * 128:(n + 1) * 128])
            P.store(O["dl_p"][e, h], S[:, :])

        def head_sample(h, wv, B):
            QKV, SZ, acc = B["QKV"], B["SZ"], B["acc"]
            K.fo_sq, K.fo_ss, K.fo_on = B["fo"]
            xs_ = AFa.alloc([16])
            xtm = AFa.alloc([128])
            for j in range(3):
                ps = psn()
                for k in range(8):
                    P.mm(ps[:, 0:16], wv[:, k, j * 128:(j + 1) * 128], HT[:, k, T:TA], start=(k == 0), stop=(k == 7))
                P.copy(xs_[:, :], ps[:, 0:16], eng="act")
                cidx = j * 4 + h
                cb = CBT[:, cidx, :].rearrange("p (s j) -> p s j", j=3)
                P.ts(acc[:, 0:16], cb[:, :, 0], CWT[:, cidx, 0:1], ALU.mult)
                P.stt(acc[:, 0:16], cb[:, :, 1], CWT[:, cidx, 1:2], acc[:, 0:16], ALU.mult, ALU.add)
                P.stt(acc[:, 0:16], cb[:, :, 2], CWT[:, cidx, 2:3], acc[:, 0:16], ALU.mult, ALU.add)
                P.stt(acc[:, 0:16], xs_[:, :], CWT[:, cidx, 3:4], acc[:, 0:16], ALU.mult, ALU.add)
                P.act(QKV[j][:, 0:16], acc[:, 0:16], AF.Silu)
                pt = psn()
                P.tr(pt[0:16, 0:128], xs_[:, :], ident[:])
                P.copy(xtm[0:16, :], pt[0:16, 0:128])
                P.store(O["cv_s"][e, :, 2, j * 512 + h * 128:j * 512 + (h + 1) * 128], xtm[0:16, :])
            l2n(B, 16)
            ps = psn()
            for k in range(8):
                P.mm(ps[:, 0:16], wv[:, k, 384:512], HT[:, k, T:TA], start=(k == 0), stop=(k == 7))
            P.act(SZ[:, 0:16], ps[:, 0:16], AF.Silu)
            DGs = AFa.alloc([16])
            P.ts(DGs[0:16, :], ident[0:16, 0:16], EGS[0:16, h:h + 1], ALU.mult)
            ps = psn()
            P.mm(ps[:, 0:16], ones[0:16, :], DGs[0:16, :])
            EGB = AFa.alloc([16])
            P.copy(EGB[:, :], ps[:, 0:16])
            sample_state_step(K, QKV[0][:, 0:16], QKV[1][:, 0:16], QKV[2][:, 0:16],
                              EGS[0:16, h:h + 1], NEGS[0:16, h:h + 1], BETA[0:16, 16, h:h + 1], EGB,
                              (lambda s, h=h: I["sssm"][e, s, h]), (lambda s, h=h: O["dl_s"][e, s, h]),
                              GNB, SZ[:, 0:16], OT[:, 4 + h, T:TA], True, False)

        NPAR = K.cfg.get("gdn_par", 2)
        for hp in range(0, 4, NPAR):
            wvs = [wv512(next_w(("eb", e, hp + i), prefetch=(i == 0))) for i in range(NPAR)]
            with Scope():
                Bs = [alloc_bufs() for _ in range(NPAR)]
                with Scope():
                    Ms = [alloc_mats() for _ in range(NPAR)]
                    streams = []
                    for i in range(NPAR):
                        K.p0, K.pn = (8 // NPAR) * i, 8 // NPAR
                        cap = P.capture()
                        head_prompt(hp + i, wvs[i], Bs[i], Ms[i])
                        P.end_capture()
                        streams.append(cap)
                    K.p0, K.pn = 0, 8
                    P.merge(streams)
                for i in range(NPAR):
                    with Scope():
                        head_sample(hp + i, wvs[i], Bs[i])


def rms_feature_gate(K, o, w, gcol, psz_fn, dest, tagf):
    P, AFa, ABa, psn, onesb = K.P, K.AFa, K.ABa, K.psn, K.onesb
    sq = K.ep_sq
    P.act(sq[:, 0:w], o[:, 0:w], AF.Square)
    ps = psn()
    P.mm(ps[:, 0:w], onesb[:], sq[:, 0:w])
    rs = K.ep_rs
    P.ts(rs[:, 0:w], ps[:, 0:w], 1.0 / 128, ALU.mult, EPS, ALU.add)
    P.act(rs[:, 0:w], rs[:, 0:w], AF.Sqrt)
    P.recip(rs[:, 0:w], rs[:, 0:w])
    P.tt(o[:, 0:w], o[:, 0:w], rs[:, 0:w], ALU.mult)
    sz = psz_fn()
    P.stt(dest, o[:, 0:w], gcol, sz, ALU.mult, ALU.mult)


def even_layer(K, li):
    P, I, O = K.P, K.I, K.O
    XT, HT, OT, AFa, ABa, PS, psn, Scope = K.XT, K.HT, K.OT, K.AFa, K.ABa, K.PS, K.psn, K.Scope
    next_w, wv512, v3, ident, ones, onesb, trib = K.next_w, K.wv512, K.v3, K.ident, K.ones, K.onesb, K.trib
    cfg = K.cfg
    e = li // 2
    lam_init = 0.8 - 0.6 * math.exp(-0.3 * li)

    with Scope():
        GQK = AFa.alloc([256])
        P.load(GQK[:, 0:64], I["qng"][e].partition_broadcast(128))
        P.load(GQK[:, 64:128], I["qng"][e].partition_broadcast(128))
        P.load(GQK[:, 128:192], I["kng"][e].partition_broadcast(128))
        P.load(GQK[:, 192:256], I["kng"][e].partition_broadcast(128))
        LV = AFa.alloc([256])
        P.load(LV[:, :], I["lamv"][e].partition_broadcast(128))
        lp = AFa.alloc([128])
        lv4 = LV[:, :].rearrange("p (a b d) -> p a b d", a=2, b=2)
        P.tt(v3(lp[:, :], 64), lv4[:, :, 0, :], lv4[:, :, 1, :], ALU.mult)
        l2 = AFa.alloc([2])
        P.red(l2[:, :], v3(lp[:, :], 64))
        P.act(l2[:, :], l2[:, :], AF.Exp)
        NLAM = AFa.alloc([1])
        P.tt(NLAM[:, :], l2[:, 1:2], l2[:, 0:1], ALU.subtract)
        P.ts(NLAM[:, :], NLAM[:, :], -lam_init, ALU.add)
        SUBG = AFa.alloc([1])
        P.load(SUBG[:, :], I["sublng"][e])
        P.ts(SUBG[:, :], SUBG[:, :], 1.0 - lam_init, ALU.mult)
        ABIAS = AFa.alloc([73])
        P.load(ABIAS[:, :], I["c_abias"])
        QS = AFa.alloc([4, 128])
        KS = AFa.alloc([4, 128])
        VS = AFa.alloc([4, 128])
        ZS = AFa.alloc([4, 16])
        VST = AFa.alloc([4, 16])
        K.ep_sq = ABa.alloc([512])
        K.ep_rs = AFa.alloc([512])

        for h in range(4 if cfg.get("attn", True) else 0):
            wv = wv512(next_w(("ea", e, h)))
            with Scope():
                QT_ = ABa.alloc([T])
                KT_ = ABa.alloc([T])
                VTM = ABa.alloc([NT, 128])
                NI = 4
                nsq = [AFa.alloc([256]) for _ in range(NI)]
                nss = [AFa.alloc([4]) for _ in range(NI)]
                nqk = [AFa.alloc([256]) for _ in range(NI)]
                nvf = [AFa.alloc([128]) for _ in range(NI)]

                def tile_ops(n, bi):
                    rows = 128 if n < 16 else 16
                    c0 = n * 128
                    ps = psn()
                    for k in range(8):
                        P.mm(ps[0:rows, 0:384], HT[:, k, c0:c0 + rows], wv[:, k, 0:384], start=(k == 0), stop=(k == 7))
                    sq, ss, qkn, vf = nsq[bi], nss[bi], nqk[bi], nvf[bi]
                    P.act(sq[0:rows, :], ps[0:rows, 0:256], AF.Square)
                    P.red(ss[0:rows, :], v3(sq[0:rows, :], 64))
                    P.ts(ss[0:rows, :], ss[0:rows, :], 1.0 / 64, ALU.mult, EPS, ALU.add)
                    P.act(ss[0:rows, :], ss[0:rows, :], AF.Sqrt)
                    P.recip(ss[0:rows, :], ss[0:rows, :])
                    P.tt(v3(qkn[0:rows, :], 64), v3(ps[0:rows, 0:256], 64),
                         ss[0:rows, :].unsqueeze(2).to_broadcast([rows, 4, 64]), ALU.mult)
                    P.tt(qkn[0:rows, :], qkn[0:rows, :], GQK[0:rows, :], ALU.mult)
                    if n < 16:
                        P.store(O["nk_p"][e, c0:c0 + 128, h * 128:(h + 1) * 128], qkn[:, 128:256])
                        P.copy(vf[:, :], ps[:, 256:384], eng="act")
                        P.store(O["nv_p"][e, c0:c0 + 128, h * 128:(h + 1) * 128], vf[:, :])
                        P.copy(VTM[:, n, :], vf[:, :])
                        pst = psn()
                        P.tr(pst[:, 0:128], qkn[:, 0:128], ident[:])
                        P.tr(pst[:, 128:256], qkn[:, 128:256], ident[:])
                        P.copy(QT_[:, c0:c0 + 128], pst[:, 0:128], eng="act")
                        P.copy(KT_[:, c0:c0 + 128], pst[:, 128:256], eng="act")
                    else:
                        P.copy(QS[0:16, h, :], qkn[0:16, 0:128])
                        P.copy(KS[0:16, h, :], qkn[0:16, 128:256])
                        P.copy(VS[0:16, h, :], ps[0:16, 256:384], eng="act")
                        pst = psn()
                        P.tr(pst[:, 0:16], VS[0:16, h, :], ident[0:16, 0:16])
                        P.copy(VST[:, h, :], pst[:, 0:16])

                for n0 in range(0, 17, NI):
                    streams = []
                    for bi, n in enumerate(range(n0, min(n0 + NI, 17))):
                        K.p0, K.pn = 2 * bi, 2
                        cap = P.capture()
                        tile_ops(n, bi)
                        P.end_capture()
                        streams.append(cap)
                    K.p0, K.pn = 0, 8
                    P.merge(streams)
                ps = psn()
                for k in range(8):
                    P.mm(ps[:, 0:16], wv[:, k, 384:512], HT[:, k, T:TA], start=(k == 0), stop=(k == 7))
                P.act(ZS[:, h, :], ps[:, 0:16], AF.Silu)
                if cfg.get("attn_stage", 3) < 2:
                    continue
                K.p0, K.pn = 0, 4
                O1, O2, D1, D2 = PS[4], PS[5], PS[6], PS[7]
                Eb = [[ABa.alloc([512]) for _ in range(2)] for _ in range(2)]
                of = AFa.alloc([512])
                t1 = AFa.alloc([512])
                t2 = AFa.alloc([512])
                szb = AFa.alloc([512])
                iters = [(qb, kb) for qb in range(4) for kb in range(4 * qb + 4)]
                SB = [(PS[0], PS[1]), (PS[2], PS[3])]

                def emit_S(idx):
                    qb, kb = iters[idx]
                    q0 = qb * 512
                    cs = 128 * max(kb - 4 * qb, 0)
                    s1, s2 = SB[idx % 2]
                    P.mm(s1[:, cs:512], KT_[0:64, kb * 128:(kb + 1) * 128], QT_[0:64, q0 + cs:q0 + 512])
                    P.mm(s2[:, cs:512], KT_[64:128, kb * 128:(kb + 1) * 128], QT_[64:128, q0 + cs:q0 + 512])

                emit_S(0)
                for idx, (qb, kb) in enumerate(iters):
                    q0 = qb * 512
                    nkb = 4 * qb + 4
                    j = kb - 4 * qb
                    cs = 128 * max(j, 0)
                    s1, s2 = SB[idx % 2]
                    if idx + 1 < len(iters):
                        emit_S(idx + 1)
                    e1, e2 = Eb[idx % 2]
                    if h == 0:
                        for sbk in range(cs // 128, 4):
                            d = (4 * qb + sbk) - kb
                            cc = slice(sbk * 128, (sbk + 1) * 128)
                            P.act(e1[:, cc], s1[:, cc], AF.Exp, bias=ABIAS[:, d:d + 1], scale=0.125)
                            P.act(e2[:, cc], s2[:, cc], AF.Exp, bias=ABIAS[:, d:d + 1], scale=0.125)
                    else:
                        d = 4 * qb - kb + 3
                        bcol = 16 + (h - 1) * 19 + d
                        P.act(e1[:, cs:512], s1[:, cs:512], AF.Exp, bias=ABIAS[:, bcol:bcol + 1], scale=0.125)
                        P.act(e2[:, cs:512], s2[:, cs:512], AF.Exp, bias=ABIAS[:, bcol:bcol + 1], scale=0.125)
                    if j >= 0:
                        P.tt(e1[:, cs:cs + 128], e1[:, cs:cs + 128], trib[:], ALU.mult)
                        P.tt(e2[:, cs:cs + 128], e2[:, cs:cs + 128], trib[:], ALU.mult)
                    st, sp = (kb == 0), (kb == nkb - 1)
                    P.mm(O1[:, cs:512], VTM[:, kb, :], e1[:, cs:512], start=st, stop=sp)
                    P.mm(D1[:, cs:512], onesb[:], e1[:, cs:512], start=st, stop=sp)
                    P.mm(O2[:, cs:512], VTM[:, kb, :], e2[:, cs:512], start=st, stop=sp)
                    P.mm(D2[:, cs:512], onesb[:], e2[:, cs:512], start=st, stop=sp)
                    if kb != nkb - 1:
                        continue
                    if cfg.get("attn_stage", 3) < 3:
                        continue
                    P.recip(t1[:, :], D1[:, :])
                    P.tt(t1[:, :], O1[:, :], t1[:, :], ALU.mult)
                    P.recip(t2[:, :], D2[:, :])
                    P.tt(t2[:, :], O2[:, :], t2[:, :], ALU.mult)
                    P.stt(of[:, :], t2[:, :], NLAM[:, 0:1], t1[:, :], ALU.mult, ALU.add)
                    K.forced = [s1, s2]

                    def psz(q0=q0, wv=wv):
                        pz = psn()
                        for k in range(8):
                            P.mm(pz[:, :], wv[:, k, 384:512], HT[:, k, q0:q0 + 512], start=(k == 0), stop=(k == 7))
                        P.act(szb[:, :], pz[:, :], AF.Silu)
                        return szb[:, :]
                    rms_feature_gate(K, of, 512, SUBG[:, 0:1], psz, OT[:, h, q0:q0 + 512], "a")
                    assert not K.forced
                K.p0, K.pn = 0, 8

        if cfg.get("sample_attn", True) and cfg.get("attn", True):
            with Scope():
                PTB = AFa.alloc([256])
                ptb_i = PTB[:, :].bitcast(I32)
                P.load(ptb_i, I["pt"].partition_broadcast(128))
                PTF = AFa.alloc([256])
                P.copy(PTF[:, :], ptb_i)
                IO = AFa.alloc([1])
                P.load(IO[:, :], I["c_iota"])
                P.ts(PTF[:, :], PTF[:, :], 128.0, ALU.mult, IO[:, 0:1], ALU.add)
                if e > 0:
                    P.ts(PTF[:, :], PTF[:, :], float(e * NPHYS * 128), ALU.add)
                IDX = AFa.alloc([256])
                idx_i = IDX[:, :].bitcast(I32)
                P.copy(idx_i, PTF[:, :])
                ALB = AFa.alloc([NPG, 4])
                P.load(ALB[:, :, :], v3(I["c_alibis"], 4))
                SEL = AFa.alloc([128])
                sp_ = AFa.alloc([512])
                P.tt(sp_[0:16, :], QS[0:16, :, :].rearrange("p h d -> p (h d)"), KS[0:16, :, :].rearrange("p h d -> p (h d)"), ALU.mult)
                ES = AFa.alloc([8])
                P.red(ES[0:16, :], v3(sp_[0:16, :], 64))
                P.act(ES[0:16, :], ES[0:16, :], AF.Exp, scale=0.125)
                QB = AFa.alloc([512])
                NPB = 8
                KP = [ABa.alloc([512]) for _ in range(NPB)]
                VP = [ABa.alloc([512]) for _ in range(NPB)]
                PPb = ABa.alloc([NPG, 4])
                prod = AFa.alloc([512])
                SC = AFa.alloc([NPG, 8])
                EE = AFa.alloc([NPG, 8])
                PP = AFa.alloc([NPG, 4])
                rsum = AFa.alloc([8])
                rinv = AFa.alloc([8])
                esb = AFa.alloc([8])
                psl = AFa.alloc([4])
                tv = AFa.alloc([4])
                OAS = AFa.alloc([4, 16])
                psO = PS[7]
                K.p0, K.pn = 0, 7
                ck = I["ck"].rearrange("e r c -> (e r) c")
                cvv = I["cvv"].rearrange("e r c -> (e r) c")
                for s in range(NS):
                    P.ts(SEL[0:16, :], ones[0:16, :], ident[0:16, s:s + 1], ALU.mult)
                    pq = psn()
                    P.mm(pq[:, :], SEL[0:16, :], QS[0:16, :, :].rearrange("p h d -> p (h d)"))
                    P.copy(QB[:, :], pq[:, :], eng="act")
                    for pg in range(NPG):
                        kp = KP[(s * NPG + pg) % NPB]
                        P.gather(kp[:, :], ck, idx_i[:, s * NPG + pg:s * NPG + pg + 1])
                        P.tt(prod[:, :], kp[:, :], QB[:, :], ALU.mult)
                        P.red(SC[:, pg, :], v3(prod[:, :], 64))
                    sc4 = SC[:, :, :].rearrange("p g (h t) -> p g h t", t=2)
                    P.stt(sc4, sc4, 0.125, ALB[:, :, :].unsqueeze(3).to_broadcast([128, NPG, 4, 2]), ALU.mult, ALU.add)
                    P.act(EE[:, :, :], SC[:, :, :], AF.Exp)
                    P.red(rsum[:, :], EE[:, :, :].rearrange("p g e -> p e g"))
                    pt_ = psn()
                    P.mm(pt_[:, 0:8], ones[:, :], rsum[:, :], start=True, stop=False)
                    P.mm(pt_[:, 0:8], SEL[0:16, :], ES[0:16, :], start=False, stop=True)
                    P.mm(pt_[:, 8:16], SEL[0:16, :], ES[0:16, :])
                    P.recip(rinv[:, :], pt_[:, 0:8])
                    P.tt(esb[:, :], pt_[:, 8:16], rinv[:, :], ALU.mult)
                    es2 = esb[:, :].rearrange("p (h t) -> p h t", t=2)
                    P.stt(psl[:, :], es2[:, :, 1], K.NLAM[:, 0:1] if False else NLAM[:, 0:1], es2[:, :, 0], ALU.mult, ALU.add)
                    P.tt(EE[:, :, :], EE[:, :, :], rinv[:, :].unsqueeze(1).to_broadcast([128, NPG, 8]), ALU.mult)
                    ee4 = EE[:, :, :].rearrange("p g (h t) -> p g h t", t=2)
                    P.stt(PP[:, :, :], ee4[:, :, :, 1], NLAM[:, 0:1], ee4[:, :, :, 0], ALU.mult, ALU.add)
                    P.copy(PPb[:, :, :], PP[:, :, :], eng="act")
                    for pg in range(NPG):
                        vp = VP[(s * NPG + pg) % NPB]
                        P.gather(vp[:, :], cvv, idx_i[:, s * NPG + pg:s * NPG + pg + 1])
                        for h in range(4):
                            P.mm(psO[:, h * 16 + s:h * 16 + s + 1], vp[:, h * 128:(h + 1) * 128], PPb[:, pg, h:h + 1],
                                 start=(pg == 0), stop=(pg == NPG - 1))
                    P.tt(tv[:, :], VST[:, :, s], psl[:, :], ALU.mult)
                    P.tt(OAS[:, :, s], v3(psO[:, 0:64], 16)[:, :, s], tv[:, :], ALU.add)
                K.p0, K.pn = 0, 8
                P.store(O["nk_s"][e].rearrange("s (h d) -> s h d", d=128), KS[0:16, :, :])
                P.store(O["nv_s"][e].rearrange("s (h d) -> s h d", d=128), VS[0:16, :, :])
                for h in range(4):
                    rms_feature_gate(K, OAS[:, h, :], 16, SUBG[:, 0:1], (lambda h=h: ZS[:, h, :]), OT[:, h, T:TA], "as")

    if not (cfg.get("sample_attn", True) and cfg.get("attn", True)):
        for h_ in range(4):
            P.memset(OT[:, h_, T:TA], 0.0)
    if cfg.get("gdn", True):
        gdn_heads(K, li)


def ret_heads(K, li):
    P, I, O = K.P, K.I, K.O
    XT, HT, OT, AFa, ABa, PS, psn, Scope = K.XT, K.HT, K.OT, K.AFa, K.ABa, K.PS, K.psn, K.Scope
    next_w, wv512, v3, ident, ones, onesb = K.next_w, K.wv512, K.v3, K.ident, K.ones, K.onesb
    o_ = li // 2
    with Scope():
        DTC = AFa.alloc([4, 128])
        P.load(DTC[:, :, :], I["c_dtc"].rearrange("h j i -> j h i"))
        KDC = AFa.alloc([4])
        EGC = AFa.alloc([4])
        P.load(KDC[:, :], I["c_kdc"])
        P.load(EGC[:, :], I["c_egc"])
        GND = AFa.alloc([128])
        P.load(GND[:, :], I["gnd"][o_].partition_broadcast(128))
        K.MASK16 = AFa.alloc([16, 16])
        P.load(K.MASK16[:, :, :], v3(I["c_mask16"], 16))

        def alloc_bufs():
            B = {}
            B["S"] = AFa.alloc([128])
            B["QT"] = AFa.alloc([512])
            B["KTf"] = AFa.alloc([512])
            B["SZ"] = AFa.alloc([512])
            B["mats"] = [AFa.alloc([128]) for _ in range(5)]
            B["fo"] = (AFa.alloc([128]), AFa.alloc([2]), AFa.alloc([128]))
            return B

        def head_prompt(h, wv, B):
            g128 = math.exp(128.0 * LOG_GAMMA[h])
            S, QT, KTf, SZ = B["S"], B["QT"], B["KTf"], B["SZ"]
            V, KD, AQ, O1, OO = B["mats"]
            K.fo_sq, K.fo_ss, K.fo_on = B["fo"]
            P.memset(S[:, :], 0.0)
            for tb in range(4):
                c0 = tb * 512
                for j, dst in ((0, QT), (1, KTf)):
                    ps = psn()
                    for k in range(8):
                        P.mm(ps[:, :], wv[:, k, j * 128:(j + 1) * 128], HT[:, k, c0:c0 + 512], start=(k == 0), stop=(k == 7))
                    P.copy(dst[:, :], ps[:, :], eng=("act" if j else "dve"))
                ps = psn()
                for k in range(8):
                    P.mm(ps[:, :], wv[:, k, 384:512], HT[:, k, c0:c0 + 512], start=(k == 0), stop=(k == 7))
                P.act(SZ[:, :], ps[:, :], AF.Silu)
                for tl in range(4):
                    n = tb * 4 + tl
                    cc = slice(tl * 128, (tl + 1) * 128)
                    ps = psn()
                    for k in range(8):
                        P.mm(ps[:, 0:256], HT[:, k, n * 128:(n + 1) * 128], wv[:, k, 128:384], start=(k == 0), stop=(k == 7))
                    P.copy(V[:, :], ps[:, 128:256], eng="act")
                    P.ts(KD[:, :], ps[:, 0:128], KDC[:, h:h + 1], ALU.mult)
                    pa = psn()
                    P.mm(pa[:, 0:128], KTf[:, cc], QT[:, cc])
                    P.tt(AQ[:, :], pa[:, 0:128], DTC[:, h, :], ALU.mult)
                    p2 = psn()
                    P.mm(p2[:, 0:128], QT[:, cc], S[:, :])
                    P.ts(O1[:, :], p2[:, 0:128], EGC[:, h:h + 1], ALU.mult)
                    p4 = psn()
                    P.mm(p4[:, 0:128], AQ[:, :], V[:, :])
                    P.tt(OO[:, :], p4[:, 0:128], O1[:, :], ALU.add)
                    p3 = psn()
                    P.mm(p3[:, 0:128], KD[:, :], V[:, :])
                    P.stt(S[:, :], S[:, :], g128, p3[:, 0:128], ALU.mult, ALU.add)
                    finish_o_ln(K, OO[:, :], 128, GND, SZ[:, cc], OT[:, 4 + h, n * 128:(n + 1) * 128])
            P.store(O["rt_p"][o_, h], S[:, :])

        def head_sample(h, wv, B):
            g1 = math.exp(LOG_GAMMA[h])
            SZ = B["SZ"]
            K.fo_sq, K.fo_ss, K.fo_on = B["fo"]
            QS_ = [AFa.alloc([16]) for _ in range(3)]
            for j in range(3):
                ps = psn()
                for k in range(8):
                    P.mm(ps[:, 0:16], wv[:, k, j * 128:(j + 1) * 128], HT[:, k, T:TA], start=(k == 0), stop=(k == 7))
                if j == 1:
                    P.ts(QS_[j][:, :], ps[:, 0:16], 128.0 ** -0.5, ALU.mult)
                else:
                    P.copy(QS_[j][:, :], ps[:, 0:16], eng="act")
            ps = psn()
            for k in range(8):
                P.mm(ps[:, 0:16], wv[:, k, 384:512], HT[:, k, T:TA], start=(k == 0), stop=(k == 7))
            P.act(SZ[:, 0:16], ps[:, 0:16], AF.Silu)
            sample_state_step(K, QS_[0][:, :], QS_[1][:, :], QS_[2][:, :], g1, None, None, g1,
                              (lambda s, h=h: I["sret"][o_, s, h]), (lambda s, h=h: O["rt_s"][o_, s, h]),
                              GND, SZ[:, 0:16], OT[:, 4 + h, T:TA], False, True)

        NPAR = 2
        for hp in range(0, 4, NPAR):
            wvs = [wv512(next_w(("od", o_, hp + i), prefetch=(i == 0))) for i in range(NPAR)]
            with Scope():
                Bs = [alloc_bufs() for _ in range(NPAR)]
                streams = []
                for i in range(NPAR):
                    K.p0, K.pn = (8 // NPAR) * i, 8 // NPAR
                    cap = P.capture()
                    head_prompt(hp + i, wvs[i], Bs[i])
                    P.end_capture()
                    streams.append(cap)
                K.p0, K.pn = 0, 8
                P.merge(streams)
                for i in range(NPAR):
                    with Scope():
                        head_sample(hp + i, wvs[i], Bs[i])


def s5_mixer(K, li):
    P, I, O = K.P, K.I, K.O
    XT, HT, OT, AFa, ABa, PS, psn, Scope = K.XT, K.HT, K.OT, K.AFa, K.ABa, K.PS, K.psn, K.Scope
    v3, ident, ones, onesb, wv512 = K.v3, K.ident, K.ones, K.onesb, K.wv512
    o_ = li // 2
    TC = 512
    with Scope():
        WU = ABa.alloc([4096])
        P.load(WU[:, :], I["wino"][o_, 0], eng="pool")
        wu = wv512(WU)
        ARE, AIM, LDT = AFa.alloc([16]), AFa.alloc([16]), AFa.alloc([16])
        P.load(ARE[:, :], I["s5are"][o_].rearrange("(t l) -> l t", l=128), slow=True)
        P.load(AIM[:, :], I["s5aim"][o_].rearrange("(t l) -> l t", l=128), slow=True)
        ldt2 = I["s5ldt"][o_].rearrange("o (t two) -> o two t", two=2)
        P.load(LDT[0:64, :], ldt2[:, 0, :].partition_broadcast(64), slow=True)
        P.load(LDT[64:128, :], ldt2[:, 1, :].partition_broadcast(64), slow=True)
        P.act(LDT[:, :], LDT[:, :], AF.Exp)
        LRE, TH, RR = AFa.alloc([16]), AFa.alloc([16]), AFa.alloc([16])
        P.tt(LRE[:, :], ARE[:, :], LDT[:, :], ALU.mult)
        P.tt(TH[:, :], AIM[:, :], LDT[:, :], ALU.mult)
        P.act(RR[:, :], LRE[:, :], AF.Exp)
        HPI = AFa.alloc([1])
        P.memset(HPI[:, :], math.pi / 2)
        C1, S1, t1_, t2_ = AFa.alloc([16]), AFa.alloc([16]), AFa.alloc([16]), AFa.alloc([16])
        P.act(S1[:, :], TH[:, :], AF.Sin, scale=1.0 / 16)
        P.act(C1[:, :], TH[:, :], AF.Sin, scale=1.0 / 16, bias=HPI[:, 0:1])
        for _ in range(4):
            P.tt(t1_[:, :], C1[:, :], C1[:, :], ALU.mult)
            P.tt(t2_[:, :], S1[:, :], S1[:, :], ALU.mult)
            P.stt(S1[:, :], C1[:, :], 2.0, S1[:, :], ALU.mult, ALU.mult)
            P.tt(C1[:, :], t1_[:, :], t2_[:, :], ALU.subtract)
        LBR, LBI, NLBI = AFa.alloc([16]), AFa.alloc([16]), AFa.alloc([16])
        P.tt(LBR[:, :], RR[:, :], C1[:, :], ALU.mult)
        P.tt(LBI[:, :], RR[:, :], S1[:, :], ALU.mult)
        P.ts(NLBI[:, :], LBI[:, :], -1.0, ALU.mult)
        DEN, FRE, FIM, LM1 = AFa.alloc([16]), AFa.alloc([16]), AFa.alloc([16]), AFa.alloc([16])
        P.tt(DEN[:, :], ARE[:, :], ARE[:, :], ALU.mult)
        P.tt(t1_[:, :], AIM[:, :], AIM[:, :], ALU.mult)
        P.tt(DEN[:, :], DEN[:, :], t1_[:, :], ALU.add)
        P.recip(DEN[:, :], DEN[:, :])
        P.ts(LM1[:, :], LBR[:, :], -1.0, ALU.add)
        P.tt(FRE[:, :], LM1[:, :], ARE[:, :], ALU.mult)
        P.tt(t1_[:, :], LBI[:, :], AIM[:, :], ALU.mult)
        P.tt(FRE[:, :], FRE[:, :], t1_[:, :], ALU.add)
        P.tt(FRE[:, :], FRE[:, :], DEN[:, :], ALU.mult)
        P.tt(FIM[:, :], LBI[:, :], ARE[:, :], ALU.mult)
        P.tt(t1_[:, :], LM1[:, :], AIM[:, :], ALU.mult)
        P.tt(FIM[:, :], FIM[:, :], t1_[:, :], ALU.subtract)
        P.tt(FIM[:, :], FIM[:, :], DEN[:, :], ALU.mult)
        BRE, BIM, BBR, BBI, tb_ = [AFa.alloc([16, 16]) for _ in range(5)]
        P.load(BRE[:, :, :], I["s5bre"][o_].rearrange("(t l) c -> l t c", l=128))
        P.load(BIM[:, :, :], I["s5bim"][o_].rearrange("(t l) c -> l t c", l=128))
        frb = FRE[:, :].unsqueeze(2).to_broadcast([128, 16, 16])
        fib = FIM[:, :].unsqueeze(2).to_broadcast([128, 16, 16])
        P.tt(BBR[:, :, :], BRE[:, :, :], frb, ALU.mult)
        P.tt(tb_[:, :, :], BIM[:, :, :], fib, ALU.mult)
        P.tt(BBR[:, :, :], BBR[:, :, :], tb_[:, :, :], ALU.subtract)
        P.tt(BBI[:, :, :], BRE[:, :, :], fib, ALU.mult)
        P.tt(tb_[:, :, :], BIM[:, :, :], frb, ALU.mult)
        P.tt(BBI[:, :, :], BBI[:, :, :], tb_[:, :, :], ALU.add)
        DSK = AFa.alloc([4])
        P.load(DSK[:, :], I["s5d"][o_].rearrange("(c p) -> p c", p=128), slow=True)
        XPall = AFa.alloc([16, 2])
        UT = AFa.alloc([4, TC])
        UTs = AFa.alloc([16])
        EC, ESn = AFa.alloc([TC]), AFa.alloc([TC])
        W = [AFa.alloc([TC]) for _ in range(4)]
        Bm, Cin = AFa.alloc([128]), AFa.alloc([128])
        CCr, CCi, CMK = AFa.alloc([64]), AFa.alloc([64]), AFa.alloc([8])
        P.load(CMK[:, :], I["c_cmask"])
        BLr, BLi, CLr, CLi = [AFa.alloc([128]) for _ in range(4)]
        XP = AFa.alloc([2])
        nsm = AFa.alloc([1])
        xs0 = AFa.alloc([2, 16])
        xtm = AFa.alloc([128])
        xsn = AFa.alloc([2, 16])
        xso = AFa.alloc([128])
        yt = AFa.alloc([TC])
        YA = [PS[0], PS[1], PS[2], PS[3]]
        YS = PS[4]
        K.p0, K.pn = 5, 3
        for ch in range(4):
            for tc in range(4):
                ps = psn()
                for k in range(8):
                    P.mm(ps[:, :], wu[:, k, ch * 128:(ch + 1) * 128], HT[:, k, tc * TC:(tc + 1) * TC], start=(k == 0), stop=(k == 7))
                P.copy(UT[:, tc, :], ps[:, :], eng="act")
            ps = psn()
            for k in range(8):
                P.mm(ps[:, 0:16], wu[:, k, ch * 128:(ch + 1) * 128], HT[:, k, T:TA], start=(k == 0), stop=(k == 7))
            P.copy(UTs[:, :], ps[:, 0:16], eng="act")
            P.load(CCr[:, :], I["s5cre"][o_][ch * 128:(ch + 1) * 128, :])
            P.load(CCi[:, :], I["s5cim"][o_][ch * 128:(ch + 1) * 128, :])
            for li_ in range(4):
                lt = ch * 4 + li_
                off = li_ * 32
                for src, dst, neg in ((BBR, BLr, False), (BBI, BLi, False)):
                    P.memset(Bm[:, :], 0.0)
                    P.copy(Bm[0:64, off:off + 16], src[0:64, lt, :])
                    P.copy(Bm[64:128, off + 16:off + 32], src[64:128, lt, :])
                    pt = psn()
                    P.tr(pt[:, 0:128], Bm[:, :], ident[:])
                    P.copy(dst[:, :], pt[:, 0:128], eng="act")
                for nm, dst, neg in (("s5cre", CLr, False), ("s5cim", CLi, True)):
                    ccs = CCr if nm == "s5cre" else CCi
                    P.ts(Cin[:, 0:64], ccs[:, :], CMK[:, li_ * 2:li_ * 2 + 1], ALU.mult)
                    P.ts(Cin[:, 64:128], ccs[:, :], CMK[:, li_ * 2 + 1:li_ * 2 + 2], ALU.mult)
                    pt = psn()
                    P.tr(pt[:, 0:128], Cin[:, :], ident[:])
                    if neg:
                        P.ts(dst[:, :], pt[:, 0:128], -1.0, ALU.mult)
                    else:
                        P.copy(dst[:, :], pt[:, 0:128], eng="act")
                P.copy(EC[:, 0:1], C1[:, lt:lt + 1])
                P.copy(ESn[:, 0:1], S1[:, lt:lt + 1])
                m = 1
                while m < TC:
                    cm, sm = EC[:, m - 1:m], ESn[:, m - 1:m]
                    P.ts(nsm[:, :], sm, -1.0, ALU.mult)
                    P.ts(EC[:, m:2 * m], EC[:, 0:m], cm, ALU.mult)
                    P.stt(EC[:, m:2 * m], ESn[:, 0:m], nsm[:, 0:1], EC[:, m:2 * m], ALU.mult, ALU.add)
                    P.ts(ESn[:, m:2 * m], EC[:, 0:m], sm, ALU.mult)
                    P.stt(ESn[:, m:2 * m], ESn[:, 0:m], cm, ESn[:, m:2 * m], ALU.mult, ALU.add)
                    m *= 2
                Rb = RR[:, lt:lt + 1].to_broadcast([128, TC])
                P.memset(XP[:, :], 0.0)
                for tc in range(4):
                    pbr = psn()
                    P.mm(pbr[:, :], BLr[:, :], UT[:, tc, :])
                    pbi = psn()
                    P.mm(pbi[:, :], BLi[:, :], UT[:, tc, :])
                    vr, vi, zr, zi = W
                    P.tt(vr[:, :], pbr[:, :], EC[:, :], ALU.mult)
                    P.tt(zr[:, :], pbi[:, :], ESn[:, :], ALU.mult)
                    P.tt(vr[:, :], vr[:, :], zr[:, :], ALU.add)
                    P.tt(vi[:, :], pbi[:, :], EC[:, :], ALU.mult)
                    P.tt(zr[:, :], pbr[:, :], ESn[:, :], ALU.mult)
                    P.tt(vi[:, :], vi[:, :], zr[:, :], ALU.subtract)
                    P.scan(zr[:, :], Rb, vr[:, :], XP[:, 0:1])
                    P.scan(zi[:, :], Rb, vi[:, :], XP[:, 1:2])
                    P.tt(vr[:, :], zr[:, :], EC[:, :], ALU.mult)
                    P.tt(yt[:, :], zi[:, :], ESn[:, :], ALU.mult)
                    P.tt(vr[:, :], vr[:, :], yt[:, :], ALU.subtract)
                    P.tt(vi[:, :], zr[:, :], ESn[:, :], ALU.mult)
                    P.tt(yt[:, :], zi[:, :], EC[:, :], ALU.mult)
                    P.tt(vi[:, :], vi[:, :], yt[:, :], ALU.add)
                    P.copy(XP[:, 0:1], vr[:, TC - 1:TC])
                    P.copy(XP[:, 1:2], vi[:, TC - 1:TC])
                    P.mm(YA[tc][:, :], CLr[:, :], vr[:, :], start=(li_ == 0), stop=False)
                    P.mm(YA[tc][:, :], CLi[:, :], vi[:, :], start=False, stop=(li_ == 3))
                P.copy(XPall[:, lt, :], XP[:, :])
                for ri, nm in ((0, "scre"), (1, "scim")):
                    P.load(xtm[0:16, :], I[nm][o_][:, lt * 128:(lt + 1) * 128])
                    pt = psn()
                    P.tr(pt[:, 0:16], xtm[0:16, :], ident[0:16, 0:16])
                    P.copy(xs0[:, ri, :], pt[:, 0:16])
                pbr = psn()
                P.mm(pbr[:, 0:16], BLr[:, :], UTs[:, :])
                P.mm(pbr[:, 16:32], BLi[:, :], UTs[:, :])
                P.ts(xsn[:, 0, :], xs0[:, 0, :], LBR[:, lt:lt + 1], ALU.mult)
                P.stt(xsn[:, 0, :], xs0[:, 1, :], NLBI[:, lt:lt + 1], xsn[:, 0, :], ALU.mult, ALU.add)
                P.tt(xsn[:, 0, :], xsn[:, 0, :], pbr[:, 0:16], ALU.add)
                P.ts(xsn[:, 1, :], xs0[:, 1, :], LBR[:, lt:lt + 1], ALU.mult)
                P.stt(xsn[:, 1, :], xs0[:, 0, :], LBI[:, lt:lt + 1], xsn[:, 1, :], ALU.mult, ALU.add)
                P.tt(xsn[:, 1, :], xsn[:, 1, :], pbr[:, 16:32], ALU.add)
                P.mm(YS[:, 0:16], CLr[:, :], xsn[:, 0, :], start=(li_ == 0), stop=False)
                P.mm(YS[:, 0:16], CLi[:, :], xsn[:, 1, :], start=False, stop=(li_ == 3))
                for ri, nm in ((0, "s5r_s"), (1, "s5i_s")):
                    pt = psn()
                    P.tr(pt[0:16, 0:128], xsn[:, ri, :], ident[:])
                    P.copy(xso[0:16, :], pt[0:16, 0:128])
                    P.store(O[nm][o_][:, lt * 128:(lt + 1) * 128], xso[0:16, :])
            for tc in range(4):
                P.stt(yt[:, :], UT[:, tc, :], DSK[:, ch:ch + 1], YA[tc][:, :], ALU.mult, ALU.add)
                P.act(OT[:, ch, tc * TC:(tc + 1) * TC], yt[:, :], AF.Gelu_apprx_tanh)
            P.stt(yt[:, 0:16], UTs[:, :], DSK[:, ch:ch + 1], YS[:, 0:16], ALU.mult, ALU.add)
            P.act(OT[:, ch, T:TA], yt[:, 0:16], AF.Gelu_apprx_tanh)
        P.store(O["s5r_p"][o_].rearrange("(t l) -> l t", l=128), XPall[:, :, 0], slow=True)
        P.store(O["s5i_p"][o_].rearrange("(t l) -> l t", l=128), XPall[:, :, 1], slow=True)
        K.p0, K.pn = 0, 8
    with Scope():
        WZ = ABa.alloc([4096])
        WG = ABa.alloc([2048])
        P.load(WZ[:, :], I["wino"][o_, 1], eng="pool")
        P.load(WG[:, :], I["wglu"][o_], eng="pool")
        wz = wv512(WZ)
        wg = WG[:, :].rearrange("p (k c) -> p k c", c=512)
        sg = [AFa.alloc([512]) for _ in range(4)]
        szb = AFa.alloc([512])
        for (c0, w) in BLOCKS:
            pgs = []
            for cp in range(4):
                pg = PS[cp]
                for k in range(4):
                    P.mm(pg[:, 0:w], wg[:, k, cp * 128:(cp + 1) * 128], OT[:, k, c0:c0 + w], start=(k == 0), stop=(k == 3))
                pgs.append(pg)
            for cp in range(4):
                P.act(sg[cp][:, 0:w], pgs[cp][:, 0:w], AF.Sigmoid)
            for cp in range(4):
                pz = PS[4 + cp]
                for k in range(8):
                    P.mm(pz[:, 0:w], wz[:, k, cp * 128:(cp + 1) * 128], HT[:, k, c0:c0 + w], start=(k == 0), stop=(k == 7))
                P.act(szb[:, 0:w], pz[:, 0:w], AF.Silu)
                P.tt(sg[cp][:, 0:w], sg[cp][:, 0:w], szb[:, 0:w], ALU.mult)
                P.tt(OT[:, cp, c0:c0 + w], OT[:, cp, c0:c0 + w], sg[cp][:, 0:w], ALU.mult)


def odd_layer(K, li):
    if K.cfg.get("s5", True):
        s5_mixer(K, li)
    if K.cfg.get("ret", True):
        ret_heads(K, li)


def build(cfg=None):
    cfg = cfg or {}
    nlayers = cfg.get("layers", 4)
    nc = bass.Bass("TRN2", target_bir_lowering=False)
    specs = INPUT_SPECS
    if cfg.get("small_cache"):
        specs = [(n, ((2, 256, 512) if n in ("ck", "cvv") else s), dt) for n, s, dt in INPUT_SPECS]
    I = {n: nc.dram_tensor(n, list(s), dt, kind="ExternalInput").ap() for n, s, dt in specs}
    O = {n: nc.dram_tensor(n, list(s), F32, kind="ExternalOutput").ap() for n, s in OUTPUT_SPECS}
    if cfg.get("dbg"):
        O["dbg_xt"] = nc.dram_tensor("dbg_xt", [128, 8 * TA], F32, kind="ExternalOutput").ap()
        O["dbg_ht"] = nc.dram_tensor("dbg_ht", [128, 8 * TA], BF16, kind="ExternalOutput").ap()
        O["dbg_ot"] = nc.dram_tensor("dbg_ot", [128, 8 * TA], BF16, kind="ExternalOutput").ap()
    K = Ctx()
    with ExitStack() as es:
        P = Prog(nc, es)
        P.limit = cfg.get("max_ops")
        if cfg.get("trace"):
            P.trace = []
        K.P = P

        def sbt(name, shape, dt=F32):
            return es.enter_context(nc.sbuf_tensor(name, shape, dt))

        XT = sbt("XT", [128, 8, TA])
        HT = sbt("HT", [128, 8, TA], BF16)
        OT = sbt("OT", [128, 8, TA], BF16)
        WB = [sbt("WB0", [128, 4096], BF16), sbt("WB1", [128, 4096], BF16)]
        AFa = Arena(sbt("AFa", [128, NF]), NF)
        ABa = Arena(sbt("ABa", [128, NB], BF16), NB)
        ident = sbt("ident", [128, 128])
        ones = sbt("ones", [128, 128])
        onesb = sbt("onesb", [128, 128], BF16)
        trib = sbt("trib", [128, 128], BF16)
        CT = sbt("CT", [128, 8, 17], BF16)
        MODT = sbt("MODT", [128, 24, 17])
        GS = sbt("GS", [128, 8, 17])
        BADA = sbt("BADA", [128, 24])
        NG = sbt("NG", [128, 8])
        PS = [es.enter_context(nc.psum_tensor("ps%d" % i, [128, 512], F32)) for i in range(8)]
        K.pi, K.p0, K.pn = 0, 0, 8

        K.forced = []

        def psn():
            if K.forced:
                return K.forced.pop(0)
            b = PS[K.p0 + (K.pi % K.pn)]
            K.pi += 1
            return b

        class Scope:
            def __enter__(self):
                AFa.stack.append(AFa.top)
                ABa.stack.append(ABa.top)

            def __exit__(self, *a):
                P.barrier()
                AFa.top = AFa.stack.pop()
                ABa.top = ABa.stack.pop()
                return False

        def v3(ap, b):
            return ap.rearrange("p (a b) -> p a b", b=b)

        wlist = []
        for li in range(nlayers):
            for blk in range(6):
                wlist.append((("ada", li, blk), I["wada"][li, blk], 4096))
            if li % 2 == 0:
                e = li // 2
                if cfg.get("attn", True):
                    for h in range(4):
                        wlist.append((("ea", e, h), I["wine"][e, h], 4096))
                if cfg.get("gdn", True):
                    wlist.append((("wab", e), I["wab"][e], 64))
                    for h in range(4):
                        wlist.append((("eb", e, h), I["wine"][e, 4 + h], 4096))
                for blk in range(2):
                    wlist.append((("eo", e, blk), I["woute"][e, blk], 4096))
            else:
                o = li // 2
                if cfg.get("ret", True):
                    for h in range(4):
                        wlist.append((("od", o, h), I["wino"][o, 2 + h], 4096))
                for blk in range(2):
                    wlist.append((("oo", o, blk), I["wouto"][o, blk], 4096))
        K.wi, K.wissued = 0, 0

        def w_issue(i):
            tag, src, n = wlist[i]
            P.load(WB[i % 2][:, 0:n], src, eng="pool")

        def next_w(tag, prefetch=True):
            i = K.wi
            assert wlist[i][0] == tag, (wlist[i][0], tag)
            if K.wissued <= i:
                w_issue(i)
                K.wissued = i + 1
            if prefetch and i + 1 < len(wlist) and K.wissued <= i + 1:
                w_issue(i + 1)
                K.wissued = i + 2
            K.wi += 1
            return WB[i % 2]

        def wv512(wb):
            return wb[:, 0:4096].rearrange("p (k c) -> p k c", c=512)

        P.load(ident[:], I["c_ident"])
        P.load(ones[:], I["c_ones"])
        P.load(onesb[:], I["c_onesb"])
        P.load(trib[:], I["c_trib"])
        with Scope():
            xin = [AFa.alloc([1024]) for _ in range(2)]
            for n in range(17):
                rows = 128 if n < 16 else 16
                src = I["xp"][n * 128:(n + 1) * 128, :] if n < 16 else I["xs"]
                xt = xin[n % 2]
                P.load(xt[0:rows, :], src)
                for half in range(2):
                    ps = psn()
                    for j in range(4):
                        k = half * 4 + j
                        P.tr(ps[:, j * 128:j * 128 + rows], xt[0:rows, k * 128:(k + 1) * 128], ident[0:rows, 0:rows])
                    P.copy(XT[:, half * 4:(half + 1) * 4, n * 128:n * 128 + rows], v3(ps[:, :], 128)[:, :, 0:rows],
                           eng=("act" if half else "dve"))
            cvt = AFa.alloc([1024])
            P.load(cvt[0:17, :], I["cv"])
            P.act(cvt[0:17, :], cvt[0:17, :], AF.Silu)
            ps = psn()
            for k in range(8):
                P.tr(ps[:, k * 32:k * 32 + 17], cvt[0:17, k * 128:(k + 1) * 128], ident[0:17, 0:17])
            P.copy(CT[:], v3(ps[:, 0:256], 32)[:, :, 0:17])

        def modulation(li):
            P.load(BADA[:], I["bada"][li].rearrange("(c p) -> p c", p=128), slow=True)
            P.load(NG[:], I["normg"][li].rearrange("(c p) -> p c", p=128), slow=True)
            for blk in range(6):
                wv = wv512(next_w(("ada", li, blk)))
                ps = psn()
                for j in range(4):
                    for k in range(8):
                        P.mm(ps[:, j * 32:j * 32 + 17], wv[:, k, j * 128:(j + 1) * 128], CT[:, k, :],
                             start=(k == 0), stop=(k == 7))
                P.tt(MODT[:, blk * 4:(blk + 1) * 4, :], v3(ps[:, 0:128], 32)[:, :, 0:17],
                     BADA[:, blk * 4:(blk + 1) * 4].unsqueeze(2).to_broadcast([128, 4, 17]), ALU.add)
            P.ts(GS[:], MODT[:, 8:16, :], 1.0, ALU.add)
            P.tt(GS[:], GS[:], NG[:].unsqueeze(2).to_broadcast([128, 8, 17]), ALU.mult)

        def norm_ht():
            with Scope():
                RS = AFa.alloc([TA])
                sqb = [ABa.alloc([512]) for _ in range(2)]
                for (c0, w) in BLOCKS:
                    ps = psn()
                    for k in range(8):
                        sq = sqb[k % 2]
                        P.act(sq[:, 0:w], XT[:, k, c0:c0 + w], AF.Square)
                        P.mm(ps[:, 0:w], onesb[:], sq[:, 0:w], start=(k == 0), stop=(k == 7))
                    P.ts(RS[:, c0:c0 + w], ps[:, 0:w], 1.0 / D, ALU.mult, EPS, ALU.add)
                    P.act(RS[:, c0:c0 + w], RS[:, c0:c0 + w], AF.Sqrt)
                    P.recip(RS[:, c0:c0 + w], RS[:, c0:c0 + w])
                tmp = [AFa.alloc([512]) for _ in range(2)]
                i = 0
                for (c0, w) in BLOCKS:
                    for k in range(8):
                        t = tmp[i % 2]
                        i += 1
                        P.tt(t[:, 0:w], XT[:, k, c0:c0 + w], RS[:, c0:c0 + w], ALU.mult)
                        if c0 < T:
                            P.act(HT[:, k, c0:c0 + w], t[:, 0:w], AF.Identity, bias=MODT[:, k, 0:1], scale=GS[:, k, 0:1])
                        else:
                            P.tt(t[:, 0:w], t[:, 0:w], GS[:, k, 1:17], ALU.mult)
                            P.tt(HT[:, k, c0:c0 + w], t[:, 0:w], MODT[:, k, 1:17], ALU.add)

        def out_proj(tag, idx):
            with Scope():
                ts_ = AFa.alloc([16])
                for blk in range(2):
                    wv = wv512(next_w((tag, idx, blk)))
                    for j in range(4):
                        fc = blk * 4 + j
                        for (c0, w) in BLOCKS:
                            ps = psn()
                            for k in range(8):
                                P.mm(ps[:, 0:w], wv[:, k, j * 128:(j + 1) * 128], OT[:, k, c0:c0 + w],
                                     start=(k == 0), stop=(k == 7))
                            if c0 < T:
                                P.stt(XT[:, fc, c0:c0 + w], ps[:, 0:w], MODT[:, 16 + fc, 0:1], XT[:, fc, c0:c0 + w],
                                      ALU.mult, ALU.add)
                            else:
                                P.tt(ts_[:, :], ps[:, 0:16], MODT[:, 16 + fc, 1:17], ALU.mult)
                                P.tt(XT[:, fc, c0:c0 + 16], XT[:, fc, c0:c0 + 16], ts_[:, :], ALU.add)

        def final_out():
            with Scope():
                yb = [AFa.alloc([1024]) for _ in range(2)]
                for n in range(17):
                    rows = 128 if n < 16 else 16
                    yt = yb[n % 2]
                    for half in range(2):
                        ps = psn()
                        for j in range(4):
                            k = half * 4 + j
                            P.tr(ps[0:rows, j * 128:(j + 1) * 128], XT[:, k, n * 128:n * 128 + rows], ident[:])
                        P.copy(yt[0:rows, half * 512:(half + 1) * 512], ps[0:rows, :], eng=("act" if half else "dve"))
                    dst = O["y_p"][n * 128:(n + 1) * 128, :] if n < 16 else O["y_s"]
                    P.store(dst, yt[0:rows, :])

        def dbg_dump():
            if cfg.get("dbg"):
                P.store(O["dbg_xt"], XT[:].rearrange("p k t -> p (k t)"))
                P.store(O["dbg_ht"], HT[:].rearrange("p k t -> p (k t)"))
                P.store(O["dbg_ot"], OT[:].rearrange("p k t -> p (k t)"))

        K.__dict__.update(locals())
        for li in range(nlayers):
            modulation(li)
            norm_ht()
            if cfg.get("zero_ot", False) or True:
                pass
            if cfg.get("mods_only"):
                continue
            if not (cfg.get("attn", True) and cfg.get("gdn", True) and cfg.get("s5", True) and cfg.get("ret", True)):
                for k_ in range(8):
                    for (c0_, w_) in BLOCKS:
                        P.memset(OT[:, k_, c0_:c0_ + w_], 0.0)
            if li % 2 == 0:
                even_layer(K, li)
                out_proj("eo", li // 2)
            else:
                odd_layer(K, li)
                out_proj("oo", li // 2)
        final_out()
        dbg_dump()
        P.barrier()
        P.emit()
    K.n_instr = P.n_instr
    K.peaks = (AFa.peak, ABa.peak)
    return nc, K


def _wblocks(w, col_lists):
    kd = w.shape[0] // 128
    out = []
    for cols in col_lists:
        blk = w[:, cols].reshape(kd, 128, len(cols)).transpose(1, 0, 2).reshape(128, kd * len(cols))
        out.append(blk)
    return np.ascontiguousarray(np.stack(out, 0))


def _even_cols():
    lists = []
    for h in range(4):
        lists.append(np.concatenate([np.arange(h * 128, (h + 1) * 128) + off for off in (0, 512, 1024, 1536)]))
    for h in range(4):
        lists.append(np.concatenate([np.arange(h * 128, (h + 1) * 128) + off for off in (2048, 2560, 3072, 3584)]))
    return lists


def _odd_cols():
    lists = [np.arange(0, 512), np.arange(512, 1024)]
    for h in range(4):
        lists.append(np.concatenate([np.arange(h * 128, (h + 1) * 128) + off for off in (1024, 1536, 2048, 2560)]))
    return lists


def make_shared(inp):
    f = lambda a: np.ascontiguousarray(np.asarray(a, dtype=np.float32))
    sh = {}
    sh["ck"] = f(inp["cache_k"]).reshape(2, NPHYS * 128, 512)
    sh["cvv"] = f(inp["cache_v"]).reshape(2, NPHYS * 128, 512)
    sh["normg"] = f(inp["norm_g"])
    w_ada = f(inp["w_ada"])
    sh["wada"] = np.stack([_wblocks(w_ada[l], [np.arange(b * 512, (b + 1) * 512) for b in range(6)]) for l in range(4)], 0)
    sh["bada"] = f(inp["b_ada"])
    w_in_e = f(inp["w_in_e"])
    sh["wine"] = np.stack([_wblocks(w_in_e[e], _even_cols()) for e in range(2)], 0)
    sh["wab"] = np.stack([_wblocks(w_in_e[e], [np.arange(4096, 4104)])[0] for e in range(2)], 0)
    w_out_e = f(inp["w_out_e"])
    sh["woute"] = np.stack([_wblocks(w_out_e[e], [np.arange(0, 512), np.arange(512, 1024)]) for e in range(2)], 0)
    sh["qng"] = f(inp["qn_g"]).reshape(2, 1, 64)
    sh["kng"] = f(inp["kn_g"]).reshape(2, 1, 64)
    sh["lamv"] = np.ascontiguousarray(np.stack([f(inp["lam_q1"]), f(inp["lam_k1"]), f(inp["lam_q2"]), f(inp["lam_k2"])], 1)).reshape(2, 1, 256)
    sh["sublng"] = f(inp["subln_g"]).reshape(2, 128, 1)
    sh["convw"] = f(inp["conv_w"])
    sh["alog"] = f(inp["a_log"]).reshape(2, 1, 4)
    sh["dtb"] = f(inp["dt_bias"]).reshape(2, 1, 4)
    sh["gnb"] = f(inp["gn_b"]).reshape(2, 1, 128)
    w_in_o = f(inp["w_in_o"])
    sh["wino"] = np.stack([_wblocks(w_in_o[o], _odd_cols()) for o in range(2)], 0)
    w_out_o = f(inp["w_out_o"])
    sh["wouto"] = np.stack([_wblocks(w_out_o[o], [np.arange(0, 512), np.arange(512, 1024)]) for o in range(2)], 0)
    sh["s5are"] = f(inp["s5_a_re"]).reshape(2, 2048)
    sh["s5aim"] = f(inp["s5_a_im"]).reshape(2, 2048)
    sh["s5bre"] = f(inp["s5_b_re"]).reshape(2, 2048, 16)
    sh["s5bim"] = f(inp["s5_b_im"]).reshape(2, 2048, 16)
    sh["s5cre"] = f(inp["s5_c_re"]).reshape(2, 512, 64)
    sh["s5cim"] = f(inp["s5_c_im"]).reshape(2, 512, 64)
    sh["s5d"] = f(inp["s5_d"])
    sh["s5ldt"] = f(inp["s5_log_dt"]).reshape(2, 1, 32)
    w_glu = f(inp["w_glu"])
    sh["wglu"] = np.stack([_wblocks(w_glu[o], [np.arange(0, 512)])[0] for o in range(2)], 0)
    sh["gnd"] = f(inp["gn_d"]).reshape(2, 1, 128)
    sh.update(make_consts2(make_consts()))
    return sh


def make_core_inputs(inp, sh, c):
    f = lambda a: np.ascontiguousarray(np.asarray(a, dtype=np.float32))
    m = dict(sh)
    s0, s1 = c * NS, (c + 1) * NS
    m["xp"] = f(inp["x_prompt"][c])
    m["xs"] = f(inp["x_sample"][s0:s1, 0, :])
    m["cv"] = f(np.concatenate([np.asarray(inp["c_prompt"])[c:c + 1], np.asarray(inp["c_sample"])[s0:s1]], 0))
    m["pt"] = np.ascontiguousarray(np.asarray(inp["page_table"], dtype=np.int32)[s0:s1]).reshape(1, NS * NPG)
    m["sconv"] = f(np.asarray(inp["state_b_conv"])[:, s0:s1]).reshape(2, NS * 3, 1536)
    m["sssm"] = f(np.asarray(inp["state_b_ssm"])[:, s0:s1])
    m["scre"] = f(np.asarray(inp["state_c_re"])[:, s0:s1]).reshape(2, NS, 2048)
    m["scim"] = f(np.asarray(inp["state_c_im"])[:, s0:s1]).reshape(2, NS, 2048)
    m["sret"] = f(np.asarray(inp["state_d_ret"])[:, s0:s1])
    return m


_CACHE = {}


def kernel(**inp):
    ncores = 8
    if "nc" not in _CACHE:
        _CACHE["nc"] = build()[0]
    nc = _CACHE["nc"]
    sh = make_shared(inp)
    in_maps = [make_core_inputs(inp, sh, c) for c in range(ncores)]
    res = run_bass_kernel_spmd(nc, in_maps, core_ids=list(range(ncores)))
    R = res.results
    cat = lambda name, ax: np.concatenate([np.asarray(R[c][name]) for c in range(ncores)], axis=ax)
    stk = lambda name: np.stack([np.asarray(R[c][name]) for c in range(ncores)], 0)
    y_p = stk("y_p")
    y_s = cat("y_s", 0).reshape(ncores * NS, 1, D)
    nk_p = stk("nk_p").transpose(1, 0, 2, 3).reshape(2, ncores, T, 4, 128)
    nv_p = stk("nv_p").transpose(1, 0, 2, 3).reshape(2, ncores, T, 4, 128)
    nk_s = cat("nk_s", 1).reshape(2, ncores * NS, 1, 4, 128)
    nv_s = cat("nv_s", 1).reshape(2, ncores * NS, 1, 4, 128)
    cv_p = stk("cv_p").transpose(1, 0, 2, 3)
    cv_s = cat("cv_s", 1)
    dl_p = stk("dl_p").transpose(1, 0, 2, 3, 4)
    dl_s = cat("dl_s", 1)
    s5r_p = stk("s5r_p").transpose(1, 0, 2).reshape(2, ncores, 32, 64)
    s5i_p = stk("s5i_p").transpose(1, 0, 2).reshape(2, ncores, 32, 64)
    s5r_s = cat("s5r_s", 1).reshape(2, ncores * NS, 32, 64)
    s5i_s = cat("s5i_s", 1).reshape(2, ncores * NS, 32, 64)
    rt_p = stk("rt_p").transpose(1, 0, 2, 3, 4)
    rt_s = cat("rt_s", 1)
    outs = (y_p, y_s, nk_p, nv_p, nk_s, nv_s, cv_p, cv_s, dl_p, dl_s, s5r_p, s5i_p, s5r_s, s5i_s, rt_p, rt_s)
    return tuple(np.ascontiguousarray(o, dtype=np.float32) for o in outs)
```
